# Optimizing a Trainium2 kernel written in Bass

```python
import jax, jax.numpy as jnp
from jax import lax
import numpy as np

D_MODEL = 1024
BATCH = 8
SEQ = 4096
DEPTH = 2

CHUNK = 64
N_MIXERS = 2
DEEPNORM_ALPHA = (2 * DEPTH) ** 0.25
DEEPNORM_BETA = (8 * DEPTH) ** -0.25
LN_EPS = 1e-5
POOL_WINDOWS = (2, 4, 8, 16)
POOL_GROUPS = len(POOL_WINDOWS)
POOL_GROUP_DIM = D_MODEL // POOL_GROUPS
SSM_EXPAND = 2
SSM_D_INNER = SSM_EXPAND * D_MODEL
SSM_HEAD_DIM = 64
SSM_N_HEADS = SSM_D_INNER // SSM_HEAD_DIM
SSM_N_GROUPS = 8
SSM_HEADS_PER_GROUP = SSM_N_HEADS // SSM_N_GROUPS
SSM_D_STATE = 128
SSM_CONV = 4
SSM_CONV_DIM = SSM_D_INNER + 2 * SSM_N_GROUPS * SSM_D_STATE
SSM_IN_DIM = SSM_D_INNER + SSM_CONV_DIM + SSM_N_HEADS
SSM_NORM_EPS = 1e-5
MEM_LEN = 256
XA_HEADS = 4
XA_HEAD_DIM = D_MODEL // XA_HEADS
N_EXPERTS = 32
TOP_K = 4
D_FF = D_MODEL
SWIGLU_LIMIT = 7.0
SWIGLU_ALPHA = 1.702
MOE_BLOCK = 128

N_POOL_LAYERS = (DEPTH + 1) // 2
N_SSM_LAYERS = DEPTH // 2

kernel_name = "hybrid_pool_ssd_memxattn_moe_deepnorm"

F32 = jnp.float32


def layer_norm(x, g, b):
    xf = x.astype(F32)
    mu = jnp.mean(xf, -1, keepdims=True)
    var = jnp.mean(jnp.square(xf - mu), -1, keepdims=True)
    y = (xf - mu) * lax.rsqrt(var + LN_EPS) * g.astype(F32) + b.astype(F32)
    return y.astype(x.dtype)


def trailing_mean_minus_self(u, w):
    S = u.shape[1]
    uf = u.astype(F32)
    cs = jnp.pad(jnp.cumsum(uf, axis=1), ((0, 0), (1, 0), (0, 0)))
    upper = cs[:, 1:]
    lower = jnp.pad(cs[:, :S + 1 - w], ((0, 0), (w - 1, 0), (0, 0)))
    count = jnp.minimum(jnp.arange(S) + 1, w).astype(F32)[None, :, None]
    return ((upper - lower) / count - uf).astype(u.dtype)


def pool_mixer(x, w_in, w_grp, scale, w_out):
    Bsz, S, _ = x.shape
    u = (x @ w_in).reshape(Bsz, S, POOL_GROUPS, POOL_GROUP_DIM)
    m = jnp.stack([trailing_mean_minus_self(u[:, :, g], w) for g, w in enumerate(POOL_WINDOWS)], axis=2)
    y = jnp.einsum('bsgc,gcd->bsgd', m, w_grp).reshape(Bsz, S, D_MODEL) * scale
    return y @ w_out


def causal_depthwise_conv(u, w, b):
    K, C = w.shape
    y = lax.conv_general_dilated(u, w[:, None, :], window_strides=(1,), padding=[(K - 1, 0)],
                                 dimension_numbers=('NWC', 'WIO', 'NWC'), feature_group_count=C)
    return y + b


def ssd_scan(xh, dt, A, Bm, Cm):
    Bsz, S, H, P = xh.shape
    G, R, N, Q = SSM_N_GROUPS, SSM_HEADS_PER_GROUP, SSM_D_STATE, CHUNK
    nc = S // Q
    xd = (xh.astype(F32) * dt[..., None]).reshape(Bsz, nc, Q, G, R, P)
    a_cs = jnp.cumsum((dt * A).reshape(Bsz, nc, Q, G, R), axis=2)
    Bc = Bm.astype(F32).reshape(Bsz, nc, Q, G, N)
    Cc = Cm.astype(F32).reshape(Bsz, nc, Q, G, N)
    seg = a_cs[:, :, :, None] - a_cs[:, :, None, :]
    causal = jnp.tril(jnp.ones((Q, Q), bool))[:, :, None, None]
    L = jnp.exp(jnp.where(causal, seg, -jnp.inf))
    cb = jnp.einsum('bcign,bcjgn->bcijg', Cc, Bc)
    y_diag = jnp.einsum('bcijgr,bcjgrp->bcigrp', cb[..., None] * L, xd)
    decay_to_end = jnp.exp(a_cs[:, :, -1:] - a_cs)
    states = jnp.einsum('bcqgn,bcqgrp->bcgrpn', Bc, xd * decay_to_end[..., None])
    chunk_decay = jnp.exp(a_cs[:, :, -1])

    def step(h, inp):
        s_c, d_c = inp
        return h * d_c[..., None, None] + s_c, h

    h0 = jnp.zeros((Bsz, G, R, P, N), F32)
    _, prev = lax.scan(step, h0, (jnp.moveaxis(states, 1, 0), jnp.moveaxis(chunk_decay, 1, 0)))
    prev = jnp.moveaxis(prev, 0, 1)
    y_off = jnp.einsum('bcqgn,bcgrpn->bcqgrp', Cc, prev) * jnp.exp(a_cs)[..., None]
    return (y_diag + y_off).reshape(Bsz, S, H, P)


def ssd_mixer(x, w_in, conv_w, conv_b, dt_bias, a_log, d_skip, norm_g, w_out):
    Bsz, S, _ = x.shape
    zxbcdt = x @ w_in
    z = zxbcdt[..., :SSM_D_INNER]
    xbc = zxbcdt[..., SSM_D_INNER:SSM_D_INNER + SSM_CONV_DIM]
    dt_raw = zxbcdt[..., SSM_D_INNER + SSM_CONV_DIM:]
    xbc = jax.nn.silu(causal_depthwise_conv(xbc, conv_w, conv_b))
    GN = SSM_N_GROUPS * SSM_D_STATE
    xs = xbc[..., :SSM_D_INNER]
    Bm = xbc[..., SSM_D_INNER:SSM_D_INNER + GN].reshape(Bsz, S, SSM_N_GROUPS, SSM_D_STATE)
    Cm = xbc[..., SSM_D_INNER + GN:].reshape(Bsz, S, SSM_N_GROUPS, SSM_D_STATE)
    dt = jax.nn.softplus(dt_raw.astype(F32) + dt_bias.astype(F32))
    A = -jnp.exp(a_log.astype(F32))
    xh = xs.reshape(Bsz, S, SSM_N_HEADS, SSM_HEAD_DIM)
    y = ssd_scan(xh, dt, A, Bm, Cm) + d_skip.astype(F32)[:, None] * xh.astype(F32)
    y = y.reshape(Bsz, S, SSM_D_INNER) * jax.nn.silu(z.astype(F32))
    yg = y.reshape(Bsz, S, SSM_N_GROUPS, SSM_D_INNER // SSM_N_GROUPS)
    yg = yg * lax.rsqrt(jnp.mean(jnp.square(yg), -1, keepdims=True) + SSM_NORM_EPS)
    y = (yg.reshape(Bsz, S, SSM_D_INNER) * norm_g.astype(F32)).astype(x.dtype)
    return y @ w_out


def memory_cross_attention(x, mem, wq, wk, wv, wo):
    Bsz, S, _ = x.shape
    M = mem.shape[1]
    q = (x @ wq).reshape(Bsz, S, XA_HEADS, XA_HEAD_DIM).astype(F32)
    k = (mem @ wk).reshape(Bsz, M, XA_HEADS, XA_HEAD_DIM).astype(F32)
    v = (mem @ wv).reshape(Bsz, M, XA_HEADS, XA_HEAD_DIM)
    s = jnp.einsum('bshd,bmhd->bhsm', q, k) * (XA_HEAD_DIM ** -0.5)
    p = jax.nn.softmax(s, axis=-1).astype(x.dtype)
    o = jnp.einsum('bhsm,bmhd->bshd', p, v).reshape(Bsz, S, D_MODEL)
    return o @ wo


def moe_ffn(x, w_router, b_router, w_gate_up, b_gate_up, w_down, b_down):
    Bsz, S, Dm = x.shape
    xt = x.reshape(-1, Dm)
    T = xt.shape[0]
    logits = (xt @ w_router + b_router).astype(F32)
    top_vals, top_idx = lax.top_k(logits, TOP_K)
    gates = jax.nn.softmax(top_vals, axis=-1)
    n_assign = T * TOP_K
    flat_e = top_idx.reshape(-1).astype(jnp.int32)
    flat_tok = jnp.repeat(jnp.arange(T, dtype=jnp.int32), TOP_K)
    flat_g = gates.reshape(-1)
    order = jnp.argsort(flat_e)
    sorted_e, sorted_tok, sorted_g = flat_e[order], flat_tok[order], flat_g[order]
    counts = jnp.zeros((N_EXPERTS,), jnp.int32).at[flat_e].add(1)
    padded = (counts + MOE_BLOCK - 1) // MOE_BLOCK * MOE_BLOCK
    starts = jnp.cumsum(counts) - counts
    padded_ends = jnp.cumsum(padded)
    padded_starts = padded_ends - padded
    dest = padded_starts[sorted_e] + (jnp.arange(n_assign, dtype=jnp.int32) - starts[sorted_e])
    n_blocks = -(-n_assign // MOE_BLOCK) + N_EXPERTS
    n_rows = n_blocks * MOE_BLOCK
    row_tok = jnp.zeros((n_rows,), jnp.int32).at[dest].set(sorted_tok)
    row_gate = jnp.zeros((n_rows,), F32).at[dest].set(sorted_g)
    block_start = jnp.arange(n_blocks, dtype=jnp.int32) * MOE_BLOCK
    block_e = jnp.minimum(jnp.searchsorted(padded_ends, block_start, side='right'), N_EXPERTS - 1)

    def expert_block(args):
        e, tok = args
        h = xt[tok] @ w_gate_up[e] + b_gate_up[e]
        gate = jnp.minimum(h[:, 0::2], SWIGLU_LIMIT)
        up = jnp.clip(h[:, 1::2], -SWIGLU_LIMIT, SWIGLU_LIMIT)
        act = (up + 1) * (gate * jax.nn.sigmoid(SWIGLU_ALPHA * gate))
        return act @ w_down[e] + b_down[e]

    y_rows = lax.map(expert_block, (block_e, row_tok.reshape(n_blocks, MOE_BLOCK)))
    y_rows = y_rows.reshape(n_rows, Dm) * row_gate[:, None].astype(x.dtype)
    y = jnp.zeros_like(xt).at[row_tok].add(y_rows)
    return y.reshape(Bsz, S, Dm)


def setup_inputs(seed: int = 0) -> dict:
    key = jax.random.key(seed)
    ks = jax.random.split(key, 32)
    nrm = lambda k, shape, s: jax.random.normal(k, shape, F32) * s
    beta = DEEPNORM_BETA
    NA, NB = N_POOL_LAYERS, N_SSM_LAYERS
    dt0 = jnp.exp(jax.random.uniform(ks[8], (NB, SSM_N_HEADS), F32, np.log(1e-3), np.log(1e-1)))
    return {
        "x": nrm(ks[0], (BATCH, SEQ, D_MODEL), 1.0),
        "mem": nrm(ks[1], (BATCH, MEM_LEN, D_MODEL), 1.0),
        "pool_w_in": nrm(ks[2], (NA, D_MODEL, D_MODEL), D_MODEL ** -0.5),
        "pool_w_grp": nrm(ks[3], (NA, POOL_GROUPS, POOL_GROUP_DIM, POOL_GROUP_DIM), POOL_GROUP_DIM ** -0.5),
        "pool_scale": 1.0 + nrm(ks[4], (NA, D_MODEL), 0.1),
        "pool_w_out": nrm(ks[5], (NA, D_MODEL, D_MODEL), beta * D_MODEL ** -0.5),
        "ssm_w_in": nrm(ks[6], (NB, D_MODEL, SSM_IN_DIM), D_MODEL ** -0.5),
        "ssm_conv_w": nrm(ks[7], (NB, SSM_CONV, SSM_CONV_DIM), SSM_CONV ** -0.5),
        "ssm_conv_b": nrm(ks[9], (NB, SSM_CONV_DIM), 0.02),
        "ssm_dt_bias": dt0 + jnp.log(-jnp.expm1(-dt0)),
        "ssm_a_log": jnp.log(jax.random.uniform(ks[10], (NB, SSM_N_HEADS), F32, 1.0, 16.0)),
        "ssm_d": 1.0 + nrm(ks[11], (NB, SSM_N_HEADS), 0.1),
        "ssm_norm_g": 1.0 + nrm(ks[12], (NB, SSM_D_INNER), 0.1),
        "ssm_w_out": nrm(ks[13], (NB, SSM_D_INNER, D_MODEL), beta * SSM_D_INNER ** -0.5),
        "xa_wq": nrm(ks[14], (DEPTH, D_MODEL, D_MODEL), D_MODEL ** -0.5),
        "xa_wk": nrm(ks[15], (DEPTH, D_MODEL, D_MODEL), D_MODEL ** -0.5),
        "xa_wv": nrm(ks[16], (DEPTH, D_MODEL, D_MODEL), beta * D_MODEL ** -0.5),
        "xa_wo": nrm(ks[17], (DEPTH, D_MODEL, D_MODEL), beta * D_MODEL ** -0.5),
        "moe_w_router": nrm(ks[18], (DEPTH, D_MODEL, N_EXPERTS), D_MODEL ** -0.5),
        "moe_b_router": nrm(ks[19], (DEPTH, N_EXPERTS), 0.01),
        "moe_w_gate_up": nrm(ks[20], (DEPTH, N_EXPERTS, D_MODEL, 2 * D_FF), beta * D_MODEL ** -0.5),
        "moe_b_gate_up": nrm(ks[21], (DEPTH, N_EXPERTS, 2 * D_FF), 0.02),
        "moe_w_down": nrm(ks[22], (DEPTH, N_EXPERTS, D_FF, D_MODEL), beta * D_FF ** -0.5),
        "moe_b_down": nrm(ks[23], (DEPTH, N_EXPERTS, D_MODEL), 0.02),
        "ln_mix_g": 1.0 + nrm(ks[24], (DEPTH, D_MODEL), 0.1),
        "ln_mix_b": nrm(ks[25], (DEPTH, D_MODEL), 0.02),
        "ln_xa_g": 1.0 + nrm(ks[26], (DEPTH, D_MODEL), 0.1),
        "ln_xa_b": nrm(ks[27], (DEPTH, D_MODEL), 0.02),
        "ln_ffn_g": 1.0 + nrm(ks[28], (DEPTH, D_MODEL), 0.1),
        "ln_ffn_b": nrm(ks[29], (DEPTH, D_MODEL), 0.02),
    }


def reference(x, mem, pool_w_in, pool_w_grp, pool_scale, pool_w_out,
              ssm_w_in, ssm_conv_w, ssm_conv_b, ssm_dt_bias, ssm_a_log, ssm_d, ssm_norm_g, ssm_w_out,
              xa_wq, xa_wk, xa_wv, xa_wo,
              moe_w_router, moe_b_router, moe_w_gate_up, moe_b_gate_up, moe_w_down, moe_b_down,
              ln_mix_g, ln_mix_b, ln_xa_g, ln_xa_b, ln_ffn_g, ln_ffn_b):
    h = x
    for i in range(DEPTH):
        j = i // N_MIXERS
        if i % N_MIXERS == 0:
            mix = pool_mixer(h, pool_w_in[j], pool_w_grp[j], pool_scale[j], pool_w_out[j])
        else:
            mix = ssd_mixer(h, ssm_w_in[j], ssm_conv_w[j], ssm_conv_b[j], ssm_dt_bias[j],
                            ssm_a_log[j], ssm_d[j], ssm_norm_g[j], ssm_w_out[j])
        h = layer_norm(DEEPNORM_ALPHA * h + mix, ln_mix_g[i], ln_mix_b[i])
        xa = memory_cross_attention(h, mem, xa_wq[i], xa_wk[i], xa_wv[i], xa_wo[i])
        h = layer_norm(DEEPNORM_ALPHA * h + xa, ln_xa_g[i], ln_xa_b[i])
        ff = moe_ffn(h, moe_w_router[i], moe_b_router[i], moe_w_gate_up[i], moe_b_gate_up[i],
                     moe_w_down[i], moe_b_down[i])
        h = layer_norm(DEEPNORM_ALPHA * h + ff, ln_ffn_g[i], ln_ffn_b[i])
    return h
```

```python
from contextlib import ExitStack
import numpy as np
import concourse.bass as bass
import concourse.mybir as mybir
from concourse.bass_utils import run_bass_kernel_spmd

F32 = mybir.dt.float32
F32R = mybir.dt.float32r
FAST = True


RT = F32R if FAST else F32


def R(ap):
    return ap


I32 = mybir.dt.int32
AF = mybir.ActivationFunctionType
ALU = mybir.AluOpType
AX = mybir.AxisListType

T = 4096
D = 1024
NTL = 32
ALPHA = float(4 ** 0.25)
EPS = 1e-5
CAP = 768
NE = 32
DEBUG = False

ENGS = ("pe", "act", "dve", "pool", "sp")


class Buf:
    __slots__ = ("name", "t", "w", "r", "dsem", "local")

    def __init__(self, name, t):
        self.name = name
        self.t = t
        self.w = {}
        self.r = {}
        self.dsem = None
        self.local = False

    def __getitem__(self, idx):
        return self.t[idx]


class Rot:
    def __init__(self, bufs):
        self.b = bufs
        self.i = 0

    def next(self):
        b = self.b[self.i % len(self.b)]
        self.i += 1
        return b


class Sched:
    def __init__(self, nc, stack):
        self.nc = nc
        self.stack = stack
        self.sems = {}
        self.cnt = {}
        for e in ENGS:
            self._mksem(e)
        self.seen = {e: {} for e in ENGS}
        self.prog = {e: [] for e in ENGS}
        self.nbuf = 0
        self.phase_stack = None
        self.free_dsems = []

    def _mksem(self, key):
        h = self.stack.enter_context(self.nc.semaphore(f"s_{key}"))
        self.sems[key] = h
        self.cnt[key] = 0

    def begin_phase(self):
        self.phase_stack = ExitStack()
        self.phase_dsems = []

    def end_phase(self):
        self.barrier()
        self.phase_stack.close()
        self.phase_stack = None
        self.free_dsems.extend(self.phase_dsems)
        self.phase_dsems = []

    def sbuf(self, name, shape, dtype=F32, glob=False):
        st = self.stack if (glob or self.phase_stack is None) else self.phase_stack
        self.uid = getattr(self, "uid", 0) + 1
        name = f"{name}_u{self.uid}"
        t = st.enter_context(self.nc.sbuf_tensor(name, list(shape), dtype))
        b = Buf(name, t)
        b.local = st is not self.stack
        return b

    def psum(self, name, shape, dtype=F32):
        st = self.stack if self.phase_stack is None else self.phase_stack
        self.uid = getattr(self, "uid", 0) + 1
        name = f"{name}_u{self.uid}"
        t = st.enter_context(self.nc.psum_tensor(name, list(shape), dtype))
        return Buf(name, t)

    def _deps(self, e, reads, writes):
        need = {}
        for b in reads:
            for k, v in b.w.items():
                if need.get(k, 0) < v:
                    need[k] = v
        for b in writes:
            for k, v in b.w.items():
                if need.get(k, 0) < v:
                    need[k] = v
            for k, v in b.r.items():
                if need.get(k, 0) < v:
                    need[k] = v
        waits = []
        seen = self.seen[e]
        for k, v in need.items():
            if e == "pe" and k == "pe":
                continue
            if seen.get(k, 0) < v:
                seen[k] = v
                waits.append((k, v))
        return waits

    def op(self, e, fn, reads=(), writes=()):
        waits = self._deps(e, reads, writes)
        self.cnt[e] += 1
        v = self.cnt[e]
        self.prog[e].append((waits, fn, (e, 1)))
        for b in writes:
            b.w[e] = v
        for b in reads:
            b.r[e] = v

    def dma(self, q, fn, reads=(), writes=(), owner=None, group=False):
        if owner is None:
            owner = writes[0]
        if owner.dsem is None:
            if self.free_dsems:
                owner.dsem = self.free_dsems.pop()
            else:
                self.nbuf += 1
                owner.dsem = f"d{self.nbuf}"
                self._mksem(owner.dsem)
            if owner.local:
                self.phase_dsems.append(owner.dsem)
        k = owner.dsem
        waits = self._deps(q, reads, writes)
        if (not group) and self.cnt[k] > 0 and self.seen[q].get(k, 0) < self.cnt[k]:
            self.seen[q][k] = self.cnt[k]
            waits.append((k, self.cnt[k]))
        self.cnt[k] += 16
        v = self.cnt[k]
        self.prog[q].append((waits, fn, (k, 16)))
        for b in writes:
            b.w[k] = v
        for b in reads:
            b.r[k] = v

    def barrier(self):
        for e in ENGS:
            waits = []
            for k, v in self.cnt.items():
                if v > 0 and self.seen[e].get(k, 0) < v and not (k == e):
                    self.seen[e][k] = v
                    waits.append((k, v))
            if waits:
                self.prog[e].append((waits, None, None))

    def emit(self):
        nc = self.nc
        sems = self.sems
        prog = self.prog
        with nc.Block() as block:
            def mk(ekey):
                def body(engine):
                    for waits, fn, inc in prog[ekey]:
                        for k, v in waits:
                            engine.wait_ge(sems[k], v)
                        if fn is not None:
                            fn(engine).then_inc(sems[inc[0]], inc[1])
                return body
            block.tensor(mk("pe"))
            block.scalar(mk("act"))
            block.vector(mk("dve"))
            block.gpsimd(mk("pool"))
            block.sync(mk("sp"))


def MM(S, ob, o, lb, l, rb, r, start=True, stop=True):
    S.op("pe", lambda e: e.matmul(o, l, r, start=start, stop=stop), reads=[lb, rb], writes=[ob])


def TRP(S, ob, o, ib, i, idb, idap):
    S.op("pe", lambda e: e.transpose(o, i, idap), reads=[ib, idb], writes=[ob])


def TT(S, eng, ob, o, ab, a, bb, b, op):
    S.op(eng, lambda e: e.tensor_tensor(out=o, in0=a, in1=b, op=op), reads=[ab, bb], writes=[ob])


def TS(S, eng, ob, o, ab, a, s1, s2, op0, op1=None, xr=()):
    if op1 is None:
        S.op(eng, lambda e: e.tensor_scalar(out=o, in0=a, scalar1=s1, scalar2=None, op0=op0),
             reads=[ab, *xr], writes=[ob])
    else:
        S.op(eng, lambda e: e.tensor_scalar(out=o, in0=a, scalar1=s1, scalar2=s2, op0=op0, op1=op1),
             reads=[ab, *xr], writes=[ob])


def STT(S, ob, o, ab, a, sc, bb, b, op0, op1, xr=(), accum=None, xw=()):
    if accum is None:
        S.op("dve", lambda e: e.scalar_tensor_tensor(out=o, in0=a, scalar=sc, in1=b, op0=op0, op1=op1),
             reads=[ab, bb, *xr], writes=[ob])
    else:
        S.op("dve", lambda e: e.scalar_tensor_tensor(out=o, in0=a, scalar=sc, in1=b, op0=op0, op1=op1,
                                                     accum_out=accum),
             reads=[ab, bb, *xr], writes=[ob, *xw])


def ACT(S, ob, o, ab, a, func, bias=None, scale=None, xr=(), accum=None, xw=()):
    kw = {}
    if bias is not None:
        kw["bias"] = bias
    if scale is not None:
        kw["scale"] = scale
    if accum is not None:
        kw["accum_out"] = accum
    S.op("act", lambda e: e.activation(out=o, in_=a, func=func, **kw), reads=[ab, *xr], writes=[ob, *xw])


def CP(S, eng, ob, o, ab, a):
    if eng == "act":
        S.op("act", lambda e: e.activation(out=o, in_=a, func=AF.Identity), reads=[ab], writes=[ob])
    else:
        S.op(eng, lambda e: e.tensor_copy(out=o, in_=a), reads=[ab], writes=[ob])


def LOAD(S, q, db, d, sb, s, group=False):
    S.dma(q, lambda e: e.dma_start(out=d, in_=s), reads=[sb], writes=[db], owner=db, group=group)


def STORE(S, q, db, d, sb, s):
    S.dma(q, lambda e: e.dma_start(out=d, in_=s), reads=[sb], writes=[db], owner=sb)


CST = {}


def _cst_layout():
    off = 0
    for name, n in [("ident", 128), ("onesD", 128), ("ones", 128), ("ones256", 128), ("ustrict", 128),
                    ("triinc", 128), ("blockones", 128), ("strictlt", 128), ("mask01", 128),
                    ("csel0", 128), ("csel1", 128), ("rcnt", 64), ("iotaec", 32)]:
        CST[name] = (off, n)
        off += n
    return off


NCST = _cst_layout()


def make_consts():
    c = np.zeros((128, NCST), np.float32)
    i = np.arange(128)
    ch = i // 64

    def put(name, arr):
        o, n = CST[name]
        c[:, o:o + n] = arr
    put("ident", np.eye(128))
    put("onesD", np.full((128, 128), 1.0 / D))
    put("ones", np.ones((128, 128)))
    put("ones256", np.full((128, 128), 1.0 / 256))
    put("ustrict", (i[:, None] < i[None, :]))
    same = ch[:, None] == ch[None, :]
    put("triinc", (i[:, None] <= i[None, :]) & same)
    put("blockones", same)
    put("strictlt", (i[None, :] < i[:, None]) & same)
    put("mask01", (i[:, None] <= i[None, :]) & same)
    put("csel0", np.repeat((i < 64)[:, None], 128, 1))
    put("csel1", np.repeat((i >= 64)[:, None], 128, 1))
    rc = np.zeros((128, 4, 16))
    for g, w in enumerate((2, 4, 8, 16)):
        rc[:, g, :] = 1.0 / np.minimum(np.arange(16) + 1, w)
    put("rcnt", rc.reshape(128, 64))
    put("iotaec", np.repeat((np.arange(32) * CAP)[None, :], 128, 0))
    return c


PV = {}


def _pv_layout():
    off = 0
    items = [("pool_scale", 8)]
    for l in range(2):
        items += [(f"ln_mix_g{l}", 8), (f"ln_mix_b{l}", 8), (f"ln_xa_g{l}", 8), (f"ln_xa_b{l}", 8)]
    items += [("conv_w", 128), ("conv_b", 32), ("norm_g", 16), ("dskip", 16)]
    for l in range(2):
        items += [(f"bg{l}", 256), (f"bu{l}", 256)]
    for name, n in items:
        PV[name] = (off, n)
        off += n
    return off


NPV = _pv_layout()


def fm(v):
    return np.ascontiguousarray(v.reshape(-1, 128).T)


def make_pv(inp):
    p = np.zeros((128, NPV), np.float32)

    def put(name, arr):
        o, n = PV[name]
        p[:, o:o + n] = arr.reshape(128, n)
    put("pool_scale", fm(inp["pool_scale"][0]))
    for l in range(2):
        put(f"ln_mix_g{l}", fm(inp["ln_mix_g"][l]))
        put(f"ln_mix_b{l}", fm(inp["ln_mix_b"][l]))
        put(f"ln_xa_g{l}", fm(inp["ln_xa_g"][l]))
        put(f"ln_xa_b{l}", fm(inp["ln_xa_b"][l]))
    cw = inp["ssm_conv_w"][0]
    put("conv_w", np.stack([fm(cw[k]) for k in range(4)], axis=2))
    put("conv_b", fm(inp["ssm_conv_b"][0]))
    put("norm_g", fm(inp["ssm_norm_g"][0]))
    put("dskip", fm(np.repeat(inp["ssm_d"][0], 64)))
    for l in range(2):
        b = inp["moe_b_gate_up"][l]
        put(f"bg{l}", np.stack([fm(b[e, 0::2]) for e in range(NE)], axis=1))
        put(f"bu{l}", np.stack([fm(b[e, 1::2]) for e in range(NE)], axis=1))
    return p


class K:
    pass


def build(phases=None, dbg=False):
    nc = bass.Bass("TRN2", target_bir_lowering=False)
    k = K()
    k.nc = nc
    k.phases = phases

    def din(name, shape, dtype=F32):
        return Buf(name, nc.dram_tensor(name, list(shape), dtype, kind="ExternalInput"))

    def dscr(name, shape, dtype=F32):
        return Buf(name, nc.dram_tensor(name, list(shape), dtype,
                                        kind="ExternalOutput" if dbg else "Internal"))

    k.xT = din("xT", [D, T])
    k.memT = din("memT", [D, 256])
    k.cst = din("cst", [128, NCST])
    k.pv = din("pv", [128, NPV])
    k.pool_w_in = din("pool_w_in", [D, D])
    k.pool_w_grp = din("pool_w_grp", [4, 256, 256])
    k.pool_w_out = din("pool_w_out", [D, D])
    k.ssm_w_in = din("ssm_w_in", [D, 6176])
    k.ssm_w_out = din("ssm_w_out", [2048, D])
    k.rowp = din("rowp", [1, 64])
    k.xa_wq = din("xa_wq", [2, D, D])
    k.xa_wk = din("xa_wk", [2, D, D])
    k.xa_wv = din("xa_wv", [2, D, D])
    k.xa_wo = din("xa_wo", [2, D, D])
    k.w_router = din("moe_w_router", [2, D, NE])
    k.b_router = din("moe_b_router", [2, NE])
    k.w_gu = din("moe_w_gate_up", [2, NE, D, 2 * D])
    k.w_dn = din("moe_w_down", [2, NE, D, D])
    k.b_dn = din("moe_b_down", [2, NE, D])
    k.ln_ffn_g = din("ln_ffn_g", [2, D])
    k.ln_ffn_b = din("ln_ffn_b", [2, D])
    k.out = Buf("out", nc.dram_tensor("out", [T, D], F32, kind="ExternalOutput"))
    k.HA = dscr("HA", [D, T])
    k.HB = dscr("HB", [D, T])
    k.HC = dscr("HC", [D, T])
    k.HTOK = dscr("HTOK", [T, D])
    k.ZSD = dscr("ZSD", [NTL, 128, 16, 128])
    k.XBCD = dscr("XBCD", [NTL, 128, 32, 128])
    k.XS = dscr("XS", [NE * CAP, D])
    k.YS = dscr("YS", [NE * CAP, D])

    with ExitStack() as st:
        S = Sched(nc, st)
        k.S = S
        k.cs = S.sbuf("cst_s", [128, NCST], glob=True)
        k.pvs = S.sbuf("pv_s", [128, NPV], glob=True)
        k.dest = S.sbuf("dest_i", [128, NTL, 4], I32, glob=True)
        k.gates = S.sbuf("gates", [128, NTL, 4], glob=True)
        LOAD(S, "sp", k.cs, k.cs[:, :], k.cst, k.cst.t.ap())
        k.csr = S.sbuf("csr_s", [128, 128], RT, glob=True)
        LOAD(S, "pool" if FAST else "sp", k.csr, k.csr[:, :], k.cst,
             k.cst.t.ap()[:, CST["onesD"][0]:CST["onesD"][0] + 128])
        LOAD(S, "sp", k.pvs, k.pvs[:, :], k.pv, k.pv.t.ap())

        def want(p):
            return phases is None or p in phases

        if want("pool0"):
            phase_pool(k, k.xT, k.HA)
        if want("xa0"):
            phase_xattn(k, 0, k.HA, k.HB, k.HTOK)
        if want("moe0"):
            phase_moe(k, 0, k.HB, k.HTOK, k.HC, None)
        if want("ssd1"):
            phase_ssd_a(k, k.HC)
            phase_ssd_b(k, k.HC, k.HA)
        if want("xa1"):
            phase_xattn(k, 1, k.HA, k.HB, k.HTOK)
        if want("moe1"):
            phase_moe(k, 1, k.HB, k.HTOK, None, k.out)
        S.barrier()
        S.emit()
    return nc


def C(k, name):
    o, n = CST[name]
    return k.cs[:, o:o + n]


def P(k, name, j=None):
    o, n = PV[name]
    if j is None:
        return k.pvs[:, o:o + n]
    return k.pvs[:, o + j:o + j + 1]


def fmview(buf):
    return buf.t.ap().rearrange("(c p) t -> p c t", p=128)


def wview(ap2d):
    return ap2d.rearrange("(c p) f -> p c f", p=128)


class WStream:
    def __init__(self, S, n=4):
        self.S = S
        self.rot = Rot([S.sbuf(f"wt{i}", [128, 8, 512], RT) for i in range(n)])

    def get(self, wbuf, view, k0, c0, ncols=512, nk=8):
        b = self.rot.next()
        LOAD(self.S, "pool" if FAST else "sp", b, R(b[:, 0:nk, 0:ncols]), wbuf, view[:, k0:k0 + nk, c0:c0 + ncols])
        return b


def proj_fm(k, ws, psr, wbuf, view, xb, x_of_kc, N, n_oc, evac, kchunks=8, col0=0):
    S = k.S
    for wc in range((n_oc + 3) // 4):
        wts = [ws.get(wbuf, view, kh * 8, col0 + wc * 512) for kh in range(kchunks // 8)]
        for o4 in range(min(4, n_oc - wc * 4)):
            oc = wc * 4 + o4
            ps = psr.next()
            for kc in range(kchunks):
                MM(S, ps, ps[:, 0:N], wts[kc // 8], R(wts[kc // 8][:, kc % 8, o4 * 128:(o4 + 1) * 128]),
                   xb, R(x_of_kc(kc)), start=(kc == 0), stop=(kc == kchunks - 1))
            evac(oc, ps, ps[:, 0:N])


def proj_fm16(k, ws, psr, wbuf, view, xb, x_of_kc, N, n_oc, evac):
    S = k.S
    for wc in range(n_oc // 4):
        pss = [psr.next() for _ in range(4)]
        for kh in range(2):
            w = ws.get(wbuf, view, kh * 8, wc * 512)
            for o4 in range(4):
                for kc in range(8):
                    MM(S, pss[o4], pss[o4][:, 0:N], w, w[:, kc, o4 * 128:(o4 + 1) * 128], xb, x_of_kc(kh * 8 + kc),
                       start=(kh == 0 and kc == 0), stop=(kh == 1 and kc == 7))
        for o4 in range(4):
            evac(wc * 4 + o4, pss[o4], pss[o4][:, 0:N])


def ln_fm(k, psr, Z, N, gname, bname, OUT, scr):
    S = k.S
    onesD = C(k, "onesD")
    pm = psr.next()
    for c in range(8):
        MM(S, pm, pm[:, 0:N], k.cs, onesD, Z, Z[:, c, 0:N], start=(c == 0), stop=(c == 7))
    mean = k.ln_mean
    CP(S, "act", mean, mean[:, 0:N], pm, pm[:, 0:N])
    TT(S, "dve", Z, Z[:, :, 0:N], Z, Z[:, :, 0:N], mean, mean[:, 0:N].unsqueeze(1).broadcast_to([128, 8, N]),
       ALU.subtract)
    ACT(S, scr, scr[:, :, 0:N], Z, Z[:, :, 0:N], AF.Square)
    pv = psr.next()
    for c in range(8):
        MM(S, pv, pv[:, 0:N], k.csr, k.csr[:, 0:128], scr, scr[:, c, 0:N], start=(c == 0), stop=(c == 7))
    rstd = k.ln_rstd
    ACT(S, rstd, rstd[:, 0:N], pv, pv[:, 0:N], AF.Sqrt, bias=k.eps_t[:, 0:1], xr=[k.eps_t])
    S.op("dve", lambda e: e.reciprocal(out=rstd[:, 0:N], in_=rstd[:, 0:N]), reads=[rstd], writes=[rstd])
    TT(S, "dve", Z, Z[:, :, 0:N], Z, Z[:, :, 0:N], rstd, rstd[:, 0:N].unsqueeze(1).broadcast_to([128, 8, N]),
       ALU.mult)
    for c in range(8):
        ACT(S, OUT, OUT[:, c, 0:N], Z, Z[:, c, 0:N], AF.Identity, bias=P(k, bname, c), scale=P(k, gname, c),
            xr=[k.pvs])


def common_small(k):
    S = k.S
    k.ln_mean = S.sbuf("ln_mean", [128, 512])
    k.ln_rstd = S.sbuf("ln_rstd", [128, 512])
    k.eps_t = S.sbuf("eps_t", [128, 1])
    S.op("dve", lambda e: e.memset(k.eps_t[:, :], EPS), writes=[k.eps_t])


def phase_pool(k, HIN, HOUT):
    S = k.S
    S.begin_phase()
    common_small(k)
    N = 512
    ws = WStream(S, 2)
    psr = Rot([S.psum(f"ps{i}", [128, 512]) for i in range(8)])
    xr = Rot([S.sbuf(f"xt{i}", [128, 8, N]) for i in range(2)])
    U = Rot([S.sbuf(f"U{i}", [128, 8, 16 + N]) for i in range(2)])
    TA = S.sbuf("TA", [128, 2, 16 + N])
    TB = S.sbuf("TB", [128, 2, 16 + N])
    M = S.sbuf("M", [128, 8, N], RT)
    Y = S.sbuf("Y", [128, 8, N], RT)
    Z = S.sbuf("Z", [128, 8, N])
    xtr = S.sbuf("xtr", [128, 8, N], RT)
    wg = S.sbuf("wg", [128, 4, 2, 256], RT)
    LOAD(S, "pool" if FAST else "sp", wg, R(wg[:, :, :, :]), k.pool_w_grp,
         k.pool_w_grp.t.ap().rearrange("g (kc p) d -> p g kc d", p=128))
    vin = wview(k.pool_w_in.t.ap())
    vout = wview(k.pool_w_out.t.ap())
    hin = fmview(HIN)
    hout = fmview(HOUT)
    o_rc, _ = CST["rcnt"]
    prevU = None
    for tb in range(T // N):
        xt = xr.next()
        LOAD(S, "sp", xt, xt[:, :, :], HIN, hin[:, :, tb * N:(tb + 1) * N])
        LOAD(S, "pool" if FAST else "sp", xtr, R(xtr[:, :, :]), HIN, hin[:, :, tb * N:(tb + 1) * N])
        u = U.next()
        if tb == 0:
            S.op("pool", lambda e, u=u: e.memset(u[:, :, 0:16], 0.0), writes=[u])
        else:
            CP(S, "dve", u, u[:, :, 0:16], prevU, prevU[:, :, N:N + 16])

        def ev_u(oc, ps, pa, u=u):
            CP(S, "act", u, u[:, oc, 16:16 + N], ps, pa)
        proj_fm(k, ws, psr, k.pool_w_in, vin, xtr, lambda kc: xtr[:, kc, :], N, 8, ev_u)
        W_ = 16 + N
        for g in range(4):
            w = 2 << g
            cur_b, cur = u, (lambda a, b, g=g: u[:, 2 * g:2 * g + 2, a:b])
            sh = 1
            tgt = [TA, TB]
            for step in range(g + 1):
                t_ = tgt[step % 2]
                lo = 2 * sh - 1
                TT(S, "dve", t_, t_[:, :, lo:W_], cur_b, cur(lo, W_), cur_b, cur(lo - sh, W_ - sh), ALU.add)
                cur_b, cur = t_, (lambda a, b, t_=t_: t_[:, :, a:b])
                sh *= 2
            if tb == 0:
                rc = k.cs[:, o_rc + g * 16:o_rc + (g + 1) * 16]
                TS(S, "dve", cur_b, cur(32, W_), cur_b, cur(32, W_), 1.0 / w, None, ALU.mult)
                TT(S, "dve", cur_b, cur(16, 32), cur_b, cur(16, 32), k.cs,
                   rc.unsqueeze(1).broadcast_to([128, 2, 16]), ALU.mult)
                TT(S, "dve", M, R(M[:, 2 * g:2 * g + 2, :]), cur_b, cur(16, W_), u, u[:, 2 * g:2 * g + 2, 16:W_],
                   ALU.subtract)
            else:
                STT(S, M, R(M[:, 2 * g:2 * g + 2, :]), cur_b, cur(16, W_), 1.0 / w, u, u[:, 2 * g:2 * g + 2, 16:W_],
                    ALU.mult, ALU.subtract)
        for g in range(4):
            for j in range(2):
                ps = psr.next()
                for kc in range(2):
                    MM(S, ps, ps[:, 0:N], wg, R(wg[:, g, kc, j * 128:(j + 1) * 128]), M, R(M[:, 2 * g + kc, :]),
                       start=(kc == 0), stop=(kc == 1))
                oc = 2 * g + j
                ACT(S, Y, R(Y[:, oc, :]), ps, ps[:, 0:N], AF.Identity, scale=P(k, "pool_scale", oc), xr=[k.pvs])

        def ev_z(oc, ps, pa, xt=xt):
            STT(S, Z, Z[:, oc, :], xt, xt[:, oc, :], ALPHA, ps, pa, ALU.mult, ALU.add)
        proj_fm(k, ws, psr, k.pool_w_out, vout, Y, lambda kc: Y[:, kc, :], N, 8, ev_z)
        o = Z
        ln_fm(k, psr, Z, N, "ln_mix_g0", "ln_mix_b0", o, M)
        STORE(S, "act", HOUT, hout[:, :, tb * N:(tb + 1) * N], o, o[:, :, :])
        prevU = u
    S.end_phase()


def phase_xattn(k, l, HIN, HOUT, HTOK):
    S = k.S
    S.begin_phase()
    common_small(k)
    N = 512
    ws = WStream(S, 2)
    psr = Rot([S.psum(f"ps{i}", [128, 512]) for i in range(8)])
    xr = Rot([S.sbuf(f"xt{i}", [128, 8, N]) for i in range(2)])
    memT = S.sbuf("memT", [128, 8, 256], RT)
    kT = S.sbuf("kT", [128, 8, 256], RT)
    v = S.sbuf("v", [128, 2, D], RT)
    qT = S.sbuf("qT", [128, 8, N], RT)
    Pm = Rot([S.sbuf(f"Pm{i}", [128, 4, 256]) for i in range(2)])
    pT = S.sbuf("pT", [128, 2, 4, N], RT)
    OT = S.sbuf("OT", [128, 8, N], RT)
    Z = S.sbuf("Z", [128, 8, N])
    xtr = S.sbuf("xtr", [128, 8, N], RT)
    htk = Rot([S.sbuf(f"htk{i}", [128, D]) for i in range(2)])
    sm = Rot([S.sbuf(f"sm{i}", [128, 16]) for i in range(2)])
    ident = C(k, "ident")
    LOAD(S, "pool" if FAST else "sp", memT, R(memT[:, :, :]), k.memT, fmview(k.memT))
    vq = wview(k.xa_wq.t.ap()[l])
    vk = wview(k.xa_wk.t.ap()[l])
    vv = wview(k.xa_wv.t.ap()[l])
    vo = wview(k.xa_wo.t.ap()[l])

    def ev_k(oc, ps, pa):
        CP(S, "act", kT, R(kT[:, oc, :]), ps, pa)
    proj_fm(k, ws, psr, k.xa_wk, vk, memT, lambda kc: memT[:, kc, :], 256, 8, ev_k)
    for wc in range(2):
        w = ws.get(k.xa_wv, vv, 0, wc * 512)
        for mc in range(2):
            ps = psr.next()
            for kc in range(8):
                MM(S, ps, ps[:, :], memT, R(memT[:, kc, mc * 128:(mc + 1) * 128]), w, R(w[:, kc, :]),
                   start=(kc == 0), stop=(kc == 7))
            CP(S, "act", v, R(v[:, mc, wc * 512:(wc + 1) * 512]), ps, ps[:, :])
    hin = fmview(HIN)
    hout = fmview(HOUT)
    gname, bname = f"ln_xa_g{l}", f"ln_xa_b{l}"
    for tb in range(T // N):
        xt = xr.next()
        LOAD(S, "sp", xt, xt[:, :, :], HIN, hin[:, :, tb * N:(tb + 1) * N])
        LOAD(S, "pool" if FAST else "sp", xtr, R(xtr[:, :, :]), HIN, hin[:, :, tb * N:(tb + 1) * N])

        def ev_q(oc, ps, pa):
            S.op("act", lambda e: e.activation(out=R(qT[:, oc, :]), in_=pa, func=AF.Identity, scale=0.0625),
                 reads=[ps], writes=[qT])
        proj_fm(k, ws, psr, k.xa_wq, vq, xtr, lambda kc: xtr[:, kc, :], N, 8, ev_q)
        for tt in range(4):
            tsl = slice(tt * 128, (tt + 1) * 128)
            pm = Pm.next()
            s_ = sm.next()
            banks = [psr.next(), psr.next()]
            for h in range(4):
                ps = banks[h // 2]
                for dc in range(2):
                    MM(S, ps, ps[:, (h % 2) * 256:(h % 2 + 1) * 256], qT, R(qT[:, 2 * h + dc, tsl]),
                       kT, R(kT[:, 2 * h + dc, :]), start=(dc == 0), stop=(dc == 1))
            for b2 in range(2):
                ps = banks[b2]
                S.op("dve", lambda e, ps=ps, s_=s_, b2=b2: e.tensor_reduce(
                    out=s_[:, 2 * b2:2 * b2 + 2], in_=ps[:, :].rearrange("p (a b) -> p a b", a=2),
                    axis=AX.X, op=ALU.max), reads=[ps], writes=[s_])
            TS(S, "dve", s_, s_[:, 4:8], s_, s_[:, 0:4], -1.0, None, ALU.mult)
            for h in range(4):
                ps = banks[h // 2]
                ACT(S, pm, pm[:, h, :], ps, ps[:, (h % 2) * 256:(h % 2 + 1) * 256], AF.Exp,
                    bias=s_[:, 4 + h:5 + h], xr=[s_], accum=s_[:, 8 + h:9 + h], xw=[s_])
            S.op("dve", lambda e, s_=s_: e.reciprocal(out=s_[:, 12:16], in_=s_[:, 8:12]), reads=[s_], writes=[s_])
            TT(S, "dve", pm, pm[:, :, :], pm, pm[:, :, :], s_, s_[:, 12:16].unsqueeze(2).broadcast_to([128, 4, 256]),
               ALU.mult)
            for mc in range(2):
                ps = psr.next()
                for h in range(4):
                    TRP(S, ps, ps[:, h * 128:(h + 1) * 128], pm, pm[:, h, mc * 128:(mc + 1) * 128], k.cs, ident)
                CP(S, "act" if mc == 0 else "dve", pT, R(pT[:, mc, :, tsl]), ps,
                   ps[:, :].rearrange("p (h t) -> p h t", h=4))
        for h in range(4):
            for dc in range(2):
                ps = psr.next()
                for mc in range(2):
                    MM(S, ps, ps[:, :], v, R(v[:, mc, h * 256 + dc * 128:h * 256 + (dc + 1) * 128]),
                       pT, R(pT[:, mc, h, :]), start=(mc == 0), stop=(mc == 1))
                CP(S, "act", OT, R(OT[:, 2 * h + dc, :]), ps, ps[:, :])

        def ev_z(oc, ps, pa, xt=xt):
            STT(S, Z, Z[:, oc, :], xt, xt[:, oc, :], ALPHA, ps, pa, ALU.mult, ALU.add)
        proj_fm(k, ws, psr, k.xa_wo, vo, OT, lambda kc: OT[:, kc, :], N, 8, ev_z)
        o = Z
        ln_fm(k, psr, Z, N, gname, bname, o, OT)
        STORE(S, "act", HOUT, hout[:, :, tb * N:(tb + 1) * N], o, o[:, :, :])
        for tt in range(4):
            ht = htk.next()
            for half in range(2):
                ps = psr.next()
                for c4 in range(4):
                    c = half * 4 + c4
                    TRP(S, ps, ps[:, c4 * 128:(c4 + 1) * 128], o, o[:, c, tt * 128:(tt + 1) * 128], k.cs, ident)
                CP(S, "act" if half == 0 else "dve", ht, ht[:, half * 512:(half + 1) * 512], ps, ps[:, :])
            r0 = tb * N + tt * 128
            STORE(S, "act", HTOK, HTOK.t.ap()[r0:r0 + 128, :], ht, ht[:, :])
    S.end_phase()


def phase_moe(k, l, HFM, HTOK, HOUT_FM, OUT_TOK):
    S = k.S
    ident = C(k, "ident")
    S.begin_phase()
    psr = Rot([S.psum(f"ps{i}", [128, 512]) for i in range(8)])
    xr = Rot([S.sbuf(f"xt{i}", [128, 8, 512]) for i in range(2)])
    hr = Rot([S.sbuf(f"hrow{i}", [128, D]) for i in range(3)])
    wr = S.sbuf("wr", [128, 8, NE])
    brt = S.sbuf("brt", [128, NE])
    cum = S.sbuf("cum", [128, NE])
    lgr = Rot([S.sbuf(f"lg{i}", [128, NE]) for i in range(2)])
    t8r = Rot([S.sbuf(f"t8{i}", [128, 16]) for i in range(2)])
    mkr = Rot([S.sbuf(f"mk{i}", [128, NE]) for i in range(2)])
    pcr = Rot([S.sbuf(f"pc{i}", [128, NE]) for i in range(2)])
    jnk = S.sbuf("jnk", [128, NE])
    dfr = Rot([S.sbuf(f"df{i}", [128, 4]) for i in range(2)])
    LOAD(S, "sp", wr, wr[:, :, :], k.w_router, wview(k.w_router.t.ap()[l]))
    LOAD(S, "sp", brt, brt[:, :], k.b_router, k.b_router.t.ap()[l:l + 1, :].broadcast_to([128, NE]))
    S.op("dve", lambda e: e.memset(cum[:, :], 0.0), writes=[cum])
    hfm = fmview(HFM)
    ustrict = C(k, "ustrict")
    ones = C(k, "ones")
    iotaec = C(k, "iotaec")
    xt = None
    for ti in range(NTL):
        if ti % 4 == 0:
            xt = xr.next()
            LOAD(S, "sp", xt, xt[:, :, :], HFM, hfm[:, :, ti * 128:ti * 128 + 512])
        tsl = slice((ti % 4) * 128, (ti % 4 + 1) * 128)
        hrow = hr.next()
        LOAD(S, "sp", hrow, hrow[:, :], HTOK, HTOK.t.ap()[ti * 128:(ti + 1) * 128, :])
        ps = psr.next()
        for kc in range(8):
            MM(S, ps, ps[:, 0:NE], xt, xt[:, kc, tsl], wr, wr[:, kc, :], start=(kc == 0), stop=(kc == 7))
        lg = lgr.next()
        TT(S, "dve", lg, lg[:, :], ps, ps[:, 0:NE], brt, brt[:, :], ALU.add)
        t8 = t8r.next()
        S.op("dve", lambda e, t8=t8, lg=lg: e.max(out=t8[:, 0:8], in_=lg[:, :]), reads=[lg], writes=[t8])
        mk = mkr.next()
        TS(S, "dve", mk, mk[:, :], lg, lg[:, :], t8[:, 3:4], None, ALU.is_ge, xr=[t8])
        pp = psr.next()
        MM(S, pp, pp[:, 0:NE], k.cs, ustrict, mk, mk[:, :])
        MM(S, pp, pp[:, NE:2 * NE], k.cs, ones, mk, mk[:, :])
        pc = pcr.next()
        TT(S, "dve", pc, pc[:, :], pp, pp[:, 0:NE], cum, cum[:, :], ALU.add)
        TT(S, "dve", pc, pc[:, :], pc, pc[:, :], k.cs, iotaec, ALU.add)
        TT(S, "dve", cum, cum[:, :], pp, pp[:, NE:2 * NE], cum, cum[:, :], ALU.add)
        df = dfr.next()
        for kk in range(4):
            STT(S, jnk, jnk[:, :], lg, lg[:, :], t8[:, kk:kk + 1], pc, pc[:, :], ALU.is_equal, ALU.mult,
                xr=[t8], accum=df[:, kk:kk + 1], xw=[df])
        CP(S, "dve", k.dest, k.dest[:, ti, :], df, df[:, :])
        TS(S, "dve", t8, t8[:, 8:9], t8, t8[:, 0:1], -1.0, None, ALU.mult)
        ACT(S, t8, t8[:, 10:14], t8, t8[:, 0:4], AF.Exp, bias=t8[:, 8:9], accum=t8[:, 9:10])
        S.op("dve", lambda e, t8=t8: e.reciprocal(out=t8[:, 14:15], in_=t8[:, 9:10]), reads=[t8], writes=[t8])
        TS(S, "dve", k.gates, k.gates[:, ti, :], t8, t8[:, 10:14], t8[:, 14:15], None, ALU.mult)
        for kk in range(4):
            S.dma("pool", lambda e, hrow=hrow, ti=ti, kk=kk: e.indirect_dma_start(
                out=k.XS.t.ap(), out_offset=bass.IndirectOffsetOnAxis(ap=k.dest[:, ti, kk:kk + 1], axis=0),
                in_=hrow[:, :], in_offset=None), reads=[hrow, k.dest], writes=[k.XS], owner=hrow, group=(kk > 0))
    S.end_phase()
    if k.phases is not None and f"moe{l}_route_only" in k.phases:
        return
    S.begin_phase()
    ws = WStream(S, 4)
    psr = Rot([S.psum(f"ps{i}", [128, 512]) for i in range(8)])
    xrow = Rot([S.sbuf(f"xrow{i}", [128, D]) for i in range(2)])
    yrow = Rot([S.sbuf(f"yrow{i}", [128, D]) for i in range(2)])
    XT = S.sbuf("XsT", [128, 8, CAP], RT)
    AT = S.sbuf("actT", [128, 8, CAP], RT)
    gp = Rot([S.sbuf(f"gp{i}", [128, CAP // 2]) for i in range(2)])
    sg = Rot([S.sbuf(f"sg{i}", [128, CAP // 2]) for i in range(2)])
    up = Rot([S.sbuf(f"up{i}", [128, CAP // 2]) for i in range(2)])
    bdr = Rot([S.sbuf(f"bd{i}", [1, D]) for i in range(2)])
    o_bg, _ = PV[f"bg{l}"]
    o_bu, _ = PV[f"bu{l}"]
    HN = CAP // 2
    NRB = CAP // 128
    ones = C(k, "ones")
    for e_ in range(NE):
        bd = bdr.next()
        LOAD(S, "sp", bd, bd[:, :], k.b_dn, k.b_dn.t.ap()[l, e_:e_ + 1, :])
        for rb in range(NRB):
            xw = xrow.next()
            r0 = e_ * CAP + rb * 128
            LOAD(S, "sp", xw, xw[:, :], k.XS, k.XS.t.ap()[r0:r0 + 128, :])
            for half in range(2):
                ps = psr.next()
                for c4 in range(4):
                    c = half * 4 + c4
                    TRP(S, ps, ps[:, c4 * 128:(c4 + 1) * 128], xw, xw[:, c * 128:(c + 1) * 128], k.cs, ident)
                CP(S, "act" if half == 0 else "dve", XT, R(XT[:, half * 4:half * 4 + 4, rb * 128:(rb + 1) * 128]),
                   ps, ps[:, :].rearrange("p (c t) -> p c t", c=4))
        vgu = wview(k.w_gu.t.ap()[l, e_])
        for wc in range(4):
            w = ws.get(k.w_gu, vgu, 0, wc * 512)
            for f2 in range(2):
                fc = wc * 2 + f2
                base = f2 * 256
                for hf in range(2):
                    cs_ = slice(hf * HN, (hf + 1) * HN)
                    pg = psr.next()
                    pu = psr.next()
                    for kc in range(8):
                        MM(S, pg, pg[:, 0:HN], w, R(w[:, kc, base:base + 256:2]), XT, R(XT[:, kc, cs_]),
                           start=(kc == 0), stop=(kc == 7))
                    for kc in range(8):
                        MM(S, pu, pu[:, 0:HN], w, R(w[:, kc, base + 1:base + 256:2]), XT, R(XT[:, kc, cs_]),
                           start=(kc == 0), stop=(kc == 7))
                    g_ = gp.next()
                    s_ = sg.next()
                    u_ = up.next()
                    bgc = k.pvs[:, o_bg + e_ * 8 + fc:o_bg + e_ * 8 + fc + 1]
                    buc = k.pvs[:, o_bu + e_ * 8 + fc:o_bu + e_ * 8 + fc + 1]
                    TS(S, "dve", g_, g_[:, :], pg, pg[:, 0:HN], bgc, 7.0, ALU.add, ALU.min, xr=[k.pvs])
                    ACT(S, s_, s_[:, :], g_, g_[:, :], AF.Sigmoid, scale=1.702)
                    TS(S, "dve", u_, u_[:, :], pu, pu[:, 0:HN], buc, 7.0, ALU.add, ALU.min, xr=[k.pvs])
                    TS(S, "dve", u_, u_[:, :], u_, u_[:, :], -7.0, 1.0, ALU.max, ALU.add)
                    TT(S, "dve", g_, g_[:, :], g_, g_[:, :], s_, s_[:, :], ALU.mult)
                    TT(S, "dve", AT, R(AT[:, fc, cs_]), g_, g_[:, :], u_, u_[:, :], ALU.mult)
        vdn = wview(k.w_dn.t.ap()[l, e_])
        wd = [ws.get(k.w_dn, vdn, 0, oc * 512) for oc in range(2)]
        for rb in range(NRB):
            yw = yrow.next()
            for oc in range(2):
                ps = psr.next()
                for fc in range(8):
                    MM(S, ps, ps[:, :], AT, R(AT[:, fc, rb * 128:(rb + 1) * 128]), wd[oc], R(wd[oc][:, fc, :]),
                       start=(fc == 0), stop=False)
                MM(S, ps, ps[:, :], k.cs, ones[0:1, :], bd, bd[0:1, oc * 512:(oc + 1) * 512], start=False, stop=True)
                CP(S, "act", yw, yw[:, oc * 512:(oc + 1) * 512], ps, ps[:, :])
            r0 = e_ * CAP + rb * 128
            STORE(S, "act", k.YS, k.YS.t.ap()[r0:r0 + 128, :], yw, yw[:, :])
    S.end_phase()
    S.begin_phase()
    psr = Rot([S.psum(f"ps{i}", [128, 512]) for i in range(8)])
    hr = Rot([S.sbuf(f"hrow{i}", [128, D]) for i in range(2)])
    yk = Rot([S.sbuf(f"yk{i}", [128, D]) for i in range(8)])
    acc = Rot([S.sbuf(f"acc{i}", [128, D]) for i in range(2)])
    ofm = Rot([S.sbuf(f"ofm{i}", [128, 8, 128]) for i in range(2)])
    gB = S.sbuf("gB", [128, D])
    bB = S.sbuf("bB", [128, D])
    st_ = Rot([S.sbuf(f"st{i}", [128, 16]) for i in range(2)])
    eps_t = S.sbuf("eps_t", [128, 1])
    S.op("dve", lambda e: e.memset(eps_t[:, :], EPS), writes=[eps_t])
    LOAD(S, "sp", gB, gB[:, :], k.ln_ffn_g, k.ln_ffn_g.t.ap()[l:l + 1, :].broadcast_to([128, D]))
    LOAD(S, "sp", bB, bB[:, :], k.ln_ffn_b, k.ln_ffn_b.t.ap()[l:l + 1, :].broadcast_to([128, D]))
    for ti in range(NTL):
        hrow = hr.next()
        LOAD(S, "sp", hrow, hrow[:, :], HTOK, HTOK.t.ap()[ti * 128:(ti + 1) * 128, :])
        ys = []
        for kk in range(4):
            y = yk.next()
            S.dma("pool", lambda e, y=y, ti=ti, kk=kk: e.indirect_dma_start(
                out=y[:, :], out_offset=None, in_=k.YS.t.ap(),
                in_offset=bass.IndirectOffsetOnAxis(ap=k.dest[:, ti, kk:kk + 1], axis=0)),
                reads=[k.YS, k.dest], writes=[y], owner=y)
            ys.append(y)
        a = acc.next()
        TS(S, "dve", a, a[:, :], ys[0], ys[0][:, :], k.gates[:, ti, 0:1], None, ALU.mult, xr=[k.gates])
        for kk in range(1, 4):
            STT(S, a, a[:, :], ys[kk], ys[kk][:, :], k.gates[:, ti, kk:kk + 1], a, a[:, :], ALU.mult, ALU.add,
                xr=[k.gates])
        STT(S, a, a[:, :], hrow, hrow[:, :], ALPHA, a, a[:, :], ALU.mult, ALU.add)
        s_ = st_.next()
        for hf in range(2):
            S.op("dve", lambda e, s_=s_, a=a, hf=hf: e.bn_stats(out=s_[:, hf * 6:(hf + 1) * 6],
                                                               in_=a[:, hf * 512:(hf + 1) * 512]),
                 reads=[a], writes=[s_])
        S.op("dve", lambda e, s_=s_: e.bn_aggr(out=s_[:, 12:14], in_=s_[:, 0:12]), reads=[s_], writes=[s_])
        ACT(S, s_, s_[:, 14:15], s_, s_[:, 13:14], AF.Sqrt, bias=eps_t[:, 0:1], xr=[eps_t])
        S.op("dve", lambda e, s_=s_: e.reciprocal(out=s_[:, 15:16], in_=s_[:, 14:15]), reads=[s_], writes=[s_])
        TS(S, "dve", a, a[:, :], a, a[:, :], s_[:, 12:13], s_[:, 15:16], ALU.subtract, ALU.mult, xr=[s_])
        TT(S, "dve", a, a[:, :], a, a[:, :], gB, gB[:, :], ALU.mult)
        TT(S, "dve", a, a[:, :], a, a[:, :], bB, bB[:, :], ALU.add)
        if OUT_TOK is not None:
            STORE(S, "act", OUT_TOK, OUT_TOK.t.ap()[ti * 128:(ti + 1) * 128, :], a, a[:, :])
        if HOUT_FM is not None:
            o = ofm.next()
            for half in range(2):
                ps = psr.next()
                for c4 in range(4):
                    c = half * 4 + c4
                    TRP(S, ps, ps[:, c4 * 128:(c4 + 1) * 128], a, a[:, c * 128:(c + 1) * 128], k.cs, ident)
                CP(S, "act", o, o[:, half * 4:half * 4 + 4, :], ps, ps[:, :].rearrange("p (c t) -> p c t", c=4))
            STORE(S, "act", HOUT_FM, fmview(HOUT_FM)[:, :, ti * 128:(ti + 1) * 128], o, o[:, :, :])
    S.end_phase()


def phase_ssd_a(k, HIN):
    S = k.S
    S.begin_phase()
    N = 512
    ws = WStream(S, 3)
    psr = Rot([S.psum(f"ps{i}", [128, 512]) for i in range(8)])
    xr = Rot([S.sbuf(f"xtr{i}", [128, 8, N], RT) for i in range(2)])
    zst = Rot([S.sbuf(f"zst{i}", [128, N]) for i in range(4)])
    ucr = Rot([S.sbuf(f"uc{i}", [128, 3 + N]) for i in range(4)])
    acr = Rot([S.sbuf(f"ac{i}", [128, N]) for i in range(4)])
    xst = Rot([S.sbuf(f"xst{i}", [128, N]) for i in range(4)])
    halo = S.sbuf("halo", [128, 32, 4])
    S.op("dve", lambda e: e.memset(halo[:, :, :], 0.0), writes=[halo])
    vin = wview(k.ssm_w_in.t.ap())
    hin = fmview(HIN)
    o_cw, _ = PV["conv_w"]
    zsd = k.ZSD.t.ap()
    xbd = k.XBCD.t.ap()
    for tb in range(T // N):
        xtr = xr.next()
        LOAD(S, "pool" if FAST else "sp", xtr, xtr[:, :, :], HIN, hin[:, :, tb * N:(tb + 1) * N])

        def ev_zs(oc, ps, pa, tb=tb):
            z_ = zst.next()
            ACT(S, z_, z_[:, :], ps, pa, AF.Silu)
            STORE(S, "act", k.ZSD, zsd[tb * 4:(tb + 1) * 4, :, oc, :].rearrange("j p t -> p j t"),
                  z_, z_[:, :].rearrange("p (j t) -> p j t", j=4))
        proj_fm(k, ws, psr, k.ssm_w_in, vin, xtr, lambda kc, xtr=xtr: xtr[:, kc, :], N, 16, ev_zs, col0=0)

        def ev_u(c, ps, pa, tb=tb):
            uc = ucr.next()
            CP(S, "dve", uc, uc[:, 0:3], halo, halo[:, c, 0:3])
            CP(S, "act", uc, uc[:, 3:3 + N], ps, pa)
            CP(S, "dve", halo, halo[:, c, 0:3], uc, uc[:, N:N + 3])
            acc = acr.next()
            wcol = lambda kk: k.pvs[:, o_cw + c * 4 + kk:o_cw + c * 4 + kk + 1]
            TS(S, "dve", acc, acc[:, :], uc, uc[:, 0:N], wcol(0), P(k, "conv_b", c), ALU.mult, ALU.add, xr=[k.pvs])
            for kk in range(1, 4):
                STT(S, acc, acc[:, :], uc, uc[:, kk:kk + N], wcol(kk), acc, acc[:, :], ALU.mult, ALU.add, xr=[k.pvs])
            x_ = xst.next()
            ACT(S, x_, x_[:, :], acc, acc[:, :], AF.Silu)
            STORE(S, "act", k.XBCD, xbd[tb * 4:(tb + 1) * 4, :, c, :].rearrange("j p t -> p j t"),
                  x_, x_[:, :].rearrange("p (j t) -> p j t", j=4))
        proj_fm(k, ws, psr, k.ssm_w_in, vin, xtr, lambda kc, xtr=xtr: xtr[:, kc, :], N, 32, ev_u, col0=2048)
    S.end_phase()


def phase_ssd_b(k, HIN, HOUT):
    S = k.S
    S.begin_phase()
    common_small(k)
    N = 128
    ws = WStream(S, 2)
    psr = Rot([S.psum(f"ps{i}", [128, 512]) for i in range(4)])
    SPS = S.psum("SPS", [128, 2048])
    xr = Rot([S.sbuf(f"xt{i}", [128, 8, N]) for i in range(2)])
    wdt = S.sbuf("wdt", [128, 8, 32])
    rowb = S.sbuf("rowb", [128, 64])
    Abc = S.sbuf("Abc", [128, 32])
    one_t = S.sbuf("one_t", [128, 1])
    zsr = Rot([S.sbuf(f"ZS{i}", [128, 16, N]) for i in range(2)])
    xbr = Rot([S.sbuf(f"XBC{i}", [128, 32, N]) for i in range(2)])
    XD = S.sbuf("XD", [128, 2048])
    XDD = S.sbuf("XDD", [128, 2048])
    BTK = S.sbuf("BTK", [128, 8, 128])
    tk = Rot([S.sbuf(f"tk{i}", [128, 8, 32]) for i in range(2)])
    cdr = Rot([S.sbuf(f"cdr{i}", [128, 64]) for i in range(2)])
    arr = Rot([S.sbuf(f"ar{i}", [128, 4, 128]) for i in range(2)])
    cbr = Rot([S.sbuf(f"cbm{i}", [128, 128]) for i in range(2)])
    mtr = Rot([S.sbuf(f"mt{i}", [128, 4, 128]) for i in range(2)])
    cer = Rot([S.sbuf(f"ce{i}", [128, 4, 128]) for i in range(2)])
    H = [S.sbuf(f"H{i}", [128, 2048]) for i in range(2)]
    Yt = S.sbuf("Yt", [128, 16, N])
    SQ = S.sbuf("SQ", [128, 16, N])
    rst = S.sbuf("rst", [128, 8, N])
    Z = S.sbuf("Z", [128, 8, N])
    SCR = S.sbuf("SCR", [128, 8, N], RT)
    YR = S.sbuf("YR", [128, 16, N], RT)
    ident = C(k, "ident")
    triinc, blockones, strictlt, mask01 = C(k, "triinc"), C(k, "blockones"), C(k, "strictlt"), C(k, "mask01")
    csel = [C(k, "csel0"), C(k, "csel1")]
    ones, ones256 = C(k, "ones"), C(k, "ones256")
    vin = wview(k.ssm_w_in.t.ap())
    vout = wview(k.ssm_w_out.t.ap())
    hin = fmview(HIN)
    hout = fmview(HOUT)
    o_cw, _ = PV["conv_w"]
    LOAD(S, "sp", wdt, wdt[:, :, :], k.ssm_w_in, vin[:, :, 6144:6176])
    LOAD(S, "sp", rowb, rowb[:, :], k.rowp, k.rowp.t.ap()[0:1, :].broadcast_to([128, 64]))
    S.op("dve", lambda e: e.memset(one_t[:, :], 1.0), writes=[one_t])
    S.op("dve", lambda e: e.memset(H[0][:, :], 0.0), writes=[H[0]])
    ACT(S, Abc, Abc[:, :], rowb, rowb[:, 32:64], AF.Exp)
    TS(S, "dve", Abc, Abc[:, :], Abc, Abc[:, :], -1.0, None, ALU.mult)
    for tb in range(T // N):
        xt = xr.next()
        LOAD(S, "sp", xt, xt[:, :, :], HIN, hin[:, :, tb * N:(tb + 1) * N])
        ZS = zsr.next()
        XBC = xbr.next()
        LOAD(S, "sp", ZS, ZS[:, :, :], k.ZSD, k.ZSD.t.ap()[tb])
        LOAD(S, "sp", XBC, XBC[:, :, :], k.XBCD, k.XBCD.t.ap()[tb])
        t_ = tk.next()
        dt, a_, acs, dte, w1, tmp = (t_[:, i, :] for i in range(6))
        ps = psr.next()
        for kc in range(8):
            MM(S, ps, ps[:, 0:32], xt, xt[:, kc, :], wdt, wdt[:, kc, :], start=(kc == 0), stop=(kc == 7))
        TT(S, "dve", t_, tmp, ps, ps[:, 0:32], rowb, rowb[:, 0:32], ALU.add)
        ACT(S, t_, tmp, t_, tmp, AF.Exp)
        ACT(S, t_, dt, t_, tmp, AF.Ln, bias=one_t[:, 0:1], xr=[one_t])
        TT(S, "dve", t_, a_, t_, dt, Abc, Abc[:, :], ALU.mult)
        pa = psr.next()
        MM(S, pa, pa[:, 0:32], k.cs, triinc, t_, a_)
        MM(S, pa, pa[:, 32:64], k.cs, blockones, t_, a_)
        MM(S, pa, pa[:, 64:96], k.cs, csel[0], t_, a_)
        MM(S, pa, pa[:, 96:128], k.cs, csel[1], t_, a_)
        CP(S, "act", t_, acs, pa, pa[:, 0:32])
        TT(S, "dve", t_, dte, pa, pa[:, 32:64], t_, acs, ALU.subtract)
        ACT(S, t_, dte, t_, dte, AF.Exp)
        TT(S, "dve", t_, w1, t_, dt, t_, dte, ALU.mult)
        cd = cdr.next()
        ACT(S, cd, cd[:, :], pa, pa[:, 64:128], AF.Exp)
        for q in range(4):
            ps = psr.next()
            for j in range(4):
                TRP(S, ps, ps[:, j * 128:(j + 1) * 128], XBC, XBC[:, q * 4 + j, :], k.cs, ident)
            pv3 = ps[:, :].rearrange("p (h d) -> p h d", h=8)
            for dst, sc in ((XD, dt), (XDD, w1)):
                TT(S, "dve", dst, dst[:, q * 512:(q + 1) * 512].rearrange("p (h d) -> p h d", h=8), ps, pv3,
                   t_, sc[:, q * 8:(q + 1) * 8].unsqueeze(2).broadcast_to([128, 8, 64]), ALU.mult)
        for q in range(2):
            ps = psr.next()
            for j in range(4):
                TRP(S, ps, ps[:, j * 128:(j + 1) * 128], XBC, XBC[:, 16 + q * 4 + j, :], k.cs, ident)
            CP(S, "act", BTK, BTK[:, q * 4:(q + 1) * 4, :], ps, ps[:, :].rearrange("p (g n) -> p g n", g=4))
        H0, H1 = H[0], H[1]

        def states(cp):
            for g in range(8):
                MM(S, SPS, SPS[:, g * 256:(g + 1) * 256], BTK, BTK[cp * 64:(cp + 1) * 64, g, :],
                   XDD, XDD[cp * 64:(cp + 1) * 64, g * 256:(g + 1) * 256])

        def hupd(dst, src, cp):
            TT(S, "dve", dst, dst[:, :].rearrange("p (h d) -> p h d", h=32), src,
               src[:, :].rearrange("p (h d) -> p h d", h=32), cd,
               cd[:, cp * 32:(cp + 1) * 32].unsqueeze(2).broadcast_to([128, 32, 64]), ALU.mult)
            TT(S, "dve", dst, dst[:, :], SPS, SPS[:, :], dst, dst[:, :], ALU.add)
        states(0)
        hupd(H1, H0, 0)
        Hc = [H0, H1]
        for g in range(8):
            ar = arr.next()
            TT(S, "dve", ar, ar[:, :, :], k.cs, triinc.unsqueeze(1).broadcast_to([128, 4, 128]), t_,
               a_[:, 4 * g:4 * g + 4].unsqueeze(2).broadcast_to([128, 4, 128]), ALU.mult)
            ar2 = ar[:, :, :].rearrange("p h i -> p (h i)")
            pcb = psr.next()
            MM(S, pcb, pcb[:, 0:128], XBC, XBC[:, 16 + g, :], XBC, XBC[:, 24 + g, :])
            cbm = cbr.next()
            TT(S, "dve", cbm, cbm[:, :], pcb, pcb[:, 0:128], k.cs, mask01, ALU.mult)
            pseg = psr.next()
            MM(S, pseg, pseg[:, :], k.cs, strictlt, ar, ar2)
            mt = mtr.next()
            ACT(S, mt, mt[:, :, :], pseg, pseg[:, :].rearrange("p (h i) -> p h i", h=4), AF.Exp)
            TT(S, "dve", mt, mt[:, :, :], mt, mt[:, :, :], cbm, cbm[:, :].unsqueeze(1).broadcast_to([128, 4, 128]),
               ALU.mult)
            pacs = psr.next()
            MM(S, pacs, pacs[:, :], k.cs, ones, ar, ar2)
            ce = cer.next()
            ACT(S, ce, ce[:, :, :], pacs, pacs[:, :].rearrange("p (h i) -> p h i", h=4), AF.Exp)
            TT(S, "dve", ce, ce[:, :, :], ce, ce[:, :, :], XBC,
               XBC[:, 24 + g, :].unsqueeze(1).broadcast_to([128, 4, 128]), ALU.mult)
            py = psr.next()
            for r in range(4):
                hd = 4 * g + r
                prt = slice((r % 2) * 64, (r % 2) * 64 + 64)
                c0 = (r // 2) * 128
                MM(S, py, py[prt, c0:c0 + 128], XD, XD[:, hd * 64:(hd + 1) * 64], mt, mt[:, r, :],
                   start=True, stop=False)
                for cp in range(2):
                    MM(S, py, py[prt, c0 + cp * 64:c0 + (cp + 1) * 64], Hc[cp], Hc[cp][:, hd * 64:(hd + 1) * 64],
                       ce, ce[:, r, cp * 64:(cp + 1) * 64], start=False, stop=(cp == 1))
            for j in range(2):
                c = 2 * g + j
                STT(S, Yt, Yt[:, c, :], XBC, XBC[:, c, :], P(k, "dskip", c), py, py[:, j * 128:(j + 1) * 128],
                    ALU.mult, ALU.add, xr=[k.pvs])
        states(1)
        hupd(H0, H1, 1)
        TT(S, "dve", Yt, Yt[:, :, :], Yt, Yt[:, :, :], ZS, ZS[:, :, :], ALU.mult)
        ACT(S, SQ, SQ[:, :, :], Yt, Yt[:, :, :], AF.Square)
        for b2 in range(2):
            pm = psr.next()
            for g4 in range(4):
                g = b2 * 4 + g4
                for j in range(2):
                    MM(S, pm, pm[:, g4 * 128:(g4 + 1) * 128], k.cs, ones256, SQ, SQ[:, 2 * g + j, :],
                       start=(j == 0), stop=(j == 1))
            ACT(S, rst, rst[:, b2 * 4:(b2 + 1) * 4, :], pm, pm[:, :].rearrange("p (g t) -> p g t", g=4), AF.Sqrt,
                bias=k.eps_t[:, 0:1], xr=[k.eps_t])
        S.op("dve", lambda e: e.reciprocal(out=rst[:, :, :], in_=rst[:, :, :]), reads=[rst], writes=[rst])
        Yt4 = Yt[:, :, :].rearrange("p (g j) t -> p g j t", j=2)
        TT(S, "dve", Yt, Yt4, Yt, Yt4, rst, rst[:, :, :].unsqueeze(2).broadcast_to([128, 8, 2, N]), ALU.mult)
        TT(S, "dve", YR, YR[:, :, :], Yt, Yt[:, :, :], k.pvs,
           P(k, "norm_g").unsqueeze(2).broadcast_to([128, 16, N]), ALU.mult)

        def ev_z(oc, ps, pa, xt=xt):
            STT(S, Z, Z[:, oc, :], xt, xt[:, oc, :], ALPHA, ps, pa, ALU.mult, ALU.add)
        proj_fm16(k, ws, psr, k.ssm_w_out, vout, YR, lambda kc: YR[:, kc, :], N, 8, ev_z)
        o = Z
        ln_fm(k, psr, Z, N, "ln_mix_g1", "ln_mix_b1", o, SCR)
        STORE(S, "act", HOUT, hout[:, :, tb * N:(tb + 1) * N], o, o[:, :, :])
    S.end_phase()


_NC_CACHE = {}


def make_in_maps(inp):
    cst = make_consts()
    pv = make_pv(inp)
    rowp = np.concatenate([inp["ssm_dt_bias"][0], inp["ssm_a_log"][0]])[None, :].astype(np.float32)
    shared = {
        "cst": cst, "pv": pv, "rowp": rowp,
        "pool_w_in": inp["pool_w_in"][0], "pool_w_grp": inp["pool_w_grp"][0], "pool_w_out": inp["pool_w_out"][0],
        "ssm_w_in": inp["ssm_w_in"][0], "ssm_w_out": inp["ssm_w_out"][0],
        "xa_wq": inp["xa_wq"], "xa_wk": inp["xa_wk"], "xa_wv": inp["xa_wv"], "xa_wo": inp["xa_wo"],
        "moe_w_router": inp["moe_w_router"], "moe_b_router": inp["moe_b_router"],
        "moe_w_gate_up": inp["moe_w_gate_up"], "moe_w_down": inp["moe_w_down"], "moe_b_down": inp["moe_b_down"],
        "ln_ffn_g": inp["ln_ffn_g"], "ln_ffn_b": inp["ln_ffn_b"],
    }
    shared = {kk: np.ascontiguousarray(np.asarray(vv, dtype=np.float32)) for kk, vv in shared.items()}
    maps = []
    for b in range(8):
        m = dict(shared)
        m["xT"] = np.ascontiguousarray(np.asarray(inp["x"][b]).T)
        m["memT"] = np.ascontiguousarray(np.asarray(inp["mem"][b]).T)
        maps.append(m)
    return maps


def kernel(**inputs):
    inp = {kk: np.asarray(vv) for kk, vv in inputs.items()}
    if "nc" not in _NC_CACHE:
        _NC_CACHE["nc"] = build()
    nc = _NC_CACHE["nc"]
    maps = make_in_maps(inp)
    res = run_bass_kernel_spmd(nc, maps, core_ids=list(range(8)))
    return np.stack([res.results[b]["out"] for b in range(8)], axis=0).astype(np.float32)
```

```python
from contextlib import ExitStack
import numpy as np
import concourse.bass as bass
import concourse.mybir as mybir
from concourse.bass_utils import run_bass_kernel_spmd

F32 = mybir.dt.float32
F32R = mybir.dt.float32r
FAST = True


RT = F32R if FAST else F32


def R(ap):
    return ap


I32 = mybir.dt.int32
AF = mybir.ActivationFunctionType
ALU = mybir.AluOpType
AX = mybir.AxisListType

T = 4096
D = 1024
NTL = 32
ALPHA = float(4 ** 0.25)
EPS = 1e-5
CAP = 768
NE = 32
DEBUG = False

ENGS = ("pe", "act", "dve", "pool", "sp")


class Buf:
    __slots__ = ("name", "t", "w", "r", "dsem", "local")

    def __init__(self, name, t):
        self.name = name
        self.t = t
        self.w = {}
        self.r = {}
        self.dsem = None
        self.local = False

    def __getitem__(self, idx):
        return self.t[idx]


class Rot:
    def __init__(self, bufs):
        self.b = bufs
        self.i = 0

    def next(self):
        b = self.b[self.i % len(self.b)]
        self.i += 1
        return b


class Sched:
    def __init__(self, nc, stack):
        self.nc = nc
        self.stack = stack
        self.sems = {}
        self.cnt = {}
        for e in ENGS:
            self._mksem(e)
        self.seen = {e: {} for e in ENGS}
        self.prog = {e: [] for e in ENGS}
        self.nbuf = 0
        self.phase_stack = None
        self.free_dsems = []

    def _mksem(self, key):
        h = self.stack.enter_context(self.nc.semaphore(f"s_{key}"))
        self.sems[key] = h
        self.cnt[key] = 0

    def begin_phase(self):
        self.phase_stack = ExitStack()
        self.phase_dsems = []

    def end_phase(self):
        self.barrier()
        self.phase_stack.close()
        self.phase_stack = None
        self.free_dsems.extend(self.phase_dsems)
        self.phase_dsems = []

    def sbuf(self, name, shape, dtype=F32, glob=False):
        st = self.stack if (glob or self.phase_stack is None) else self.phase_stack
        self.uid = getattr(self, "uid", 0) + 1
        name = f"{name}_u{self.uid}"
        t = st.enter_context(self.nc.sbuf_tensor(name, list(shape), dtype))
        b = Buf(name, t)
        b.local = st is not self.stack
        return b

    def psum(self, name, shape, dtype=F32):
        st = self.stack if self.phase_stack is None else self.phase_stack
        self.uid = getattr(self, "uid", 0) + 1
        name = f"{name}_u{self.uid}"
        t = st.enter_context(self.nc.psum_tensor(name, list(shape), dtype))
        return Buf(name, t)

    def _deps(self, e, reads, writes):
        need = {}
        for b in reads:
            for k, v in b.w.items():
                if need.get(k, 0) < v:
                    need[k] = v
        for b in writes:
            for k, v in b.w.items():
                if need.get(k, 0) < v:
                    need[k] = v
            for k, v in b.r.items():
                if need.get(k, 0) < v:
                    need[k] = v
        waits = []
        seen = self.seen[e]
        for k, v in need.items():
            if e == "pe" and k == "pe":
                continue
            if seen.get(k, 0) < v:
                seen[k] = v
                waits.append((k, v))
        return waits

    def op(self, e, fn, reads=(), writes=()):
        waits = self._deps(e, reads, writes)
        self.cnt[e] += 1
        v = self.cnt[e]
        self.prog[e].append((waits, fn, (e, 1)))
        for b in writes:
            b.w[e] = v
        for b in reads:
            b.r[e] = v

    def dma(self, q, fn, reads=(), writes=(), owner=None, group=False):
        if owner is None:
            owner = writes[0]
        if owner.dsem is None:
            if self.free_dsems:
                owner.dsem = self.free_dsems.pop()
            else:
                self.nbuf += 1
                owner.dsem = f"d{self.nbuf}"
                self._mksem(owner.dsem)
            if owner.local:
                self.phase_dsems.append(owner.dsem)
        k = owner.dsem
        waits = self._deps(q, reads, writes)
        if (not group) and self.cnt[k] > 0 and self.seen[q].get(k, 0) < self.cnt[k]:
            self.seen[q][k] = self.cnt[k]
            waits.append((k, self.cnt[k]))
        self.cnt[k] += 16
        v = self.cnt[k]
        self.prog[q].append((waits, fn, (k, 16)))
        for b in writes:
            b.w[k] = v
        for b in reads:
            b.r[k] = v

    def barrier(self):
        for e in ENGS:
            waits = []
            for k, v in self.cnt.items():
                if v > 0 and self.seen[e].get(k, 0) < v and not (k == e):
                    self.seen[e][k] = v
                    waits.append((k, v))
            if waits:
                self.prog[e].append((waits, None, None))

    def emit(self):
        nc = self.nc
        sems = self.sems
        prog = self.prog
        with nc.Block() as block:
            def mk(ekey):
                def body(engine):
                    for waits, fn, inc in prog[ekey]:
                        for k, v in waits:
                            engine.wait_ge(sems[k], v)
                        if fn is not None:
                            fn(engine).then_inc(sems[inc[0]], inc[1])
                return body
            block.tensor(mk("pe"))
            block.scalar(mk("act"))
            block.vector(mk("dve"))
            block.gpsimd(mk("pool"))
            block.sync(mk("sp"))


def MM(S, ob, o, lb, l, rb, r, start=True, stop=True):
    S.op("pe", lambda e: e.matmul(o, l, r, start=start, stop=stop), reads=[lb, rb], writes=[ob])


def TRP(S, ob, o, ib, i, idb, idap):
    S.op("pe", lambda e: e.transpose(o, i, idap), reads=[ib, idb], writes=[ob])


def TT(S, eng, ob, o, ab, a, bb, b, op):
    S.op(eng, lambda e: e.tensor_tensor(out=o, in0=a, in1=b, op=op), reads=[ab, bb], writes=[ob])


def TS(S, eng, ob, o, ab, a, s1, s2, op0, op1=None, xr=()):
    if op1 is None:
        S.op(eng, lambda e: e.tensor_scalar(out=o, in0=a, scalar1=s1, scalar2=None, op0=op0),
             reads=[ab, *xr], writes=[ob])
    else:
        S.op(eng, lambda e: e.tensor_scalar(out=o, in0=a, scalar1=s1, scalar2=s2, op0=op0, op1=op1),
             reads=[ab, *xr], writes=[ob])


def STT(S, ob, o, ab, a, sc, bb, b, op0, op1, xr=(), accum=None, xw=()):
    if accum is None:
        S.op("dve", lambda e: e.scalar_tensor_tensor(out=o, in0=a, scalar=sc, in1=b, op0=op0, op1=op1),
             reads=[ab, bb, *xr], writes=[ob])
    else:
        S.op("dve", lambda e: e.scalar_tensor_tensor(out=o, in0=a, scalar=sc, in1=b, op0=op0, op1=op1,
                                                     accum_out=accum),
             reads=[ab, bb, *xr], writes=[ob, *xw])


def ACT(S, ob, o, ab, a, func, bias=None, scale=None, xr=(), accum=None, xw=()):
    kw = {}
    if bias is not None:
        kw["bias"] = bias
    if scale is not None:
        kw["scale"] = scale
    if accum is not None:
        kw["accum_out"] = accum
    S.op("act", lambda e: e.activation(out=o, in_=a, func=func, **kw), reads=[ab, *xr], writes=[ob, *xw])


def CP(S, eng, ob, o, ab, a):
    if eng == "act":
        S.op("act", lambda e: e.activation(out=o, in_=a, func=AF.Identity), reads=[ab], writes=[ob])
    else:
        S.op(eng, lambda e: e.tensor_copy(out=o, in_=a), reads=[ab], writes=[ob])


def LOAD(S, q, db, d, sb, s, group=False):
    S.dma(q, lambda e: e.dma_start(out=d, in_=s), reads=[sb], writes=[db], owner=db, group=group)


def STORE(S, q, db, d, sb, s):
    S.dma(q, lambda e: e.dma_start(out=d, in_=s), reads=[sb], writes=[db], owner=sb)


CST = {}


def _cst_layout():
    off = 0
    for name, n in [("ident", 128), ("onesD", 128), ("ones", 128), ("ones256", 128), ("ustrict", 128),
                    ("triinc", 128), ("blockones", 128), ("strictlt", 128), ("mask01", 128),
                    ("csel0", 128), ("csel1", 128), ("rcnt", 64), ("iotaec", 32)]:
        CST[name] = (off, n)
        off += n
    return off


NCST = _cst_layout()


def make_consts():
    c = np.zeros((128, NCST), np.float32)
    i = np.arange(128)
    ch = i // 64

    def put(name, arr):
        o, n = CST[name]
        c[:, o:o + n] = arr
    put("ident", np.eye(128))
    put("onesD", np.full((128, 128), 1.0 / D))
    put("ones", np.ones((128, 128)))
    put("ones256", np.full((128, 128), 1.0 / 256))
    put("ustrict", (i[:, None] < i[None, :]))
    same = ch[:, None] == ch[None, :]
    put("triinc", (i[:, None] <= i[None, :]) & same)
    put("blockones", same)
    put("strictlt", (i[None, :] < i[:, None]) & same)
    put("mask01", (i[:, None] <= i[None, :]) & same)
    put("csel0", np.repeat((i < 64)[:, None], 128, 1))
    put("csel1", np.repeat((i >= 64)[:, None], 128, 1))
    rc = np.zeros((128, 4, 16))
    for g, w in enumerate((2, 4, 8, 16)):
        rc[:, g, :] = 1.0 / np.minimum(np.arange(16) + 1, w)
    put("rcnt", rc.reshape(128, 64))
    put("iotaec", np.repeat((np.arange(32) * CAP)[None, :], 128, 0))
    return c


PV = {}


def _pv_layout():
    off = 0
    items = [("pool_scale", 8)]
    for l in range(2):
        items += [(f"ln_mix_g{l}", 8), (f"ln_mix_b{l}", 8), (f"ln_xa_g{l}", 8), (f"ln_xa_b{l}", 8)]
    items += [("conv_w", 128), ("conv_b", 32), ("norm_g", 16), ("dskip", 16)]
    for l in range(2):
        items += [(f"bg{l}", 256), (f"bu{l}", 256)]
    for name, n in items:
        PV[name] = (off, n)
        off += n
    return off


NPV = _pv_layout()


def fm(v):
    return np.ascontiguousarray(v.reshape(-1, 128).T)


def make_pv(inp):
    p = np.zeros((128, NPV), np.float32)

    def put(name, arr):
        o, n = PV[name]
        p[:, o:o + n] = arr.reshape(128, n)
    put("pool_scale", fm(inp["pool_scale"][0]))
    for l in range(2):
        put(f"ln_mix_g{l}", fm(inp["ln_mix_g"][l]))
        put(f"ln_mix_b{l}", fm(inp["ln_mix_b"][l]))
        put(f"ln_xa_g{l}", fm(inp["ln_xa_g"][l]))
        put(f"ln_xa_b{l}", fm(inp["ln_xa_b"][l]))
    cw = inp["ssm_conv_w"][0]
    put("conv_w", np.stack([fm(cw[k]) for k in range(4)], axis=2))
    put("conv_b", fm(inp["ssm_conv_b"][0]))
    put("norm_g", fm(inp["ssm_norm_g"][0]))
    put("dskip", fm(np.repeat(inp["ssm_d"][0], 64)))
    for l in range(2):
        b = inp["moe_b_gate_up"][l]
        put(f"bg{l}", np.stack([fm(b[e, 0::2]) for e in range(NE)], axis=1))
        put(f"bu{l}", np.stack([fm(b[e, 1::2]) for e in range(NE)], axis=1))
    return p


class K:
    pass


def build(phases=None, dbg=False):
    nc = bass.Bass("TRN2", target_bir_lowering=False)
    k = K()
    k.nc = nc
    k.phases = phases

    def din(name, shape, dtype=F32):
        return Buf(name, nc.dram_tensor(name, list(shape), dtype, kind="ExternalInput"))

    def dscr(name, shape, dtype=F32):
        return Buf(name, nc.dram_tensor(name, list(shape), dtype,
                                        kind="ExternalOutput" if dbg else "Internal"))

    k.xT = din("xT", [D, T])
    k.memT = din("memT", [D, 256])
    k.cst = din("cst", [128, NCST])
    k.pv = din("pv", [128, NPV])
    k.pool_w_in = din("pool_w_in", [D, D])
    k.pool_w_grp = din("pool_w_grp", [4, 256, 256])
    k.pool_w_out = din("pool_w_out", [D, D])
    k.ssm_w_in = din("ssm_w_in", [D, 6176])
    k.ssm_w_out = din("ssm_w_out", [2048, D])
    k.rowp = din("rowp", [1, 64])
    k.xa_wq = din("xa_wq", [2, D, D])
    k.xa_wk = din("xa_wk", [2, D, D])
    k.xa_wv = din("xa_wv", [2, D, D])
    k.xa_wo = din("xa_wo", [2, D, D])
    k.w_router = din("moe_w_router", [2, D, NE])
    k.b_router = din("moe_b_router", [2, NE])
    k.w_gu = din("moe_w_gate_up", [2, NE, D, 2 * D])
    k.w_dn = din("moe_w_down", [2, NE, D, D])
    k.b_dn = din("moe_b_down", [2, NE, D])
    k.ln_mix_g = din("ln_mix_g", [2, D])
    k.ln_mix_b = din("ln_mix_b", [2, D])
    k.ln_ffn_g = din("ln_ffn_g", [2, D])
    k.ln_ffn_b = din("ln_ffn_b", [2, D])
    k.out = Buf("out", nc.dram_tensor("out", [T, D], F32, kind="ExternalOutput"))
    k.HA = dscr("HA", [D, T])
    k.HB = dscr("HB", [D, T])
    k.HC = dscr("HC", [D, T])
    k.HTOK = dscr("HTOK", [T, D])
    k.HCTOK = dscr("HCTOK", [T, D])
    k.ZSD = dscr("ZSD", [2048, T])
    k.XBCD = dscr("XBCD", [4096, T])
    k.XS = dscr("XS", [NE * CAP, D])
    k.YS = dscr("YS", [NE * CAP, D])

    with ExitStack() as st:
        S = Sched(nc, st)
        k.S = S
        k.cs = S.sbuf("cst_s", [128, NCST], glob=True)
        k.pvs = S.sbuf("pv_s", [128, NPV], glob=True)
        k.dest = S.sbuf("dest_i", [128, NTL, 4], I32, glob=True)
        k.gates = S.sbuf("gates", [128, NTL, 4], glob=True)
        LOAD(S, "sp", k.cs, k.cs[:, :], k.cst, k.cst.t.ap())
        k.csr = S.sbuf("csr_s", [128, 384], RT, glob=True)
        for ci, cn in enumerate(("onesD", "strictlt", "ones")):
            LOAD(S, "pool" if FAST else "sp", k.csr, k.csr[:, ci * 128:(ci + 1) * 128], k.cst,
                 k.cst.t.ap()[:, CST[cn][0]:CST[cn][0] + 128], group=(ci > 0))
        LOAD(S, "sp", k.pvs, k.pvs[:, :], k.pv, k.pv.t.ap())

        def want(p):
            return phases is None or p in phases

        if want("pool0"):
            phase_pool(k, k.xT, k.HA)
        if want("xa0"):
            phase_xattn(k, 0, k.HA, k.HB, k.HTOK)
        if want("moe0"):
            phase_moe(k, 0, k.HB, k.HTOK, k.HC, None)
        if want("ssd1"):
            phase_ssd_a(k, k.HC)
            phase_ssd_b(k, k.HC, k.HA)
        if want("xa1"):
            phase_xattn(k, 1, k.HA, k.HB, k.HTOK)
        if want("moe1"):
            phase_moe(k, 1, k.HB, k.HTOK, None, k.out)
        S.barrier()
        S.emit()
    return nc


def C(k, name):
    o, n = CST[name]
    return k.cs[:, o:o + n]


def P(k, name, j=None):
    o, n = PV[name]
    if j is None:
        return k.pvs[:, o:o + n]
    return k.pvs[:, o + j:o + j + 1]


def fmview(buf):
    return buf.t.ap().rearrange("(c p) t -> p c t", p=128)


def wview(ap2d):
    return ap2d.rearrange("(c p) f -> p c f", p=128)


class WStream:
    def __init__(self, S, n=4):
        self.S = S
        self.rot = Rot([S.sbuf(f"wt{i}", [128, 8, 512], RT) for i in range(n)])

    def get(self, wbuf, view, k0, c0, ncols=512, nk=8):
        b = self.rot.next()
        LOAD(self.S, "pool" if FAST else "sp", b, R(b[:, 0:nk, 0:ncols]), wbuf, view[:, k0:k0 + nk, c0:c0 + ncols])
        return b


def proj_fm(k, ws, psr, wbuf, view, xb, x_of_kc, N, n_oc, evac, kchunks=8, col0=0):
    S = k.S
    for wc in range((n_oc + 3) // 4):
        wts = [ws.get(wbuf, view, kh * 8, col0 + wc * 512) for kh in range(kchunks // 8)]
        for o4 in range(min(4, n_oc - wc * 4)):
            oc = wc * 4 + o4
            ps = psr.next()
            for kc in range(kchunks):
                MM(S, ps, ps[:, 0:N], wts[kc // 8], R(wts[kc // 8][:, kc % 8, o4 * 128:(o4 + 1) * 128]),
                   xb, R(x_of_kc(kc)), start=(kc == 0), stop=(kc == kchunks - 1))
            evac(oc, ps, ps[:, 0:N])


def proj_fm16(k, ws, psr, wbuf, view, xb, x_of_kc, N, n_oc, evac):
    S = k.S
    for wc in range(n_oc // 4):
        pss = [psr.next() for _ in range(4)]
        for kh in range(2):
            w = ws.get(wbuf, view, kh * 8, wc * 512)
            for o4 in range(4):
                for kc in range(8):
                    MM(S, pss[o4], pss[o4][:, 0:N], w, w[:, kc, o4 * 128:(o4 + 1) * 128], xb, x_of_kc(kh * 8 + kc),
                       start=(kh == 0 and kc == 0), stop=(kh == 1 and kc == 7))
        for o4 in range(4):
            evac(wc * 4 + o4, pss[o4], pss[o4][:, 0:N])


def ln_fm(k, psr, Z, N, gname, bname, OUT, scr):
    S = k.S
    onesD = C(k, "onesD")
    pm = psr.next()
    for c in range(8):
        MM(S, pm, pm[:, 0:N], k.cs, onesD, Z, Z[:, c, 0:N], start=(c == 0), stop=(c == 7))
    mean = k.ln_mean
    CP(S, "act", mean, mean[:, 0:N], pm, pm[:, 0:N])
    TT(S, "dve", Z, Z[:, :, 0:N], Z, Z[:, :, 0:N], mean, mean[:, 0:N].unsqueeze(1).broadcast_to([128, 8, N]),
       ALU.subtract)
    ACT(S, scr, scr[:, :, 0:N], Z, Z[:, :, 0:N], AF.Square)
    pv = psr.next()
    for c in range(8):
        MM(S, pv, pv[:, 0:N], k.csr, k.csr[:, 0:128], scr, scr[:, c, 0:N], start=(c == 0), stop=(c == 7))
    rstd = k.ln_rstd
    ACT(S, rstd, rstd[:, 0:N], pv, pv[:, 0:N], AF.Sqrt, bias=k.eps_t[:, 0:1], xr=[k.eps_t])
    S.op("dve", lambda e: e.reciprocal(out=rstd[:, 0:N], in_=rstd[:, 0:N]), reads=[rstd], writes=[rstd])
    TT(S, "dve", Z, Z[:, :, 0:N], Z, Z[:, :, 0:N], rstd, rstd[:, 0:N].unsqueeze(1).broadcast_to([128, 8, N]),
       ALU.mult)
    for c in range(8):
        ACT(S, OUT, OUT[:, c, 0:N], Z, Z[:, c, 0:N], AF.Identity, bias=P(k, bname, c), scale=P(k, gname, c),
            xr=[k.pvs])


def common_small(k):
    S = k.S
    k.ln_mean = S.sbuf("ln_mean", [128, 512])
    k.ln_rstd = S.sbuf("ln_rstd", [128, 512])
    k.eps_t = S.sbuf("eps_t", [128, 1])
    S.op("dve", lambda e: e.memset(k.eps_t[:, :], EPS), writes=[k.eps_t])


def phase_pool(k, HIN, HOUT):
    S = k.S
    S.begin_phase()
    common_small(k)
    N = 512
    ws = WStream(S, 2)
    psr = Rot([S.psum(f"ps{i}", [128, 512]) for i in range(8)])
    xr = Rot([S.sbuf(f"xt{i}", [128, 8, N]) for i in range(2)])
    U = Rot([S.sbuf(f"U{i}", [128, 8, 16 + N]) for i in range(2)])
    TA = S.sbuf("TA", [128, 2, 16 + N])
    TB = S.sbuf("TB", [128, 2, 16 + N])
    M = S.sbuf("M", [128, 8, N], RT)
    Y = S.sbuf("Y", [128, 8, N], RT)
    Z = S.sbuf("Z", [128, 8, N])
    xtr = S.sbuf("xtr", [128, 8, N], RT)
    wg = S.sbuf("wg", [128, 4, 2, 256], RT)
    LOAD(S, "pool" if FAST else "sp", wg, R(wg[:, :, :, :]), k.pool_w_grp,
         k.pool_w_grp.t.ap().rearrange("g (kc p) d -> p g kc d", p=128))
    vin = wview(k.pool_w_in.t.ap())
    vout = wview(k.pool_w_out.t.ap())
    hin = fmview(HIN)
    hout = fmview(HOUT)
    o_rc, _ = CST["rcnt"]
    prevU = None
    for tb in range(T // N):
        xt = xr.next()
        LOAD(S, "sp", xt, xt[:, :, :], HIN, hin[:, :, tb * N:(tb + 1) * N])
        LOAD(S, "pool" if FAST else "sp", xtr, R(xtr[:, :, :]), HIN, hin[:, :, tb * N:(tb + 1) * N])
        u = U.next()
        if tb == 0:
            S.op("pool", lambda e, u=u: e.memset(u[:, :, 0:16], 0.0), writes=[u])
        else:
            CP(S, "dve", u, u[:, :, 0:16], prevU, prevU[:, :, N:N + 16])

        def ev_u(oc, ps, pa, u=u):
            CP(S, "act", u, u[:, oc, 16:16 + N], ps, pa)
        proj_fm(k, ws, psr, k.pool_w_in, vin, xtr, lambda kc: xtr[:, kc, :], N, 8, ev_u)
        W_ = 16 + N
        for g in range(4):
            w = 2 << g
            cur_b, cur = u, (lambda a, b, g=g: u[:, 2 * g:2 * g + 2, a:b])
            sh = 1
            tgt = [TA, TB]
            for step in range(g + 1):
                t_ = tgt[step % 2]
                lo = 2 * sh - 1
                TT(S, "dve", t_, t_[:, :, lo:W_], cur_b, cur(lo, W_), cur_b, cur(lo - sh, W_ - sh), ALU.add)
                cur_b, cur = t_, (lambda a, b, t_=t_: t_[:, :, a:b])
                sh *= 2
            if tb == 0:
                rc = k.cs[:, o_rc + g * 16:o_rc + (g + 1) * 16]
                TS(S, "dve", cur_b, cur(32, W_), cur_b, cur(32, W_), 1.0 / w, None, ALU.mult)
                TT(S, "dve", cur_b, cur(16, 32), cur_b, cur(16, 32), k.cs,
                   rc.unsqueeze(1).broadcast_to([128, 2, 16]), ALU.mult)
                TT(S, "dve", M, R(M[:, 2 * g:2 * g + 2, :]), cur_b, cur(16, W_), u, u[:, 2 * g:2 * g + 2, 16:W_],
                   ALU.subtract)
            else:
                STT(S, M, R(M[:, 2 * g:2 * g + 2, :]), cur_b, cur(16, W_), 1.0 / w, u, u[:, 2 * g:2 * g + 2, 16:W_],
                    ALU.mult, ALU.subtract)
        for g in range(4):
            for j in range(2):
                ps = psr.next()
                for kc in range(2):
                    MM(S, ps, ps[:, 0:N], wg, R(wg[:, g, kc, j * 128:(j + 1) * 128]), M, R(M[:, 2 * g + kc, :]),
                       start=(kc == 0), stop=(kc == 1))
                oc = 2 * g + j
                ACT(S, Y, R(Y[:, oc, :]), ps, ps[:, 0:N], AF.Identity, scale=P(k, "pool_scale", oc), xr=[k.pvs])

        def ev_z(oc, ps, pa, xt=xt):
            STT(S, Z, Z[:, oc, :], xt, xt[:, oc, :], ALPHA, ps, pa, ALU.mult, ALU.add)
        proj_fm(k, ws, psr, k.pool_w_out, vout, Y, lambda kc: Y[:, kc, :], N, 8, ev_z)
        o = Z
        ln_fm(k, psr, Z, N, "ln_mix_g0", "ln_mix_b0", o, M)
        STORE(S, "act", HOUT, hout[:, :, tb * N:(tb + 1) * N], o, o[:, :, :])
        prevU = u
    S.end_phase()


def phase_xattn(k, l, HIN, HOUT, HTOK):
    S = k.S
    S.begin_phase()
    common_small(k)
    N = 512
    ws = WStream(S, 2)
    psr = Rot([S.psum(f"ps{i}", [128, 512]) for i in range(8)])
    xr = Rot([S.sbuf(f"xt{i}", [128, 8, N]) for i in range(2)])
    memT = S.sbuf("memT", [128, 8, 256], RT)
    kT = S.sbuf("kT", [128, 8, 256], RT)
    v = S.sbuf("v", [128, 2, D], RT)
    qT = S.sbuf("qT", [128, 8, N], RT)
    Pm = Rot([S.sbuf(f"Pm{i}", [128, 4, 256]) for i in range(2)])
    pT = S.sbuf("pT", [128, 2, 4, N], RT)
    OT = S.sbuf("OT", [128, 8, N], RT)
    Z = S.sbuf("Z", [128, 8, N])
    xtr = S.sbuf("xtr", [128, 8, N], RT)
    htk = Rot([S.sbuf(f"htk{i}", [128, D]) for i in range(2)])
    sm = Rot([S.sbuf(f"sm{i}", [128, 16]) for i in range(2)])
    ident = C(k, "ident")
    LOAD(S, "pool" if FAST else "sp", memT, R(memT[:, :, :]), k.memT, fmview(k.memT))
    vq = wview(k.xa_wq.t.ap()[l])
    vk = wview(k.xa_wk.t.ap()[l])
    vv = wview(k.xa_wv.t.ap()[l])
    vo = wview(k.xa_wo.t.ap()[l])

    def ev_k(oc, ps, pa):
        CP(S, "act", kT, R(kT[:, oc, :]), ps, pa)
    proj_fm(k, ws, psr, k.xa_wk, vk, memT, lambda kc: memT[:, kc, :], 256, 8, ev_k)
    for wc in range(2):
        w = ws.get(k.xa_wv, vv, 0, wc * 512)
        for mc in range(2):
            ps = psr.next()
            for kc in range(8):
                MM(S, ps, ps[:, :], memT, R(memT[:, kc, mc * 128:(mc + 1) * 128]), w, R(w[:, kc, :]),
                   start=(kc == 0), stop=(kc == 7))
            CP(S, "act", v, R(v[:, mc, wc * 512:(wc + 1) * 512]), ps, ps[:, :])
    hin = fmview(HIN)
    hout = fmview(HOUT)
    gname, bname = f"ln_xa_g{l}", f"ln_xa_b{l}"
    for tb in range(T // N):
        xt = xr.next()
        LOAD(S, "sp", xt, xt[:, :, :], HIN, hin[:, :, tb * N:(tb + 1) * N])
        LOAD(S, "pool" if FAST else "sp", xtr, R(xtr[:, :, :]), HIN, hin[:, :, tb * N:(tb + 1) * N])

        def ev_q(oc, ps, pa):
            S.op("act", lambda e: e.activation(out=R(qT[:, oc, :]), in_=pa, func=AF.Identity, scale=0.0625),
                 reads=[ps], writes=[qT])
        proj_fm(k, ws, psr, k.xa_wq, vq, xtr, lambda kc: xtr[:, kc, :], N, 8, ev_q)
        for tt in range(4):
            tsl = slice(tt * 128, (tt + 1) * 128)
            pm = Pm.next()
            s_ = sm.next()
            banks = [psr.next(), psr.next()]
            for h in range(4):
                ps = banks[h // 2]
                for dc in range(2):
                    MM(S, ps, ps[:, (h % 2) * 256:(h % 2 + 1) * 256], qT, R(qT[:, 2 * h + dc, tsl]),
                       kT, R(kT[:, 2 * h + dc, :]), start=(dc == 0), stop=(dc == 1))
            for b2 in range(2):
                ps = banks[b2]
                S.op("dve", lambda e, ps=ps, s_=s_, b2=b2: e.tensor_reduce(
                    out=s_[:, 2 * b2:2 * b2 + 2], in_=ps[:, :].rearrange("p (a b) -> p a b", a=2),
                    axis=AX.X, op=ALU.max), reads=[ps], writes=[s_])
            TS(S, "dve", s_, s_[:, 4:8], s_, s_[:, 0:4], -1.0, None, ALU.mult)
            for h in range(4):
                ps = banks[h // 2]
                ACT(S, pm, pm[:, h, :], ps, ps[:, (h % 2) * 256:(h % 2 + 1) * 256], AF.Exp,
                    bias=s_[:, 4 + h:5 + h], xr=[s_], accum=s_[:, 8 + h:9 + h], xw=[s_])
            S.op("dve", lambda e, s_=s_: e.reciprocal(out=s_[:, 12:16], in_=s_[:, 8:12]), reads=[s_], writes=[s_])
            TT(S, "dve", pm, pm[:, :, :], pm, pm[:, :, :], s_, s_[:, 12:16].unsqueeze(2).broadcast_to([128, 4, 256]),
               ALU.mult)
            for mc in range(2):
                ps = psr.next()
                for h in range(4):
                    TRP(S, ps, ps[:, h * 128:(h + 1) * 128], pm, pm[:, h, mc * 128:(mc + 1) * 128], k.cs, ident)
                CP(S, "act" if mc == 0 else "dve", pT, R(pT[:, mc, :, tsl]), ps,
                   ps[:, :].rearrange("p (h t) -> p h t", h=4))
        for h in range(4):
            for dc in range(2):
                ps = psr.next()
                for mc in range(2):
                    MM(S, ps, ps[:, :], v, R(v[:, mc, h * 256 + dc * 128:h * 256 + (dc + 1) * 128]),
                       pT, R(pT[:, mc, h, :]), start=(mc == 0), stop=(mc == 1))
                CP(S, "act", OT, R(OT[:, 2 * h + dc, :]), ps, ps[:, :])

        def ev_z(oc, ps, pa, xt=xt):
            STT(S, Z, Z[:, oc, :], xt, xt[:, oc, :], ALPHA, ps, pa, ALU.mult, ALU.add)
        proj_fm(k, ws, psr, k.xa_wo, vo, OT, lambda kc: OT[:, kc, :], N, 8, ev_z)
        o = Z
        ln_fm(k, psr, Z, N, gname, bname, o, OT)
        STORE(S, "act", HOUT, hout[:, :, tb * N:(tb + 1) * N], o, o[:, :, :])
        for tt in range(4):
            ht = htk.next()
            for half in range(2):
                ps = psr.next()
                for c4 in range(4):
                    c = half * 4 + c4
                    TRP(S, ps, ps[:, c4 * 128:(c4 + 1) * 128], o, o[:, c, tt * 128:(tt + 1) * 128], k.cs, ident)
                CP(S, "act" if half == 0 else "dve", ht, ht[:, half * 512:(half + 1) * 512], ps, ps[:, :])
            r0 = tb * N + tt * 128
            STORE(S, "act", HTOK, HTOK.t.ap()[r0:r0 + 128, :], ht, ht[:, :])
    S.end_phase()


def phase_moe(k, l, HFM, HTOK, HOUT_FM, OUT_TOK):
    S = k.S
    ident = C(k, "ident")
    S.begin_phase()
    psr = Rot([S.psum(f"ps{i}", [128, 512]) for i in range(8)])
    xr = Rot([S.sbuf(f"xt{i}", [128, 8, 512]) for i in range(2)])
    hr = Rot([S.sbuf(f"hrow{i}", [128, D]) for i in range(3)])
    wr = S.sbuf("wr", [128, 8, NE])
    brt = S.sbuf("brt", [128, NE])
    cum = S.sbuf("cum", [128, NE])
    lgr = Rot([S.sbuf(f"lg{i}", [128, NE]) for i in range(2)])
    t8r = Rot([S.sbuf(f"t8{i}", [128, 16]) for i in range(2)])
    mkr = Rot([S.sbuf(f"mk{i}", [128, NE]) for i in range(2)])
    pcr = Rot([S.sbuf(f"pc{i}", [128, NE]) for i in range(2)])
    jnk = S.sbuf("jnk", [128, NE])
    dfr = Rot([S.sbuf(f"df{i}", [128, 4]) for i in range(2)])
    LOAD(S, "sp", wr, wr[:, :, :], k.w_router, wview(k.w_router.t.ap()[l]))
    LOAD(S, "sp", brt, brt[:, :], k.b_router, k.b_router.t.ap()[l:l + 1, :].broadcast_to([128, NE]))
    S.op("dve", lambda e: e.memset(cum[:, :], 0.0), writes=[cum])
    hfm = fmview(HFM)
    ustrict = C(k, "ustrict")
    ones = C(k, "ones")
    iotaec = C(k, "iotaec")
    xt = None
    for ti in range(NTL):
        if ti % 4 == 0:
            xt = xr.next()
            LOAD(S, "sp", xt, xt[:, :, :], HFM, hfm[:, :, ti * 128:ti * 128 + 512])
        tsl = slice((ti % 4) * 128, (ti % 4 + 1) * 128)
        hrow = hr.next()
        LOAD(S, "sp", hrow, hrow[:, :], HTOK, HTOK.t.ap()[ti * 128:(ti + 1) * 128, :])
        ps = psr.next()
        for kc in range(8):
            MM(S, ps, ps[:, 0:NE], xt, xt[:, kc, tsl], wr, wr[:, kc, :], start=(kc == 0), stop=(kc == 7))
        lg = lgr.next()
        TT(S, "dve", lg, lg[:, :], ps, ps[:, 0:NE], brt, brt[:, :], ALU.add)
        t8 = t8r.next()
        S.op("dve", lambda e, t8=t8, lg=lg: e.max(out=t8[:, 0:8], in_=lg[:, :]), reads=[lg], writes=[t8])
        mk = mkr.next()
        TS(S, "dve", mk, mk[:, :], lg, lg[:, :], t8[:, 3:4], None, ALU.is_ge, xr=[t8])
        pp = psr.next()
        MM(S, pp, pp[:, 0:NE], k.cs, ustrict, mk, mk[:, :])
        MM(S, pp, pp[:, NE:2 * NE], k.cs, ones, mk, mk[:, :])
        pc = pcr.next()
        TT(S, "dve", pc, pc[:, :], pp, pp[:, 0:NE], cum, cum[:, :], ALU.add)
        TT(S, "dve", pc, pc[:, :], pc, pc[:, :], k.cs, iotaec, ALU.add)
        TT(S, "dve", cum, cum[:, :], pp, pp[:, NE:2 * NE], cum, cum[:, :], ALU.add)
        df = dfr.next()
        for kk in range(4):
            STT(S, jnk, jnk[:, :], lg, lg[:, :], t8[:, kk:kk + 1], pc, pc[:, :], ALU.is_equal, ALU.mult,
                xr=[t8], accum=df[:, kk:kk + 1], xw=[df])
        TS(S, "dve", df, df[:, :], df, df[:, :], 0.0, float(NE * CAP - 1), ALU.max, ALU.min)
        CP(S, "dve", k.dest, k.dest[:, ti, :], df, df[:, :])
        TS(S, "dve", t8, t8[:, 8:9], t8, t8[:, 0:1], -1.0, None, ALU.mult)
        ACT(S, t8, t8[:, 10:14], t8, t8[:, 0:4], AF.Exp, bias=t8[:, 8:9], accum=t8[:, 9:10])
        S.op("dve", lambda e, t8=t8: e.reciprocal(out=t8[:, 14:15], in_=t8[:, 9:10]), reads=[t8], writes=[t8])
        TS(S, "dve", k.gates, k.gates[:, ti, :], t8, t8[:, 10:14], t8[:, 14:15], None, ALU.mult)
        for kk in range(4):
            S.dma("pool", lambda e, hrow=hrow, ti=ti, kk=kk: e.indirect_dma_start(
                out=k.XS.t.ap(), out_offset=bass.IndirectOffsetOnAxis(ap=k.dest[:, ti, kk:kk + 1], axis=0),
                in_=hrow[:, :], in_offset=None),
                reads=[hrow, k.dest, k.gates], writes=[k.XS], owner=hrow, group=(kk > 0))
    S.end_phase()
    if k.phases is not None and f"moe{l}_route_only" in k.phases:
        return
    S.begin_phase()
    ws = WStream(S, 4)
    psr = Rot([S.psum(f"ps{i}", [128, 512]) for i in range(8)])
    xrow = Rot([S.sbuf(f"xrow{i}", [128, D]) for i in range(2)])
    yrow = Rot([S.sbuf(f"yrow{i}", [128, D]) for i in range(2)])
    XT = S.sbuf("XsT", [128, 8, CAP], RT)
    AT = S.sbuf("actT", [128, 8, CAP], RT)
    gp = Rot([S.sbuf(f"gp{i}", [128, CAP // 2]) for i in range(2)])
    sg = Rot([S.sbuf(f"sg{i}", [128, CAP // 2]) for i in range(2)])
    up = Rot([S.sbuf(f"up{i}", [128, CAP // 2]) for i in range(2)])
    bdr = Rot([S.sbuf(f"bd{i}", [1, D]) for i in range(2)])
    o_bg, _ = PV[f"bg{l}"]
    o_bu, _ = PV[f"bu{l}"]
    HN = CAP // 2
    NRB = CAP // 128
    ones = C(k, "ones")
    for e_ in range(NE):
        bd = bdr.next()
        LOAD(S, "sp", bd, bd[:, :], k.b_dn, k.b_dn.t.ap()[l, e_:e_ + 1, :])
        for rb in range(NRB):
            xw = xrow.next()
            r0 = e_ * CAP + rb * 128
            LOAD(S, "sp", xw, xw[:, :], k.XS, k.XS.t.ap()[r0:r0 + 128, :])
            for half in range(2):
                ps = psr.next()
                for c4 in range(4):
                    c = half * 4 + c4
                    TRP(S, ps, ps[:, c4 * 128:(c4 + 1) * 128], xw, xw[:, c * 128:(c + 1) * 128], k.cs, ident)
                CP(S, "act" if half == 0 else "dve", XT, R(XT[:, half * 4:half * 4 + 4, rb * 128:(rb + 1) * 128]),
                   ps, ps[:, :].rearrange("p (c t) -> p c t", c=4))
        vgu = wview(k.w_gu.t.ap()[l, e_])
        for wc in range(4):
            w = ws.get(k.w_gu, vgu, 0, wc * 512)
            for f2 in range(2):
                fc = wc * 2 + f2
                base = f2 * 256
                for hf in range(2):
                    cs_ = slice(hf * HN, (hf + 1) * HN)
                    pg = psr.next()
                    pu = psr.next()
                    for kc in range(8):
                        MM(S, pg, pg[:, 0:HN], w, R(w[:, kc, base:base + 256:2]), XT, R(XT[:, kc, cs_]),
                           start=(kc == 0), stop=(kc == 7))
                    for kc in range(8):
                        MM(S, pu, pu[:, 0:HN], w, R(w[:, kc, base + 1:base + 256:2]), XT, R(XT[:, kc, cs_]),
                           start=(kc == 0), stop=(kc == 7))
                    g_ = gp.next()
                    s_ = sg.next()
                    u_ = up.next()
                    bgc = k.pvs[:, o_bg + e_ * 8 + fc:o_bg + e_ * 8 + fc + 1]
                    buc = k.pvs[:, o_bu + e_ * 8 + fc:o_bu + e_ * 8 + fc + 1]
                    TS(S, "dve", g_, g_[:, :], pg, pg[:, 0:HN], bgc, 7.0, ALU.add, ALU.min, xr=[k.pvs])
                    ACT(S, s_, s_[:, :], g_, g_[:, :], AF.Sigmoid, scale=1.702)
                    TS(S, "dve", u_, u_[:, :], pu, pu[:, 0:HN], buc, 7.0, ALU.add, ALU.min, xr=[k.pvs])
                    TS(S, "dve", u_, u_[:, :], u_, u_[:, :], -7.0, 1.0, ALU.max, ALU.add)
                    TT(S, "dve", g_, g_[:, :], g_, g_[:, :], s_, s_[:, :], ALU.mult)
                    TT(S, "dve", AT, R(AT[:, fc, cs_]), g_, g_[:, :], u_, u_[:, :], ALU.mult)
        vdn = wview(k.w_dn.t.ap()[l, e_])
        wd = [ws.get(k.w_dn, vdn, 0, oc * 512) for oc in range(2)]
        for rb in range(NRB):
            yw = yrow.next()
            for oc in range(2):
                ps = psr.next()
                for fc in range(8):
                    MM(S, ps, ps[:, :], AT, R(AT[:, fc, rb * 128:(rb + 1) * 128]), wd[oc], R(wd[oc][:, fc, :]),
                       start=(fc == 0), stop=False)
                MM(S, ps, ps[:, :], k.cs, ones[0:1, :], bd, bd[0:1, oc * 512:(oc + 1) * 512], start=False, stop=True)
                CP(S, "act", yw, yw[:, oc * 512:(oc + 1) * 512], ps, ps[:, :])
            r0 = e_ * CAP + rb * 128
            STORE(S, "act", k.YS, k.YS.t.ap()[r0:r0 + 128, :], yw, yw[:, :])
    S.end_phase()
    S.begin_phase()
    psr = Rot([S.psum(f"ps{i}", [128, 512]) for i in range(8)])
    hr = Rot([S.sbuf(f"hrow{i}", [128, D]) for i in range(2)])
    yk = Rot([S.sbuf(f"yk{i}", [128, D]) for i in range(8)])
    acc = Rot([S.sbuf(f"acc{i}", [128, D]) for i in range(2)])
    ofm = Rot([S.sbuf(f"ofm{i}", [128, 8, 128]) for i in range(2)])
    gB = S.sbuf("gB", [128, D])
    bB = S.sbuf("bB", [128, D])
    st_ = Rot([S.sbuf(f"st{i}", [128, 16]) for i in range(2)])
    eps_t = S.sbuf("eps_t", [128, 1])
    S.op("dve", lambda e: e.memset(eps_t[:, :], EPS), writes=[eps_t])
    LOAD(S, "sp", gB, gB[:, :], k.ln_ffn_g, k.ln_ffn_g.t.ap()[l:l + 1, :].broadcast_to([128, D]))
    LOAD(S, "sp", bB, bB[:, :], k.ln_ffn_b, k.ln_ffn_b.t.ap()[l:l + 1, :].broadcast_to([128, D]))
    for ti in range(NTL):
        hrow = hr.next()
        LOAD(S, "sp", hrow, hrow[:, :], HTOK, HTOK.t.ap()[ti * 128:(ti + 1) * 128, :])
        ys = []
        for kk in range(4):
            y = yk.next()
            S.dma("pool", lambda e, y=y, ti=ti, kk=kk: e.indirect_dma_start(
                out=y[:, :], out_offset=None, in_=k.YS.t.ap(),
                in_offset=bass.IndirectOffsetOnAxis(ap=k.dest[:, ti, kk:kk + 1], axis=0)),
                reads=[k.YS, k.dest], writes=[y], owner=y)
            ys.append(y)
        a = acc.next()
        TS(S, "dve", a, a[:, :], ys[0], ys[0][:, :], k.gates[:, ti, 0:1], None, ALU.mult, xr=[k.gates])
        for kk in range(1, 4):
            STT(S, a, a[:, :], ys[kk], ys[kk][:, :], k.gates[:, ti, kk:kk + 1], a, a[:, :], ALU.mult, ALU.add,
                xr=[k.gates])
        STT(S, a, a[:, :], hrow, hrow[:, :], ALPHA, a, a[:, :], ALU.mult, ALU.add)
        s_ = st_.next()
        for hf in range(2):
            S.op("dve", lambda e, s_=s_, a=a, hf=hf: e.bn_stats(out=s_[:, hf * 6:(hf + 1) * 6],
                                                               in_=a[:, hf * 512:(hf + 1) * 512]),
                 reads=[a], writes=[s_])
        S.op("dve", lambda e, s_=s_: e.bn_aggr(out=s_[:, 12:14], in_=s_[:, 0:12]), reads=[s_], writes=[s_])
        ACT(S, s_, s_[:, 14:15], s_, s_[:, 13:14], AF.Sqrt, bias=eps_t[:, 0:1], xr=[eps_t])
        S.op("dve", lambda e, s_=s_: e.reciprocal(out=s_[:, 15:16], in_=s_[:, 14:15]), reads=[s_], writes=[s_])
        TS(S, "dve", a, a[:, :], a, a[:, :], s_[:, 12:13], s_[:, 15:16], ALU.subtract, ALU.mult, xr=[s_])
        TT(S, "dve", a, a[:, :], a, a[:, :], gB, gB[:, :], ALU.mult)
        TT(S, "dve", a, a[:, :], a, a[:, :], bB, bB[:, :], ALU.add)
        if OUT_TOK is not None:
            STORE(S, "act", OUT_TOK, OUT_TOK.t.ap()[ti * 128:(ti + 1) * 128, :], a, a[:, :])
        if HOUT_FM is not None:
            STORE(S, "act", k.HCTOK, k.HCTOK.t.ap()[ti * 128:(ti + 1) * 128, :], a, a[:, :])
            o = ofm.next()
            for half in range(2):
                ps = psr.next()
                for c4 in range(4):
                    c = half * 4 + c4
                    TRP(S, ps, ps[:, c4 * 128:(c4 + 1) * 128], a, a[:, c * 128:(c + 1) * 128], k.cs, ident)
                CP(S, "act", o, o[:, half * 4:half * 4 + 4, :], ps, ps[:, :].rearrange("p (c t) -> p c t", c=4))
            STORE(S, "act", HOUT_FM, fmview(HOUT_FM)[:, :, ti * 128:(ti + 1) * 128], o, o[:, :, :])
    S.end_phase()


def phase_ssd_a(k, HIN):
    S = k.S
    S.begin_phase()
    N = 512
    ws = WStream(S, 3)
    psr = Rot([S.psum(f"ps{i}", [128, 512]) for i in range(8)])
    xr = Rot([S.sbuf(f"xtr{i}", [128, 8, N], RT) for i in range(2)])
    zst = Rot([S.sbuf(f"zst{i}", [128, N]) for i in range(4)])
    ucr = Rot([S.sbuf(f"uc{i}", [128, 3 + N]) for i in range(4)])
    acr = Rot([S.sbuf(f"ac{i}", [128, N]) for i in range(4)])
    xst = Rot([S.sbuf(f"xst{i}", [128, N]) for i in range(4)])
    halo = S.sbuf("halo", [128, 32, 4])
    S.op("dve", lambda e: e.memset(halo[:, :, :], 0.0), writes=[halo])
    vin = wview(k.ssm_w_in.t.ap())
    hin = fmview(HIN)
    o_cw, _ = PV["conv_w"]
    zsd = k.ZSD.t.ap()
    xbd = k.XBCD.t.ap()
    for tb in range(T // N):
        xtr = xr.next()
        LOAD(S, "pool" if FAST else "sp", xtr, xtr[:, :, :], HIN, hin[:, :, tb * N:(tb + 1) * N])

        def ev_zs(oc, ps, pa, tb=tb):
            z_ = zst.next()
            ACT(S, z_, z_[:, :], ps, pa, AF.Silu)
            STORE(S, "act", k.ZSD, zsd[oc * 128:(oc + 1) * 128, tb * N:(tb + 1) * N], z_, z_[:, :])
        proj_fm(k, ws, psr, k.ssm_w_in, vin, xtr, lambda kc, xtr=xtr: xtr[:, kc, :], N, 16, ev_zs, col0=0)

        pend = []

        def flush():
            while pend:
                pend.pop(0)()

        def ev_u(c, ps, pa, tb=tb):
            uc = ucr.next()
            CP(S, "dve", uc, uc[:, 0:3], halo, halo[:, c, 0:3])
            CP(S, "act", uc, uc[:, 3:3 + N], ps, pa)
            flush()
            CP(S, "dve", halo, halo[:, c, 0:3], uc, uc[:, N:N + 3])
            acc = acr.next()
            wcol = lambda kk: k.pvs[:, o_cw + c * 4 + kk:o_cw + c * 4 + kk + 1]
            TS(S, "dve", acc, acc[:, :], uc, uc[:, 0:N], wcol(0), P(k, "conv_b", c), ALU.mult, ALU.add, xr=[k.pvs])
            for kk in range(1, 4):
                STT(S, acc, acc[:, :], uc, uc[:, kk:kk + N], wcol(kk), acc, acc[:, :], ALU.mult, ALU.add, xr=[k.pvs])

            def fin(c=c, acc=acc, tb=tb):
                x_ = xst.next()
                ACT(S, x_, x_[:, :], acc, acc[:, :], AF.Silu)
                STORE(S, "act", k.XBCD, xbd[c * 128:(c + 1) * 128, tb * N:(tb + 1) * N], x_, x_[:, :])
            pend.append(fin)
        proj_fm(k, ws, psr, k.ssm_w_in, vin, xtr, lambda kc, xtr=xtr: xtr[:, kc, :], N, 32, ev_u, col0=2048)
        flush()
    S.end_phase()


def phase_ssd_b(k, HIN, HOUT):
    S = k.S
    S.begin_phase()
    common_small(k)
    N = 128
    ws = WStream(S, 2)
    psr = Rot([S.psum(f"ps{i}", [128, 512]) for i in range(4)])
    SPS = S.psum("SPS", [128, 2048])
    xr = Rot([S.sbuf(f"xt{i}", [128, 8, N]) for i in range(2)])
    wdt = S.sbuf("wdt", [128, 8, 32])
    rowb = S.sbuf("rowb", [128, 64])
    Abc = S.sbuf("Abc", [128, 32])
    one_t = S.sbuf("one_t", [128, 1])
    zsr = Rot([S.sbuf(f"ZS{i}", [128, 16, N]) for i in range(2)])
    xbr = Rot([S.sbuf(f"XBC{i}", [128, 32, N]) for i in range(2)])
    XD = S.sbuf("XD", [128, 2048])
    XDD = S.sbuf("XDD", [128, 2048])
    BTK = S.sbuf("BTK", [128, 8, 128])
    tk = Rot([S.sbuf(f"tk{i}", [128, 8, 32]) for i in range(2)])
    cdr = Rot([S.sbuf(f"cdr{i}", [128, 64]) for i in range(2)])
    arr = Rot([S.sbuf(f"ar{i}", [128, 4, 128], RT) for i in range(2)])
    cbr = Rot([S.sbuf(f"cbm{i}", [128, 128]) for i in range(2)])
    mtr = Rot([S.sbuf(f"mt{i}", [128, 4, 128]) for i in range(2)])
    cer = Rot([S.sbuf(f"ce{i}", [128, 4, 128]) for i in range(2)])
    H = [S.sbuf(f"H{i}", [128, 2048]) for i in range(2)]
    Yt = S.sbuf("Yt", [128, 16, N])
    SQ = S.sbuf("SQ", [128, 16, N])
    rst = S.sbuf("rst", [128, 8, N])
    YR = S.sbuf("YR", [128, 16, N], RT)
    xtk = S.sbuf("xtk", [128, D])
    ZT = S.sbuf("ZT", [128, D])
    gB = S.sbuf("gBm", [128, D])
    bB = S.sbuf("bBm", [128, D])
    st_ = Rot([S.sbuf(f"stm{i}", [128, 16]) for i in range(2)])
    LOAD(S, "sp", gB, gB[:, :], k.ln_mix_g, k.ln_mix_g.t.ap()[1:2, :].broadcast_to([128, D]))
    LOAD(S, "sp", bB, bB[:, :], k.ln_mix_b, k.ln_mix_b.t.ap()[1:2, :].broadcast_to([128, D]))
    ident = C(k, "ident")
    triinc, blockones, strictlt, mask01 = C(k, "triinc"), C(k, "blockones"), C(k, "strictlt"), C(k, "mask01")
    csel = [C(k, "csel0"), C(k, "csel1")]
    ones, ones256 = C(k, "ones"), C(k, "ones256")
    vin = wview(k.ssm_w_in.t.ap())
    vout = wview(k.ssm_w_out.t.ap())
    hin = fmview(HIN)
    hout = fmview(HOUT)
    o_cw, _ = PV["conv_w"]
    LOAD(S, "sp", wdt, wdt[:, :, :], k.ssm_w_in, vin[:, :, 6144:6176])
    LOAD(S, "sp", rowb, rowb[:, :], k.rowp, k.rowp.t.ap()[0:1, :].broadcast_to([128, 64]))
    S.op("dve", lambda e: e.memset(one_t[:, :], 1.0), writes=[one_t])
    S.op("dve", lambda e: e.memset(H[0][:, :], 0.0), writes=[H[0]])
    ACT(S, Abc, Abc[:, :], rowb, rowb[:, 32:64], AF.Exp)
    TS(S, "dve", Abc, Abc[:, :], Abc, Abc[:, :], -1.0, None, ALU.mult)
    for tb in range(T // N):
        xt = xr.next()
        LOAD(S, "sp", xt, xt[:, :, :], HIN, hin[:, :, tb * N:(tb + 1) * N])
        ZS = zsr.next()
        XBC = xbr.next()
        for q4 in range(2):
            LOAD(S, "sp", ZS, ZS[:, q4 * 8:(q4 + 1) * 8, :], k.ZSD,
                 fmview(k.ZSD)[:, q4 * 8:(q4 + 1) * 8, tb * N:(tb + 1) * N], group=(q4 > 0))
        for q4 in range(4):
            LOAD(S, "sp", XBC, XBC[:, q4 * 8:(q4 + 1) * 8, :], k.XBCD,
                 fmview(k.XBCD)[:, q4 * 8:(q4 + 1) * 8, tb * N:(tb + 1) * N], group=(q4 > 0))
        t_ = tk.next()
        dt, a_, acs, dte, w1, tmp = (t_[:, i, :] for i in range(6))
        ps = psr.next()
        for kc in range(8):
            MM(S, ps, ps[:, 0:32], xt, xt[:, kc, :], wdt, wdt[:, kc, :], start=(kc == 0), stop=(kc == 7))
        TT(S, "dve", t_, tmp, ps, ps[:, 0:32], rowb, rowb[:, 0:32], ALU.add)
        ACT(S, t_, tmp, t_, tmp, AF.Exp)
        ACT(S, t_, dt, t_, tmp, AF.Ln, bias=one_t[:, 0:1], xr=[one_t])
        TT(S, "dve", t_, a_, t_, dt, Abc, Abc[:, :], ALU.mult)
        pa = psr.next()
        MM(S, pa, pa[:, 0:32], k.cs, triinc, t_, a_)
        MM(S, pa, pa[:, 32:64], k.cs, blockones, t_, a_)
        MM(S, pa, pa[:, 64:96], k.cs, csel[0], t_, a_)
        MM(S, pa, pa[:, 96:128], k.cs, csel[1], t_, a_)
        CP(S, "act", t_, acs, pa, pa[:, 0:32])
        TT(S, "dve", t_, dte, pa, pa[:, 32:64], t_, acs, ALU.subtract)
        ACT(S, t_, dte, t_, dte, AF.Exp)
        TT(S, "dve", t_, w1, t_, dt, t_, dte, ALU.mult)
        cd = cdr.next()
        ACT(S, cd, cd[:, :], pa, pa[:, 64:128], AF.Exp)
        for q in range(4):
            ps = psr.next()
            for j in range(4):
                TRP(S, ps, ps[:, j * 128:(j + 1) * 128], XBC, XBC[:, q * 4 + j, :], k.cs, ident)
            pv3 = ps[:, :].rearrange("p (h d) -> p h d", h=8)
            for dst, sc in ((XD, dt), (XDD, w1)):
                TT(S, "dve", dst, dst[:, q * 512:(q + 1) * 512].rearrange("p (h d) -> p h d", h=8), ps, pv3,
                   t_, sc[:, q * 8:(q + 1) * 8].unsqueeze(2).broadcast_to([128, 8, 64]), ALU.mult)
        for q in range(2):
            ps = psr.next()
            for j in range(4):
                TRP(S, ps, ps[:, j * 128:(j + 1) * 128], XBC, XBC[:, 16 + q * 4 + j, :], k.cs, ident)
            CP(S, "act", BTK, BTK[:, q * 4:(q + 1) * 4, :], ps, ps[:, :].rearrange("p (g n) -> p g n", g=4))
        H0, H1 = H[0], H[1]

        def states(cp):
            for g in range(8):
                MM(S, SPS, SPS[:, g * 256:(g + 1) * 256], BTK, BTK[cp * 64:(cp + 1) * 64, g, :],
                   XDD, XDD[cp * 64:(cp + 1) * 64, g * 256:(g + 1) * 256])

        def hupd(dst, src, cp):
            TT(S, "dve", dst, dst[:, :].rearrange("p (h d) -> p h d", h=32), src,
               src[:, :].rearrange("p (h d) -> p h d", h=32), cd,
               cd[:, cp * 32:(cp + 1) * 32].unsqueeze(2).broadcast_to([128, 32, 64]), ALU.mult)
            TT(S, "dve", dst, dst[:, :], SPS, SPS[:, :], dst, dst[:, :], ALU.add)
        states(0)
        hupd(H1, H0, 0)
        Hc = [H0, H1]
        def stage1(g):
            ar = arr.next()
            TT(S, "dve", ar, ar[:, :, :], k.cs, triinc.unsqueeze(1).broadcast_to([128, 4, 128]), t_,
               a_[:, 4 * g:4 * g + 4].unsqueeze(2).broadcast_to([128, 4, 128]), ALU.mult)
            ar2 = ar[:, :, :].rearrange("p h i -> p (h i)")
            pcb = psr.next()
            MM(S, pcb, pcb[:, 0:128], XBC, XBC[:, 16 + g, :], XBC, XBC[:, 24 + g, :])
            pseg = psr.next()
            MM(S, pseg, pseg[:, :], k.csr, k.csr[:, 128:256], ar, ar2)
            pacs = psr.next()
            MM(S, pacs, pacs[:, :], k.csr, k.csr[:, 256:384], ar, ar2)
            cbm = cbr.next()
            TT(S, "dve", cbm, cbm[:, :], pcb, pcb[:, 0:128], k.cs, mask01, ALU.mult)
            mt = mtr.next()
            ACT(S, mt, mt[:, :, :], pseg, pseg[:, :].rearrange("p (h i) -> p h i", h=4), AF.Exp)
            ce = cer.next()
            ACT(S, ce, ce[:, :, :], pacs, pacs[:, :].rearrange("p (h i) -> p h i", h=4), AF.Exp)
            TT(S, "dve", mt, mt[:, :, :], mt, mt[:, :, :], cbm, cbm[:, :].unsqueeze(1).broadcast_to([128, 4, 128]),
               ALU.mult)
            TT(S, "dve", ce, ce[:, :, :], ce, ce[:, :, :], XBC,
               XBC[:, 24 + g, :].unsqueeze(1).broadcast_to([128, 4, 128]), ALU.mult)
            return mt, ce

        def stage2(g, mt, ce):
            py = psr.next()
            for r in range(4):
                hd = 4 * g + r
                prt = slice((r % 2) * 64, (r % 2) * 64 + 64)
                c0 = (r // 2) * 128
                MM(S, py, py[prt, c0:c0 + 128], XD, XD[:, hd * 64:(hd + 1) * 64], mt, mt[:, r, :],
                   start=True, stop=False)
                for cp in range(2):
                    MM(S, py, py[prt, c0 + cp * 64:c0 + (cp + 1) * 64], Hc[cp], Hc[cp][:, hd * 64:(hd + 1) * 64],
                       ce, ce[:, r, cp * 64:(cp + 1) * 64], start=False, stop=(cp == 1))
            for j in range(2):
                c = 2 * g + j
                STT(S, Yt, Yt[:, c, :], XBC, XBC[:, c, :], P(k, "dskip", c), py, py[:, j * 128:(j + 1) * 128],
                    ALU.mult, ALU.add, xr=[k.pvs])
        cur = stage1(0)
        for g in range(8):
            nxt = stage1(g + 1) if g < 7 else None
            stage2(g, *cur)
            cur = nxt
        states(1)
        hupd(H0, H1, 1)
        TT(S, "dve", Yt, Yt[:, :, :], Yt, Yt[:, :, :], ZS, ZS[:, :, :], ALU.mult)
        ACT(S, SQ, SQ[:, :, :], Yt, Yt[:, :, :], AF.Square)
        for b2 in range(2):
            pm = psr.next()
            for g4 in range(4):
                g = b2 * 4 + g4
                for j in range(2):
                    MM(S, pm, pm[:, g4 * 128:(g4 + 1) * 128], k.cs, ones256, SQ, SQ[:, 2 * g + j, :],
                       start=(j == 0), stop=(j == 1))
            ACT(S, rst, rst[:, b2 * 4:(b2 + 1) * 4, :], pm, pm[:, :].rearrange("p (g t) -> p g t", g=4), AF.Sqrt,
                bias=k.eps_t[:, 0:1], xr=[k.eps_t])
        S.op("dve", lambda e: e.reciprocal(out=rst[:, :, :], in_=rst[:, :, :]), reads=[rst], writes=[rst])
        Yt4 = Yt[:, :, :].rearrange("p (g j) t -> p g j t", j=2)
        TT(S, "dve", Yt, Yt4, Yt, Yt4, rst, rst[:, :, :].unsqueeze(2).broadcast_to([128, 8, 2, N]), ALU.mult)
        TT(S, "dve", YR, YR[:, :, :], Yt, Yt[:, :, :], k.pvs,
           P(k, "norm_g").unsqueeze(2).broadcast_to([128, 16, N]), ALU.mult)

        LOAD(S, "sp", xtk, xtk[:, :], k.HCTOK, k.HCTOK.t.ap()[tb * N:(tb + 1) * N, :])
        for wc in range(2):
            ps = psr.next()
            for kh in range(2):
                w = ws.get(k.ssm_w_out, vout, kh * 8, wc * 512)
                for kc in range(8):
                    MM(S, ps, ps[:, :], YR, YR[:, kh * 8 + kc, :], w, w[:, kc, :],
                       start=(kh == 0 and kc == 0), stop=(kh == 1 and kc == 7))
            STT(S, ZT, ZT[:, wc * 512:(wc + 1) * 512], xtk, xtk[:, wc * 512:(wc + 1) * 512], ALPHA, ps, ps[:, :],
                ALU.mult, ALU.add)
        s_ = st_.next()
        for hf in range(2):
            S.op("dve", lambda e, s_=s_, hf=hf: e.bn_stats(out=s_[:, hf * 6:(hf + 1) * 6],
                                                          in_=ZT[:, hf * 512:(hf + 1) * 512]),
                 reads=[ZT], writes=[s_])
        S.op("dve", lambda e, s_=s_: e.bn_aggr(out=s_[:, 12:14], in_=s_[:, 0:12]), reads=[s_], writes=[s_])
        ACT(S, s_, s_[:, 14:15], s_, s_[:, 13:14], AF.Sqrt, bias=k.eps_t[:, 0:1], xr=[k.eps_t])
        S.op("dve", lambda e, s_=s_: e.reciprocal(out=s_[:, 15:16], in_=s_[:, 14:15]), reads=[s_], writes=[s_])
        TS(S, "dve", ZT, ZT[:, :], ZT, ZT[:, :], s_[:, 12:13], s_[:, 15:16], ALU.subtract, ALU.mult, xr=[s_])
        TT(S, "dve", ZT, ZT[:, :], ZT, ZT[:, :], gB, gB[:, :], ALU.mult)
        TT(S, "dve", ZT, ZT[:, :], ZT, ZT[:, :], bB, bB[:, :], ALU.add)
        for half in range(2):
            ps = psr.next()
            for c4 in range(4):
                c = half * 4 + c4
                TRP(S, ps, ps[:, c4 * 128:(c4 + 1) * 128], ZT, ZT[:, c * 128:(c + 1) * 128], k.cs, ident)
            CP(S, "act", SQ, SQ[:, half * 4:half * 4 + 4, :], ps, ps[:, :].rearrange("p (c t) -> p c t", c=4))
        STORE(S, "act", HOUT, hout[:, :, tb * N:(tb + 1) * N], SQ, SQ[:, 0:8, :])
    S.end_phase()


_NC_CACHE = {}


def make_in_maps(inp):
    cst = make_consts()
    pv = make_pv(inp)
    rowp = np.concatenate([inp["ssm_dt_bias"][0], inp["ssm_a_log"][0]])[None, :].astype(np.float32)
    shared = {
        "cst": cst, "pv": pv, "rowp": rowp,
        "pool_w_in": inp["pool_w_in"][0], "pool_w_grp": inp["pool_w_grp"][0], "pool_w_out": inp["pool_w_out"][0],
        "ssm_w_in": inp["ssm_w_in"][0], "ssm_w_out": inp["ssm_w_out"][0],
        "xa_wq": inp["xa_wq"], "xa_wk": inp["xa_wk"], "xa_wv": inp["xa_wv"], "xa_wo": inp["xa_wo"],
        "moe_w_router": inp["moe_w_router"], "moe_b_router": inp["moe_b_router"],
        "moe_w_gate_up": inp["moe_w_gate_up"], "moe_w_down": inp["moe_w_down"], "moe_b_down": inp["moe_b_down"],
        "ln_ffn_g": inp["ln_ffn_g"], "ln_ffn_b": inp["ln_ffn_b"],
        "ln_mix_g": inp["ln_mix_g"], "ln_mix_b": inp["ln_mix_b"],
    }
    shared = {kk: np.ascontiguousarray(np.asarray(vv, dtype=np.float32)) for kk, vv in shared.items()}
    maps = []
    for b in range(8):
        m = dict(shared)
        m["xT"] = np.ascontiguousarray(np.asarray(inp["x"][b]).T)
        m["memT"] = np.ascontiguousarray(np.asarray(inp["mem"][b]).T)
        maps.append(m)
    return maps


def kernel(**inputs):
    inp = {kk: np.asarray(vv) for kk, vv in inputs.items()}
    if "nc" not in _NC_CACHE:
        _NC_CACHE["nc"] = build()
    nc = _NC_CACHE["nc"]
    maps = make_in_maps(inp)
    res = run_bass_kernel_spmd(nc, maps, core_ids=list(range(8)))
    return np.stack([res.results[b]["out"] for b in range(8)], axis=0).astype(np.float32)
```

```python
from contextlib import ExitStack
import numpy as np
import concourse.bass as bass
import concourse.mybir as mybir
from concourse.bass_utils import run_bass_kernel_spmd

F32 = mybir.dt.float32
F32R = mybir.dt.float32r
FAST = True


RT = F32R if FAST else F32


def R(ap):
    return ap


I32 = mybir.dt.int32
AF = mybir.ActivationFunctionType
ALU = mybir.AluOpType
AX = mybir.AxisListType

T = 4096
D = 1024
NTL = 32
ALPHA = float(4 ** 0.25)
EPS = 1e-5
CAP = 768
NE = 32
DEBUG = False

ENGS = ("pe", "act", "dve", "pool", "sp")


class Buf:
    __slots__ = ("name", "t", "w", "r", "dsem", "local")

    def __init__(self, name, t):
        self.name = name
        self.t = t
        self.w = {}
        self.r = {}
        self.dsem = None
        self.local = False

    def __getitem__(self, idx):
        return self.t[idx]


class Rot:
    def __init__(self, bufs):
        self.b = bufs
        self.i = 0

    def next(self):
        b = self.b[self.i % len(self.b)]
        self.i += 1
        return b


class Sched:
    def __init__(self, nc, stack):
        self.nc = nc
        self.stack = stack
        self.sems = {}
        self.cnt = {}
        for e in ENGS:
            self._mksem(e)
        self.seen = {e: {} for e in ENGS}
        self.prog = {e: [] for e in ENGS}
        self.nbuf = 0
        self.phase_stack = None
        self.free_dsems = []

    def _mksem(self, key):
        h = self.stack.enter_context(self.nc.semaphore(f"s_{key}"))
        self.sems[key] = h
        self.cnt[key] = 0

    def begin_phase(self):
        self.phase_stack = ExitStack()
        self.phase_dsems = []

    def end_phase(self):
        self.barrier()
        self.phase_stack.close()
        self.phase_stack = None
        self.free_dsems.extend(self.phase_dsems)
        self.phase_dsems = []

    def sbuf(self, name, shape, dtype=F32, glob=False):
        st = self.stack if (glob or self.phase_stack is None) else self.phase_stack
        self.uid = getattr(self, "uid", 0) + 1
        name = f"{name}_u{self.uid}"
        t = st.enter_context(self.nc.sbuf_tensor(name, list(shape), dtype))
        b = Buf(name, t)
        b.local = st is not self.stack
        return b

    def psum(self, name, shape, dtype=F32):
        st = self.stack if self.phase_stack is None else self.phase_stack
        self.uid = getattr(self, "uid", 0) + 1
        name = f"{name}_u{self.uid}"
        t = st.enter_context(self.nc.psum_tensor(name, list(shape), dtype))
        return Buf(name, t)

    def _deps(self, e, reads, writes):
        need = {}
        for b in reads:
            for k, v in b.w.items():
                if need.get(k, 0) < v:
                    need[k] = v
        for b in writes:
            for k, v in b.w.items():
                if need.get(k, 0) < v:
                    need[k] = v
            for k, v in b.r.items():
                if need.get(k, 0) < v:
                    need[k] = v
        waits = []
        seen = self.seen[e]
        for k, v in need.items():
            if e == "pe" and k == "pe":
                continue
            if seen.get(k, 0) < v:
                seen[k] = v
                waits.append((k, v))
        return waits

    def op(self, e, fn, reads=(), writes=()):
        waits = self._deps(e, reads, writes)
        self.cnt[e] += 1
        v = self.cnt[e]
        self.prog[e].append((waits, fn, (e, 1)))
        for b in writes:
            b.w[e] = v
        for b in reads:
            b.r[e] = v

    def dma(self, q, fn, reads=(), writes=(), owner=None, group=False):
        if owner is None:
            owner = writes[0]
        if owner.dsem is None:
            if self.free_dsems:
                owner.dsem = self.free_dsems.pop()
            else:
                self.nbuf += 1
                owner.dsem = f"d{self.nbuf}"
                self._mksem(owner.dsem)
            if owner.local:
                self.phase_dsems.append(owner.dsem)
        k = owner.dsem
        waits = self._deps(q, reads, writes)
        if (not group) and self.cnt[k] > 0 and self.seen[q].get(k, 0) < self.cnt[k]:
            self.seen[q][k] = self.cnt[k]
            waits.append((k, self.cnt[k]))
        self.cnt[k] += 16
        v = self.cnt[k]
        self.prog[q].append((waits, fn, (k, 16)))
        for b in writes:
            b.w[k] = v
        for b in reads:
            b.r[k] = v

    def barrier(self):
        for e in ENGS:
            waits = []
            for k, v in self.cnt.items():
                if v > 0 and self.seen[e].get(k, 0) < v and not (k == e):
                    self.seen[e][k] = v
                    waits.append((k, v))
            if waits:
                self.prog[e].append((waits, None, None))

    def emit(self):
        nc = self.nc
        sems = self.sems
        prog = self.prog
        with nc.Block() as block:
            def mk(ekey):
                def body(engine):
                    for waits, fn, inc in prog[ekey]:
                        for k, v in waits:
                            engine.wait_ge(sems[k], v)
                        if fn is not None:
                            fn(engine).then_inc(sems[inc[0]], inc[1])
                return body
            block.tensor(mk("pe"))
            block.scalar(mk("act"))
            block.vector(mk("dve"))
            block.gpsimd(mk("pool"))
            block.sync(mk("sp"))


def MM(S, ob, o, lb, l, rb, r, start=True, stop=True):
    S.op("pe", lambda e: e.matmul(o, l, r, start=start, stop=stop), reads=[lb, rb], writes=[ob])


def TRP(S, ob, o, ib, i, idb, idap):
    S.op("pe", lambda e: e.transpose(o, i, idap), reads=[ib, idb], writes=[ob])


def TT(S, eng, ob, o, ab, a, bb, b, op):
    S.op(eng, lambda e: e.tensor_tensor(out=o, in0=a, in1=b, op=op), reads=[ab, bb], writes=[ob])


def TS(S, eng, ob, o, ab, a, s1, s2, op0, op1=None, xr=()):
    if op1 is None:
        S.op(eng, lambda e: e.tensor_scalar(out=o, in0=a, scalar1=s1, scalar2=None, op0=op0),
             reads=[ab, *xr], writes=[ob])
    else:
        S.op(eng, lambda e: e.tensor_scalar(out=o, in0=a, scalar1=s1, scalar2=s2, op0=op0, op1=op1),
             reads=[ab, *xr], writes=[ob])


def STT(S, ob, o, ab, a, sc, bb, b, op0, op1, xr=(), accum=None, xw=()):
    if accum is None:
        S.op("dve", lambda e: e.scalar_tensor_tensor(out=o, in0=a, scalar=sc, in1=b, op0=op0, op1=op1),
             reads=[ab, bb, *xr], writes=[ob])
    else:
        S.op("dve", lambda e: e.scalar_tensor_tensor(out=o, in0=a, scalar=sc, in1=b, op0=op0, op1=op1,
                                                     accum_out=accum),
             reads=[ab, bb, *xr], writes=[ob, *xw])


def ACT(S, ob, o, ab, a, func, bias=None, scale=None, xr=(), accum=None, xw=()):
    kw = {}
    if bias is not None:
        kw["bias"] = bias
    if scale is not None:
        kw["scale"] = scale
    if accum is not None:
        kw["accum_out"] = accum
    S.op("act", lambda e: e.activation(out=o, in_=a, func=func, **kw), reads=[ab, *xr], writes=[ob, *xw])


def CP(S, eng, ob, o, ab, a):
    if eng == "act":
        S.op("act", lambda e: e.activation(out=o, in_=a, func=AF.Identity), reads=[ab], writes=[ob])
    else:
        S.op(eng, lambda e: e.tensor_copy(out=o, in_=a), reads=[ab], writes=[ob])


def LOAD(S, q, db, d, sb, s, group=False):
    S.dma(q, lambda e: e.dma_start(out=d, in_=s), reads=[sb], writes=[db], owner=db, group=group)


def STORE(S, q, db, d, sb, s):
    S.dma(q, lambda e: e.dma_start(out=d, in_=s), reads=[sb], writes=[db], owner=sb)


CST = {}


def _cst_layout():
    off = 0
    for name, n in [("ident", 128), ("onesD", 128), ("ones", 128), ("ones256", 128), ("ustrict", 128),
                    ("triinc", 128), ("blockones", 128), ("strictlt", 128), ("mask01", 128),
                    ("csel0", 128), ("csel1", 128), ("rcnt", 64), ("iotaec", 32)]:
        CST[name] = (off, n)
        off += n
    return off


NCST = _cst_layout()


def make_consts():
    c = np.zeros((128, NCST), np.float32)
    i = np.arange(128)
    ch = i // 64

    def put(name, arr):
        o, n = CST[name]
        c[:, o:o + n] = arr
    put("ident", np.eye(128))
    put("onesD", np.full((128, 128), 1.0 / D))
    put("ones", np.ones((128, 128)))
    put("ones256", np.full((128, 128), 1.0 / 256))
    put("ustrict", (i[:, None] < i[None, :]))
    same = ch[:, None] == ch[None, :]
    put("triinc", (i[:, None] <= i[None, :]) & same)
    put("blockones", same)
    put("strictlt", (i[None, :] < i[:, None]) & same)
    put("mask01", (i[:, None] <= i[None, :]) & same)
    put("csel0", np.repeat((i < 64)[:, None], 128, 1))
    put("csel1", np.repeat((i >= 64)[:, None], 128, 1))
    rc = np.zeros((128, 4, 16))
    for g, w in enumerate((2, 4, 8, 16)):
        rc[:, g, :] = 1.0 / np.minimum(np.arange(16) + 1, w)
    put("rcnt", rc.reshape(128, 64))
    put("iotaec", np.repeat((np.arange(32) * CAP)[None, :], 128, 0))
    return c


PV = {}


def _pv_layout():
    off = 0
    items = [("pool_scale", 8)]
    for l in range(2):
        items += [(f"ln_mix_g{l}", 8), (f"ln_mix_b{l}", 8), (f"ln_xa_g{l}", 8), (f"ln_xa_b{l}", 8)]
    items += [("conv_w", 128), ("conv_b", 32), ("norm_g", 16), ("dskip", 16)]
    for l in range(2):
        items += [(f"bg{l}", 256), (f"bu{l}", 256)]
    for name, n in items:
        PV[name] = (off, n)
        off += n
    return off


NPV = _pv_layout()


def fm(v):
    return np.ascontiguousarray(v.reshape(-1, 128).T)


def make_pv(inp):
    p = np.zeros((128, NPV), np.float32)

    def put(name, arr):
        o, n = PV[name]
        p[:, o:o + n] = arr.reshape(128, n)
    put("pool_scale", fm(inp["pool_scale"][0]))
    for l in range(2):
        put(f"ln_mix_g{l}", fm(inp["ln_mix_g"][l]))
        put(f"ln_mix_b{l}", fm(inp["ln_mix_b"][l]))
        put(f"ln_xa_g{l}", fm(inp["ln_xa_g"][l]))
        put(f"ln_xa_b{l}", fm(inp["ln_xa_b"][l]))
    cw = inp["ssm_conv_w"][0]
    put("conv_w", np.stack([fm(cw[k]) for k in range(4)], axis=2))
    put("conv_b", fm(inp["ssm_conv_b"][0]))
    put("norm_g", fm(inp["ssm_norm_g"][0]))
    put("dskip", fm(np.repeat(inp["ssm_d"][0], 64)))
    for l in range(2):
        b = inp["moe_b_gate_up"][l]
        put(f"bg{l}", np.stack([fm(b[e, 0::2]) for e in range(NE)], axis=1))
        put(f"bu{l}", np.stack([fm(b[e, 1::2]) for e in range(NE)], axis=1))
    return p


class K:
    pass


def build(phases=None, dbg=False):
    nc = bass.Bass("TRN2", target_bir_lowering=False)
    k = K()
    k.nc = nc
    k.phases = phases

    def din(name, shape, dtype=F32):
        return Buf(name, nc.dram_tensor(name, list(shape), dtype, kind="ExternalInput"))

    def dscr(name, shape, dtype=F32):
        return Buf(name, nc.dram_tensor(name, list(shape), dtype,
                                        kind="ExternalOutput" if dbg else "Internal"))

    k.xT = din("xT", [D, T])
    k.memT = din("memT", [D, 256])
    k.cst = din("cst", [128, NCST])
    k.pv = din("pv", [128, NPV])
    k.pool_w_in = din("pool_w_in", [D, D])
    k.pool_w_grp = din("pool_w_grp", [4, 256, 256])
    k.pool_w_out = din("pool_w_out", [D, D])
    k.ssm_w_in = din("ssm_w_in", [D, 6176])
    k.ssm_w_out = din("ssm_w_out", [2048, D])
    k.rowp = din("rowp", [1, 64])
    k.xa_wq = din("xa_wq", [2, D, D])
    k.xa_wk = din("xa_wk", [2, D, D])
    k.xa_wv = din("xa_wv", [2, D, D])
    k.xa_wo = din("xa_wo", [2, D, D])
    k.w_router = din("moe_w_router", [2, D, NE])
    k.b_router = din("moe_b_router", [2, NE])
    k.w_gu = din("moe_w_gate_up", [2, NE, D, 2 * D])
    k.w_dn = din("moe_w_down", [2, NE, D, D])
    k.b_dn = din("moe_b_down", [2, NE, D])
    k.ln_mix_g = din("ln_mix_g", [2, D])
    k.ln_mix_b = din("ln_mix_b", [2, D])
    k.ln_ffn_g = din("ln_ffn_g", [2, D])
    k.ln_ffn_b = din("ln_ffn_b", [2, D])
    k.out = Buf("out", nc.dram_tensor("out", [T, D], F32, kind="ExternalOutput"))
    k.HA = dscr("HA", [D, T])
    k.HB = dscr("HB", [D, T])
    k.HC = dscr("HC", [D, T])
    k.HTOK = dscr("HTOK", [T, D])
    k.HCTOK = dscr("HCTOK", [T, D])
    k.ZSD = dscr("ZSD", [2048, T])
    k.XBCD = dscr("XBCD", [4096, T])
    k.XS = dscr("XS", [NE * CAP, D])
    k.YS = dscr("YS", [NE * CAP, D])

    with ExitStack() as st:
        S = Sched(nc, st)
        k.S = S
        k.cs = S.sbuf("cst_s", [128, NCST], glob=True)
        k.pvs = S.sbuf("pv_s", [128, NPV], glob=True)
        k.dest = S.sbuf("dest_i", [128, NTL, 4], I32, glob=True)
        k.gates = S.sbuf("gates", [128, NTL, 4], glob=True)
        LOAD(S, "sp", k.cs, k.cs[:, :], k.cst, k.cst.t.ap())
        k.csr = S.sbuf("csr_s", [128, 384], RT, glob=True)
        for ci, cn in enumerate(("onesD", "strictlt", "ones")):
            LOAD(S, "pool" if FAST else "sp", k.csr, k.csr[:, ci * 128:(ci + 1) * 128], k.cst,
                 k.cst.t.ap()[:, CST[cn][0]:CST[cn][0] + 128], group=(ci > 0))
        LOAD(S, "sp", k.pvs, k.pvs[:, :], k.pv, k.pv.t.ap())

        def want(p):
            return phases is None or p in phases

        if want("pool0"):
            phase_pool(k, k.xT, k.HA)
        if want("xa0"):
            phase_xattn(k, 0, k.HA, k.HB, k.HTOK)
        if want("moe0"):
            phase_moe(k, 0, k.HB, k.HTOK, k.HC, None)
        if want("ssd1"):
            phase_ssd_a(k, k.HC)
            phase_ssd_b(k, k.HC, k.HA)
        if want("xa1"):
            phase_xattn(k, 1, k.HA, k.HB, k.HTOK)
        if want("moe1"):
            phase_moe(k, 1, k.HB, k.HTOK, None, k.out)
        S.barrier()
        S.emit()
    return nc


def C(k, name):
    o, n = CST[name]
    return k.cs[:, o:o + n]


def P(k, name, j=None):
    o, n = PV[name]
    if j is None:
        return k.pvs[:, o:o + n]
    return k.pvs[:, o + j:o + j + 1]


def fmview(buf):
    return buf.t.ap().rearrange("(c p) t -> p c t", p=128)


def wview(ap2d):
    return ap2d.rearrange("(c p) f -> p c f", p=128)


class WStream:
    def __init__(self, S, n=4):
        self.S = S
        self.rot = Rot([S.sbuf(f"wt{i}", [128, 8, 512], RT) for i in range(n)])

    def get(self, wbuf, view, k0, c0, ncols=512, nk=8):
        b = self.rot.next()
        LOAD(self.S, "pool" if FAST else "sp", b, R(b[:, 0:nk, 0:ncols]), wbuf, view[:, k0:k0 + nk, c0:c0 + ncols])
        return b


def proj_fm(k, ws, psr, wbuf, view, xb, x_of_kc, N, n_oc, evac, kchunks=8, col0=0):
    S = k.S
    for wc in range((n_oc + 3) // 4):
        wts = [ws.get(wbuf, view, kh * 8, col0 + wc * 512) for kh in range(kchunks // 8)]
        for o4 in range(min(4, n_oc - wc * 4)):
            oc = wc * 4 + o4
            ps = psr.next()
            for kc in range(kchunks):
                MM(S, ps, ps[:, 0:N], wts[kc // 8], R(wts[kc // 8][:, kc % 8, o4 * 128:(o4 + 1) * 128]),
                   xb, R(x_of_kc(kc)), start=(kc == 0), stop=(kc == kchunks - 1))
            evac(oc, ps, ps[:, 0:N])


def proj_fm16(k, ws, psr, wbuf, view, xb, x_of_kc, N, n_oc, evac):
    S = k.S
    for wc in range(n_oc // 4):
        pss = [psr.next() for _ in range(4)]
        for kh in range(2):
            w = ws.get(wbuf, view, kh * 8, wc * 512)
            for o4 in range(4):
                for kc in range(8):
                    MM(S, pss[o4], pss[o4][:, 0:N], w, w[:, kc, o4 * 128:(o4 + 1) * 128], xb, x_of_kc(kh * 8 + kc),
                       start=(kh == 0 and kc == 0), stop=(kh == 1 and kc == 7))
        for o4 in range(4):
            evac(wc * 4 + o4, pss[o4], pss[o4][:, 0:N])


def ln_fm(k, psr, Z, N, gname, bname, OUT, scr):
    S = k.S
    onesD = C(k, "onesD")
    pm = psr.next()
    for c in range(8):
        MM(S, pm, pm[:, 0:N], k.cs, onesD, Z, Z[:, c, 0:N], start=(c == 0), stop=(c == 7))
    mean = k.ln_mean
    CP(S, "act", mean, mean[:, 0:N], pm, pm[:, 0:N])
    TT(S, "dve", Z, Z[:, :, 0:N], Z, Z[:, :, 0:N], mean, mean[:, 0:N].unsqueeze(1).broadcast_to([128, 8, N]),
       ALU.subtract)
    ACT(S, scr, scr[:, :, 0:N], Z, Z[:, :, 0:N], AF.Square)
    pv = psr.next()
    for c in range(8):
        MM(S, pv, pv[:, 0:N], k.csr, k.csr[:, 0:128], scr, scr[:, c, 0:N], start=(c == 0), stop=(c == 7))
    rstd = k.ln_rstd
    ACT(S, rstd, rstd[:, 0:N], pv, pv[:, 0:N], AF.Sqrt, bias=k.eps_t[:, 0:1], xr=[k.eps_t])
    S.op("dve", lambda e: e.reciprocal(out=rstd[:, 0:N], in_=rstd[:, 0:N]), reads=[rstd], writes=[rstd])
    TT(S, "dve", Z, Z[:, :, 0:N], Z, Z[:, :, 0:N], rstd, rstd[:, 0:N].unsqueeze(1).broadcast_to([128, 8, N]),
       ALU.mult)
    for c in range(8):
        ACT(S, OUT, OUT[:, c, 0:N], Z, Z[:, c, 0:N], AF.Identity, bias=P(k, bname, c), scale=P(k, gname, c),
            xr=[k.pvs])


def common_small(k):
    S = k.S
    k.ln_mean = S.sbuf("ln_mean", [128, 512])
    k.ln_rstd = S.sbuf("ln_rstd", [128, 512])
    k.eps_t = S.sbuf("eps_t", [128, 1])
    S.op("dve", lambda e: e.memset(k.eps_t[:, :], EPS), writes=[k.eps_t])


def phase_pool(k, HIN, HOUT):
    S = k.S
    S.begin_phase()
    common_small(k)
    N = 512
    ws = WStream(S, 2)
    psr = Rot([S.psum(f"ps{i}", [128, 512]) for i in range(8)])
    xr = Rot([S.sbuf(f"xt{i}", [128, 8, N]) for i in range(2)])
    U = Rot([S.sbuf(f"U{i}", [128, 8, 16 + N]) for i in range(2)])
    TA = S.sbuf("TA", [128, 2, 16 + N])
    TB = S.sbuf("TB", [128, 2, 16 + N])
    M = S.sbuf("M", [128, 8, N], RT)
    Y = S.sbuf("Y", [128, 8, N], RT)
    Z = S.sbuf("Z", [128, 8, N])
    xtr = S.sbuf("xtr", [128, 8, N], RT)
    wg = S.sbuf("wg", [128, 4, 2, 256], RT)
    LOAD(S, "pool" if FAST else "sp", wg, R(wg[:, :, :, :]), k.pool_w_grp,
         k.pool_w_grp.t.ap().rearrange("g (kc p) d -> p g kc d", p=128))
    vin = wview(k.pool_w_in.t.ap())
    vout = wview(k.pool_w_out.t.ap())
    hin = fmview(HIN)
    hout = fmview(HOUT)
    o_rc, _ = CST["rcnt"]
    prevU = None
    for tb in range(T // N):
        xt = xr.next()
        LOAD(S, "sp", xt, xt[:, :, :], HIN, hin[:, :, tb * N:(tb + 1) * N])
        LOAD(S, "pool" if FAST else "sp", xtr, R(xtr[:, :, :]), HIN, hin[:, :, tb * N:(tb + 1) * N])
        u = U.next()
        if tb == 0:
            S.op("pool", lambda e, u=u: e.memset(u[:, :, 0:16], 0.0), writes=[u])
        else:
            CP(S, "dve", u, u[:, :, 0:16], prevU, prevU[:, :, N:N + 16])

        def ev_u(oc, ps, pa, u=u):
            CP(S, "act", u, u[:, oc, 16:16 + N], ps, pa)
        proj_fm(k, ws, psr, k.pool_w_in, vin, xtr, lambda kc: xtr[:, kc, :], N, 8, ev_u)
        W_ = 16 + N
        for g in range(4):
            w = 2 << g
            cur_b, cur = u, (lambda a, b, g=g: u[:, 2 * g:2 * g + 2, a:b])
            sh = 1
            tgt = [TA, TB]
            for step in range(g + 1):
                t_ = tgt[step % 2]
                lo = 2 * sh - 1
                TT(S, "dve", t_, t_[:, :, lo:W_], cur_b, cur(lo, W_), cur_b, cur(lo - sh, W_ - sh), ALU.add)
                cur_b, cur = t_, (lambda a, b, t_=t_: t_[:, :, a:b])
                sh *= 2
            if tb == 0:
                rc = k.cs[:, o_rc + g * 16:o_rc + (g + 1) * 16]
                TS(S, "dve", cur_b, cur(32, W_), cur_b, cur(32, W_), 1.0 / w, None, ALU.mult)
                TT(S, "dve", cur_b, cur(16, 32), cur_b, cur(16, 32), k.cs,
                   rc.unsqueeze(1).broadcast_to([128, 2, 16]), ALU.mult)
                TT(S, "dve", M, R(M[:, 2 * g:2 * g + 2, :]), cur_b, cur(16, W_), u, u[:, 2 * g:2 * g + 2, 16:W_],
                   ALU.subtract)
            else:
                STT(S, M, R(M[:, 2 * g:2 * g + 2, :]), cur_b, cur(16, W_), 1.0 / w, u, u[:, 2 * g:2 * g + 2, 16:W_],
                    ALU.mult, ALU.subtract)
        for g in range(4):
            for j in range(2):
                ps = psr.next()
                for kc in range(2):
                    MM(S, ps, ps[:, 0:N], wg, R(wg[:, g, kc, j * 128:(j + 1) * 128]), M, R(M[:, 2 * g + kc, :]),
                       start=(kc == 0), stop=(kc == 1))
                oc = 2 * g + j
                ACT(S, Y, R(Y[:, oc, :]), ps, ps[:, 0:N], AF.Identity, scale=P(k, "pool_scale", oc), xr=[k.pvs])

        def ev_z(oc, ps, pa, xt=xt):
            STT(S, Z, Z[:, oc, :], xt, xt[:, oc, :], ALPHA, ps, pa, ALU.mult, ALU.add)
        proj_fm(k, ws, psr, k.pool_w_out, vout, Y, lambda kc: Y[:, kc, :], N, 8, ev_z)
        o = Z
        ln_fm(k, psr, Z, N, "ln_mix_g0", "ln_mix_b0", o, M)
        STORE(S, "act", HOUT, hout[:, :, tb * N:(tb + 1) * N], o, o[:, :, :])
        prevU = u
    S.end_phase()


def phase_xattn(k, l, HIN, HOUT, HTOK):
    S = k.S
    S.begin_phase()
    common_small(k)
    N = 512
    ws = WStream(S, 2)
    psr = Rot([S.psum(f"ps{i}", [128, 512]) for i in range(8)])
    xr = Rot([S.sbuf(f"xt{i}", [128, 8, N]) for i in range(2)])
    memT = S.sbuf("memT", [128, 8, 256], RT)
    kT = S.sbuf("kT", [128, 8, 256], RT)
    v = S.sbuf("v", [128, 2, D], RT)
    qT = S.sbuf("qT", [128, 8, N], RT)
    Pm = Rot([S.sbuf(f"Pm{i}", [128, 4, 256]) for i in range(2)])
    pT = S.sbuf("pT", [128, 2, 4, N], RT)
    OT = S.sbuf("OT", [128, 8, N], RT)
    Z = S.sbuf("Z", [128, 8, N])
    xtr = S.sbuf("xtr", [128, 8, N], RT)
    htk = Rot([S.sbuf(f"htk{i}", [128, D]) for i in range(2)])
    sm = Rot([S.sbuf(f"sm{i}", [128, 16]) for i in range(2)])
    ident = C(k, "ident")
    LOAD(S, "pool" if FAST else "sp", memT, R(memT[:, :, :]), k.memT, fmview(k.memT))
    vq = wview(k.xa_wq.t.ap()[l])
    vk = wview(k.xa_wk.t.ap()[l])
    vv = wview(k.xa_wv.t.ap()[l])
    vo = wview(k.xa_wo.t.ap()[l])

    def ev_k(oc, ps, pa):
        CP(S, "act", kT, R(kT[:, oc, :]), ps, pa)
    proj_fm(k, ws, psr, k.xa_wk, vk, memT, lambda kc: memT[:, kc, :], 256, 8, ev_k)
    for wc in range(2):
        w = ws.get(k.xa_wv, vv, 0, wc * 512)
        for mc in range(2):
            ps = psr.next()
            for kc in range(8):
                MM(S, ps, ps[:, :], memT, R(memT[:, kc, mc * 128:(mc + 1) * 128]), w, R(w[:, kc, :]),
                   start=(kc == 0), stop=(kc == 7))
            CP(S, "act", v, R(v[:, mc, wc * 512:(wc + 1) * 512]), ps, ps[:, :])
    hin = fmview(HIN)
    hout = fmview(HOUT)
    gname, bname = f"ln_xa_g{l}", f"ln_xa_b{l}"
    for tb in range(T // N):
        xt = xr.next()
        LOAD(S, "sp", xt, xt[:, :, :], HIN, hin[:, :, tb * N:(tb + 1) * N])
        LOAD(S, "pool" if FAST else "sp", xtr, R(xtr[:, :, :]), HIN, hin[:, :, tb * N:(tb + 1) * N])

        def ev_q(oc, ps, pa):
            S.op("act", lambda e: e.activation(out=R(qT[:, oc, :]), in_=pa, func=AF.Identity, scale=0.0625),
                 reads=[ps], writes=[qT])
        proj_fm(k, ws, psr, k.xa_wq, vq, xtr, lambda kc: xtr[:, kc, :], N, 8, ev_q)
        def xa_stage_a(tt):
                tsl = slice(tt * 128, (tt + 1) * 128)
                pm = Pm.next()
                s_ = sm.next()
                banks = [psr.next(), psr.next()]
                for h in range(4):
                    ps = banks[h // 2]
                    for dc in range(2):
                        MM(S, ps, ps[:, (h % 2) * 256:(h % 2 + 1) * 256], qT, R(qT[:, 2 * h + dc, tsl]),
                           kT, R(kT[:, 2 * h + dc, :]), start=(dc == 0), stop=(dc == 1))
                for b2 in range(2):
                    ps = banks[b2]
                    S.op("dve", lambda e, ps=ps, s_=s_, b2=b2: e.tensor_reduce(
                        out=s_[:, 2 * b2:2 * b2 + 2], in_=ps[:, :].rearrange("p (a b) -> p a b", a=2),
                        axis=AX.X, op=ALU.max), reads=[ps], writes=[s_])
                TS(S, "dve", s_, s_[:, 4:8], s_, s_[:, 0:4], -1.0, None, ALU.mult)
                for h in range(4):
                    ps = banks[h // 2]
                    ACT(S, pm, pm[:, h, :], ps, ps[:, (h % 2) * 256:(h % 2 + 1) * 256], AF.Exp,
                        bias=s_[:, 4 + h:5 + h], xr=[s_], accum=s_[:, 8 + h:9 + h], xw=[s_])
                S.op("dve", lambda e, s_=s_: e.reciprocal(out=s_[:, 12:16], in_=s_[:, 8:12]), reads=[s_], writes=[s_])
                TT(S, "dve", pm, pm[:, :, :], pm, pm[:, :, :], s_, s_[:, 12:16].unsqueeze(2).broadcast_to([128, 4, 256]),
                   ALU.mult)
                return tsl, pm

        def xa_stage_b(tt, tsl, pm):
                for mc in range(2):
                    ps = psr.next()
                    for h in range(4):
                        TRP(S, ps, ps[:, h * 128:(h + 1) * 128], pm, pm[:, h, mc * 128:(mc + 1) * 128], k.cs, ident)
                    CP(S, "act" if mc == 0 else "dve", pT, R(pT[:, mc, :, tsl]), ps,
                       ps[:, :].rearrange("p (h t) -> p h t", h=4))
        cur = xa_stage_a(0)
        for tt in range(4):
            nxt = xa_stage_a(tt + 1) if tt < 3 else None
            xa_stage_b(tt, *cur)
            cur = nxt
        for h in range(4):
            for dc in range(2):
                ps = psr.next()
                for mc in range(2):
                    MM(S, ps, ps[:, :], v, R(v[:, mc, h * 256 + dc * 128:h * 256 + (dc + 1) * 128]),
                       pT, R(pT[:, mc, h, :]), start=(mc == 0), stop=(mc == 1))
                CP(S, "act", OT, R(OT[:, 2 * h + dc, :]), ps, ps[:, :])

        def ev_z(oc, ps, pa, xt=xt):
            STT(S, Z, Z[:, oc, :], xt, xt[:, oc, :], ALPHA, ps, pa, ALU.mult, ALU.add)
        proj_fm(k, ws, psr, k.xa_wo, vo, OT, lambda kc: OT[:, kc, :], N, 8, ev_z)
        o = Z
        ln_fm(k, psr, Z, N, gname, bname, o, OT)
        STORE(S, "act", HOUT, hout[:, :, tb * N:(tb + 1) * N], o, o[:, :, :])
        for tt in range(4):
            ht = htk.next()
            for half in range(2):
                ps = psr.next()
                for c4 in range(4):
                    c = half * 4 + c4
                    TRP(S, ps, ps[:, c4 * 128:(c4 + 1) * 128], o, o[:, c, tt * 128:(tt + 1) * 128], k.cs, ident)
                CP(S, "act" if half == 0 else "dve", ht, ht[:, half * 512:(half + 1) * 512], ps, ps[:, :])
            r0 = tb * N + tt * 128
            STORE(S, "act", HTOK, HTOK.t.ap()[r0:r0 + 128, :], ht, ht[:, :])
    S.end_phase()


def phase_moe(k, l, HFM, HTOK, HOUT_FM, OUT_TOK):
    S = k.S
    ident = C(k, "ident")
    S.begin_phase()
    psr = Rot([S.psum(f"ps{i}", [128, 512]) for i in range(8)])
    xr = Rot([S.sbuf(f"xt{i}", [128, 8, 512]) for i in range(2)])
    hr = Rot([S.sbuf(f"hrow{i}", [128, D]) for i in range(3)])
    wr = S.sbuf("wr", [128, 8, NE])
    brt = S.sbuf("brt", [128, NE])
    cum = S.sbuf("cum", [128, NE])
    lgr = Rot([S.sbuf(f"lg{i}", [128, NE]) for i in range(2)])
    t8r = Rot([S.sbuf(f"t8{i}", [128, 16]) for i in range(2)])
    mkr = Rot([S.sbuf(f"mk{i}", [128, NE]) for i in range(2)])
    pcr = Rot([S.sbuf(f"pc{i}", [128, NE]) for i in range(2)])
    jnk = S.sbuf("jnk", [128, NE])
    dfr = Rot([S.sbuf(f"df{i}", [128, 4]) for i in range(2)])
    LOAD(S, "sp", wr, wr[:, :, :], k.w_router, wview(k.w_router.t.ap()[l]))
    LOAD(S, "sp", brt, brt[:, :], k.b_router, k.b_router.t.ap()[l:l + 1, :].broadcast_to([128, NE]))
    S.op("dve", lambda e: e.memset(cum[:, :], 0.0), writes=[cum])
    hfm = fmview(HFM)
    ustrict = C(k, "ustrict")
    ones = C(k, "ones")
    iotaec = C(k, "iotaec")
    xcur = [None]

    def rt_stage_a(ti):
            if ti % 4 == 0:
                xcur[0] = xr.next()
                LOAD(S, "sp", xcur[0], xcur[0][:, :, :], HFM, hfm[:, :, ti * 128:ti * 128 + 512])
            xt = xcur[0]
            tsl = slice((ti % 4) * 128, (ti % 4 + 1) * 128)
            hrow = hr.next()
            LOAD(S, "sp", hrow, hrow[:, :], HTOK, HTOK.t.ap()[ti * 128:(ti + 1) * 128, :])
            ps = psr.next()
            for kc in range(8):
                MM(S, ps, ps[:, 0:NE], xt, xt[:, kc, tsl], wr, wr[:, kc, :], start=(kc == 0), stop=(kc == 7))
            lg = lgr.next()
            TT(S, "dve", lg, lg[:, :], ps, ps[:, 0:NE], brt, brt[:, :], ALU.add)
            t8 = t8r.next()
            S.op("dve", lambda e, t8=t8, lg=lg: e.max(out=t8[:, 0:8], in_=lg[:, :]), reads=[lg], writes=[t8])
            mk = mkr.next()
            TS(S, "dve", mk, mk[:, :], lg, lg[:, :], t8[:, 3:4], None, ALU.is_ge, xr=[t8])
            pp = psr.next()
            MM(S, pp, pp[:, 0:NE], k.cs, ustrict, mk, mk[:, :])
            MM(S, pp, pp[:, NE:2 * NE], k.cs, ones, mk, mk[:, :])
            return hrow, lg, t8, pp

    def rt_stage_b(ti, hrow, lg, t8, pp):
            pc = pcr.next()
            TT(S, "dve", pc, pc[:, :], pp, pp[:, 0:NE], cum, cum[:, :], ALU.add)
            TT(S, "dve", pc, pc[:, :], pc, pc[:, :], k.cs, iotaec, ALU.add)
            TT(S, "dve", cum, cum[:, :], pp, pp[:, NE:2 * NE], cum, cum[:, :], ALU.add)
            df = dfr.next()
            for kk in range(4):
                STT(S, jnk, jnk[:, :], lg, lg[:, :], t8[:, kk:kk + 1], pc, pc[:, :], ALU.is_equal, ALU.mult,
                    xr=[t8], accum=df[:, kk:kk + 1], xw=[df])
            TS(S, "dve", df, df[:, :], df, df[:, :], 0.0, float(NE * CAP - 1), ALU.max, ALU.min)
            CP(S, "dve", k.dest, k.dest[:, ti, :], df, df[:, :])
            TS(S, "dve", t8, t8[:, 8:9], t8, t8[:, 0:1], -1.0, None, ALU.mult)
            ACT(S, t8, t8[:, 10:14], t8, t8[:, 0:4], AF.Exp, bias=t8[:, 8:9], accum=t8[:, 9:10])
            S.op("dve", lambda e, t8=t8: e.reciprocal(out=t8[:, 14:15], in_=t8[:, 9:10]), reads=[t8], writes=[t8])
            TS(S, "dve", k.gates, k.gates[:, ti, :], t8, t8[:, 10:14], t8[:, 14:15], None, ALU.mult)
            for kk in range(4):
                S.dma("pool", lambda e, hrow=hrow, ti=ti, kk=kk: e.indirect_dma_start(
                    out=k.XS.t.ap(), out_offset=bass.IndirectOffsetOnAxis(ap=k.dest[:, ti, kk:kk + 1], axis=0),
                    in_=hrow[:, :], in_offset=None),
                    reads=[hrow, k.dest, k.gates], writes=[k.XS], owner=hrow, group=(kk > 0))
    cur = rt_stage_a(0)
    for ti in range(NTL):
        nxt = rt_stage_a(ti + 1) if ti < NTL - 1 else None
        rt_stage_b(ti, *cur)
        cur = nxt
    S.end_phase()
    if k.phases is not None and f"moe{l}_route_only" in k.phases:
        return
    S.begin_phase()
    ws = WStream(S, 4)
    psr = Rot([S.psum(f"ps{i}", [128, 512]) for i in range(8)])
    xrow = Rot([S.sbuf(f"xrow{i}", [128, D]) for i in range(2)])
    yrow = Rot([S.sbuf(f"yrow{i}", [128, D]) for i in range(2)])
    XT = S.sbuf("XsT", [128, 8, CAP], RT)
    AT = S.sbuf("actT", [128, 8, CAP], RT)
    gp = Rot([S.sbuf(f"gp{i}", [128, CAP // 2]) for i in range(2)])
    sg = Rot([S.sbuf(f"sg{i}", [128, CAP // 2]) for i in range(2)])
    up = Rot([S.sbuf(f"up{i}", [128, CAP // 2]) for i in range(2)])
    bdr = Rot([S.sbuf(f"bd{i}", [1, D], RT) for i in range(2)])
    o_bg, _ = PV[f"bg{l}"]
    o_bu, _ = PV[f"bu{l}"]
    HN = CAP // 2
    NRB = CAP // 128
    ones = C(k, "ones")
    for e_ in range(NE):
        bd = bdr.next()
        LOAD(S, "pool" if FAST else "sp", bd, bd[:, :], k.b_dn, k.b_dn.t.ap()[l, e_:e_ + 1, :])
        for rb in range(NRB):
            xw = xrow.next()
            r0 = e_ * CAP + rb * 128
            LOAD(S, "sp", xw, xw[:, :], k.XS, k.XS.t.ap()[r0:r0 + 128, :])
            for half in range(2):
                ps = psr.next()
                for c4 in range(4):
                    c = half * 4 + c4
                    TRP(S, ps, ps[:, c4 * 128:(c4 + 1) * 128], xw, xw[:, c * 128:(c + 1) * 128], k.cs, ident)
                CP(S, "act" if half == 0 else "dve", XT, R(XT[:, half * 4:half * 4 + 4, rb * 128:(rb + 1) * 128]),
                   ps, ps[:, :].rearrange("p (c t) -> p c t", c=4))
        vgu = wview(k.w_gu.t.ap()[l, e_])
        for wc in range(4):
            w = ws.get(k.w_gu, vgu, 0, wc * 512)
            for f2 in range(2):
                fc = wc * 2 + f2
                base = f2 * 256
                for hf in range(2):
                    cs_ = slice(hf * HN, (hf + 1) * HN)
                    pg = psr.next()
                    pu = psr.next()
                    for kc in range(8):
                        MM(S, pg, pg[:, 0:HN], w, R(w[:, kc, base:base + 256:2]), XT, R(XT[:, kc, cs_]),
                           start=(kc == 0), stop=(kc == 7))
                    for kc in range(8):
                        MM(S, pu, pu[:, 0:HN], w, R(w[:, kc, base + 1:base + 256:2]), XT, R(XT[:, kc, cs_]),
                           start=(kc == 0), stop=(kc == 7))
                    g_ = gp.next()
                    s_ = sg.next()
                    u_ = up.next()
                    bgc = k.pvs[:, o_bg + e_ * 8 + fc:o_bg + e_ * 8 + fc + 1]
                    buc = k.pvs[:, o_bu + e_ * 8 + fc:o_bu + e_ * 8 + fc + 1]
                    TS(S, "dve", g_, g_[:, :], pg, pg[:, 0:HN], bgc, 7.0, ALU.add, ALU.min, xr=[k.pvs])
                    ACT(S, s_, s_[:, :], g_, g_[:, :], AF.Sigmoid, scale=1.702)
                    TS(S, "dve", u_, u_[:, :], pu, pu[:, 0:HN], buc, 7.0, ALU.add, ALU.min, xr=[k.pvs])
                    TS(S, "dve", u_, u_[:, :], u_, u_[:, :], -7.0, 1.0, ALU.max, ALU.add)
                    TT(S, "dve", g_, g_[:, :], g_, g_[:, :], s_, s_[:, :], ALU.mult)
                    TT(S, "dve", AT, R(AT[:, fc, cs_]), g_, g_[:, :], u_, u_[:, :], ALU.mult)
        vdn = wview(k.w_dn.t.ap()[l, e_])
        wd = [ws.get(k.w_dn, vdn, 0, oc * 512) for oc in range(2)]
        for rb in range(NRB):
            yw = yrow.next()
            for oc in range(2):
                ps = psr.next()
                for fc in range(8):
                    MM(S, ps, ps[:, :], AT, R(AT[:, fc, rb * 128:(rb + 1) * 128]), wd[oc], R(wd[oc][:, fc, :]),
                       start=(fc == 0), stop=False)
                MM(S, ps, ps[:, :], k.csr, k.csr[0:1, 256:384], bd, bd[0:1, oc * 512:(oc + 1) * 512],
                   start=False, stop=True)
                CP(S, "act", yw, yw[:, oc * 512:(oc + 1) * 512], ps, ps[:, :])
            r0 = e_ * CAP + rb * 128
            STORE(S, "act", k.YS, k.YS.t.ap()[r0:r0 + 128, :], yw, yw[:, :])
    S.end_phase()
    S.begin_phase()
    psr = Rot([S.psum(f"ps{i}", [128, 512]) for i in range(8)])
    hr = Rot([S.sbuf(f"hrow{i}", [128, D]) for i in range(2)])
    yk = Rot([S.sbuf(f"yk{i}", [128, D]) for i in range(8)])
    acc = Rot([S.sbuf(f"acc{i}", [128, D]) for i in range(2)])
    ofm = Rot([S.sbuf(f"ofm{i}", [128, 8, 128]) for i in range(2)])
    gB = S.sbuf("gB", [128, D])
    bB = S.sbuf("bB", [128, D])
    st_ = Rot([S.sbuf(f"st{i}", [128, 16]) for i in range(2)])
    eps_t = S.sbuf("eps_t", [128, 1])
    S.op("dve", lambda e: e.memset(eps_t[:, :], EPS), writes=[eps_t])
    LOAD(S, "sp", gB, gB[:, :], k.ln_ffn_g, k.ln_ffn_g.t.ap()[l:l + 1, :].broadcast_to([128, D]))
    LOAD(S, "sp", bB, bB[:, :], k.ln_ffn_b, k.ln_ffn_b.t.ap()[l:l + 1, :].broadcast_to([128, D]))
    for ti in range(NTL):
        hrow = hr.next()
        LOAD(S, "sp", hrow, hrow[:, :], HTOK, HTOK.t.ap()[ti * 128:(ti + 1) * 128, :])
        ys = []
        for kk in range(4):
            y = yk.next()
            S.dma("pool", lambda e, y=y, ti=ti, kk=kk: e.indirect_dma_start(
                out=y[:, :], out_offset=None, in_=k.YS.t.ap(),
                in_offset=bass.IndirectOffsetOnAxis(ap=k.dest[:, ti, kk:kk + 1], axis=0)),
                reads=[k.YS, k.dest], writes=[y], owner=y)
            ys.append(y)
        a = acc.next()
        TS(S, "dve", a, a[:, :], ys[0], ys[0][:, :], k.gates[:, ti, 0:1], None, ALU.mult, xr=[k.gates])
        for kk in range(1, 4):
            STT(S, a, a[:, :], ys[kk], ys[kk][:, :], k.gates[:, ti, kk:kk + 1], a, a[:, :], ALU.mult, ALU.add,
                xr=[k.gates])
        STT(S, a, a[:, :], hrow, hrow[:, :], ALPHA, a, a[:, :], ALU.mult, ALU.add)
        s_ = st_.next()
        for hf in range(2):
            S.op("dve", lambda e, s_=s_, a=a, hf=hf: e.bn_stats(out=s_[:, hf * 6:(hf + 1) * 6],
                                                               in_=a[:, hf * 512:(hf + 1) * 512]),
                 reads=[a], writes=[s_])
        S.op("dve", lambda e, s_=s_: e.bn_aggr(out=s_[:, 12:14], in_=s_[:, 0:12]), reads=[s_], writes=[s_])
        ACT(S, s_, s_[:, 14:15], s_, s_[:, 13:14], AF.Sqrt, bias=eps_t[:, 0:1], xr=[eps_t])
        S.op("dve", lambda e, s_=s_: e.reciprocal(out=s_[:, 15:16], in_=s_[:, 14:15]), reads=[s_], writes=[s_])
        TS(S, "dve", a, a[:, :], a, a[:, :], s_[:, 12:13], s_[:, 15:16], ALU.subtract, ALU.mult, xr=[s_])
        TT(S, "dve", a, a[:, :], a, a[:, :], gB, gB[:, :], ALU.mult)
        TT(S, "dve", a, a[:, :], a, a[:, :], bB, bB[:, :], ALU.add)
        if OUT_TOK is not None:
            STORE(S, "act", OUT_TOK, OUT_TOK.t.ap()[ti * 128:(ti + 1) * 128, :], a, a[:, :])
        if HOUT_FM is not None:
            STORE(S, "act", k.HCTOK, k.HCTOK.t.ap()[ti * 128:(ti + 1) * 128, :], a, a[:, :])
            o = ofm.next()
            for half in range(2):
                ps = psr.next()
                for c4 in range(4):
                    c = half * 4 + c4
                    TRP(S, ps, ps[:, c4 * 128:(c4 + 1) * 128], a, a[:, c * 128:(c + 1) * 128], k.cs, ident)
                CP(S, "act", o, o[:, half * 4:half * 4 + 4, :], ps, ps[:, :].rearrange("p (c t) -> p c t", c=4))
            STORE(S, "act", HOUT_FM, fmview(HOUT_FM)[:, :, ti * 128:(ti + 1) * 128], o, o[:, :, :])
    S.end_phase()


def phase_ssd_a(k, HIN):
    S = k.S
    S.begin_phase()
    N = 512
    ws = WStream(S, 3)
    psr = Rot([S.psum(f"ps{i}", [128, 512]) for i in range(8)])
    xr = Rot([S.sbuf(f"xtr{i}", [128, 8, N], RT) for i in range(2)])
    zst = Rot([S.sbuf(f"zst{i}", [128, N]) for i in range(4)])
    ucr = Rot([S.sbuf(f"uc{i}", [128, 3 + N]) for i in range(4)])
    acr = Rot([S.sbuf(f"ac{i}", [128, N]) for i in range(4)])
    xst = Rot([S.sbuf(f"xst{i}", [128, N]) for i in range(4)])
    halo = S.sbuf("halo", [128, 32, 4])
    S.op("dve", lambda e: e.memset(halo[:, :, :], 0.0), writes=[halo])
    vin = wview(k.ssm_w_in.t.ap())
    hin = fmview(HIN)
    o_cw, _ = PV["conv_w"]
    zsd = k.ZSD.t.ap()
    xbd = k.XBCD.t.ap()
    for tb in range(T // N):
        xtr = xr.next()
        LOAD(S, "pool" if FAST else "sp", xtr, xtr[:, :, :], HIN, hin[:, :, tb * N:(tb + 1) * N])

        def ev_zs(oc, ps, pa, tb=tb):
            z_ = zst.next()
            ACT(S, z_, z_[:, :], ps, pa, AF.Silu)
            STORE(S, "act", k.ZSD, zsd[oc * 128:(oc + 1) * 128, tb * N:(tb + 1) * N], z_, z_[:, :])
        proj_fm(k, ws, psr, k.ssm_w_in, vin, xtr, lambda kc, xtr=xtr: xtr[:, kc, :], N, 16, ev_zs, col0=0)

        pend = []

        def flush():
            while pend:
                pend.pop(0)()

        def ev_u(c, ps, pa, tb=tb):
            uc = ucr.next()
            CP(S, "dve", uc, uc[:, 0:3], halo, halo[:, c, 0:3])
            CP(S, "act", uc, uc[:, 3:3 + N], ps, pa)
            flush()
            CP(S, "dve", halo, halo[:, c, 0:3], uc, uc[:, N:N + 3])
            acc = acr.next()
            wcol = lambda kk: k.pvs[:, o_cw + c * 4 + kk:o_cw + c * 4 + kk + 1]
            ACT(S, acc, acc[:, :], uc, uc[:, 0:N], AF.Identity, bias=P(k, "conv_b", c), scale=wcol(0), xr=[k.pvs])
            for kk in range(1, 4):
                STT(S, acc, acc[:, :], uc, uc[:, kk:kk + N], wcol(kk), acc, acc[:, :], ALU.mult, ALU.add, xr=[k.pvs])

            def fin(c=c, acc=acc, tb=tb):
                x_ = xst.next()
                ACT(S, x_, x_[:, :], acc, acc[:, :], AF.Silu)
                STORE(S, "act", k.XBCD, xbd[c * 128:(c + 1) * 128, tb * N:(tb + 1) * N], x_, x_[:, :])
            pend.append(fin)
        proj_fm(k, ws, psr, k.ssm_w_in, vin, xtr, lambda kc, xtr=xtr: xtr[:, kc, :], N, 32, ev_u, col0=2048)
        flush()
    S.end_phase()


def phase_ssd_b(k, HIN, HOUT):
    S = k.S
    S.begin_phase()
    common_small(k)
    N = 128
    ws = WStream(S, 2)
    psr = Rot([S.psum(f"ps{i}", [128, 512]) for i in range(4)])
    SPS = S.psum("SPS", [128, 2048])
    xr = Rot([S.sbuf(f"xt{i}", [128, 8, N]) for i in range(2)])
    wdt = S.sbuf("wdt", [128, 8, 32])
    rowb = S.sbuf("rowb", [128, 64])
    Abc = S.sbuf("Abc", [128, 32])
    one_t = S.sbuf("one_t", [128, 1])
    zsr = Rot([S.sbuf(f"ZS{i}", [128, 16, N]) for i in range(2)])
    xbr = Rot([S.sbuf(f"XBC{i}", [128, 32, N]) for i in range(2)])
    XD = S.sbuf("XD", [128, 2048])
    XDD = S.sbuf("XDD", [128, 2048])
    BTK = S.sbuf("BTK", [128, 8, 128])
    tk = Rot([S.sbuf(f"tk{i}", [128, 8, 32]) for i in range(2)])
    cdr = Rot([S.sbuf(f"cdr{i}", [128, 64]) for i in range(2)])
    arr = Rot([S.sbuf(f"ar{i}", [128, 4, 128], RT) for i in range(2)])
    cbr = Rot([S.sbuf(f"cbm{i}", [128, 128]) for i in range(2)])
    mtr = Rot([S.sbuf(f"mt{i}", [128, 4, 128]) for i in range(2)])
    cer = Rot([S.sbuf(f"ce{i}", [128, 4, 128]) for i in range(2)])
    H = [S.sbuf(f"H{i}", [128, 2048]) for i in range(2)]
    Yt = S.sbuf("Yt", [128, 16, N])
    SQ = S.sbuf("SQ", [128, 16, N])
    rst = S.sbuf("rst", [128, 8, N])
    YR = S.sbuf("YR", [128, 16, N], RT)
    xtk = S.sbuf("xtk", [128, D])
    ZT = S.sbuf("ZT", [128, D])
    gB = S.sbuf("gBm", [128, D])
    bB = S.sbuf("bBm", [128, D])
    st_ = Rot([S.sbuf(f"stm{i}", [128, 16]) for i in range(2)])
    LOAD(S, "sp", gB, gB[:, :], k.ln_mix_g, k.ln_mix_g.t.ap()[1:2, :].broadcast_to([128, D]))
    LOAD(S, "sp", bB, bB[:, :], k.ln_mix_b, k.ln_mix_b.t.ap()[1:2, :].broadcast_to([128, D]))
    ident = C(k, "ident")
    triinc, blockones, strictlt, mask01 = C(k, "triinc"), C(k, "blockones"), C(k, "strictlt"), C(k, "mask01")
    csel = [C(k, "csel0"), C(k, "csel1")]
    ones, ones256 = C(k, "ones"), C(k, "ones256")
    vin = wview(k.ssm_w_in.t.ap())
    vout = wview(k.ssm_w_out.t.ap())
    hin = fmview(HIN)
    hout = fmview(HOUT)
    o_cw, _ = PV["conv_w"]
    LOAD(S, "sp", wdt, wdt[:, :, :], k.ssm_w_in, vin[:, :, 6144:6176])
    LOAD(S, "sp", rowb, rowb[:, :], k.rowp, k.rowp.t.ap()[0:1, :].broadcast_to([128, 64]))
    S.op("dve", lambda e: e.memset(one_t[:, :], 1.0), writes=[one_t])
    S.op("dve", lambda e: e.memset(H[0][:, :], 0.0), writes=[H[0]])
    ACT(S, Abc, Abc[:, :], rowb, rowb[:, 32:64], AF.Exp)
    TS(S, "dve", Abc, Abc[:, :], Abc, Abc[:, :], -1.0, None, ALU.mult)
    for tb in range(T // N):
        xt = xr.next()
        LOAD(S, "sp", xt, xt[:, :, :], HIN, hin[:, :, tb * N:(tb + 1) * N])
        ZS = zsr.next()
        XBC = xbr.next()
        for q4 in range(2):
            LOAD(S, "sp", ZS, ZS[:, q4 * 8:(q4 + 1) * 8, :], k.ZSD,
                 fmview(k.ZSD)[:, q4 * 8:(q4 + 1) * 8, tb * N:(tb + 1) * N], group=(q4 > 0))
        for q4 in range(4):
            LOAD(S, "sp", XBC, XBC[:, q4 * 8:(q4 + 1) * 8, :], k.XBCD,
                 fmview(k.XBCD)[:, q4 * 8:(q4 + 1) * 8, tb * N:(tb + 1) * N], group=(q4 > 0))
        t_ = tk.next()
        dt, a_, acs, dte, w1, tmp = (t_[:, i, :] for i in range(6))
        ps = psr.next()
        for kc in range(8):
            MM(S, ps, ps[:, 0:32], xt, xt[:, kc, :], wdt, wdt[:, kc, :], start=(kc == 0), stop=(kc == 7))
        TT(S, "dve", t_, tmp, ps, ps[:, 0:32], rowb, rowb[:, 0:32], ALU.add)
        ACT(S, t_, tmp, t_, tmp, AF.Exp)
        ACT(S, t_, dt, t_, tmp, AF.Ln, bias=one_t[:, 0:1], xr=[one_t])
        TT(S, "dve", t_, a_, t_, dt, Abc, Abc[:, :], ALU.mult)
        pa = psr.next()
        MM(S, pa, pa[:, 0:32], k.cs, triinc, t_, a_)
        MM(S, pa, pa[:, 32:64], k.cs, blockones, t_, a_)
        MM(S, pa, pa[:, 64:96], k.cs, csel[0], t_, a_)
        MM(S, pa, pa[:, 96:128], k.cs, csel[1], t_, a_)
        CP(S, "act", t_, acs, pa, pa[:, 0:32])
        TT(S, "dve", t_, dte, pa, pa[:, 32:64], t_, acs, ALU.subtract)
        ACT(S, t_, dte, t_, dte, AF.Exp)
        TT(S, "dve", t_, w1, t_, dt, t_, dte, ALU.mult)
        cd = cdr.next()
        ACT(S, cd, cd[:, :], pa, pa[:, 64:128], AF.Exp)
        for q in range(4):
            ps = psr.next()
            for j in range(4):
                TRP(S, ps, ps[:, j * 128:(j + 1) * 128], XBC, XBC[:, q * 4 + j, :], k.cs, ident)
            pv3 = ps[:, :].rearrange("p (h d) -> p h d", h=8)
            for dst, sc in ((XD, dt), (XDD, w1)):
                TT(S, "dve", dst, dst[:, q * 512:(q + 1) * 512].rearrange("p (h d) -> p h d", h=8), ps, pv3,
                   t_, sc[:, q * 8:(q + 1) * 8].unsqueeze(2).broadcast_to([128, 8, 64]), ALU.mult)
        for q in range(2):
            ps = psr.next()
            for j in range(4):
                TRP(S, ps, ps[:, j * 128:(j + 1) * 128], XBC, XBC[:, 16 + q * 4 + j, :], k.cs, ident)
            CP(S, "act", BTK, BTK[:, q * 4:(q + 1) * 4, :], ps, ps[:, :].rearrange("p (g n) -> p g n", g=4))
        H0, H1 = H[0], H[1]

        def states(cp):
            for g in range(8):
                MM(S, SPS, SPS[:, g * 256:(g + 1) * 256], BTK, BTK[cp * 64:(cp + 1) * 64, g, :],
                   XDD, XDD[cp * 64:(cp + 1) * 64, g * 256:(g + 1) * 256])

        def hupd(dst, src, cp):
            TT(S, "dve", dst, dst[:, :].rearrange("p (h d) -> p h d", h=32), src,
               src[:, :].rearrange("p (h d) -> p h d", h=32), cd,
               cd[:, cp * 32:(cp + 1) * 32].unsqueeze(2).broadcast_to([128, 32, 64]), ALU.mult)
            TT(S, "dve", dst, dst[:, :], SPS, SPS[:, :], dst, dst[:, :], ALU.add)
        states(0)
        hupd(H1, H0, 0)
        Hc = [H0, H1]
        def stage1(g):
            ar = arr.next()
            TT(S, "dve", ar, ar[:, :, :], k.cs, triinc.unsqueeze(1).broadcast_to([128, 4, 128]), t_,
               a_[:, 4 * g:4 * g + 4].unsqueeze(2).broadcast_to([128, 4, 128]), ALU.mult)
            ar2 = ar[:, :, :].rearrange("p h i -> p (h i)")
            pcb = psr.next()
            MM(S, pcb, pcb[:, 0:128], XBC, XBC[:, 16 + g, :], XBC, XBC[:, 24 + g, :])
            pseg = psr.next()
            MM(S, pseg, pseg[:, :], k.csr, k.csr[:, 128:256], ar, ar2)
            pacs = psr.next()
            MM(S, pacs, pacs[:, :], k.csr, k.csr[:, 256:384], ar, ar2)
            cbm = cbr.next()
            TT(S, "dve", cbm, cbm[:, :], pcb, pcb[:, 0:128], k.cs, mask01, ALU.mult)
            mt = mtr.next()
            ACT(S, mt, mt[:, :, :], pseg, pseg[:, :].rearrange("p (h i) -> p h i", h=4), AF.Exp)
            ce = cer.next()
            ACT(S, ce, ce[:, :, :], pacs, pacs[:, :].rearrange("p (h i) -> p h i", h=4), AF.Exp)
            TT(S, "dve", mt, mt[:, :, :], mt, mt[:, :, :], cbm, cbm[:, :].unsqueeze(1).broadcast_to([128, 4, 128]),
               ALU.mult)
            TT(S, "dve", ce, ce[:, :, :], ce, ce[:, :, :], XBC,
               XBC[:, 24 + g, :].unsqueeze(1).broadcast_to([128, 4, 128]), ALU.mult)
            return mt, ce

        def stage2(g, mt, ce):
            py = psr.next()
            for r in range(4):
                hd = 4 * g + r
                prt = slice((r % 2) * 64, (r % 2) * 64 + 64)
                c0 = (r // 2) * 128
                MM(S, py, py[prt, c0:c0 + 128], XD, XD[:, hd * 64:(hd + 1) * 64], mt, mt[:, r, :],
                   start=True, stop=False)
                for cp in range(2):
                    MM(S, py, py[prt, c0 + cp * 64:c0 + (cp + 1) * 64], Hc[cp], Hc[cp][:, hd * 64:(hd + 1) * 64],
                       ce, ce[:, r, cp * 64:(cp + 1) * 64], start=False, stop=(cp == 1))
            for j in range(2):
                c = 2 * g + j
                STT(S, Yt, Yt[:, c, :], XBC, XBC[:, c, :], P(k, "dskip", c), py, py[:, j * 128:(j + 1) * 128],
                    ALU.mult, ALU.add, xr=[k.pvs])
        cur = stage1(0)
        for g in range(8):
            nxt = stage1(g + 1) if g < 7 else None
            stage2(g, *cur)
            cur = nxt
        states(1)
        hupd(H0, H1, 1)
        TT(S, "dve", Yt, Yt[:, :, :], Yt, Yt[:, :, :], ZS, ZS[:, :, :], ALU.mult)
        ACT(S, SQ, SQ[:, :, :], Yt, Yt[:, :, :], AF.Square)
        for b2 in range(2):
            pm = psr.next()
            for g4 in range(4):
                g = b2 * 4 + g4
                for j in range(2):
                    MM(S, pm, pm[:, g4 * 128:(g4 + 1) * 128], k.cs, ones256, SQ, SQ[:, 2 * g + j, :],
                       start=(j == 0), stop=(j == 1))
            ACT(S, rst, rst[:, b2 * 4:(b2 + 1) * 4, :], pm, pm[:, :].rearrange("p (g t) -> p g t", g=4), AF.Sqrt,
                bias=k.eps_t[:, 0:1], xr=[k.eps_t])
        S.op("dve", lambda e: e.reciprocal(out=rst[:, :, :], in_=rst[:, :, :]), reads=[rst], writes=[rst])
        Yt4 = Yt[:, :, :].rearrange("p (g j) t -> p g j t", j=2)
        TT(S, "dve", Yt, Yt4, Yt, Yt4, rst, rst[:, :, :].unsqueeze(2).broadcast_to([128, 8, 2, N]), ALU.mult)
        TT(S, "dve", YR, YR[:, :, :], Yt, Yt[:, :, :], k.pvs,
           P(k, "norm_g").unsqueeze(2).broadcast_to([128, 16, N]), ALU.mult)

        LOAD(S, "sp", xtk, xtk[:, :], k.HCTOK, k.HCTOK.t.ap()[tb * N:(tb + 1) * N, :])
        for wc in range(2):
            ps = psr.next()
            for kh in range(2):
                w = ws.get(k.ssm_w_out, vout, kh * 8, wc * 512)
                for kc in range(8):
                    MM(S, ps, ps[:, :], YR, YR[:, kh * 8 + kc, :], w, w[:, kc, :],
                       start=(kh == 0 and kc == 0), stop=(kh == 1 and kc == 7))
            STT(S, ZT, ZT[:, wc * 512:(wc + 1) * 512], xtk, xtk[:, wc * 512:(wc + 1) * 512], ALPHA, ps, ps[:, :],
                ALU.mult, ALU.add)
        s_ = st_.next()
        for hf in range(2):
            S.op("dve", lambda e, s_=s_, hf=hf: e.bn_stats(out=s_[:, hf * 6:(hf + 1) * 6],
                                                          in_=ZT[:, hf * 512:(hf + 1) * 512]),
                 reads=[ZT], writes=[s_])
        S.op("dve", lambda e, s_=s_: e.bn_aggr(out=s_[:, 12:14], in_=s_[:, 0:12]), reads=[s_], writes=[s_])
        ACT(S, s_, s_[:, 14:15], s_, s_[:, 13:14], AF.Sqrt, bias=k.eps_t[:, 0:1], xr=[k.eps_t])
        S.op("dve", lambda e, s_=s_: e.reciprocal(out=s_[:, 15:16], in_=s_[:, 14:15]), reads=[s_], writes=[s_])
        TS(S, "dve", ZT, ZT[:, :], ZT, ZT[:, :], s_[:, 12:13], s_[:, 15:16], ALU.subtract, ALU.mult, xr=[s_])
        TT(S, "dve", ZT, ZT[:, :], ZT, ZT[:, :], gB, gB[:, :], ALU.mult)
        TT(S, "dve", ZT, ZT[:, :], ZT, ZT[:, :], bB, bB[:, :], ALU.add)
        for half in range(2):
            ps = psr.next()
            for c4 in range(4):
                c = half * 4 + c4
                TRP(S, ps, ps[:, c4 * 128:(c4 + 1) * 128], ZT, ZT[:, c * 128:(c + 1) * 128], k.cs, ident)
            CP(S, "act", SQ, SQ[:, half * 4:half * 4 + 4, :], ps, ps[:, :].rearrange("p (c t) -> p c t", c=4))
        STORE(S, "act", HOUT, hout[:, :, tb * N:(tb + 1) * N], SQ, SQ[:, 0:8, :])
    S.end_phase()


_NC_CACHE = {}


def make_in_maps(inp):
    cst = make_consts()
    pv = make_pv(inp)
    rowp = np.concatenate([inp["ssm_dt_bias"][0], inp["ssm_a_log"][0]])[None, :].astype(np.float32)
    shared = {
        "cst": cst, "pv": pv, "rowp": rowp,
        "pool_w_in": inp["pool_w_in"][0], "pool_w_grp": inp["pool_w_grp"][0], "pool_w_out": inp["pool_w_out"][0],
        "ssm_w_in": inp["ssm_w_in"][0], "ssm_w_out": inp["ssm_w_out"][0],
        "xa_wq": inp["xa_wq"], "xa_wk": inp["xa_wk"], "xa_wv": inp["xa_wv"], "xa_wo": inp["xa_wo"],
        "moe_w_router": inp["moe_w_router"], "moe_b_router": inp["moe_b_router"],
        "moe_w_gate_up": inp["moe_w_gate_up"], "moe_w_down": inp["moe_w_down"], "moe_b_down": inp["moe_b_down"],
        "ln_ffn_g": inp["ln_ffn_g"], "ln_ffn_b": inp["ln_ffn_b"],
        "ln_mix_g": inp["ln_mix_g"], "ln_mix_b": inp["ln_mix_b"],
    }
    shared = {kk: np.ascontiguousarray(np.asarray(vv, dtype=np.float32)) for kk, vv in shared.items()}
    maps = []
    for b in range(8):
        m = dict(shared)
        m["xT"] = np.ascontiguousarray(np.asarray(inp["x"][b]).T)
        m["memT"] = np.ascontiguousarray(np.asarray(inp["mem"][b]).T)
        maps.append(m)
    return maps


def kernel(**inputs):
    inp = {kk: np.asarray(vv) for kk, vv in inputs.items()}
    if "nc" not in _NC_CACHE:
        _NC_CACHE["nc"] = build()
    nc = _NC_CACHE["nc"]
    maps = make_in_maps(inp)
    res = run_bass_kernel_spmd(nc, maps, core_ids=list(range(8)))
    return np.stack([res.results[b]["out"] for b in range(8)], axis=0).astype(np.float32)
```

```python
from contextlib import ExitStack
import numpy as np
import concourse.bass as bass
import concourse.mybir as mybir
from concourse.bass_utils import run_bass_kernel_spmd

F32 = mybir.dt.float32
F32R = mybir.dt.float32r
FAST = True


RT = F32R if FAST else F32


def R(ap):
    return ap


I32 = mybir.dt.int32
AF = mybir.ActivationFunctionType
ALU = mybir.AluOpType
AX = mybir.AxisListType

T = 4096
D = 1024
NTL = 32
ALPHA = float(4 ** 0.25)
EPS = 1e-5
CAP = 768
NE = 32
DEBUG = False

ENGS = ("pe", "act", "dve", "pool", "sp")


class Buf:
    __slots__ = ("name", "t", "w", "r", "dsem", "local")

    def __init__(self, name, t):
        self.name = name
        self.t = t
        self.w = {}
        self.r = {}
        self.dsem = None
        self.local = False

    def __getitem__(self, idx):
        return self.t[idx]


class Rot:
    def __init__(self, bufs):
        self.b = bufs
        self.i = 0

    def next(self):
        b = self.b[self.i % len(self.b)]
        self.i += 1
        return b


class Sched:
    def __init__(self, nc, stack):
        self.nc = nc
        self.stack = stack
        self.sems = {}
        self.cnt = {}
        for e in ENGS:
            self._mksem(e)
        self.seen = {e: {} for e in ENGS}
        self.prog = {e: [] for e in ENGS}
        self.nbuf = 0
        self.phase_stack = None
        self.free_dsems = []

    def _mksem(self, key):
        h = self.stack.enter_context(self.nc.semaphore(f"s_{key}"))
        self.sems[key] = h
        self.cnt[key] = 0

    def begin_phase(self):
        self.phase_stack = ExitStack()
        self.phase_dsems = []

    def end_phase(self):
        self.barrier()
        self.phase_stack.close()
        self.phase_stack = None
        self.free_dsems.extend(self.phase_dsems)
        self.phase_dsems = []

    def sbuf(self, name, shape, dtype=F32, glob=False):
        st = self.stack if (glob or self.phase_stack is None) else self.phase_stack
        self.uid = getattr(self, "uid", 0) + 1
        name = f"{name}_u{self.uid}"
        t = st.enter_context(self.nc.sbuf_tensor(name, list(shape), dtype))
        b = Buf(name, t)
        b.local = st is not self.stack
        return b

    def psum(self, name, shape, dtype=F32):
        st = self.stack if self.phase_stack is None else self.phase_stack
        self.uid = getattr(self, "uid", 0) + 1
        name = f"{name}_u{self.uid}"
        t = st.enter_context(self.nc.psum_tensor(name, list(shape), dtype))
        return Buf(name, t)

    def _deps(self, e, reads, writes):
        need = {}
        for b in reads:
            for k, v in b.w.items():
                if need.get(k, 0) < v:
                    need[k] = v
        for b in writes:
            for k, v in b.w.items():
                if need.get(k, 0) < v:
                    need[k] = v
            for k, v in b.r.items():
                if need.get(k, 0) < v:
                    need[k] = v
        waits = []
        seen = self.seen[e]
        for k, v in need.items():
            if e == "pe" and k == "pe":
                continue
            if seen.get(k, 0) < v:
                seen[k] = v
                waits.append((k, v))
        return waits

    def op(self, e, fn, reads=(), writes=()):
        waits = self._deps(e, reads, writes)
        self.cnt[e] += 1
        v = self.cnt[e]
        self.prog[e].append((waits, fn, (e, 1)))
        for b in writes:
            b.w[e] = v
        for b in reads:
            b.r[e] = v

    def dma(self, q, fn, reads=(), writes=(), owner=None, group=False):
        if owner is None:
            owner = writes[0]
        if owner.dsem is None:
            if self.free_dsems:
                owner.dsem = self.free_dsems.pop()
            else:
                self.nbuf += 1
                owner.dsem = f"d{self.nbuf}"
                self._mksem(owner.dsem)
            if owner.local:
                self.phase_dsems.append(owner.dsem)
        k = owner.dsem
        waits = self._deps(q, reads, writes)
        if (not group) and self.cnt[k] > 0 and self.seen[q].get(k, 0) < self.cnt[k]:
            self.seen[q][k] = self.cnt[k]
            waits.append((k, self.cnt[k]))
        self.cnt[k] += 16
        v = self.cnt[k]
        self.prog[q].append((waits, fn, (k, 16)))
        for b in writes:
            b.w[k] = v
        for b in reads:
            b.r[k] = v

    def barrier(self):
        for e in ENGS:
            waits = []
            for k, v in self.cnt.items():
                if v > 0 and self.seen[e].get(k, 0) < v and not (k == e):
                    self.seen[e][k] = v
                    waits.append((k, v))
            if waits:
                self.prog[e].append((waits, None, None))

    def emit(self):
        nc = self.nc
        sems = self.sems
        prog = self.prog
        with nc.Block() as block:
            def mk(ekey):
                def body(engine):
                    for waits, fn, inc in prog[ekey]:
                        for k, v in waits:
                            engine.wait_ge(sems[k], v)
                        if fn is not None:
                            fn(engine).then_inc(sems[inc[0]], inc[1])
                return body
            block.tensor(mk("pe"))
            block.scalar(mk("act"))
            block.vector(mk("dve"))
            block.gpsimd(mk("pool"))
            block.sync(mk("sp"))


def MM(S, ob, o, lb, l, rb, r, start=True, stop=True):
    S.op("pe", lambda e: e.matmul(o, l, r, start=start, stop=stop), reads=[lb, rb], writes=[ob])


def TRP(S, ob, o, ib, i, idb, idap):
    S.op("pe", lambda e: e.transpose(o, i, idap), reads=[ib, idb], writes=[ob])


def TT(S, eng, ob, o, ab, a, bb, b, op):
    S.op(eng, lambda e: e.tensor_tensor(out=o, in0=a, in1=b, op=op), reads=[ab, bb], writes=[ob])


def TS(S, eng, ob, o, ab, a, s1, s2, op0, op1=None, xr=()):
    if op1 is None:
        S.op(eng, lambda e: e.tensor_scalar(out=o, in0=a, scalar1=s1, scalar2=None, op0=op0),
             reads=[ab, *xr], writes=[ob])
    else:
        S.op(eng, lambda e: e.tensor_scalar(out=o, in0=a, scalar1=s1, scalar2=s2, op0=op0, op1=op1),
             reads=[ab, *xr], writes=[ob])


def STT(S, ob, o, ab, a, sc, bb, b, op0, op1, xr=(), accum=None, xw=()):
    if accum is None:
        S.op("dve", lambda e: e.scalar_tensor_tensor(out=o, in0=a, scalar=sc, in1=b, op0=op0, op1=op1),
             reads=[ab, bb, *xr], writes=[ob])
    else:
        S.op("dve", lambda e: e.scalar_tensor_tensor(out=o, in0=a, scalar=sc, in1=b, op0=op0, op1=op1,
                                                     accum_out=accum),
             reads=[ab, bb, *xr], writes=[ob, *xw])


def ACT(S, ob, o, ab, a, func, bias=None, scale=None, xr=(), accum=None, xw=()):
    kw = {}
    if bias is not None:
        kw["bias"] = bias
    if scale is not None:
        kw["scale"] = scale
    if accum is not None:
        kw["accum_out"] = accum
    S.op("act", lambda e: e.activation(out=o, in_=a, func=func, **kw), reads=[ab, *xr], writes=[ob, *xw])


def CP(S, eng, ob, o, ab, a):
    if eng == "act":
        S.op("act", lambda e: e.activation(out=o, in_=a, func=AF.Identity), reads=[ab], writes=[ob])
    else:
        S.op(eng, lambda e: e.tensor_copy(out=o, in_=a), reads=[ab], writes=[ob])


def LOAD(S, q, db, d, sb, s, group=False):
    S.dma(q, lambda e: e.dma_start(out=d, in_=s), reads=[sb], writes=[db], owner=db, group=group)


def STORE(S, q, db, d, sb, s):
    S.dma(q, lambda e: e.dma_start(out=d, in_=s), reads=[sb], writes=[db], owner=sb)


CST = {}


def _cst_layout():
    off = 0
    for name, n in [("ident", 128), ("onesD", 128), ("ones", 128), ("ones256", 128), ("ustrict", 128),
                    ("triinc", 128), ("blockones", 128), ("strictlt", 128), ("mask01", 128),
                    ("csel0", 128), ("csel1", 128), ("rcnt", 64), ("iotaec", 32)]:
        CST[name] = (off, n)
        off += n
    return off


NCST = _cst_layout()


def make_consts():
    c = np.zeros((128, NCST), np.float32)
    i = np.arange(128)
    ch = i // 64

    def put(name, arr):
        o, n = CST[name]
        c[:, o:o + n] = arr
    put("ident", np.eye(128))
    put("onesD", np.full((128, 128), 1.0 / D))
    put("ones", np.ones((128, 128)))
    put("ones256", np.full((128, 128), 1.0 / 256))
    put("ustrict", (i[:, None] < i[None, :]))
    same = ch[:, None] == ch[None, :]
    put("triinc", (i[:, None] <= i[None, :]) & same)
    put("blockones", same)
    put("strictlt", (i[None, :] < i[:, None]) & same)
    put("mask01", (i[:, None] <= i[None, :]) & same)
    put("csel0", np.repeat((i < 64)[:, None], 128, 1))
    put("csel1", np.repeat((i >= 64)[:, None], 128, 1))
    rc = np.zeros((128, 4, 16))
    for g, w in enumerate((2, 4, 8, 16)):
        rc[:, g, :] = 1.0 / np.minimum(np.arange(16) + 1, w)
    put("rcnt", rc.reshape(128, 64))
    put("iotaec", np.repeat((np.arange(32) * CAP)[None, :], 128, 0))
    return c


PV = {}


def _pv_layout():
    off = 0
    items = [("pool_scale", 8)]
    for l in range(2):
        items += [(f"ln_mix_g{l}", 8), (f"ln_mix_b{l}", 8), (f"ln_xa_g{l}", 8), (f"ln_xa_b{l}", 8)]
    items += [("conv_w", 128), ("conv_b", 32), ("norm_g", 16), ("dskip", 16)]
    for l in range(2):
        items += [(f"bg{l}", 256), (f"bu{l}", 256)]
    for name, n in items:
        PV[name] = (off, n)
        off += n
    return off


NPV = _pv_layout()


def fm(v):
    return np.ascontiguousarray(v.reshape(-1, 128).T)


def make_pv(inp):
    p = np.zeros((128, NPV), np.float32)

    def put(name, arr):
        o, n = PV[name]
        p[:, o:o + n] = arr.reshape(128, n)
    put("pool_scale", fm(inp["pool_scale"][0]))
    for l in range(2):
        put(f"ln_mix_g{l}", fm(inp["ln_mix_g"][l]))
        put(f"ln_mix_b{l}", fm(inp["ln_mix_b"][l]))
        put(f"ln_xa_g{l}", fm(inp["ln_xa_g"][l]))
        put(f"ln_xa_b{l}", fm(inp["ln_xa_b"][l]))
    cw = inp["ssm_conv_w"][0]
    put("conv_w", np.stack([fm(cw[k]) for k in range(4)], axis=2))
    put("conv_b", fm(inp["ssm_conv_b"][0]))
    put("norm_g", fm(inp["ssm_norm_g"][0]))
    put("dskip", fm(np.repeat(inp["ssm_d"][0], 64)))
    for l in range(2):
        b = inp["moe_b_gate_up"][l]
        put(f"bg{l}", np.stack([fm(b[e, 0::2]) for e in range(NE)], axis=1))
        put(f"bu{l}", np.stack([fm(b[e, 1::2]) for e in range(NE)], axis=1))
    return p


class K:
    pass


def build(phases=None, dbg=False):
    nc = bass.Bass("TRN2", target_bir_lowering=False)
    k = K()
    k.nc = nc
    k.phases = phases

    def din(name, shape, dtype=F32):
        return Buf(name, nc.dram_tensor(name, list(shape), dtype, kind="ExternalInput"))

    def dscr(name, shape, dtype=F32):
        return Buf(name, nc.dram_tensor(name, list(shape), dtype,
                                        kind="ExternalOutput" if dbg else "Internal"))

    k.xT = din("xT", [D, T])
    k.memT = din("memT", [D, 256])
    k.cst = din("cst", [128, NCST])
    k.pv = din("pv", [128, NPV])
    k.pool_w_in = din("pool_w_in", [D, D])
    k.pool_w_grp = din("pool_w_grp", [4, 256, 256])
    k.pool_w_out = din("pool_w_out", [D, D])
    k.ssm_w_in = din("ssm_w_in", [D, 6176])
    k.ssm_w_out = din("ssm_w_out", [2048, D])
    k.rowp = din("rowp", [1, 64])
    k.xa_wq = din("xa_wq", [2, D, D])
    k.xa_wk = din("xa_wk", [2, D, D])
    k.xa_wv = din("xa_wv", [2, D, D])
    k.xa_wo = din("xa_wo", [2, D, D])
    k.w_router = din("moe_w_router", [2, D, NE])
    k.b_router = din("moe_b_router", [2, NE])
    k.w_gu = din("moe_w_gate_up", [2, NE, D, 2 * D])
    k.w_dn = din("moe_w_down", [2, NE, D, D])
    k.b_dn = din("moe_b_down", [2, NE, D])
    k.ln_mix_g = din("ln_mix_g", [2, D])
    k.ln_mix_b = din("ln_mix_b", [2, D])
    k.ln_ffn_g = din("ln_ffn_g", [2, D])
    k.ln_ffn_b = din("ln_ffn_b", [2, D])
    k.out = Buf("out", nc.dram_tensor("out", [T, D], F32, kind="ExternalOutput"))
    k.HA = dscr("HA", [D, T])
    k.HB = dscr("HB", [D, T])
    k.HC = dscr("HC", [D, T])
    k.HTOK = dscr("HTOK", [T, D])
    k.HCTOK = dscr("HCTOK", [T, D])
    k.ZSD = dscr("ZSD", [2048, T])
    k.XBCD = dscr("XBCD", [4096, T])
    k.XS = dscr("XS", [NE * CAP, D])
    k.YS = dscr("YS", [NE * CAP, D])

    with ExitStack() as st:
        S = Sched(nc, st)
        k.S = S
        k.cs = S.sbuf("cst_s", [128, NCST], glob=True)
        k.pvs = S.sbuf("pv_s", [128, NPV], glob=True)
        k.dest = S.sbuf("dest_i", [128, NTL, 4], I32, glob=True)
        k.gates = S.sbuf("gates", [128, NTL, 4], glob=True)
        LOAD(S, "sp", k.cs, k.cs[:, :], k.cst, k.cst.t.ap())
        k.csr = S.sbuf("csr_s", [128, 384], RT, glob=True)
        for ci, cn in enumerate(("onesD", "strictlt", "ones")):
            LOAD(S, "pool" if FAST else "sp", k.csr, k.csr[:, ci * 128:(ci + 1) * 128], k.cst,
                 k.cst.t.ap()[:, CST[cn][0]:CST[cn][0] + 128], group=(ci > 0))
        LOAD(S, "sp", k.pvs, k.pvs[:, :], k.pv, k.pv.t.ap())

        def want(p):
            return phases is None or p in phases

        if want("pool0"):
            phase_pool(k, k.xT, k.HA)
        if want("xa0"):
            phase_xattn(k, 0, k.HA, k.HB, k.HTOK)
        if want("moe0"):
            phase_moe(k, 0, k.HB, k.HTOK, k.HC, None)
        if want("ssd1"):
            phase_ssd_a(k, k.HC)
            phase_ssd_b(k, k.HC, k.HA)
        if want("xa1"):
            phase_xattn(k, 1, k.HA, k.HB, k.HTOK)
        if want("moe1"):
            phase_moe(k, 1, k.HB, k.HTOK, None, k.out)
        S.barrier()
        S.emit()
    return nc


def C(k, name):
    o, n = CST[name]
    return k.cs[:, o:o + n]


def P(k, name, j=None):
    o, n = PV[name]
    if j is None:
        return k.pvs[:, o:o + n]
    return k.pvs[:, o + j:o + j + 1]


def fmview(buf):
    return buf.t.ap().rearrange("(c p) t -> p c t", p=128)


def wview(ap2d):
    return ap2d.rearrange("(c p) f -> p c f", p=128)


class WStream:
    def __init__(self, S, n=4):
        self.S = S
        self.rot = Rot([S.sbuf(f"wt{i}", [128, 8, 512], RT) for i in range(n)])

    def get(self, wbuf, view, k0, c0, ncols=512, nk=8):
        b = self.rot.next()
        LOAD(self.S, "pool" if FAST else "sp", b, R(b[:, 0:nk, 0:ncols]), wbuf, view[:, k0:k0 + nk, c0:c0 + ncols])
        return b


def proj_fm(k, ws, psr, wbuf, view, xb, x_of_kc, N, n_oc, evac, kchunks=8, col0=0):
    S = k.S
    for wc in range((n_oc + 3) // 4):
        wts = [ws.get(wbuf, view, kh * 8, col0 + wc * 512) for kh in range(kchunks // 8)]
        for o4 in range(min(4, n_oc - wc * 4)):
            oc = wc * 4 + o4
            ps = psr.next()
            for kc in range(kchunks):
                MM(S, ps, ps[:, 0:N], wts[kc // 8], R(wts[kc // 8][:, kc % 8, o4 * 128:(o4 + 1) * 128]),
                   xb, R(x_of_kc(kc)), start=(kc == 0), stop=(kc == kchunks - 1))
            evac(oc, ps, ps[:, 0:N])


def proj_fm16(k, ws, psr, wbuf, view, xb, x_of_kc, N, n_oc, evac):
    S = k.S
    for wc in range(n_oc // 4):
        pss = [psr.next() for _ in range(4)]
        for kh in range(2):
            w = ws.get(wbuf, view, kh * 8, wc * 512)
            for o4 in range(4):
                for kc in range(8):
                    MM(S, pss[o4], pss[o4][:, 0:N], w, w[:, kc, o4 * 128:(o4 + 1) * 128], xb, x_of_kc(kh * 8 + kc),
                       start=(kh == 0 and kc == 0), stop=(kh == 1 and kc == 7))
        for o4 in range(4):
            evac(wc * 4 + o4, pss[o4], pss[o4][:, 0:N])


def ln_fm(k, psr, Z, N, gname, bname, OUT, scr):
    S = k.S
    onesD = C(k, "onesD")
    pm = psr.next()
    for c in range(8):
        MM(S, pm, pm[:, 0:N], k.cs, onesD, Z, Z[:, c, 0:N], start=(c == 0), stop=(c == 7))
    mean = k.ln_mean
    CP(S, "act", mean, mean[:, 0:N], pm, pm[:, 0:N])
    TT(S, "dve", Z, Z[:, :, 0:N], Z, Z[:, :, 0:N], mean, mean[:, 0:N].unsqueeze(1).broadcast_to([128, 8, N]),
       ALU.subtract)
    ACT(S, scr, scr[:, :, 0:N], Z, Z[:, :, 0:N], AF.Square)
    pv = psr.next()
    for c in range(8):
        MM(S, pv, pv[:, 0:N], k.csr, k.csr[:, 0:128], scr, scr[:, c, 0:N], start=(c == 0), stop=(c == 7))
    rstd = k.ln_rstd
    ACT(S, rstd, rstd[:, 0:N], pv, pv[:, 0:N], AF.Sqrt, bias=k.eps_t[:, 0:1], xr=[k.eps_t])
    S.op("dve", lambda e: e.reciprocal(out=rstd[:, 0:N], in_=rstd[:, 0:N]), reads=[rstd], writes=[rstd])
    TT(S, "dve", Z, Z[:, :, 0:N], Z, Z[:, :, 0:N], rstd, rstd[:, 0:N].unsqueeze(1).broadcast_to([128, 8, N]),
       ALU.mult)
    for c in range(8):
        ACT(S, OUT, OUT[:, c, 0:N], Z, Z[:, c, 0:N], AF.Identity, bias=P(k, bname, c), scale=P(k, gname, c),
            xr=[k.pvs])


def common_small(k):
    S = k.S
    k.ln_mean = S.sbuf("ln_mean", [128, 512])
    k.ln_rstd = S.sbuf("ln_rstd", [128, 512])
    k.eps_t = S.sbuf("eps_t", [128, 1])
    S.op("dve", lambda e: e.memset(k.eps_t[:, :], EPS), writes=[k.eps_t])


def phase_pool(k, HIN, HOUT):
    S = k.S
    S.begin_phase()
    common_small(k)
    N = 512
    ws = WStream(S, 2)
    psr = Rot([S.psum(f"ps{i}", [128, 512]) for i in range(8)])
    xr = Rot([S.sbuf(f"xt{i}", [128, 8, N]) for i in range(2)])
    U = Rot([S.sbuf(f"U{i}", [128, 8, 16 + N]) for i in range(2)])
    TA = S.sbuf("TA", [128, 2, 16 + N])
    TB = S.sbuf("TB", [128, 2, 16 + N])
    M = S.sbuf("M", [128, 8, N], RT)
    Y = S.sbuf("Y", [128, 8, N], RT)
    Z = S.sbuf("Z", [128, 8, N])
    xtr = S.sbuf("xtr", [128, 8, N], RT)
    wg = S.sbuf("wg", [128, 4, 2, 256], RT)
    LOAD(S, "pool" if FAST else "sp", wg, R(wg[:, :, :, :]), k.pool_w_grp,
         k.pool_w_grp.t.ap().rearrange("g (kc p) d -> p g kc d", p=128))
    vin = wview(k.pool_w_in.t.ap())
    vout = wview(k.pool_w_out.t.ap())
    hin = fmview(HIN)
    hout = fmview(HOUT)
    o_rc, _ = CST["rcnt"]
    prevU = None
    for tb in range(T // N):
        xt = xr.next()
        LOAD(S, "sp", xt, xt[:, :, :], HIN, hin[:, :, tb * N:(tb + 1) * N])
        LOAD(S, "pool" if FAST else "sp", xtr, R(xtr[:, :, :]), HIN, hin[:, :, tb * N:(tb + 1) * N])
        u = U.next()
        if tb == 0:
            S.op("pool", lambda e, u=u: e.memset(u[:, :, 0:16], 0.0), writes=[u])
        else:
            CP(S, "dve", u, u[:, :, 0:16], prevU, prevU[:, :, N:N + 16])

        def ev_u(oc, ps, pa, u=u):
            CP(S, "act", u, u[:, oc, 16:16 + N], ps, pa)
        proj_fm(k, ws, psr, k.pool_w_in, vin, xtr, lambda kc: xtr[:, kc, :], N, 8, ev_u)
        W_ = 16 + N
        for g in range(4):
            w = 2 << g
            cur_b, cur = u, (lambda a, b, g=g: u[:, 2 * g:2 * g + 2, a:b])
            sh = 1
            tgt = [TA, TB]
            for step in range(g + 1):
                t_ = tgt[step % 2]
                lo = 2 * sh - 1
                TT(S, "dve", t_, t_[:, :, lo:W_], cur_b, cur(lo, W_), cur_b, cur(lo - sh, W_ - sh), ALU.add)
                cur_b, cur = t_, (lambda a, b, t_=t_: t_[:, :, a:b])
                sh *= 2
            if tb == 0:
                rc = k.cs[:, o_rc + g * 16:o_rc + (g + 1) * 16]
                TS(S, "dve", cur_b, cur(32, W_), cur_b, cur(32, W_), 1.0 / w, None, ALU.mult)
                TT(S, "dve", cur_b, cur(16, 32), cur_b, cur(16, 32), k.cs,
                   rc.unsqueeze(1).broadcast_to([128, 2, 16]), ALU.mult)
                TT(S, "dve", M, R(M[:, 2 * g:2 * g + 2, :]), cur_b, cur(16, W_), u, u[:, 2 * g:2 * g + 2, 16:W_],
                   ALU.subtract)
            else:
                STT(S, M, R(M[:, 2 * g:2 * g + 2, :]), cur_b, cur(16, W_), 1.0 / w, u, u[:, 2 * g:2 * g + 2, 16:W_],
                    ALU.mult, ALU.subtract)
        for g in range(4):
            for j in range(2):
                ps = psr.next()
                for kc in range(2):
                    MM(S, ps, ps[:, 0:N], wg, R(wg[:, g, kc, j * 128:(j + 1) * 128]), M, R(M[:, 2 * g + kc, :]),
                       start=(kc == 0), stop=(kc == 1))
                oc = 2 * g + j
                ACT(S, Y, R(Y[:, oc, :]), ps, ps[:, 0:N], AF.Identity, scale=P(k, "pool_scale", oc), xr=[k.pvs])

        def ev_z(oc, ps, pa, xt=xt):
            STT(S, Z, Z[:, oc, :], xt, xt[:, oc, :], ALPHA, ps, pa, ALU.mult, ALU.add)
        proj_fm(k, ws, psr, k.pool_w_out, vout, Y, lambda kc: Y[:, kc, :], N, 8, ev_z)
        o = Z
        ln_fm(k, psr, Z, N, "ln_mix_g0", "ln_mix_b0", o, M)
        STORE(S, "act", HOUT, hout[:, :, tb * N:(tb + 1) * N], o, o[:, :, :])
        prevU = u
    S.end_phase()


def phase_xattn(k, l, HIN, HOUT, HTOK):
    S = k.S
    S.begin_phase()
    common_small(k)
    N = 512
    ws = WStream(S, 2)
    psr = Rot([S.psum(f"ps{i}", [128, 512]) for i in range(8)])
    xr = Rot([S.sbuf(f"xt{i}", [128, 8, N]) for i in range(2)])
    memT = S.sbuf("memT", [128, 8, 256], RT)
    kT = S.sbuf("kT", [128, 8, 256], RT)
    v = S.sbuf("v", [128, 2, D], RT)
    qT = S.sbuf("qT", [128, 8, N], RT)
    Pm = Rot([S.sbuf(f"Pm{i}", [128, 4, 256]) for i in range(2)])
    pT = S.sbuf("pT", [128, 2, 4, N], RT)
    OT = S.sbuf("OT", [128, 8, N], RT)
    Z = S.sbuf("Z", [128, 8, N])
    xtr = S.sbuf("xtr", [128, 8, N], RT)
    htk = Rot([S.sbuf(f"htk{i}", [128, D]) for i in range(2)])
    sm = Rot([S.sbuf(f"sm{i}", [128, 16]) for i in range(2)])
    ident = C(k, "ident")
    LOAD(S, "pool" if FAST else "sp", memT, R(memT[:, :, :]), k.memT, fmview(k.memT))
    vq = wview(k.xa_wq.t.ap()[l])
    vk = wview(k.xa_wk.t.ap()[l])
    vv = wview(k.xa_wv.t.ap()[l])
    vo = wview(k.xa_wo.t.ap()[l])

    def ev_k(oc, ps, pa):
        CP(S, "act", kT, R(kT[:, oc, :]), ps, pa)
    proj_fm(k, ws, psr, k.xa_wk, vk, memT, lambda kc: memT[:, kc, :], 256, 8, ev_k)
    for wc in range(2):
        w = ws.get(k.xa_wv, vv, 0, wc * 512)
        for mc in range(2):
            ps = psr.next()
            for kc in range(8):
                MM(S, ps, ps[:, :], memT, R(memT[:, kc, mc * 128:(mc + 1) * 128]), w, R(w[:, kc, :]),
                   start=(kc == 0), stop=(kc == 7))
            CP(S, "act", v, R(v[:, mc, wc * 512:(wc + 1) * 512]), ps, ps[:, :])
    hin = fmview(HIN)
    hout = fmview(HOUT)
    gname, bname = f"ln_xa_g{l}", f"ln_xa_b{l}"
    for tb in range(T // N):
        xt = xr.next()
        LOAD(S, "sp", xt, xt[:, :, :], HIN, hin[:, :, tb * N:(tb + 1) * N])
        LOAD(S, "pool" if FAST else "sp", xtr, R(xtr[:, :, :]), HIN, hin[:, :, tb * N:(tb + 1) * N])

        def ev_q(oc, ps, pa):
            S.op("act", lambda e: e.activation(out=R(qT[:, oc, :]), in_=pa, func=AF.Identity, scale=0.0625),
                 reads=[ps], writes=[qT])
        proj_fm(k, ws, psr, k.xa_wq, vq, xtr, lambda kc: xtr[:, kc, :], N, 8, ev_q)
        def xa_stage_a(tt):
                tsl = slice(tt * 128, (tt + 1) * 128)
                pm = Pm.next()
                s_ = sm.next()
                banks = [psr.next(), psr.next()]
                for h in range(4):
                    ps = banks[h // 2]
                    for dc in range(2):
                        MM(S, ps, ps[:, (h % 2) * 256:(h % 2 + 1) * 256], qT, R(qT[:, 2 * h + dc, tsl]),
                           kT, R(kT[:, 2 * h + dc, :]), start=(dc == 0), stop=(dc == 1))
                for b2 in range(2):
                    ps = banks[b2]
                    S.op("dve", lambda e, ps=ps, s_=s_, b2=b2: e.tensor_reduce(
                        out=s_[:, 2 * b2:2 * b2 + 2], in_=ps[:, :].rearrange("p (a b) -> p a b", a=2),
                        axis=AX.X, op=ALU.max), reads=[ps], writes=[s_])
                TS(S, "dve", s_, s_[:, 4:8], s_, s_[:, 0:4], -1.0, None, ALU.mult)
                for h in range(4):
                    ps = banks[h // 2]
                    ACT(S, pm, pm[:, h, :], ps, ps[:, (h % 2) * 256:(h % 2 + 1) * 256], AF.Exp,
                        bias=s_[:, 4 + h:5 + h], xr=[s_], accum=s_[:, 8 + h:9 + h], xw=[s_])
                S.op("dve", lambda e, s_=s_: e.reciprocal(out=s_[:, 12:16], in_=s_[:, 8:12]), reads=[s_], writes=[s_])
                TT(S, "dve", pm, pm[:, :, :], pm, pm[:, :, :], s_, s_[:, 12:16].unsqueeze(2).broadcast_to([128, 4, 256]),
                   ALU.mult)
                return tsl, pm

        def xa_stage_b(tt, tsl, pm):
                for mc in range(2):
                    ps = psr.next()
                    for h in range(4):
                        TRP(S, ps, ps[:, h * 128:(h + 1) * 128], pm, pm[:, h, mc * 128:(mc + 1) * 128], k.cs, ident)
                    CP(S, "act" if mc == 0 else "dve", pT, R(pT[:, mc, :, tsl]), ps,
                       ps[:, :].rearrange("p (h t) -> p h t", h=4))
        cur = xa_stage_a(0)
        for tt in range(4):
            nxt = xa_stage_a(tt + 1) if tt < 3 else None
            xa_stage_b(tt, *cur)
            cur = nxt
        for h in range(4):
            for dc in range(2):
                ps = psr.next()
                for mc in range(2):
                    MM(S, ps, ps[:, :], v, R(v[:, mc, h * 256 + dc * 128:h * 256 + (dc + 1) * 128]),
                       pT, R(pT[:, mc, h, :]), start=(mc == 0), stop=(mc == 1))
                CP(S, "act", OT, R(OT[:, 2 * h + dc, :]), ps, ps[:, :])

        def ev_z(oc, ps, pa, xt=xt):
            STT(S, Z, Z[:, oc, :], xt, xt[:, oc, :], ALPHA, ps, pa, ALU.mult, ALU.add)
        proj_fm(k, ws, psr, k.xa_wo, vo, OT, lambda kc: OT[:, kc, :], N, 8, ev_z)
        o = Z
        ln_fm(k, psr, Z, N, gname, bname, o, OT)
        STORE(S, "act", HOUT, hout[:, :, tb * N:(tb + 1) * N], o, o[:, :, :])
        for tt in range(4):
            ht = htk.next()
            for half in range(2):
                ps = psr.next()
                for c4 in range(4):
                    c = half * 4 + c4
                    TRP(S, ps, ps[:, c4 * 128:(c4 + 1) * 128], o, o[:, c, tt * 128:(tt + 1) * 128], k.cs, ident)
                CP(S, "act" if half == 0 else "dve", ht, ht[:, half * 512:(half + 1) * 512], ps, ps[:, :])
            r0 = tb * N + tt * 128
            STORE(S, "act", HTOK, HTOK.t.ap()[r0:r0 + 128, :], ht, ht[:, :])
    S.end_phase()


def phase_moe(k, l, HFM, HTOK, HOUT_FM, OUT_TOK):
    S = k.S
    ident = C(k, "ident")
    S.begin_phase()
    psr = Rot([S.psum(f"ps{i}", [128, 512]) for i in range(8)])
    xr = Rot([S.sbuf(f"xt{i}", [128, 8, 512]) for i in range(2)])
    hr = Rot([S.sbuf(f"hrow{i}", [128, D]) for i in range(3)])
    wr = S.sbuf("wr", [128, 8, NE])
    brt = S.sbuf("brt", [128, NE])
    cum = S.sbuf("cum", [128, NE])
    lgr = Rot([S.sbuf(f"lg{i}", [128, NE]) for i in range(2)])
    t8r = Rot([S.sbuf(f"t8{i}", [128, 16]) for i in range(2)])
    mkr = Rot([S.sbuf(f"mk{i}", [128, NE]) for i in range(2)])
    pcr = Rot([S.sbuf(f"pc{i}", [128, NE]) for i in range(2)])
    jnk = S.sbuf("jnk", [128, NE])
    dfr = Rot([S.sbuf(f"df{i}", [128, 4]) for i in range(2)])
    LOAD(S, "sp", wr, wr[:, :, :], k.w_router, wview(k.w_router.t.ap()[l]))
    LOAD(S, "sp", brt, brt[:, :], k.b_router, k.b_router.t.ap()[l:l + 1, :].broadcast_to([128, NE]))
    S.op("dve", lambda e: e.memset(cum[:, :], 0.0), writes=[cum])
    hfm = fmview(HFM)
    ustrict = C(k, "ustrict")
    ones = C(k, "ones")
    iotaec = C(k, "iotaec")
    xcur = [None]

    def rt_stage_a(ti):
            if ti % 4 == 0:
                xcur[0] = xr.next()
                LOAD(S, "sp", xcur[0], xcur[0][:, :, :], HFM, hfm[:, :, ti * 128:ti * 128 + 512])
            xt = xcur[0]
            tsl = slice((ti % 4) * 128, (ti % 4 + 1) * 128)
            hrow = hr.next()
            LOAD(S, "sp", hrow, hrow[:, :], HTOK, HTOK.t.ap()[ti * 128:(ti + 1) * 128, :])
            ps = psr.next()
            for kc in range(8):
                MM(S, ps, ps[:, 0:NE], xt, xt[:, kc, tsl], wr, wr[:, kc, :], start=(kc == 0), stop=(kc == 7))
            lg = lgr.next()
            TT(S, "dve", lg, lg[:, :], ps, ps[:, 0:NE], brt, brt[:, :], ALU.add)
            t8 = t8r.next()
            S.op("dve", lambda e, t8=t8, lg=lg: e.max(out=t8[:, 0:8], in_=lg[:, :]), reads=[lg], writes=[t8])
            mk = mkr.next()
            TS(S, "dve", mk, mk[:, :], lg, lg[:, :], t8[:, 3:4], None, ALU.is_ge, xr=[t8])
            pp = psr.next()
            MM(S, pp, pp[:, 0:NE], k.cs, ustrict, mk, mk[:, :])
            MM(S, pp, pp[:, NE:2 * NE], k.cs, ones, mk, mk[:, :])
            return hrow, lg, t8, pp

    def rt_stage_b(ti, hrow, lg, t8, pp):
            pc = pcr.next()
            TT(S, "dve", pc, pc[:, :], pp, pp[:, 0:NE], cum, cum[:, :], ALU.add)
            TT(S, "dve", pc, pc[:, :], pc, pc[:, :], k.cs, iotaec, ALU.add)
            TT(S, "dve", cum, cum[:, :], pp, pp[:, NE:2 * NE], cum, cum[:, :], ALU.add)
            df = dfr.next()
            for kk in range(4):
                STT(S, jnk, jnk[:, :], lg, lg[:, :], t8[:, kk:kk + 1], pc, pc[:, :], ALU.is_equal, ALU.mult,
                    xr=[t8], accum=df[:, kk:kk + 1], xw=[df])
            TS(S, "dve", df, df[:, :], df, df[:, :], 0.0, float(NE * CAP - 1), ALU.max, ALU.min)
            CP(S, "dve", k.dest, k.dest[:, ti, :], df, df[:, :])
            TS(S, "dve", t8, t8[:, 8:9], t8, t8[:, 0:1], -1.0, None, ALU.mult)
            ACT(S, t8, t8[:, 10:14], t8, t8[:, 0:4], AF.Exp, bias=t8[:, 8:9], accum=t8[:, 9:10])
            S.op("dve", lambda e, t8=t8: e.reciprocal(out=t8[:, 14:15], in_=t8[:, 9:10]), reads=[t8], writes=[t8])
            TS(S, "dve", k.gates, k.gates[:, ti, :], t8, t8[:, 10:14], t8[:, 14:15], None, ALU.mult)
            for kk in range(4):
                S.dma("pool", lambda e, hrow=hrow, ti=ti, kk=kk: e.indirect_dma_start(
                    out=k.XS.t.ap(), out_offset=bass.IndirectOffsetOnAxis(ap=k.dest[:, ti, kk:kk + 1], axis=0),
                    in_=hrow[:, :], in_offset=None),
                    reads=[hrow, k.dest, k.gates], writes=[k.XS], owner=hrow, group=(kk > 0))
    cur = rt_stage_a(0)
    for ti in range(NTL):
        nxt = rt_stage_a(ti + 1) if ti < NTL - 1 else None
        rt_stage_b(ti, *cur)
        cur = nxt
    S.end_phase()
    if k.phases is not None and f"moe{l}_route_only" in k.phases:
        return
    S.begin_phase()
    ws = WStream(S, 4)
    psr = Rot([S.psum(f"ps{i}", [128, 512]) for i in range(8)])
    xrow = Rot([S.sbuf(f"xrow{i}", [128, D]) for i in range(3)])
    yrow = Rot([S.sbuf(f"yrow{i}", [128, D]) for i in range(2)])
    XT = S.sbuf("XsT", [128, 8, CAP], RT)
    AT = S.sbuf("actT", [128, 8, CAP], RT)
    gp = Rot([S.sbuf(f"gp{i}", [128, CAP // 2]) for i in range(2)])
    sg = Rot([S.sbuf(f"sg{i}", [128, CAP // 2]) for i in range(2)])
    up = Rot([S.sbuf(f"up{i}", [128, CAP // 2]) for i in range(2)])
    bdr = Rot([S.sbuf(f"bd{i}", [1, D], RT) for i in range(2)])
    o_bg, _ = PV[f"bg{l}"]
    o_bu, _ = PV[f"bu{l}"]
    HN = CAP // 2
    NRB = CAP // 128
    ones = C(k, "ones")
    xtr2 = Rot([XT, S.sbuf("XsT2", [128, 8, CAP], RT)])

    def build_xt(e_):
        XT = xtr2.next()
        for rb in range(NRB):
            xw = xrow.next()
            r0 = e_ * CAP + rb * 128
            LOAD(S, "sp", xw, xw[:, :], k.XS, k.XS.t.ap()[r0:r0 + 128, :])
            for half in range(2):
                ps = psr.next()
                for c4 in range(4):
                    c = half * 4 + c4
                    TRP(S, ps, ps[:, c4 * 128:(c4 + 1) * 128], xw, xw[:, c * 128:(c + 1) * 128], k.cs, ident)
                CP(S, "act" if half == 0 else "dve", XT, R(XT[:, half * 4:half * 4 + 4, rb * 128:(rb + 1) * 128]),
                   ps, ps[:, :].rearrange("p (c t) -> p c t", c=4))
        return XT

    def gate_up(e_, XT):
        vgu = wview(k.w_gu.t.ap()[l, e_])
        for wc in range(4):
            w = ws.get(k.w_gu, vgu, 0, wc * 512)
            for f2 in range(2):
                fc = wc * 2 + f2
                base = f2 * 256
                for hf in range(2):
                    cs_ = slice(hf * HN, (hf + 1) * HN)
                    pg = psr.next()
                    pu = psr.next()
                    for kc in range(8):
                        MM(S, pg, pg[:, 0:HN], w, R(w[:, kc, base:base + 256:2]), XT, R(XT[:, kc, cs_]),
                           start=(kc == 0), stop=(kc == 7))
                    for kc in range(8):
                        MM(S, pu, pu[:, 0:HN], w, R(w[:, kc, base + 1:base + 256:2]), XT, R(XT[:, kc, cs_]),
                           start=(kc == 0), stop=(kc == 7))
                    g_ = gp.next()
                    s_ = sg.next()
                    u_ = up.next()
                    bgc = k.pvs[:, o_bg + e_ * 8 + fc:o_bg + e_ * 8 + fc + 1]
                    buc = k.pvs[:, o_bu + e_ * 8 + fc:o_bu + e_ * 8 + fc + 1]
                    TS(S, "dve", g_, g_[:, :], pg, pg[:, 0:HN], bgc, 7.0, ALU.add, ALU.min, xr=[k.pvs])
                    ACT(S, s_, s_[:, :], g_, g_[:, :], AF.Sigmoid, scale=1.702)
                    TS(S, "dve", u_, u_[:, :], pu, pu[:, 0:HN], buc, 7.0, ALU.add, ALU.min, xr=[k.pvs])
                    TS(S, "dve", u_, u_[:, :], u_, u_[:, :], -7.0, 1.0, ALU.max, ALU.add)
                    TT(S, "dve", g_, g_[:, :], g_, g_[:, :], s_, s_[:, :], ALU.mult)
                    TT(S, "dve", AT, R(AT[:, fc, cs_]), g_, g_[:, :], u_, u_[:, :], ALU.mult)

    def down(e_):
        bd = bdr.next()
        LOAD(S, "pool" if FAST else "sp", bd, bd[:, :], k.b_dn, k.b_dn.t.ap()[l, e_:e_ + 1, :])
        vdn = wview(k.w_dn.t.ap()[l, e_])
        wd = [ws.get(k.w_dn, vdn, 0, oc * 512) for oc in range(2)]
        for rb in range(NRB):
            yw = yrow.next()
            for oc in range(2):
                ps = psr.next()
                for fc in range(8):
                    MM(S, ps, ps[:, :], AT, R(AT[:, fc, rb * 128:(rb + 1) * 128]), wd[oc], R(wd[oc][:, fc, :]),
                       start=(fc == 0), stop=False)
                MM(S, ps, ps[:, :], k.csr, k.csr[0:1, 256:384], bd, bd[0:1, oc * 512:(oc + 1) * 512],
                   start=False, stop=True)
                CP(S, "act", yw, yw[:, oc * 512:(oc + 1) * 512], ps, ps[:, :])
            r0 = e_ * CAP + rb * 128
            STORE(S, "act", k.YS, k.YS.t.ap()[r0:r0 + 128, :], yw, yw[:, :])

    xt_cur = build_xt(0)
    for e_ in range(NE):
        gate_up(e_, xt_cur)
        xt_nxt = build_xt(e_ + 1) if e_ < NE - 1 else None
        down(e_)
        xt_cur = xt_nxt
    S.end_phase()
    S.begin_phase()
    psr = Rot([S.psum(f"ps{i}", [128, 512]) for i in range(8)])
    hr = Rot([S.sbuf(f"hrow{i}", [128, D]) for i in range(2)])
    yk = Rot([S.sbuf(f"yk{i}", [128, D]) for i in range(8)])
    acc = Rot([S.sbuf(f"acc{i}", [128, D]) for i in range(2)])
    ofm = Rot([S.sbuf(f"ofm{i}", [128, 8, 128]) for i in range(2)])
    gB = S.sbuf("gB", [128, D])
    bB = S.sbuf("bB", [128, D])
    st_ = Rot([S.sbuf(f"st{i}", [128, 16]) for i in range(2)])
    eps_t = S.sbuf("eps_t", [128, 1])
    S.op("dve", lambda e: e.memset(eps_t[:, :], EPS), writes=[eps_t])
    LOAD(S, "sp", gB, gB[:, :], k.ln_ffn_g, k.ln_ffn_g.t.ap()[l:l + 1, :].broadcast_to([128, D]))
    LOAD(S, "sp", bB, bB[:, :], k.ln_ffn_b, k.ln_ffn_b.t.ap()[l:l + 1, :].broadcast_to([128, D]))
    for ti in range(NTL):
        hrow = hr.next()
        LOAD(S, "sp", hrow, hrow[:, :], HTOK, HTOK.t.ap()[ti * 128:(ti + 1) * 128, :])
        ys = []
        for kk in range(4):
            y = yk.next()
            S.dma("pool", lambda e, y=y, ti=ti, kk=kk: e.indirect_dma_start(
                out=y[:, :], out_offset=None, in_=k.YS.t.ap(),
                in_offset=bass.IndirectOffsetOnAxis(ap=k.dest[:, ti, kk:kk + 1], axis=0)),
                reads=[k.YS, k.dest], writes=[y], owner=y)
            ys.append(y)
        a = acc.next()
        TS(S, "dve", a, a[:, :], ys[0], ys[0][:, :], k.gates[:, ti, 0:1], None, ALU.mult, xr=[k.gates])
        for kk in range(1, 4):
            STT(S, a, a[:, :], ys[kk], ys[kk][:, :], k.gates[:, ti, kk:kk + 1], a, a[:, :], ALU.mult, ALU.add,
                xr=[k.gates])
        STT(S, a, a[:, :], hrow, hrow[:, :], ALPHA, a, a[:, :], ALU.mult, ALU.add)
        s_ = st_.next()
        for hf in range(2):
            S.op("dve", lambda e, s_=s_, a=a, hf=hf: e.bn_stats(out=s_[:, hf * 6:(hf + 1) * 6],
                                                               in_=a[:, hf * 512:(hf + 1) * 512]),
                 reads=[a], writes=[s_])
        S.op("dve", lambda e, s_=s_: e.bn_aggr(out=s_[:, 12:14], in_=s_[:, 0:12]), reads=[s_], writes=[s_])
        ACT(S, s_, s_[:, 14:15], s_, s_[:, 13:14], AF.Sqrt, bias=eps_t[:, 0:1], xr=[eps_t])
        S.op("dve", lambda e, s_=s_: e.reciprocal(out=s_[:, 15:16], in_=s_[:, 14:15]), reads=[s_], writes=[s_])
        TS(S, "dve", a, a[:, :], a, a[:, :], s_[:, 12:13], s_[:, 15:16], ALU.subtract, ALU.mult, xr=[s_])
        TT(S, "dve", a, a[:, :], a, a[:, :], gB, gB[:, :], ALU.mult)
        TT(S, "dve", a, a[:, :], a, a[:, :], bB, bB[:, :], ALU.add)
        if OUT_TOK is not None:
            STORE(S, "act", OUT_TOK, OUT_TOK.t.ap()[ti * 128:(ti + 1) * 128, :], a, a[:, :])
        if HOUT_FM is not None:
            STORE(S, "act", k.HCTOK, k.HCTOK.t.ap()[ti * 128:(ti + 1) * 128, :], a, a[:, :])
            o = ofm.next()
            for half in range(2):
                ps = psr.next()
                for c4 in range(4):
                    c = half * 4 + c4
                    TRP(S, ps, ps[:, c4 * 128:(c4 + 1) * 128], a, a[:, c * 128:(c + 1) * 128], k.cs, ident)
                CP(S, "act", o, o[:, half * 4:half * 4 + 4, :], ps, ps[:, :].rearrange("p (c t) -> p c t", c=4))
            STORE(S, "act", HOUT_FM, fmview(HOUT_FM)[:, :, ti * 128:(ti + 1) * 128], o, o[:, :, :])
    S.end_phase()


def phase_ssd_a(k, HIN):
    S = k.S
    S.begin_phase()
    N = 512
    ws = WStream(S, 3)
    psr = Rot([S.psum(f"ps{i}", [128, 512]) for i in range(8)])
    xr = Rot([S.sbuf(f"xtr{i}", [128, 8, N], RT) for i in range(2)])
    zst = Rot([S.sbuf(f"zst{i}", [128, N]) for i in range(4)])
    ucr = Rot([S.sbuf(f"uc{i}", [128, 3 + N]) for i in range(4)])
    acr = Rot([S.sbuf(f"ac{i}", [128, N]) for i in range(4)])
    xst = Rot([S.sbuf(f"xst{i}", [128, N]) for i in range(4)])
    halo = S.sbuf("halo", [128, 32, 4])
    S.op("dve", lambda e: e.memset(halo[:, :, :], 0.0), writes=[halo])
    vin = wview(k.ssm_w_in.t.ap())
    hin = fmview(HIN)
    o_cw, _ = PV["conv_w"]
    zsd = k.ZSD.t.ap()
    xbd = k.XBCD.t.ap()
    for tb in range(T // N):
        xtr = xr.next()
        LOAD(S, "pool" if FAST else "sp", xtr, xtr[:, :, :], HIN, hin[:, :, tb * N:(tb + 1) * N])

        def ev_zs(oc, ps, pa, tb=tb):
            z_ = zst.next()
            ACT(S, z_, z_[:, :], ps, pa, AF.Silu)
            STORE(S, "act", k.ZSD, zsd[oc * 128:(oc + 1) * 128, tb * N:(tb + 1) * N], z_, z_[:, :])
        proj_fm(k, ws, psr, k.ssm_w_in, vin, xtr, lambda kc, xtr=xtr: xtr[:, kc, :], N, 16, ev_zs, col0=0)

        pend = []

        def flush():
            while pend:
                pend.pop(0)()

        def ev_u(c, ps, pa, tb=tb):
            uc = ucr.next()
            CP(S, "dve", uc, uc[:, 0:3], halo, halo[:, c, 0:3])
            CP(S, "act", uc, uc[:, 3:3 + N], ps, pa)
            flush()
            CP(S, "dve", halo, halo[:, c, 0:3], uc, uc[:, N:N + 3])
            acc = acr.next()
            wcol = lambda kk: k.pvs[:, o_cw + c * 4 + kk:o_cw + c * 4 + kk + 1]
            ACT(S, acc, acc[:, :], uc, uc[:, 0:N], AF.Identity, bias=P(k, "conv_b", c), scale=wcol(0), xr=[k.pvs])
            for kk in range(1, 4):
                STT(S, acc, acc[:, :], uc, uc[:, kk:kk + N], wcol(kk), acc, acc[:, :], ALU.mult, ALU.add, xr=[k.pvs])

            def fin(c=c, acc=acc, tb=tb):
                x_ = xst.next()
                ACT(S, x_, x_[:, :], acc, acc[:, :], AF.Silu)
                STORE(S, "act", k.XBCD, xbd[c * 128:(c + 1) * 128, tb * N:(tb + 1) * N], x_, x_[:, :])
            pend.append(fin)
        proj_fm(k, ws, psr, k.ssm_w_in, vin, xtr, lambda kc, xtr=xtr: xtr[:, kc, :], N, 32, ev_u, col0=2048)
        flush()
    S.end_phase()


def phase_ssd_b(k, HIN, HOUT):
    S = k.S
    S.begin_phase()
    common_small(k)
    N = 128
    ws = WStream(S, 2)
    psr = Rot([S.psum(f"ps{i}", [128, 512]) for i in range(4)])
    SPS = S.psum("SPS", [128, 2048])
    xr = Rot([S.sbuf(f"xt{i}", [128, 8, N]) for i in range(2)])
    wdt = S.sbuf("wdt", [128, 8, 32])
    rowb = S.sbuf("rowb", [128, 64])
    Abc = S.sbuf("Abc", [128, 32])
    one_t = S.sbuf("one_t", [128, 1])
    zsr = Rot([S.sbuf(f"ZS{i}", [128, 16, N]) for i in range(2)])
    xbr = Rot([S.sbuf(f"XBC{i}", [128, 32, N]) for i in range(2)])
    XD = S.sbuf("XD", [128, 2048])
    XDD = S.sbuf("XDD", [128, 2048])
    BTK = S.sbuf("BTK", [128, 8, 128])
    tk = Rot([S.sbuf(f"tk{i}", [128, 8, 32]) for i in range(2)])
    cdr = Rot([S.sbuf(f"cdr{i}", [128, 64]) for i in range(2)])
    arr = Rot([S.sbuf(f"ar{i}", [128, 4, 128], RT) for i in range(2)])
    cbr = Rot([S.sbuf(f"cbm{i}", [128, 128]) for i in range(2)])
    mtr = Rot([S.sbuf(f"mt{i}", [128, 4, 128]) for i in range(2)])
    cer = Rot([S.sbuf(f"ce{i}", [128, 4, 128]) for i in range(2)])
    H = [S.sbuf(f"H{i}", [128, 2048]) for i in range(2)]
    Yt = S.sbuf("Yt", [128, 16, N])
    SQ = S.sbuf("SQ", [128, 16, N])
    rst = S.sbuf("rst", [128, 8, N])
    YR = S.sbuf("YR", [128, 16, N], RT)
    xtk = S.sbuf("xtk", [128, D])
    ZT = S.sbuf("ZT", [128, D])
    gB = S.sbuf("gBm", [128, D])
    bB = S.sbuf("bBm", [128, D])
    st_ = Rot([S.sbuf(f"stm{i}", [128, 16]) for i in range(2)])
    LOAD(S, "sp", gB, gB[:, :], k.ln_mix_g, k.ln_mix_g.t.ap()[1:2, :].broadcast_to([128, D]))
    LOAD(S, "sp", bB, bB[:, :], k.ln_mix_b, k.ln_mix_b.t.ap()[1:2, :].broadcast_to([128, D]))
    ident = C(k, "ident")
    triinc, blockones, strictlt, mask01 = C(k, "triinc"), C(k, "blockones"), C(k, "strictlt"), C(k, "mask01")
    csel = [C(k, "csel0"), C(k, "csel1")]
    ones, ones256 = C(k, "ones"), C(k, "ones256")
    vin = wview(k.ssm_w_in.t.ap())
    vout = wview(k.ssm_w_out.t.ap())
    hin = fmview(HIN)
    hout = fmview(HOUT)
    o_cw, _ = PV["conv_w"]
    LOAD(S, "sp", wdt, wdt[:, :, :], k.ssm_w_in, vin[:, :, 6144:6176])
    LOAD(S, "sp", rowb, rowb[:, :], k.rowp, k.rowp.t.ap()[0:1, :].broadcast_to([128, 64]))
    S.op("dve", lambda e: e.memset(one_t[:, :], 1.0), writes=[one_t])
    S.op("dve", lambda e: e.memset(H[0][:, :], 0.0), writes=[H[0]])
    ACT(S, Abc, Abc[:, :], rowb, rowb[:, 32:64], AF.Exp)
    TS(S, "dve", Abc, Abc[:, :], Abc, Abc[:, :], -1.0, None, ALU.mult)
    for tb in range(T // N):
        xt = xr.next()
        LOAD(S, "sp", xt, xt[:, :, :], HIN, hin[:, :, tb * N:(tb + 1) * N])
        ZS = zsr.next()
        XBC = xbr.next()
        for q4 in range(2):
            LOAD(S, "sp", ZS, ZS[:, q4 * 8:(q4 + 1) * 8, :], k.ZSD,
                 fmview(k.ZSD)[:, q4 * 8:(q4 + 1) * 8, tb * N:(tb + 1) * N], group=(q4 > 0))
        for q4 in range(4):
            LOAD(S, "sp", XBC, XBC[:, q4 * 8:(q4 + 1) * 8, :], k.XBCD,
                 fmview(k.XBCD)[:, q4 * 8:(q4 + 1) * 8, tb * N:(tb + 1) * N], group=(q4 > 0))
        t_ = tk.next()
        dt, a_, acs, dte, w1, tmp = (t_[:, i, :] for i in range(6))
        ps = psr.next()
        for kc in range(8):
            MM(S, ps, ps[:, 0:32], xt, xt[:, kc, :], wdt, wdt[:, kc, :], start=(kc == 0), stop=(kc == 7))
        TT(S, "dve", t_, tmp, ps, ps[:, 0:32], rowb, rowb[:, 0:32], ALU.add)
        ACT(S, t_, tmp, t_, tmp, AF.Exp)
        ACT(S, t_, dt, t_, tmp, AF.Ln, bias=one_t[:, 0:1], xr=[one_t])
        TT(S, "dve", t_, a_, t_, dt, Abc, Abc[:, :], ALU.mult)
        pa = psr.next()
        MM(S, pa, pa[:, 0:32], k.cs, triinc, t_, a_)
        MM(S, pa, pa[:, 32:64], k.cs, blockones, t_, a_)
        MM(S, pa, pa[:, 64:96], k.cs, csel[0], t_, a_)
        MM(S, pa, pa[:, 96:128], k.cs, csel[1], t_, a_)
        CP(S, "act", t_, acs, pa, pa[:, 0:32])
        TT(S, "dve", t_, dte, pa, pa[:, 32:64], t_, acs, ALU.subtract)
        ACT(S, t_, dte, t_, dte, AF.Exp)
        TT(S, "dve", t_, w1, t_, dt, t_, dte, ALU.mult)
        cd = cdr.next()
        ACT(S, cd, cd[:, :], pa, pa[:, 64:128], AF.Exp)
        for q in range(4):
            ps = psr.next()
            for j in range(4):
                TRP(S, ps, ps[:, j * 128:(j + 1) * 128], XBC, XBC[:, q * 4 + j, :], k.cs, ident)
            pv3 = ps[:, :].rearrange("p (h d) -> p h d", h=8)
            for dst, sc in ((XD, dt), (XDD, w1)):
                TT(S, "dve", dst, dst[:, q * 512:(q + 1) * 512].rearrange("p (h d) -> p h d", h=8), ps, pv3,
                   t_, sc[:, q * 8:(q + 1) * 8].unsqueeze(2).broadcast_to([128, 8, 64]), ALU.mult)
        for q in range(2):
            ps = psr.next()
            for j in range(4):
                TRP(S, ps, ps[:, j * 128:(j + 1) * 128], XBC, XBC[:, 16 + q * 4 + j, :], k.cs, ident)
            CP(S, "act", BTK, BTK[:, q * 4:(q + 1) * 4, :], ps, ps[:, :].rearrange("p (g n) -> p g n", g=4))
        H0, H1 = H[0], H[1]

        def states(cp):
            for g in range(8):
                MM(S, SPS, SPS[:, g * 256:(g + 1) * 256], BTK, BTK[cp * 64:(cp + 1) * 64, g, :],
                   XDD, XDD[cp * 64:(cp + 1) * 64, g * 256:(g + 1) * 256])

        def hupd(dst, src, cp):
            TT(S, "dve", dst, dst[:, :].rearrange("p (h d) -> p h d", h=32), src,
               src[:, :].rearrange("p (h d) -> p h d", h=32), cd,
               cd[:, cp * 32:(cp + 1) * 32].unsqueeze(2).broadcast_to([128, 32, 64]), ALU.mult)
            TT(S, "dve", dst, dst[:, :], SPS, SPS[:, :], dst, dst[:, :], ALU.add)
        states(0)
        hupd(H1, H0, 0)
        Hc = [H0, H1]
        def stage1(g):
            ar = arr.next()
            TT(S, "dve", ar, ar[:, :, :], k.cs, triinc.unsqueeze(1).broadcast_to([128, 4, 128]), t_,
               a_[:, 4 * g:4 * g + 4].unsqueeze(2).broadcast_to([128, 4, 128]), ALU.mult)
            ar2 = ar[:, :, :].rearrange("p h i -> p (h i)")
            pcb = psr.next()
            MM(S, pcb, pcb[:, 0:128], XBC, XBC[:, 16 + g, :], XBC, XBC[:, 24 + g, :])
            pseg = psr.next()
            MM(S, pseg, pseg[:, :], k.csr, k.csr[:, 128:256], ar, ar2)
            pacs = psr.next()
            MM(S, pacs, pacs[:, :], k.csr, k.csr[:, 256:384], ar, ar2)
            cbm = cbr.next()
            TT(S, "dve", cbm, cbm[:, :], pcb, pcb[:, 0:128], k.cs, mask01, ALU.mult)
            mt = mtr.next()
            ACT(S, mt, mt[:, :, :], pseg, pseg[:, :].rearrange("p (h i) -> p h i", h=4), AF.Exp)
            ce = cer.next()
            ACT(S, ce, ce[:, :, :], pacs, pacs[:, :].rearrange("p (h i) -> p h i", h=4), AF.Exp)
            TT(S, "dve", mt, mt[:, :, :], mt, mt[:, :, :], cbm, cbm[:, :].unsqueeze(1).broadcast_to([128, 4, 128]),
               ALU.mult)
            TT(S, "dve", ce, ce[:, :, :], ce, ce[:, :, :], XBC,
               XBC[:, 24 + g, :].unsqueeze(1).broadcast_to([128, 4, 128]), ALU.mult)
            return mt, ce

        def stage2(g, mt, ce):
            py = psr.next()
            for r in range(4):
                hd = 4 * g + r
                prt = slice((r % 2) * 64, (r % 2) * 64 + 64)
                c0 = (r // 2) * 128
                MM(S, py, py[prt, c0:c0 + 128], XD, XD[:, hd * 64:(hd + 1) * 64], mt, mt[:, r, :],
                   start=True, stop=False)
                for cp in range(2):
                    MM(S, py, py[prt, c0 + cp * 64:c0 + (cp + 1) * 64], Hc[cp], Hc[cp][:, hd * 64:(hd + 1) * 64],
                       ce, ce[:, r, cp * 64:(cp + 1) * 64], start=False, stop=(cp == 1))
            for j in range(2):
                c = 2 * g + j
                STT(S, Yt, Yt[:, c, :], XBC, XBC[:, c, :], P(k, "dskip", c), py, py[:, j * 128:(j + 1) * 128],
                    ALU.mult, ALU.add, xr=[k.pvs])
        cur = stage1(0)
        for g in range(8):
            nxt = stage1(g + 1) if g < 7 else None
            stage2(g, *cur)
            cur = nxt
        states(1)
        hupd(H0, H1, 1)
        TT(S, "dve", Yt, Yt[:, :, :], Yt, Yt[:, :, :], ZS, ZS[:, :, :], ALU.mult)
        ACT(S, SQ, SQ[:, :, :], Yt, Yt[:, :, :], AF.Square)
        for b2 in range(2):
            pm = psr.next()
            for g4 in range(4):
                g = b2 * 4 + g4
                for j in range(2):
                    MM(S, pm, pm[:, g4 * 128:(g4 + 1) * 128], k.cs, ones256, SQ, SQ[:, 2 * g + j, :],
                       start=(j == 0), stop=(j == 1))
            ACT(S, rst, rst[:, b2 * 4:(b2 + 1) * 4, :], pm, pm[:, :].rearrange("p (g t) -> p g t", g=4), AF.Sqrt,
                bias=k.eps_t[:, 0:1], xr=[k.eps_t])
        S.op("dve", lambda e: e.reciprocal(out=rst[:, :, :], in_=rst[:, :, :]), reads=[rst], writes=[rst])
        Yt4 = Yt[:, :, :].rearrange("p (g j) t -> p g j t", j=2)
        TT(S, "dve", Yt, Yt4, Yt, Yt4, rst, rst[:, :, :].unsqueeze(2).broadcast_to([128, 8, 2, N]), ALU.mult)
        TT(S, "dve", YR, YR[:, :, :], Yt, Yt[:, :, :], k.pvs,
           P(k, "norm_g").unsqueeze(2).broadcast_to([128, 16, N]), ALU.mult)

        LOAD(S, "sp", xtk, xtk[:, :], k.HCTOK, k.HCTOK.t.ap()[tb * N:(tb + 1) * N, :])
        for wc in range(2):
            ps = psr.next()
            for kh in range(2):
                w = ws.get(k.ssm_w_out, vout, kh * 8, wc * 512)
                for kc in range(8):
                    MM(S, ps, ps[:, :], YR, YR[:, kh * 8 + kc, :], w, w[:, kc, :],
                       start=(kh == 0 and kc == 0), stop=(kh == 1 and kc == 7))
            STT(S, ZT, ZT[:, wc * 512:(wc + 1) * 512], xtk, xtk[:, wc * 512:(wc + 1) * 512], ALPHA, ps, ps[:, :],
                ALU.mult, ALU.add)
        s_ = st_.next()
        for hf in range(2):
            S.op("dve", lambda e, s_=s_, hf=hf: e.bn_stats(out=s_[:, hf * 6:(hf + 1) * 6],
                                                          in_=ZT[:, hf * 512:(hf + 1) * 512]),
                 reads=[ZT], writes=[s_])
        S.op("dve", lambda e, s_=s_: e.bn_aggr(out=s_[:, 12:14], in_=s_[:, 0:12]), reads=[s_], writes=[s_])
        ACT(S, s_, s_[:, 14:15], s_, s_[:, 13:14], AF.Sqrt, bias=k.eps_t[:, 0:1], xr=[k.eps_t])
        S.op("dve", lambda e, s_=s_: e.reciprocal(out=s_[:, 15:16], in_=s_[:, 14:15]), reads=[s_], writes=[s_])
        TS(S, "dve", ZT, ZT[:, :], ZT, ZT[:, :], s_[:, 12:13], s_[:, 15:16], ALU.subtract, ALU.mult, xr=[s_])
        TT(S, "dve", ZT, ZT[:, :], ZT, ZT[:, :], gB, gB[:, :], ALU.mult)
        TT(S, "dve", ZT, ZT[:, :], ZT, ZT[:, :], bB, bB[:, :], ALU.add)
        for half in range(2):
            ps = psr.next()
            for c4 in range(4):
                c = half * 4 + c4
                TRP(S, ps, ps[:, c4 * 128:(c4 + 1) * 128], ZT, ZT[:, c * 128:(c + 1) * 128], k.cs, ident)
            CP(S, "act", SQ, SQ[:, half * 4:half * 4 + 4, :], ps, ps[:, :].rearrange("p (c t) -> p c t", c=4))
        STORE(S, "act", HOUT, hout[:, :, tb * N:(tb + 1) * N], SQ, SQ[:, 0:8, :])
    S.end_phase()


_NC_CACHE = {}


def make_in_maps(inp):
    cst = make_consts()
    pv = make_pv(inp)
    rowp = np.concatenate([inp["ssm_dt_bias"][0], inp["ssm_a_log"][0]])[None, :].astype(np.float32)
    shared = {
        "cst": cst, "pv": pv, "rowp": rowp,
        "pool_w_in": inp["pool_w_in"][0], "pool_w_grp": inp["pool_w_grp"][0], "pool_w_out": inp["pool_w_out"][0],
        "ssm_w_in": inp["ssm_w_in"][0], "ssm_w_out": inp["ssm_w_out"][0],
        "xa_wq": inp["xa_wq"], "xa_wk": inp["xa_wk"], "xa_wv": inp["xa_wv"], "xa_wo": inp["xa_wo"],
        "moe_w_router": inp["moe_w_router"], "moe_b_router": inp["moe_b_router"],
        "moe_w_gate_up": inp["moe_w_gate_up"], "moe_w_down": inp["moe_w_down"], "moe_b_down": inp["moe_b_down"],
        "ln_ffn_g": inp["ln_ffn_g"], "ln_ffn_b": inp["ln_ffn_b"],
        "ln_mix_g": inp["ln_mix_g"], "ln_mix_b": inp["ln_mix_b"],
    }
    shared = {kk: np.ascontiguousarray(np.asarray(vv, dtype=np.float32)) for kk, vv in shared.items()}
    maps = []
    for b in range(8):
        m = dict(shared)
        m["xT"] = np.ascontiguousarray(np.asarray(inp["x"][b]).T)
        m["memT"] = np.ascontiguousarray(np.asarray(inp["mem"][b]).T)
        maps.append(m)
    return maps


def kernel(**inputs):
    inp = {kk: np.asarray(vv) for kk, vv in inputs.items()}
    if "nc" not in _NC_CACHE:
        _NC_CACHE["nc"] = build()
    nc = _NC_CACHE["nc"]
    maps = make_in_maps(inp)
    res = run_bass_kernel_spmd(nc, maps, core_ids=list(range(8)))
    return np.stack([res.results[b]["out"] for b in range(8)], axis=0).astype(np.float32)
```

```python
from contextlib import ExitStack
import numpy as np
import concourse.bass as bass
import concourse.mybir as mybir
from concourse.bass_utils import run_bass_kernel_spmd

F32 = mybir.dt.float32
F32R = mybir.dt.float32r
FAST = True


RT = F32R if FAST else F32


def R(ap):
    return ap


I32 = mybir.dt.int32
AF = mybir.ActivationFunctionType
ALU = mybir.AluOpType
AX = mybir.AxisListType

T = 4096
D = 1024
NTL = 32
ALPHA = float(4 ** 0.25)
EPS = 1e-5
CAP = 768
NE = 32
DEBUG = False

ENGS = ("pe", "act", "dve", "pool", "sp")


class Buf:
    __slots__ = ("name", "t", "w", "r", "dsem", "local")

    def __init__(self, name, t):
        self.name = name
        self.t = t
        self.w = {}
        self.r = {}
        self.dsem = None
        self.local = False

    def __getitem__(self, idx):
        return self.t[idx]


class Rot:
    def __init__(self, bufs):
        self.b = bufs
        self.i = 0

    def next(self):
        b = self.b[self.i % len(self.b)]
        self.i += 1
        return b


class Sched:
    def __init__(self, nc, stack):
        self.nc = nc
        self.stack = stack
        self.sems = {}
        self.cnt = {}
        for e in ENGS:
            self._mksem(e)
        self.seen = {e: {} for e in ENGS}
        self.prog = {e: [] for e in ENGS}
        self.nbuf = 0
        self.phase_stack = None
        self.free_dsems = []

    def _mksem(self, key):
        h = self.stack.enter_context(self.nc.semaphore(f"s_{key}"))
        self.sems[key] = h
        self.cnt[key] = 0

    def begin_phase(self):
        self.phase_stack = ExitStack()
        self.phase_dsems = []

    def end_phase(self):
        self.barrier()
        self.phase_stack.close()
        self.phase_stack = None
        self.free_dsems.extend(self.phase_dsems)
        self.phase_dsems = []

    def sbuf(self, name, shape, dtype=F32, glob=False):
        st = self.stack if (glob or self.phase_stack is None) else self.phase_stack
        self.uid = getattr(self, "uid", 0) + 1
        name = f"{name}_u{self.uid}"
        t = st.enter_context(self.nc.sbuf_tensor(name, list(shape), dtype))
        b = Buf(name, t)
        b.local = st is not self.stack
        return b

    def psum(self, name, shape, dtype=F32):
        st = self.stack if self.phase_stack is None else self.phase_stack
        self.uid = getattr(self, "uid", 0) + 1
        name = f"{name}_u{self.uid}"
        t = st.enter_context(self.nc.psum_tensor(name, list(shape), dtype))
        return Buf(name, t)

    def _deps(self, e, reads, writes):
        need = {}
        for b in reads:
            for k, v in b.w.items():
                if need.get(k, 0) < v:
                    need[k] = v
        for b in writes:
            for k, v in b.w.items():
                if need.get(k, 0) < v:
                    need[k] = v
            for k, v in b.r.items():
                if need.get(k, 0) < v:
                    need[k] = v
        waits = []
        seen = self.seen[e]
        for k, v in need.items():
            if e == "pe" and k == "pe":
                continue
            if seen.get(k, 0) < v:
                seen[k] = v
                waits.append((k, v))
        return waits

    def op(self, e, fn, reads=(), writes=()):
        waits = self._deps(e, reads, writes)
        self.cnt[e] += 1
        v = self.cnt[e]
        self.prog[e].append((waits, fn, (e, 1)))
        for b in writes:
            b.w[e] = v
        for b in reads:
            b.r[e] = v

    def dma(self, q, fn, reads=(), writes=(), owner=None, group=False):
        if owner is None:
            owner = writes[0]
        if owner.dsem is None:
            if self.free_dsems:
                owner.dsem = self.free_dsems.pop()
            else:
                self.nbuf += 1
                owner.dsem = f"d{self.nbuf}"
                self._mksem(owner.dsem)
            if owner.local:
                self.phase_dsems.append(owner.dsem)
        k = owner.dsem
        waits = self._deps(q, reads, writes)
        if (not group) and self.cnt[k] > 0 and self.seen[q].get(k, 0) < self.cnt[k]:
            self.seen[q][k] = self.cnt[k]
            waits.append((k, self.cnt[k]))
        self.cnt[k] += 16
        v = self.cnt[k]
        self.prog[q].append((waits, fn, (k, 16)))
        for b in writes:
            b.w[k] = v
        for b in reads:
            b.r[k] = v

    def barrier(self):
        for e in ENGS:
            waits = []
            for k, v in self.cnt.items():
                if v > 0 and self.seen[e].get(k, 0) < v and not (k == e):
                    self.seen[e][k] = v
                    waits.append((k, v))
            if waits:
                self.prog[e].append((waits, None, None))

    def emit(self):
        nc = self.nc
        sems = self.sems
        prog = self.prog
        with nc.Block() as block:
            def mk(ekey):
                def body(engine):
                    for waits, fn, inc in prog[ekey]:
                        for k, v in waits:
                            engine.wait_ge(sems[k], v)
                        if fn is not None:
                            fn(engine).then_inc(sems[inc[0]], inc[1])
                return body
            block.tensor(mk("pe"))
            block.scalar(mk("act"))
            block.vector(mk("dve"))
            block.gpsimd(mk("pool"))
            block.sync(mk("sp"))


def MM(S, ob, o, lb, l, rb, r, start=True, stop=True):
    S.op("pe", lambda e: e.matmul(o, l, r, start=start, stop=stop), reads=[lb, rb], writes=[ob])


def TRP(S, ob, o, ib, i, idb, idap):
    S.op("pe", lambda e: e.transpose(o, i, idap), reads=[ib, idb], writes=[ob])


def TT(S, eng, ob, o, ab, a, bb, b, op):
    S.op(eng, lambda e: e.tensor_tensor(out=o, in0=a, in1=b, op=op), reads=[ab, bb], writes=[ob])


def TS(S, eng, ob, o, ab, a, s1, s2, op0, op1=None, xr=()):
    if op1 is None:
        S.op(eng, lambda e: e.tensor_scalar(out=o, in0=a, scalar1=s1, scalar2=None, op0=op0),
             reads=[ab, *xr], writes=[ob])
    else:
        S.op(eng, lambda e: e.tensor_scalar(out=o, in0=a, scalar1=s1, scalar2=s2, op0=op0, op1=op1),
             reads=[ab, *xr], writes=[ob])


def STT(S, ob, o, ab, a, sc, bb, b, op0, op1, xr=(), accum=None, xw=()):
    if accum is None:
        S.op("dve", lambda e: e.scalar_tensor_tensor(out=o, in0=a, scalar=sc, in1=b, op0=op0, op1=op1),
             reads=[ab, bb, *xr], writes=[ob])
    else:
        S.op("dve", lambda e: e.scalar_tensor_tensor(out=o, in0=a, scalar=sc, in1=b, op0=op0, op1=op1,
                                                     accum_out=accum),
             reads=[ab, bb, *xr], writes=[ob, *xw])


def ACT(S, ob, o, ab, a, func, bias=None, scale=None, xr=(), accum=None, xw=()):
    kw = {}
    if bias is not None:
        kw["bias"] = bias
    if scale is not None:
        kw["scale"] = scale
    if accum is not None:
        kw["accum_out"] = accum
    S.op("act", lambda e: e.activation(out=o, in_=a, func=func, **kw), reads=[ab, *xr], writes=[ob, *xw])


def CP(S, eng, ob, o, ab, a):
    if eng == "act":
        S.op("act", lambda e: e.activation(out=o, in_=a, func=AF.Identity), reads=[ab], writes=[ob])
    else:
        S.op(eng, lambda e: e.tensor_copy(out=o, in_=a), reads=[ab], writes=[ob])


def LOAD(S, q, db, d, sb, s, group=False):
    S.dma(q, lambda e: e.dma_start(out=d, in_=s), reads=[sb], writes=[db], owner=db, group=group)


def STORE(S, q, db, d, sb, s):
    S.dma(q, lambda e: e.dma_start(out=d, in_=s), reads=[sb], writes=[db], owner=sb)


CST = {}


def _cst_layout():
    off = 0
    for name, n in [("ident", 128), ("onesD", 128), ("ones", 128), ("ones256", 128), ("ustrict", 128),
                    ("triinc", 128), ("blockones", 128), ("strictlt", 128), ("mask01", 128),
                    ("csel0", 128), ("csel1", 128), ("rcnt", 64), ("iotaec", 32)]:
        CST[name] = (off, n)
        off += n
    return off


NCST = _cst_layout()


def make_consts():
    c = np.zeros((128, NCST), np.float32)
    i = np.arange(128)
    ch = i // 64

    def put(name, arr):
        o, n = CST[name]
        c[:, o:o + n] = arr
    put("ident", np.eye(128))
    put("onesD", np.full((128, 128), 1.0 / D))
    put("ones", np.ones((128, 128)))
    put("ones256", np.full((128, 128), 1.0 / 256))
    put("ustrict", (i[:, None] < i[None, :]))
    same = ch[:, None] == ch[None, :]
    put("triinc", (i[:, None] <= i[None, :]) & same)
    put("blockones", same)
    put("strictlt", (i[None, :] < i[:, None]) & same)
    put("mask01", (i[:, None] <= i[None, :]) & same)
    put("csel0", np.repeat((i < 64)[:, None], 128, 1))
    put("csel1", np.repeat((i >= 64)[:, None], 128, 1))
    rc = np.zeros((128, 4, 16))
    for g, w in enumerate((2, 4, 8, 16)):
        rc[:, g, :] = 1.0 / np.minimum(np.arange(16) + 1, w)
    put("rcnt", rc.reshape(128, 64))
    put("iotaec", np.repeat((np.arange(32) * CAP)[None, :], 128, 0))
    return c


PV = {}


def _pv_layout():
    off = 0
    items = [("pool_scale", 8)]
    for l in range(2):
        items += [(f"ln_mix_g{l}", 8), (f"ln_mix_b{l}", 8), (f"ln_xa_g{l}", 8), (f"ln_xa_b{l}", 8)]
    items += [("conv_w", 128), ("conv_b", 32), ("norm_g", 16), ("dskip", 16)]
    for l in range(2):
        items += [(f"bg{l}", 256), (f"bu{l}", 256)]
    for name, n in items:
        PV[name] = (off, n)
        off += n
    return off


NPV = _pv_layout()


def fm(v):
    return np.ascontiguousarray(v.reshape(-1, 128).T)


def make_pv(inp):
    p = np.zeros((128, NPV), np.float32)

    def put(name, arr):
        o, n = PV[name]
        p[:, o:o + n] = arr.reshape(128, n)
    put("pool_scale", fm(inp["pool_scale"][0]))
    for l in range(2):
        put(f"ln_mix_g{l}", fm(inp["ln_mix_g"][l]))
        put(f"ln_mix_b{l}", fm(inp["ln_mix_b"][l]))
        put(f"ln_xa_g{l}", fm(inp["ln_xa_g"][l]))
        put(f"ln_xa_b{l}", fm(inp["ln_xa_b"][l]))
    cw = inp["ssm_conv_w"][0]
    put("conv_w", np.stack([fm(cw[k]) for k in range(4)], axis=2))
    put("conv_b", fm(inp["ssm_conv_b"][0]))
    put("norm_g", fm(inp["ssm_norm_g"][0]))
    put("dskip", fm(np.repeat(inp["ssm_d"][0], 64)))
    for l in range(2):
        b = inp["moe_b_gate_up"][l]
        put(f"bg{l}", np.stack([fm(b[e, 0::2]) for e in range(NE)], axis=1))
        put(f"bu{l}", np.stack([fm(b[e, 1::2]) for e in range(NE)], axis=1))
    return p


class K:
    pass


def build(phases=None, dbg=False):
    nc = bass.Bass("TRN2", target_bir_lowering=False)
    k = K()
    k.nc = nc
    k.phases = phases

    def din(name, shape, dtype=F32):
        return Buf(name, nc.dram_tensor(name, list(shape), dtype, kind="ExternalInput"))

    def dscr(name, shape, dtype=F32):
        return Buf(name, nc.dram_tensor(name, list(shape), dtype,
                                        kind="ExternalOutput" if dbg else "Internal"))

    k.xT = din("xT", [D, T])
    k.memT = din("memT", [D, 256])
    k.cst = din("cst", [128, NCST])
    k.pv = din("pv", [128, NPV])
    k.pool_w_in = din("pool_w_in", [D, D])
    k.pool_w_grp = din("pool_w_grp", [4, 256, 256])
    k.pool_w_out = din("pool_w_out", [D, D])
    k.ssm_w_in = din("ssm_w_in", [D, 6176])
    k.ssm_w_out = din("ssm_w_out", [2048, D])
    k.rowp = din("rowp", [1, 64])
    k.xa_wq = din("xa_wq", [2, D, D])
    k.xa_wk = din("xa_wk", [2, D, D])
    k.xa_wv = din("xa_wv", [2, D, D])
    k.xa_wo = din("xa_wo", [2, D, D])
    k.w_router = din("moe_w_router", [2, D, NE])
    k.b_router = din("moe_b_router", [2, NE])
    k.w_gu = din("moe_w_gate_up", [2, NE, D, 2 * D])
    k.w_dn = din("moe_w_down", [2, NE, D, D])
    k.b_dn = din("moe_b_down", [2, NE, D])
    k.ln_mix_g = din("ln_mix_g", [2, D])
    k.ln_mix_b = din("ln_mix_b", [2, D])
    k.ln_ffn_g = din("ln_ffn_g", [2, D])
    k.ln_ffn_b = din("ln_ffn_b", [2, D])
    k.out = Buf("out", nc.dram_tensor("out", [T, D], F32, kind="ExternalOutput"))
    k.HA = dscr("HA", [D, T])
    k.HB = dscr("HB", [D, T])
    k.HC = dscr("HC", [D, T])
    k.HTOK = dscr("HTOK", [T, D])
    k.HCTOK = dscr("HCTOK", [T, D])
    k.ZSD = dscr("ZSD", [2048, T])
    k.XBCD = dscr("XBCD", [4096, T])
    k.XS = dscr("XS", [NE * CAP, D])
    k.YS = dscr("YS", [NE * CAP, D])

    with ExitStack() as st:
        S = Sched(nc, st)
        k.S = S
        k.cs = S.sbuf("cst_s", [128, NCST], glob=True)
        k.pvs = S.sbuf("pv_s", [128, NPV], glob=True)
        k.dest = S.sbuf("dest_i", [128, NTL, 4], I32, glob=True)
        k.gates = S.sbuf("gates", [128, NTL, 4], glob=True)
        LOAD(S, "sp", k.cs, k.cs[:, :], k.cst, k.cst.t.ap())
        k.csr = S.sbuf("csr_s", [128, 384], RT, glob=True)
        for ci, cn in enumerate(("onesD", "strictlt", "ones")):
            LOAD(S, "pool" if FAST else "sp", k.csr, k.csr[:, ci * 128:(ci + 1) * 128], k.cst,
                 k.cst.t.ap()[:, CST[cn][0]:CST[cn][0] + 128], group=(ci > 0))
        LOAD(S, "sp", k.pvs, k.pvs[:, :], k.pv, k.pv.t.ap())

        def want(p):
            return phases is None or p in phases

        if want("pool0"):
            phase_pool(k, k.xT, k.HA)
        if want("xa0"):
            phase_xattn(k, 0, k.HA, k.HB, k.HTOK)
        if want("moe0"):
            phase_moe(k, 0, k.HB, k.HTOK, k.HC, None)
        if want("ssd1"):
            phase_ssd_a(k, k.HC)
            phase_ssd_b(k, k.HC, k.HA)
        if want("xa1"):
            phase_xattn(k, 1, k.HA, k.HB, k.HTOK)
        if want("moe1"):
            phase_moe(k, 1, k.HB, k.HTOK, None, k.out)
        S.barrier()
        S.emit()
    return nc


def C(k, name):
    o, n = CST[name]
    return k.cs[:, o:o + n]


def P(k, name, j=None):
    o, n = PV[name]
    if j is None:
        return k.pvs[:, o:o + n]
    return k.pvs[:, o + j:o + j + 1]


def fmview(buf):
    return buf.t.ap().rearrange("(c p) t -> p c t", p=128)


def wview(ap2d):
    return ap2d.rearrange("(c p) f -> p c f", p=128)


class WStream:
    def __init__(self, S, n=4):
        self.S = S
        self.rot = Rot([S.sbuf(f"wt{i}", [128, 8, 512], RT) for i in range(n)])

    def get(self, wbuf, view, k0, c0, ncols=512, nk=8):
        b = self.rot.next()
        LOAD(self.S, "pool" if FAST else "sp", b, R(b[:, 0:nk, 0:ncols]), wbuf, view[:, k0:k0 + nk, c0:c0 + ncols])
        return b


def proj_fm(k, ws, psr, wbuf, view, xb, x_of_kc, N, n_oc, evac, kchunks=8, col0=0):
    S = k.S
    for wc in range((n_oc + 3) // 4):
        wts = [ws.get(wbuf, view, kh * 8, col0 + wc * 512) for kh in range(kchunks // 8)]
        for o4 in range(min(4, n_oc - wc * 4)):
            oc = wc * 4 + o4
            ps = psr.next()
            for kc in range(kchunks):
                MM(S, ps, ps[:, 0:N], wts[kc // 8], R(wts[kc // 8][:, kc % 8, o4 * 128:(o4 + 1) * 128]),
                   xb, R(x_of_kc(kc)), start=(kc == 0), stop=(kc == kchunks - 1))
            evac(oc, ps, ps[:, 0:N])


def proj_fm16(k, ws, psr, wbuf, view, xb, x_of_kc, N, n_oc, evac):
    S = k.S
    for wc in range(n_oc // 4):
        pss = [psr.next() for _ in range(4)]
        for kh in range(2):
            w = ws.get(wbuf, view, kh * 8, wc * 512)
            for o4 in range(4):
                for kc in range(8):
                    MM(S, pss[o4], pss[o4][:, 0:N], w, w[:, kc, o4 * 128:(o4 + 1) * 128], xb, x_of_kc(kh * 8 + kc),
                       start=(kh == 0 and kc == 0), stop=(kh == 1 and kc == 7))
        for o4 in range(4):
            evac(wc * 4 + o4, pss[o4], pss[o4][:, 0:N])


def ln_fm(k, psr, Z, N, gname, bname, OUT, scr):
    S = k.S
    onesD = C(k, "onesD")
    pm = psr.next()
    for c in range(8):
        MM(S, pm, pm[:, 0:N], k.cs, onesD, Z, Z[:, c, 0:N], start=(c == 0), stop=(c == 7))
    mean = k.ln_mean
    CP(S, "act", mean, mean[:, 0:N], pm, pm[:, 0:N])
    TT(S, "dve", Z, Z[:, :, 0:N], Z, Z[:, :, 0:N], mean, mean[:, 0:N].unsqueeze(1).broadcast_to([128, 8, N]),
       ALU.subtract)
    ACT(S, scr, scr[:, :, 0:N], Z, Z[:, :, 0:N], AF.Square)
    pv = psr.next()
    for c in range(8):
        MM(S, pv, pv[:, 0:N], k.csr, k.csr[:, 0:128], scr, scr[:, c, 0:N], start=(c == 0), stop=(c == 7))
    rstd = k.ln_rstd
    ACT(S, rstd, rstd[:, 0:N], pv, pv[:, 0:N], AF.Sqrt, bias=k.eps_t[:, 0:1], xr=[k.eps_t])
    S.op("dve", lambda e: e.reciprocal(out=rstd[:, 0:N], in_=rstd[:, 0:N]), reads=[rstd], writes=[rstd])
    TT(S, "dve", Z, Z[:, :, 0:N], Z, Z[:, :, 0:N], rstd, rstd[:, 0:N].unsqueeze(1).broadcast_to([128, 8, N]),
       ALU.mult)
    for c in range(8):
        ACT(S, OUT, OUT[:, c, 0:N], Z, Z[:, c, 0:N], AF.Identity, bias=P(k, bname, c), scale=P(k, gname, c),
            xr=[k.pvs])


def common_small(k):
    S = k.S
    k.ln_mean = S.sbuf("ln_mean", [128, 512])
    k.ln_rstd = S.sbuf("ln_rstd", [128, 512])
    k.eps_t = S.sbuf("eps_t", [128, 1])
    S.op("dve", lambda e: e.memset(k.eps_t[:, :], EPS), writes=[k.eps_t])


def phase_pool(k, HIN, HOUT):
    S = k.S
    S.begin_phase()
    common_small(k)
    N = 512
    ws = WStream(S, 2)
    psr = Rot([S.psum(f"ps{i}", [128, 512]) for i in range(8)])
    xr = Rot([S.sbuf(f"xt{i}", [128, 8, N]) for i in range(2)])
    U = Rot([S.sbuf(f"U{i}", [128, 8, 16 + N]) for i in range(2)])
    TA = S.sbuf("TA", [128, 2, 16 + N])
    TB = S.sbuf("TB", [128, 2, 16 + N])
    M = S.sbuf("M", [128, 8, N], RT)
    Y = S.sbuf("Y", [128, 8, N], RT)
    Z = S.sbuf("Z", [128, 8, N])
    xtr = S.sbuf("xtr", [128, 8, N], RT)
    wg = S.sbuf("wg", [128, 4, 2, 256], RT)
    LOAD(S, "pool" if FAST else "sp", wg, R(wg[:, :, :, :]), k.pool_w_grp,
         k.pool_w_grp.t.ap().rearrange("g (kc p) d -> p g kc d", p=128))
    vin = wview(k.pool_w_in.t.ap())
    vout = wview(k.pool_w_out.t.ap())
    hin = fmview(HIN)
    hout = fmview(HOUT)
    o_rc, _ = CST["rcnt"]
    prevU = None
    for tb in range(T // N):
        xt = xr.next()
        LOAD(S, "sp", xt, xt[:, :, :], HIN, hin[:, :, tb * N:(tb + 1) * N])
        LOAD(S, "pool" if FAST else "sp", xtr, R(xtr[:, :, :]), HIN, hin[:, :, tb * N:(tb + 1) * N])
        u = U.next()
        if tb == 0:
            S.op("pool", lambda e, u=u: e.memset(u[:, :, 0:16], 0.0), writes=[u])
        else:
            CP(S, "dve", u, u[:, :, 0:16], prevU, prevU[:, :, N:N + 16])

        def ev_u(oc, ps, pa, u=u):
            CP(S, "act", u, u[:, oc, 16:16 + N], ps, pa)
        proj_fm(k, ws, psr, k.pool_w_in, vin, xtr, lambda kc: xtr[:, kc, :], N, 8, ev_u)
        W_ = 16 + N
        for g in range(4):
            w = 2 << g
            cur_b, cur = u, (lambda a, b, g=g: u[:, 2 * g:2 * g + 2, a:b])
            sh = 1
            tgt = [TA, TB]
            for step in range(g + 1):
                t_ = tgt[step % 2]
                lo = 2 * sh - 1
                TT(S, "dve", t_, t_[:, :, lo:W_], cur_b, cur(lo, W_), cur_b, cur(lo - sh, W_ - sh), ALU.add)
                cur_b, cur = t_, (lambda a, b, t_=t_: t_[:, :, a:b])
                sh *= 2
            if tb == 0:
                rc = k.cs[:, o_rc + g * 16:o_rc + (g + 1) * 16]
                TS(S, "dve", cur_b, cur(32, W_), cur_b, cur(32, W_), 1.0 / w, None, ALU.mult)
                TT(S, "dve", cur_b, cur(16, 32), cur_b, cur(16, 32), k.cs,
                   rc.unsqueeze(1).broadcast_to([128, 2, 16]), ALU.mult)
                TT(S, "dve", M, R(M[:, 2 * g:2 * g + 2, :]), cur_b, cur(16, W_), u, u[:, 2 * g:2 * g + 2, 16:W_],
                   ALU.subtract)
            else:
                STT(S, M, R(M[:, 2 * g:2 * g + 2, :]), cur_b, cur(16, W_), 1.0 / w, u, u[:, 2 * g:2 * g + 2, 16:W_],
                    ALU.mult, ALU.subtract)
        for g in range(4):
            for j in range(2):
                ps = psr.next()
                for kc in range(2):
                    MM(S, ps, ps[:, 0:N], wg, R(wg[:, g, kc, j * 128:(j + 1) * 128]), M, R(M[:, 2 * g + kc, :]),
                       start=(kc == 0), stop=(kc == 1))
                oc = 2 * g + j
                ACT(S, Y, R(Y[:, oc, :]), ps, ps[:, 0:N], AF.Identity, scale=P(k, "pool_scale", oc), xr=[k.pvs])

        def ev_z(oc, ps, pa, xt=xt):
            STT(S, Z, Z[:, oc, :], xt, xt[:, oc, :], ALPHA, ps, pa, ALU.mult, ALU.add)
        proj_fm(k, ws, psr, k.pool_w_out, vout, Y, lambda kc: Y[:, kc, :], N, 8, ev_z)
        o = Z
        ln_fm(k, psr, Z, N, "ln_mix_g0", "ln_mix_b0", o, M)
        STORE(S, "act", HOUT, hout[:, :, tb * N:(tb + 1) * N], o, o[:, :, :])
        prevU = u
    S.end_phase()


def phase_xattn(k, l, HIN, HOUT, HTOK):
    S = k.S
    S.begin_phase()
    common_small(k)
    N = 512
    ws = WStream(S, 2)
    psr = Rot([S.psum(f"ps{i}", [128, 512]) for i in range(8)])
    xr = Rot([S.sbuf(f"xt{i}", [128, 8, N]) for i in range(2)])
    memT = S.sbuf("memT", [128, 8, 256], RT)
    kT = S.sbuf("kT", [128, 8, 256], RT)
    v = S.sbuf("v", [128, 2, D], RT)
    qT = S.sbuf("qT", [128, 8, N], RT)
    Pm = Rot([S.sbuf(f"Pm{i}", [128, 4, 256]) for i in range(2)])
    pT = S.sbuf("pT", [128, 2, 4, N], RT)
    OT = S.sbuf("OT", [128, 8, N], RT)
    Z = S.sbuf("Z", [128, 8, N])
    xtr = S.sbuf("xtr", [128, 8, N], RT)
    htk = Rot([S.sbuf(f"htk{i}", [128, D]) for i in range(2)])
    sm = Rot([S.sbuf(f"sm{i}", [128, 16]) for i in range(2)])
    ident = C(k, "ident")
    LOAD(S, "pool" if FAST else "sp", memT, R(memT[:, :, :]), k.memT, fmview(k.memT))
    vq = wview(k.xa_wq.t.ap()[l])
    vk = wview(k.xa_wk.t.ap()[l])
    vv = wview(k.xa_wv.t.ap()[l])
    vo = wview(k.xa_wo.t.ap()[l])

    def ev_k(oc, ps, pa):
        CP(S, "act", kT, R(kT[:, oc, :]), ps, pa)
    proj_fm(k, ws, psr, k.xa_wk, vk, memT, lambda kc: memT[:, kc, :], 256, 8, ev_k)
    for wc in range(2):
        w = ws.get(k.xa_wv, vv, 0, wc * 512)
        for mc in range(2):
            ps = psr.next()
            for kc in range(8):
                MM(S, ps, ps[:, :], memT, R(memT[:, kc, mc * 128:(mc + 1) * 128]), w, R(w[:, kc, :]),
                   start=(kc == 0), stop=(kc == 7))
            CP(S, "act", v, R(v[:, mc, wc * 512:(wc + 1) * 512]), ps, ps[:, :])
    hin = fmview(HIN)
    hout = fmview(HOUT)
    gname, bname = f"ln_xa_g{l}", f"ln_xa_b{l}"
    for tb in range(T // N):
        xt = xr.next()
        LOAD(S, "sp", xt, xt[:, :, :], HIN, hin[:, :, tb * N:(tb + 1) * N])
        LOAD(S, "pool" if FAST else "sp", xtr, R(xtr[:, :, :]), HIN, hin[:, :, tb * N:(tb + 1) * N])

        def ev_q(oc, ps, pa):
            S.op("act", lambda e: e.activation(out=R(qT[:, oc, :]), in_=pa, func=AF.Identity, scale=0.0625),
                 reads=[ps], writes=[qT])
        proj_fm(k, ws, psr, k.xa_wq, vq, xtr, lambda kc: xtr[:, kc, :], N, 8, ev_q)
        def xa_stage_a(tt):
                tsl = slice(tt * 128, (tt + 1) * 128)
                pm = Pm.next()
                s_ = sm.next()
                banks = [psr.next(), psr.next()]
                for h in range(4):
                    ps = banks[h // 2]
                    for dc in range(2):
                        MM(S, ps, ps[:, (h % 2) * 256:(h % 2 + 1) * 256], qT, R(qT[:, 2 * h + dc, tsl]),
                           kT, R(kT[:, 2 * h + dc, :]), start=(dc == 0), stop=(dc == 1))
                for b2 in range(2):
                    ps = banks[b2]
                    S.op("dve", lambda e, ps=ps, s_=s_, b2=b2: e.tensor_reduce(
                        out=s_[:, 2 * b2:2 * b2 + 2], in_=ps[:, :].rearrange("p (a b) -> p a b", a=2),
                        axis=AX.X, op=ALU.max), reads=[ps], writes=[s_])
                TS(S, "dve", s_, s_[:, 4:8], s_, s_[:, 0:4], -1.0, None, ALU.mult)
                for h in range(4):
                    ps = banks[h // 2]
                    ACT(S, pm, pm[:, h, :], ps, ps[:, (h % 2) * 256:(h % 2 + 1) * 256], AF.Exp,
                        bias=s_[:, 4 + h:5 + h], xr=[s_], accum=s_[:, 8 + h:9 + h], xw=[s_])
                S.op("dve", lambda e, s_=s_: e.reciprocal(out=s_[:, 12:16], in_=s_[:, 8:12]), reads=[s_], writes=[s_])
                TT(S, "dve", pm, pm[:, :, :], pm, pm[:, :, :], s_, s_[:, 12:16].unsqueeze(2).broadcast_to([128, 4, 256]),
                   ALU.mult)
                return tsl, pm

        def xa_stage_b(tt, tsl, pm):
                for mc in range(2):
                    ps = psr.next()
                    for h in range(4):
                        TRP(S, ps, ps[:, h * 128:(h + 1) * 128], pm, pm[:, h, mc * 128:(mc + 1) * 128], k.cs, ident)
                    CP(S, "act" if mc == 0 else "dve", pT, R(pT[:, mc, :, tsl]), ps,
                       ps[:, :].rearrange("p (h t) -> p h t", h=4))
        cur = xa_stage_a(0)
        for tt in range(4):
            nxt = xa_stage_a(tt + 1) if tt < 3 else None
            xa_stage_b(tt, *cur)
            cur = nxt
        for h in range(4):
            for dc in range(2):
                ps = psr.next()
                for mc in range(2):
                    MM(S, ps, ps[:, :], v, R(v[:, mc, h * 256 + dc * 128:h * 256 + (dc + 1) * 128]),
                       pT, R(pT[:, mc, h, :]), start=(mc == 0), stop=(mc == 1))
                CP(S, "act", OT, R(OT[:, 2 * h + dc, :]), ps, ps[:, :])

        def ev_z(oc, ps, pa, xt=xt):
            STT(S, Z, Z[:, oc, :], xt, xt[:, oc, :], ALPHA, ps, pa, ALU.mult, ALU.add)
        proj_fm(k, ws, psr, k.xa_wo, vo, OT, lambda kc: OT[:, kc, :], N, 8, ev_z)
        o = Z
        ln_fm(k, psr, Z, N, gname, bname, o, OT)
        STORE(S, "act", HOUT, hout[:, :, tb * N:(tb + 1) * N], o, o[:, :, :])
        for tt in range(4):
            ht = htk.next()
            for half in range(2):
                ps = psr.next()
                for c4 in range(4):
                    c = half * 4 + c4
                    TRP(S, ps, ps[:, c4 * 128:(c4 + 1) * 128], o, o[:, c, tt * 128:(tt + 1) * 128], k.cs, ident)
                CP(S, "act" if half == 0 else "dve", ht, ht[:, half * 512:(half + 1) * 512], ps, ps[:, :])
            r0 = tb * N + tt * 128
            STORE(S, "act", HTOK, HTOK.t.ap()[r0:r0 + 128, :], ht, ht[:, :])
    S.end_phase()


def phase_moe(k, l, HFM, HTOK, HOUT_FM, OUT_TOK):
    S = k.S
    ident = C(k, "ident")
    S.begin_phase()
    psr = Rot([S.psum(f"ps{i}", [128, 512]) for i in range(8)])
    xr = Rot([S.sbuf(f"xt{i}", [128, 8, 512]) for i in range(2)])
    hr = Rot([S.sbuf(f"hrow{i}", [128, D]) for i in range(3)])
    wr = S.sbuf("wr", [128, 8, NE])
    brt = S.sbuf("brt", [128, NE])
    cum = S.sbuf("cum", [128, NE])
    lgr = Rot([S.sbuf(f"lg{i}", [128, NE]) for i in range(2)])
    t8r = Rot([S.sbuf(f"t8{i}", [128, 16]) for i in range(2)])
    mkr = Rot([S.sbuf(f"mk{i}", [128, NE]) for i in range(2)])
    pcr = Rot([S.sbuf(f"pc{i}", [128, NE]) for i in range(2)])
    jnk = S.sbuf("jnk", [128, NE])
    dfr = Rot([S.sbuf(f"df{i}", [128, 4]) for i in range(2)])
    LOAD(S, "sp", wr, wr[:, :, :], k.w_router, wview(k.w_router.t.ap()[l]))
    LOAD(S, "sp", brt, brt[:, :], k.b_router, k.b_router.t.ap()[l:l + 1, :].broadcast_to([128, NE]))
    S.op("dve", lambda e: e.memset(cum[:, :], 0.0), writes=[cum])
    hfm = fmview(HFM)
    ustrict = C(k, "ustrict")
    ones = C(k, "ones")
    iotaec = C(k, "iotaec")
    xcur = [None]

    def rt_stage_a(ti):
            if ti % 4 == 0:
                xcur[0] = xr.next()
                LOAD(S, "sp", xcur[0], xcur[0][:, :, :], HFM, hfm[:, :, ti * 128:ti * 128 + 512])
            xt = xcur[0]
            tsl = slice((ti % 4) * 128, (ti % 4 + 1) * 128)
            hrow = hr.next()
            LOAD(S, "sp", hrow, hrow[:, :], HTOK, HTOK.t.ap()[ti * 128:(ti + 1) * 128, :])
            ps = psr.next()
            for kc in range(8):
                MM(S, ps, ps[:, 0:NE], xt, xt[:, kc, tsl], wr, wr[:, kc, :], start=(kc == 0), stop=(kc == 7))
            lg = lgr.next()
            TT(S, "dve", lg, lg[:, :], ps, ps[:, 0:NE], brt, brt[:, :], ALU.add)
            t8 = t8r.next()
            S.op("dve", lambda e, t8=t8, lg=lg: e.max(out=t8[:, 0:8], in_=lg[:, :]), reads=[lg], writes=[t8])
            mk = mkr.next()
            TS(S, "dve", mk, mk[:, :], lg, lg[:, :], t8[:, 3:4], None, ALU.is_ge, xr=[t8])
            pp = psr.next()
            MM(S, pp, pp[:, 0:NE], k.cs, ustrict, mk, mk[:, :])
            MM(S, pp, pp[:, NE:2 * NE], k.cs, ones, mk, mk[:, :])
            return hrow, lg, t8, pp

    def rt_stage_b(ti, hrow, lg, t8, pp):
            pc = pcr.next()
            TT(S, "dve", pc, pc[:, :], pp, pp[:, 0:NE], cum, cum[:, :], ALU.add)
            TT(S, "dve", pc, pc[:, :], pc, pc[:, :], k.cs, iotaec, ALU.add)
            TT(S, "dve", cum, cum[:, :], pp, pp[:, NE:2 * NE], cum, cum[:, :], ALU.add)
            df = dfr.next()
            for kk in range(4):
                STT(S, jnk, jnk[:, :], lg, lg[:, :], t8[:, kk:kk + 1], pc, pc[:, :], ALU.is_equal, ALU.mult,
                    xr=[t8], accum=df[:, kk:kk + 1], xw=[df])
            TS(S, "dve", df, df[:, :], df, df[:, :], 0.0, float(NE * CAP - 1), ALU.max, ALU.min)
            CP(S, "dve", k.dest, k.dest[:, ti, :], df, df[:, :])
            TS(S, "dve", t8, t8[:, 8:9], t8, t8[:, 0:1], -1.0, None, ALU.mult)
            ACT(S, t8, t8[:, 10:14], t8, t8[:, 0:4], AF.Exp, bias=t8[:, 8:9], accum=t8[:, 9:10])
            S.op("dve", lambda e, t8=t8: e.reciprocal(out=t8[:, 14:15], in_=t8[:, 9:10]), reads=[t8], writes=[t8])
            TS(S, "dve", k.gates, k.gates[:, ti, :], t8, t8[:, 10:14], t8[:, 14:15], None, ALU.mult)
            for kk in range(4):
                S.dma("pool", lambda e, hrow=hrow, ti=ti, kk=kk: e.indirect_dma_start(
                    out=k.XS.t.ap(), out_offset=bass.IndirectOffsetOnAxis(ap=k.dest[:, ti, kk:kk + 1], axis=0),
                    in_=hrow[:, :], in_offset=None),
                    reads=[hrow, k.dest, k.gates], writes=[k.XS], owner=hrow, group=(kk > 0))
    cur = rt_stage_a(0)
    for ti in range(NTL):
        nxt = rt_stage_a(ti + 1) if ti < NTL - 1 else None
        rt_stage_b(ti, *cur)
        cur = nxt
    S.end_phase()
    if k.phases is not None and f"moe{l}_route_only" in k.phases:
        return
    S.begin_phase()
    ws = WStream(S, 4)
    psr = Rot([S.psum(f"ps{i}", [128, 512]) for i in range(8)])
    xrow = Rot([S.sbuf(f"xrow{i}", [128, D]) for i in range(3)])
    yrow = Rot([S.sbuf(f"yrow{i}", [128, D]) for i in range(2)])
    XT = S.sbuf("XsT", [128, 8, CAP], RT)
    AT = S.sbuf("actT", [128, 8, CAP], RT)
    gp = Rot([S.sbuf(f"gp{i}", [128, CAP // 2]) for i in range(2)])
    sg = Rot([S.sbuf(f"sg{i}", [128, CAP // 2]) for i in range(2)])
    up = Rot([S.sbuf(f"up{i}", [128, CAP // 2]) for i in range(2)])
    bdr = Rot([S.sbuf(f"bd{i}", [1, D], RT) for i in range(2)])
    o_bg, _ = PV[f"bg{l}"]
    o_bu, _ = PV[f"bu{l}"]
    HN = CAP // 2
    NRB = CAP // 128
    ones = C(k, "ones")
    xtr2 = Rot([XT, S.sbuf("XsT2", [128, 8, CAP], RT)])

    def build_xt(e_):
        XT = xtr2.next()
        for rb in range(NRB):
            xw = xrow.next()
            r0 = e_ * CAP + rb * 128
            LOAD(S, "sp", xw, xw[:, :], k.XS, k.XS.t.ap()[r0:r0 + 128, :])
            for half in range(2):
                ps = psr.next()
                for c4 in range(4):
                    c = half * 4 + c4
                    TRP(S, ps, ps[:, c4 * 128:(c4 + 1) * 128], xw, xw[:, c * 128:(c + 1) * 128], k.cs, ident)
                CP(S, "act" if half == 0 else "dve", XT, R(XT[:, half * 4:half * 4 + 4, rb * 128:(rb + 1) * 128]),
                   ps, ps[:, :].rearrange("p (c t) -> p c t", c=4))
        return XT

    def gate_up(e_, XT):
        vgu = wview(k.w_gu.t.ap()[l, e_])
        for wc in range(4):
            w = ws.get(k.w_gu, vgu, 0, wc * 512)
            for f2 in range(2):
                fc = wc * 2 + f2
                base = f2 * 256
                for hf in range(2):
                    cs_ = slice(hf * HN, (hf + 1) * HN)
                    pg = psr.next()
                    pu = psr.next()
                    for kc in range(8):
                        MM(S, pg, pg[:, 0:HN], w, R(w[:, kc, base:base + 256:2]), XT, R(XT[:, kc, cs_]),
                           start=(kc == 0), stop=(kc == 7))
                    for kc in range(8):
                        MM(S, pu, pu[:, 0:HN], w, R(w[:, kc, base + 1:base + 256:2]), XT, R(XT[:, kc, cs_]),
                           start=(kc == 0), stop=(kc == 7))
                    g_ = gp.next()
                    s_ = sg.next()
                    u_ = up.next()
                    bgc = k.pvs[:, o_bg + e_ * 8 + fc:o_bg + e_ * 8 + fc + 1]
                    buc = k.pvs[:, o_bu + e_ * 8 + fc:o_bu + e_ * 8 + fc + 1]
                    TS(S, "dve", g_, g_[:, :], pg, pg[:, 0:HN], bgc, 7.0, ALU.add, ALU.min, xr=[k.pvs])
                    ACT(S, s_, s_[:, :], g_, g_[:, :], AF.Sigmoid, scale=1.702)
                    TS(S, "dve", u_, u_[:, :], pu, pu[:, 0:HN], buc, 7.0, ALU.add, ALU.min, xr=[k.pvs])
                    TS(S, "dve", u_, u_[:, :], u_, u_[:, :], -7.0, 1.0, ALU.max, ALU.add)
                    TT(S, "dve", g_, g_[:, :], g_, g_[:, :], s_, s_[:, :], ALU.mult)
                    TT(S, "dve", AT, R(AT[:, fc, cs_]), g_, g_[:, :], u_, u_[:, :], ALU.mult)

    def down(e_):
        bd = bdr.next()
        LOAD(S, "pool" if FAST else "sp", bd, bd[:, :], k.b_dn, k.b_dn.t.ap()[l, e_:e_ + 1, :])
        vdn = wview(k.w_dn.t.ap()[l, e_])
        wd = [ws.get(k.w_dn, vdn, 0, oc * 512) for oc in range(2)]
        for rb in range(NRB):
            yw = yrow.next()
            for oc in range(2):
                ps = psr.next()
                for fc in range(8):
                    MM(S, ps, ps[:, :], AT, R(AT[:, fc, rb * 128:(rb + 1) * 128]), wd[oc], R(wd[oc][:, fc, :]),
                       start=(fc == 0), stop=False)
                MM(S, ps, ps[:, :], k.csr, k.csr[0:1, 256:384], bd, bd[0:1, oc * 512:(oc + 1) * 512],
                   start=False, stop=True)
                CP(S, "act", yw, yw[:, oc * 512:(oc + 1) * 512], ps, ps[:, :])
            r0 = e_ * CAP + rb * 128
            STORE(S, "act", k.YS, k.YS.t.ap()[r0:r0 + 128, :], yw, yw[:, :])

    xt_cur = build_xt(0)
    for e_ in range(NE):
        gate_up(e_, xt_cur)
        xt_nxt = build_xt(e_ + 1) if e_ < NE - 1 else None
        down(e_)
        xt_cur = xt_nxt
    S.end_phase()
    S.begin_phase()
    psr = Rot([S.psum(f"ps{i}", [128, 512]) for i in range(8)])
    hr = Rot([S.sbuf(f"hrow{i}", [128, D]) for i in range(2)])
    yk = Rot([S.sbuf(f"yk{i}", [128, D]) for i in range(8)])
    acc = Rot([S.sbuf(f"acc{i}", [128, D]) for i in range(2)])
    ofm = Rot([S.sbuf(f"ofm{i}", [128, 8, 128]) for i in range(2)])
    gB = S.sbuf("gB", [128, D])
    bB = S.sbuf("bB", [128, D])
    st_ = Rot([S.sbuf(f"st{i}", [128, 16]) for i in range(2)])
    eps_t = S.sbuf("eps_t", [128, 1])
    S.op("dve", lambda e: e.memset(eps_t[:, :], EPS), writes=[eps_t])
    LOAD(S, "sp", gB, gB[:, :], k.ln_ffn_g, k.ln_ffn_g.t.ap()[l:l + 1, :].broadcast_to([128, D]))
    LOAD(S, "sp", bB, bB[:, :], k.ln_ffn_b, k.ln_ffn_b.t.ap()[l:l + 1, :].broadcast_to([128, D]))
    for ti in range(NTL):
        hrow = hr.next()
        LOAD(S, "sp", hrow, hrow[:, :], HTOK, HTOK.t.ap()[ti * 128:(ti + 1) * 128, :])
        ys = []
        for kk in range(4):
            y = yk.next()
            S.dma("pool", lambda e, y=y, ti=ti, kk=kk: e.indirect_dma_start(
                out=y[:, :], out_offset=None, in_=k.YS.t.ap(),
                in_offset=bass.IndirectOffsetOnAxis(ap=k.dest[:, ti, kk:kk + 1], axis=0)),
                reads=[k.YS, k.dest], writes=[y], owner=y)
            ys.append(y)
        a = acc.next()
        TS(S, "dve", a, a[:, :], ys[0], ys[0][:, :], k.gates[:, ti, 0:1], None, ALU.mult, xr=[k.gates])
        for kk in range(1, 4):
            STT(S, a, a[:, :], ys[kk], ys[kk][:, :], k.gates[:, ti, kk:kk + 1], a, a[:, :], ALU.mult, ALU.add,
                xr=[k.gates])
        STT(S, a, a[:, :], hrow, hrow[:, :], ALPHA, a, a[:, :], ALU.mult, ALU.add)
        s_ = st_.next()
        for hf in range(2):
            S.op("dve", lambda e, s_=s_, a=a, hf=hf: e.bn_stats(out=s_[:, hf * 6:(hf + 1) * 6],
                                                               in_=a[:, hf * 512:(hf + 1) * 512]),
                 reads=[a], writes=[s_])
        S.op("dve", lambda e, s_=s_: e.bn_aggr(out=s_[:, 12:14], in_=s_[:, 0:12]), reads=[s_], writes=[s_])
        ACT(S, s_, s_[:, 14:15], s_, s_[:, 13:14], AF.Sqrt, bias=eps_t[:, 0:1], xr=[eps_t])
        S.op("dve", lambda e, s_=s_: e.reciprocal(out=s_[:, 15:16], in_=s_[:, 14:15]), reads=[s_], writes=[s_])
        TS(S, "dve", a, a[:, :], a, a[:, :], s_[:, 12:13], s_[:, 15:16], ALU.subtract, ALU.mult, xr=[s_])
        TT(S, "dve", a, a[:, :], a, a[:, :], gB, gB[:, :], ALU.mult)
        TT(S, "dve", a, a[:, :], a, a[:, :], bB, bB[:, :], ALU.add)
        if OUT_TOK is not None:
            STORE(S, "act", OUT_TOK, OUT_TOK.t.ap()[ti * 128:(ti + 1) * 128, :], a, a[:, :])
        if HOUT_FM is not None:
            STORE(S, "act", k.HCTOK, k.HCTOK.t.ap()[ti * 128:(ti + 1) * 128, :], a, a[:, :])
            o = ofm.next()
            for half in range(2):
                ps = psr.next()
                for c4 in range(4):
                    c = half * 4 + c4
                    TRP(S, ps, ps[:, c4 * 128:(c4 + 1) * 128], a, a[:, c * 128:(c + 1) * 128], k.cs, ident)
                CP(S, "act", o, o[:, half * 4:half * 4 + 4, :], ps, ps[:, :].rearrange("p (c t) -> p c t", c=4))
            STORE(S, "act", HOUT_FM, fmview(HOUT_FM)[:, :, ti * 128:(ti + 1) * 128], o, o[:, :, :])
    S.end_phase()


def phase_ssd_a(k, HIN):
    S = k.S
    S.begin_phase()
    N = 512
    ws = WStream(S, 3)
    psr = Rot([S.psum(f"ps{i}", [128, 512]) for i in range(8)])
    xr = Rot([S.sbuf(f"xtr{i}", [128, 8, N], RT) for i in range(2)])
    zst = Rot([S.sbuf(f"zst{i}", [128, N]) for i in range(4)])
    ucr = Rot([S.sbuf(f"uc{i}", [128, 3 + N]) for i in range(4)])
    acr = Rot([S.sbuf(f"ac{i}", [128, N]) for i in range(4)])
    xst = Rot([S.sbuf(f"xst{i}", [128, N]) for i in range(4)])
    halo = S.sbuf("halo", [128, 32, 4])
    S.op("dve", lambda e: e.memset(halo[:, :, :], 0.0), writes=[halo])
    vin = wview(k.ssm_w_in.t.ap())
    hin = fmview(HIN)
    o_cw, _ = PV["conv_w"]
    zsd = k.ZSD.t.ap()
    xbd = k.XBCD.t.ap()
    for tb in range(T // N):
        xtr = xr.next()
        LOAD(S, "pool" if FAST else "sp", xtr, xtr[:, :, :], HIN, hin[:, :, tb * N:(tb + 1) * N])

        def ev_zs(oc, ps, pa, tb=tb):
            z_ = zst.next()
            ACT(S, z_, z_[:, :], ps, pa, AF.Silu)
            STORE(S, "act", k.ZSD, zsd[oc * 128:(oc + 1) * 128, tb * N:(tb + 1) * N], z_, z_[:, :])
        proj_fm(k, ws, psr, k.ssm_w_in, vin, xtr, lambda kc, xtr=xtr: xtr[:, kc, :], N, 16, ev_zs, col0=0)

        pend = []

        def flush():
            while pend:
                pend.pop(0)()

        def ev_u(c, ps, pa, tb=tb):
            uc = ucr.next()
            CP(S, "dve", uc, uc[:, 0:3], halo, halo[:, c, 0:3])
            CP(S, "act", uc, uc[:, 3:3 + N], ps, pa)
            flush()
            CP(S, "dve", halo, halo[:, c, 0:3], uc, uc[:, N:N + 3])
            acc = acr.next()
            wcol = lambda kk: k.pvs[:, o_cw + c * 4 + kk:o_cw + c * 4 + kk + 1]
            ACT(S, acc, acc[:, :], uc, uc[:, 0:N], AF.Identity, bias=P(k, "conv_b", c), scale=wcol(0), xr=[k.pvs])
            for kk in range(1, 4):
                STT(S, acc, acc[:, :], uc, uc[:, kk:kk + N], wcol(kk), acc, acc[:, :], ALU.mult, ALU.add, xr=[k.pvs])

            def fin(c=c, acc=acc, tb=tb):
                x_ = xst.next()
                ACT(S, x_, x_[:, :], acc, acc[:, :], AF.Silu)
                STORE(S, "act", k.XBCD, xbd[c * 128:(c + 1) * 128, tb * N:(tb + 1) * N], x_, x_[:, :])
            pend.append(fin)
        proj_fm(k, ws, psr, k.ssm_w_in, vin, xtr, lambda kc, xtr=xtr: xtr[:, kc, :], N, 32, ev_u, col0=2048)
        flush()
    S.end_phase()


def phase_ssd_b(k, HIN, HOUT):
    S = k.S
    S.begin_phase()
    common_small(k)
    N = 128
    ws = WStream(S, 2)
    psr = Rot([S.psum(f"ps{i}", [128, 512]) for i in range(4)])
    SPS = S.psum("SPS", [128, 2048])
    xr = Rot([S.sbuf(f"xt{i}", [128, 8, N]) for i in range(2)])
    wdt = S.sbuf("wdt", [128, 8, 32])
    rowb = S.sbuf("rowb", [128, 64])
    Abc = S.sbuf("Abc", [128, 32])
    one_t = S.sbuf("one_t", [128, 1])
    zsr = Rot([S.sbuf(f"ZS{i}", [128, 16, N]) for i in range(2)])
    xbr = Rot([S.sbuf(f"XBC{i}", [128, 32, N]) for i in range(2)])
    XD = S.sbuf("XD", [128, 2048])
    XDD = S.sbuf("XDD", [128, 2048])
    BTK = S.sbuf("BTK", [128, 8, 128])
    tk = Rot([S.sbuf(f"tk{i}", [128, 8, 32]) for i in range(2)])
    cdr = Rot([S.sbuf(f"cdr{i}", [128, 64]) for i in range(2)])
    arr = Rot([S.sbuf(f"ar{i}", [128, 4, 128], RT) for i in range(2)])
    cbr = Rot([S.sbuf(f"cbm{i}", [128, 128]) for i in range(2)])
    mtr = Rot([S.sbuf(f"mt{i}", [128, 4, 128]) for i in range(2)])
    cer = Rot([S.sbuf(f"ce{i}", [128, 4, 128]) for i in range(2)])
    H = [S.sbuf(f"H{i}", [128, 2048]) for i in range(2)]
    Yt = S.sbuf("Yt", [128, 16, N])
    SQ = S.sbuf("SQ", [128, 16, N])
    rst = S.sbuf("rst", [128, 8, N])
    YR = S.sbuf("YR", [128, 16, N], RT)
    xtk = S.sbuf("xtk", [128, D])
    ZT = S.sbuf("ZT", [128, D])
    gB = S.sbuf("gBm", [128, D])
    bB = S.sbuf("bBm", [128, D])
    st_ = Rot([S.sbuf(f"stm{i}", [128, 16]) for i in range(2)])
    LOAD(S, "sp", gB, gB[:, :], k.ln_mix_g, k.ln_mix_g.t.ap()[1:2, :].broadcast_to([128, D]))
    LOAD(S, "sp", bB, bB[:, :], k.ln_mix_b, k.ln_mix_b.t.ap()[1:2, :].broadcast_to([128, D]))
    ident = C(k, "ident")
    triinc, blockones, strictlt, mask01 = C(k, "triinc"), C(k, "blockones"), C(k, "strictlt"), C(k, "mask01")
    csel = [C(k, "csel0"), C(k, "csel1")]
    ones, ones256 = C(k, "ones"), C(k, "ones256")
    vin = wview(k.ssm_w_in.t.ap())
    vout = wview(k.ssm_w_out.t.ap())
    hin = fmview(HIN)
    hout = fmview(HOUT)
    o_cw, _ = PV["conv_w"]
    LOAD(S, "sp", wdt, wdt[:, :, :], k.ssm_w_in, vin[:, :, 6144:6176])
    LOAD(S, "sp", rowb, rowb[:, :], k.rowp, k.rowp.t.ap()[0:1, :].broadcast_to([128, 64]))
    S.op("dve", lambda e: e.memset(one_t[:, :], 1.0), writes=[one_t])
    S.op("dve", lambda e: e.memset(H[0][:, :], 0.0), writes=[H[0]])
    ACT(S, Abc, Abc[:, :], rowb, rowb[:, 32:64], AF.Exp)
    TS(S, "dve", Abc, Abc[:, :], Abc, Abc[:, :], -1.0, None, ALU.mult)
    def tile_gen(tb):
        xt = xr.next()
        LOAD(S, "sp", xt, xt[:, :, :], HIN, hin[:, :, tb * N:(tb + 1) * N])
        ZS = zsr.next()
        XBC = xbr.next()
        for q4 in range(2):
            LOAD(S, "sp", ZS, ZS[:, q4 * 8:(q4 + 1) * 8, :], k.ZSD,
                 fmview(k.ZSD)[:, q4 * 8:(q4 + 1) * 8, tb * N:(tb + 1) * N], group=(q4 > 0))
        for q4 in range(4):
            LOAD(S, "sp", XBC, XBC[:, q4 * 8:(q4 + 1) * 8, :], k.XBCD,
                 fmview(k.XBCD)[:, q4 * 8:(q4 + 1) * 8, tb * N:(tb + 1) * N], group=(q4 > 0))
        t_ = tk.next()
        dt, a_, acs, dte, w1, tmp = (t_[:, i, :] for i in range(6))
        ps = psr.next()
        for kc in range(8):
            MM(S, ps, ps[:, 0:32], xt, xt[:, kc, :], wdt, wdt[:, kc, :], start=(kc == 0), stop=(kc == 7))
        TT(S, "dve", t_, tmp, ps, ps[:, 0:32], rowb, rowb[:, 0:32], ALU.add)
        ACT(S, t_, tmp, t_, tmp, AF.Exp)
        ACT(S, t_, dt, t_, tmp, AF.Ln, bias=one_t[:, 0:1], xr=[one_t])
        TT(S, "dve", t_, a_, t_, dt, Abc, Abc[:, :], ALU.mult)
        pa = psr.next()
        MM(S, pa, pa[:, 0:32], k.cs, triinc, t_, a_)
        MM(S, pa, pa[:, 32:64], k.cs, blockones, t_, a_)
        MM(S, pa, pa[:, 64:96], k.cs, csel[0], t_, a_)
        MM(S, pa, pa[:, 96:128], k.cs, csel[1], t_, a_)
        CP(S, "act", t_, acs, pa, pa[:, 0:32])
        TT(S, "dve", t_, dte, pa, pa[:, 32:64], t_, acs, ALU.subtract)
        ACT(S, t_, dte, t_, dte, AF.Exp)
        TT(S, "dve", t_, w1, t_, dt, t_, dte, ALU.mult)
        cd = cdr.next()
        ACT(S, cd, cd[:, :], pa, pa[:, 64:128], AF.Exp)
        for q in range(4):
            ps = psr.next()
            for j in range(4):
                TRP(S, ps, ps[:, j * 128:(j + 1) * 128], XBC, XBC[:, q * 4 + j, :], k.cs, ident)
            pv3 = ps[:, :].rearrange("p (h d) -> p h d", h=8)
            for dst, sc in ((XD, dt), (XDD, w1)):
                TT(S, "dve", dst, dst[:, q * 512:(q + 1) * 512].rearrange("p (h d) -> p h d", h=8), ps, pv3,
                   t_, sc[:, q * 8:(q + 1) * 8].unsqueeze(2).broadcast_to([128, 8, 64]), ALU.mult)
        for q in range(2):
            ps = psr.next()
            for j in range(4):
                TRP(S, ps, ps[:, j * 128:(j + 1) * 128], XBC, XBC[:, 16 + q * 4 + j, :], k.cs, ident)
            CP(S, "act", BTK, BTK[:, q * 4:(q + 1) * 4, :], ps, ps[:, :].rearrange("p (g n) -> p g n", g=4))
        H0, H1 = H[0], H[1]

        def states(cp):
            for g in range(8):
                MM(S, SPS, SPS[:, g * 256:(g + 1) * 256], BTK, BTK[cp * 64:(cp + 1) * 64, g, :],
                   XDD, XDD[cp * 64:(cp + 1) * 64, g * 256:(g + 1) * 256])

        def hupd(dst, src, cp):
            TT(S, "dve", dst, dst[:, :].rearrange("p (h d) -> p h d", h=32), src,
               src[:, :].rearrange("p (h d) -> p h d", h=32), cd,
               cd[:, cp * 32:(cp + 1) * 32].unsqueeze(2).broadcast_to([128, 32, 64]), ALU.mult)
            TT(S, "dve", dst, dst[:, :], SPS, SPS[:, :], dst, dst[:, :], ALU.add)
        states(0)
        hupd(H1, H0, 0)
        Hc = [H0, H1]
        yield
        def stage1(g):
            ar = arr.next()
            TT(S, "dve", ar, ar[:, :, :], k.cs, triinc.unsqueeze(1).broadcast_to([128, 4, 128]), t_,
               a_[:, 4 * g:4 * g + 4].unsqueeze(2).broadcast_to([128, 4, 128]), ALU.mult)
            ar2 = ar[:, :, :].rearrange("p h i -> p (h i)")
            pcb = psr.next()
            MM(S, pcb, pcb[:, 0:128], XBC, XBC[:, 16 + g, :], XBC, XBC[:, 24 + g, :])
            pseg = psr.next()
            MM(S, pseg, pseg[:, :], k.csr, k.csr[:, 128:256], ar, ar2)
            pacs = psr.next()
            MM(S, pacs, pacs[:, :], k.csr, k.csr[:, 256:384], ar, ar2)
            cbm = cbr.next()
            TT(S, "dve", cbm, cbm[:, :], pcb, pcb[:, 0:128], k.cs, mask01, ALU.mult)
            mt = mtr.next()
            ACT(S, mt, mt[:, :, :], pseg, pseg[:, :].rearrange("p (h i) -> p h i", h=4), AF.Exp)
            ce = cer.next()
            ACT(S, ce, ce[:, :, :], pacs, pacs[:, :].rearrange("p (h i) -> p h i", h=4), AF.Exp)
            TT(S, "dve", mt, mt[:, :, :], mt, mt[:, :, :], cbm, cbm[:, :].unsqueeze(1).broadcast_to([128, 4, 128]),
               ALU.mult)
            TT(S, "dve", ce, ce[:, :, :], ce, ce[:, :, :], XBC,
               XBC[:, 24 + g, :].unsqueeze(1).broadcast_to([128, 4, 128]), ALU.mult)
            return mt, ce

        def stage2(g, mt, ce):
            py = psr.next()
            for r in range(4):
                hd = 4 * g + r
                prt = slice((r % 2) * 64, (r % 2) * 64 + 64)
                c0 = (r // 2) * 128
                MM(S, py, py[prt, c0:c0 + 128], XD, XD[:, hd * 64:(hd + 1) * 64], mt, mt[:, r, :],
                   start=True, stop=False)
                for cp in range(2):
                    MM(S, py, py[prt, c0 + cp * 64:c0 + (cp + 1) * 64], Hc[cp], Hc[cp][:, hd * 64:(hd + 1) * 64],
                       ce, ce[:, r, cp * 64:(cp + 1) * 64], start=False, stop=(cp == 1))
            for j in range(2):
                c = 2 * g + j
                STT(S, Yt, Yt[:, c, :], XBC, XBC[:, c, :], P(k, "dskip", c), py, py[:, j * 128:(j + 1) * 128],
                    ALU.mult, ALU.add, xr=[k.pvs])
        cur = stage1(0)
        for g in range(8):
            nxt = stage1(g + 1) if g < 7 else None
            stage2(g, *cur)
            cur = nxt
        states(1)
        hupd(H0, H1, 1)
        yield
        TT(S, "dve", Yt, Yt[:, :, :], Yt, Yt[:, :, :], ZS, ZS[:, :, :], ALU.mult)
        ACT(S, SQ, SQ[:, :, :], Yt, Yt[:, :, :], AF.Square)
        for b2 in range(2):
            pm = psr.next()
            for g4 in range(4):
                g = b2 * 4 + g4
                for j in range(2):
                    MM(S, pm, pm[:, g4 * 128:(g4 + 1) * 128], k.cs, ones256, SQ, SQ[:, 2 * g + j, :],
                       start=(j == 0), stop=(j == 1))
            ACT(S, rst, rst[:, b2 * 4:(b2 + 1) * 4, :], pm, pm[:, :].rearrange("p (g t) -> p g t", g=4), AF.Ln,
                bias=k.eps_t[:, 0:1], xr=[k.eps_t])
        ACT(S, rst, rst[:, :, :], rst, rst[:, :, :], AF.Exp, scale=-0.5)
        Yt4 = Yt[:, :, :].rearrange("p (g j) t -> p g j t", j=2)
        TT(S, "dve", Yt, Yt4, Yt, Yt4, rst, rst[:, :, :].unsqueeze(2).broadcast_to([128, 8, 2, N]), ALU.mult)
        TT(S, "dve", YR, YR[:, :, :], Yt, Yt[:, :, :], k.pvs,
           P(k, "norm_g").unsqueeze(2).broadcast_to([128, 16, N]), ALU.mult)

        LOAD(S, "sp", xtk, xtk[:, :], k.HCTOK, k.HCTOK.t.ap()[tb * N:(tb + 1) * N, :])
        for wc in range(2):
            ps = psr.next()
            for kh in range(2):
                w = ws.get(k.ssm_w_out, vout, kh * 8, wc * 512)
                for kc in range(8):
                    MM(S, ps, ps[:, :], YR, YR[:, kh * 8 + kc, :], w, w[:, kc, :],
                       start=(kh == 0 and kc == 0), stop=(kh == 1 and kc == 7))
            STT(S, ZT, ZT[:, wc * 512:(wc + 1) * 512], xtk, xtk[:, wc * 512:(wc + 1) * 512], ALPHA, ps, ps[:, :],
                ALU.mult, ALU.add)
        s_ = st_.next()
        for hf in range(2):
            S.op("dve", lambda e, s_=s_, hf=hf: e.bn_stats(out=s_[:, hf * 6:(hf + 1) * 6],
                                                          in_=ZT[:, hf * 512:(hf + 1) * 512]),
                 reads=[ZT], writes=[s_])
        S.op("dve", lambda e, s_=s_: e.bn_aggr(out=s_[:, 12:14], in_=s_[:, 0:12]), reads=[s_], writes=[s_])
        ACT(S, s_, s_[:, 14:15], s_, s_[:, 13:14], AF.Sqrt, bias=k.eps_t[:, 0:1], xr=[k.eps_t])
        S.op("dve", lambda e, s_=s_: e.reciprocal(out=s_[:, 15:16], in_=s_[:, 14:15]), reads=[s_], writes=[s_])
        TS(S, "dve", ZT, ZT[:, :], ZT, ZT[:, :], s_[:, 12:13], s_[:, 15:16], ALU.subtract, ALU.mult, xr=[s_])
        TT(S, "dve", ZT, ZT[:, :], ZT, ZT[:, :], gB, gB[:, :], ALU.mult)
        TT(S, "dve", ZT, ZT[:, :], ZT, ZT[:, :], bB, bB[:, :], ALU.add)
        for half in range(2):
            ps = psr.next()
            for c4 in range(4):
                c = half * 4 + c4
                TRP(S, ps, ps[:, c4 * 128:(c4 + 1) * 128], ZT, ZT[:, c * 128:(c + 1) * 128], k.cs, ident)
            CP(S, "act", SQ, SQ[:, half * 4:half * 4 + 4, :], ps, ps[:, :].rearrange("p (c t) -> p c t", c=4))
        STORE(S, "act", HOUT, hout[:, :, tb * N:(tb + 1) * N], SQ, SQ[:, 0:8, :])
    gens = [tile_gen(tb) for tb in range(T // N)]
    next(gens[0])
    for tb in range(T // N):
        next(gens[tb])
        if tb + 1 < T // N:
            next(gens[tb + 1])
        next(gens[tb], None)
    S.end_phase()


_NC_CACHE = {}


def make_in_maps(inp):
    cst = make_consts()
    pv = make_pv(inp)
    rowp = np.concatenate([inp["ssm_dt_bias"][0], inp["ssm_a_log"][0]])[None, :].astype(np.float32)
    shared = {
        "cst": cst, "pv": pv, "rowp": rowp,
        "pool_w_in": inp["pool_w_in"][0], "pool_w_grp": inp["pool_w_grp"][0], "pool_w_out": inp["pool_w_out"][0],
        "ssm_w_in": inp["ssm_w_in"][0], "ssm_w_out": inp["ssm_w_out"][0],
        "xa_wq": inp["xa_wq"], "xa_wk": inp["xa_wk"], "xa_wv": inp["xa_wv"], "xa_wo": inp["xa_wo"],
        "moe_w_router": inp["moe_w_router"], "moe_b_router": inp["moe_b_router"],
        "moe_w_gate_up": inp["moe_w_gate_up"], "moe_w_down": inp["moe_w_down"], "moe_b_down": inp["moe_b_down"],
        "ln_ffn_g": inp["ln_ffn_g"], "ln_ffn_b": inp["ln_ffn_b"],
        "ln_mix_g": inp["ln_mix_g"], "ln_mix_b": inp["ln_mix_b"],
    }
    shared = {kk: np.ascontiguousarray(np.asarray(vv, dtype=np.float32)) for kk, vv in shared.items()}
    maps = []
    for b in range(8):
        m = dict(shared)
        m["xT"] = np.ascontiguousarray(np.asarray(inp["x"][b]).T)
        m["memT"] = np.ascontiguousarray(np.asarray(inp["mem"][b]).T)
        maps.append(m)
    return maps


def kernel(**inputs):
    inp = {kk: np.asarray(vv) for kk, vv in inputs.items()}
    if "nc" not in _NC_CACHE:
        _NC_CACHE["nc"] = build()
    nc = _NC_CACHE["nc"]
    maps = make_in_maps(inp)
    res = run_bass_kernel_spmd(nc, maps, core_ids=list(range(8)))
    return np.stack([res.results[b]["out"] for b in range(8)], axis=0).astype(np.float32)
```

```python
from contextlib import ExitStack
import numpy as np
import concourse.bass as bass
import concourse.mybir as mybir
from concourse.bass_utils import run_bass_kernel_spmd

F32 = mybir.dt.float32
F32R = mybir.dt.float32r
FAST = True


RT = F32R if FAST else F32


def R(ap):
    return ap


I32 = mybir.dt.int32
AF = mybir.ActivationFunctionType
ALU = mybir.AluOpType
AX = mybir.AxisListType

T = 4096
D = 1024
NTL = 32
ALPHA = float(4 ** 0.25)
EPS = 1e-5
CAP = 768
NE = 32
DEBUG = False

ENGS = ("pe", "act", "dve", "pool", "sp")


class Buf:
    __slots__ = ("name", "t", "w", "r", "dsem", "local")

    def __init__(self, name, t):
        self.name = name
        self.t = t
        self.w = {}
        self.r = {}
        self.dsem = None
        self.local = False

    def __getitem__(self, idx):
        return self.t[idx]


class Rot:
    def __init__(self, bufs):
        self.b = bufs
        self.i = 0

    def next(self):
        b = self.b[self.i % len(self.b)]
        self.i += 1
        return b


class Sched:
    def __init__(self, nc, stack):
        self.nc = nc
        self.stack = stack
        self.sems = {}
        self.cnt = {}
        for e in ENGS:
            self._mksem(e)
        self.seen = {e: {} for e in ENGS}
        self.prog = {e: [] for e in ENGS}
        self.nbuf = 0
        self.phase_stack = None
        self.free_dsems = []

    def _mksem(self, key):
        h = self.stack.enter_context(self.nc.semaphore(f"s_{key}"))
        self.sems[key] = h
        self.cnt[key] = 0

    def begin_phase(self):
        self.phase_stack = ExitStack()
        self.phase_dsems = []

    def end_phase(self):
        self.barrier()
        self.phase_stack.close()
        self.phase_stack = None
        self.free_dsems.extend(self.phase_dsems)
        self.phase_dsems = []

    def sbuf(self, name, shape, dtype=F32, glob=False):
        st = self.stack if (glob or self.phase_stack is None) else self.phase_stack
        self.uid = getattr(self, "uid", 0) + 1
        name = f"{name}_u{self.uid}"
        t = st.enter_context(self.nc.sbuf_tensor(name, list(shape), dtype))
        b = Buf(name, t)
        b.local = st is not self.stack
        return b

    def psum(self, name, shape, dtype=F32):
        st = self.stack if self.phase_stack is None else self.phase_stack
        self.uid = getattr(self, "uid", 0) + 1
        name = f"{name}_u{self.uid}"
        t = st.enter_context(self.nc.psum_tensor(name, list(shape), dtype))
        return Buf(name, t)

    def _deps(self, e, reads, writes):
        need = {}
        for b in reads:
            for k, v in b.w.items():
                if need.get(k, 0) < v:
                    need[k] = v
        for b in writes:
            for k, v in b.w.items():
                if need.get(k, 0) < v:
                    need[k] = v
            for k, v in b.r.items():
                if need.get(k, 0) < v:
                    need[k] = v
        waits = []
        seen = self.seen[e]
        for k, v in need.items():
            if e == "pe" and k == "pe":
                continue
            if seen.get(k, 0) < v:
                seen[k] = v
                waits.append((k, v))
        return waits

    def op(self, e, fn, reads=(), writes=()):
        waits = self._deps(e, reads, writes)
        self.cnt[e] += 1
        v = self.cnt[e]
        self.prog[e].append((waits, fn, (e, 1)))
        for b in writes:
            b.w[e] = v
        for b in reads:
            b.r[e] = v

    def dma(self, q, fn, reads=(), writes=(), owner=None, group=False):
        if owner is None:
            owner = writes[0]
        if owner.dsem is None:
            if self.free_dsems:
                owner.dsem = self.free_dsems.pop()
            else:
                self.nbuf += 1
                owner.dsem = f"d{self.nbuf}"
                self._mksem(owner.dsem)
            if owner.local:
                self.phase_dsems.append(owner.dsem)
        k = owner.dsem
        waits = self._deps(q, reads, writes)
        if (not group) and self.cnt[k] > 0 and self.seen[q].get(k, 0) < self.cnt[k]:
            self.seen[q][k] = self.cnt[k]
            waits.append((k, self.cnt[k]))
        self.cnt[k] += 16
        v = self.cnt[k]
        self.prog[q].append((waits, fn, (k, 16)))
        for b in writes:
            b.w[k] = v
        for b in reads:
            b.r[k] = v

    def barrier(self):
        for e in ENGS:
            waits = []
            for k, v in self.cnt.items():
                if v > 0 and self.seen[e].get(k, 0) < v and not (k == e):
                    self.seen[e][k] = v
                    waits.append((k, v))
            if waits:
                self.prog[e].append((waits, None, None))

    def emit(self):
        nc = self.nc
        sems = self.sems
        prog = self.prog
        with nc.Block() as block:
            def mk(ekey):
                def body(engine):
                    for waits, fn, inc in prog[ekey]:
                        for k, v in waits:
                            engine.wait_ge(sems[k], v)
                        if fn is not None:
                            fn(engine).then_inc(sems[inc[0]], inc[1])
                return body
            block.tensor(mk("pe"))
            block.scalar(mk("act"))
            block.vector(mk("dve"))
            block.gpsimd(mk("pool"))
            block.sync(mk("sp"))


def MM(S, ob, o, lb, l, rb, r, start=True, stop=True):
    S.op("pe", lambda e: e.matmul(o, l, r, start=start, stop=stop), reads=[lb, rb], writes=[ob])


def TRP(S, ob, o, ib, i, idb, idap):
    S.op("pe", lambda e: e.transpose(o, i, idap), reads=[ib, idb], writes=[ob])


def TT(S, eng, ob, o, ab, a, bb, b, op):
    S.op(eng, lambda e: e.tensor_tensor(out=o, in0=a, in1=b, op=op), reads=[ab, bb], writes=[ob])


def TS(S, eng, ob, o, ab, a, s1, s2, op0, op1=None, xr=()):
    if op1 is None:
        S.op(eng, lambda e: e.tensor_scalar(out=o, in0=a, scalar1=s1, scalar2=None, op0=op0),
             reads=[ab, *xr], writes=[ob])
    else:
        S.op(eng, lambda e: e.tensor_scalar(out=o, in0=a, scalar1=s1, scalar2=s2, op0=op0, op1=op1),
             reads=[ab, *xr], writes=[ob])


def STT(S, ob, o, ab, a, sc, bb, b, op0, op1, xr=(), accum=None, xw=()):
    if accum is None:
        S.op("dve", lambda e: e.scalar_tensor_tensor(out=o, in0=a, scalar=sc, in1=b, op0=op0, op1=op1),
             reads=[ab, bb, *xr], writes=[ob])
    else:
        S.op("dve", lambda e: e.scalar_tensor_tensor(out=o, in0=a, scalar=sc, in1=b, op0=op0, op1=op1,
                                                     accum_out=accum),
             reads=[ab, bb, *xr], writes=[ob, *xw])


def ACT(S, ob, o, ab, a, func, bias=None, scale=None, xr=(), accum=None, xw=()):
    kw = {}
    if bias is not None:
        kw["bias"] = bias
    if scale is not None:
        kw["scale"] = scale
    if accum is not None:
        kw["accum_out"] = accum
    S.op("act", lambda e: e.activation(out=o, in_=a, func=func, **kw), reads=[ab, *xr], writes=[ob, *xw])


def CP(S, eng, ob, o, ab, a):
    if eng == "act":
        S.op("act", lambda e: e.activation(out=o, in_=a, func=AF.Identity), reads=[ab], writes=[ob])
    else:
        S.op(eng, lambda e: e.tensor_copy(out=o, in_=a), reads=[ab], writes=[ob])


def LOAD(S, q, db, d, sb, s, group=False):
    S.dma(q, lambda e: e.dma_start(out=d, in_=s), reads=[sb], writes=[db], owner=db, group=group)


def STORE(S, q, db, d, sb, s):
    S.dma(q, lambda e: e.dma_start(out=d, in_=s), reads=[sb], writes=[db], owner=sb)


CST = {}


def _cst_layout():
    off = 0
    for name, n in [("ident", 128), ("onesD", 128), ("ones", 128), ("ones256", 128), ("ustrict", 128),
                    ("triinc", 128), ("blockones", 128), ("strictlt", 128), ("mask01", 128),
                    ("csel0", 128), ("csel1", 128), ("rcnt", 64), ("iotaec", 32)]:
        CST[name] = (off, n)
        off += n
    return off


NCST = _cst_layout()


def make_consts():
    c = np.zeros((128, NCST), np.float32)
    i = np.arange(128)
    ch = i // 64

    def put(name, arr):
        o, n = CST[name]
        c[:, o:o + n] = arr
    put("ident", np.eye(128))
    put("onesD", np.full((128, 128), 1.0 / D))
    put("ones", np.ones((128, 128)))
    put("ones256", np.full((128, 128), 1.0 / 256))
    put("ustrict", (i[:, None] < i[None, :]))
    same = ch[:, None] == ch[None, :]
    put("triinc", (i[:, None] <= i[None, :]) & same)
    put("blockones", same)
    put("strictlt", (i[None, :] < i[:, None]) & same)
    put("mask01", (i[:, None] <= i[None, :]) & same)
    put("csel0", np.repeat((i < 64)[:, None], 128, 1))
    put("csel1", np.repeat((i >= 64)[:, None], 128, 1))
    rc = np.zeros((128, 4, 16))
    for g, w in enumerate((2, 4, 8, 16)):
        rc[:, g, :] = 1.0 / np.minimum(np.arange(16) + 1, w)
    put("rcnt", rc.reshape(128, 64))
    put("iotaec", np.repeat((np.arange(32) * CAP)[None, :], 128, 0))
    return c


PV = {}


def _pv_layout():
    off = 0
    items = [("pool_scale", 8)]
    for l in range(2):
        items += [(f"ln_mix_g{l}", 8), (f"ln_mix_b{l}", 8), (f"ln_xa_g{l}", 8), (f"ln_xa_b{l}", 8)]
    items += [("conv_w", 128), ("conv_b", 32), ("norm_g", 16), ("dskip", 16)]
    for l in range(2):
        items += [(f"bg{l}", 256), (f"bu{l}", 256)]
    for name, n in items:
        PV[name] = (off, n)
        off += n
    return off


NPV = _pv_layout()


def fm(v):
    return np.ascontiguousarray(v.reshape(-1, 128).T)


def make_pv(inp):
    p = np.zeros((128, NPV), np.float32)

    def put(name, arr):
        o, n = PV[name]
        p[:, o:o + n] = arr.reshape(128, n)
    put("pool_scale", fm(inp["pool_scale"][0]))
    for l in range(2):
        put(f"ln_mix_g{l}", fm(inp["ln_mix_g"][l]))
        put(f"ln_mix_b{l}", fm(inp["ln_mix_b"][l]))
        put(f"ln_xa_g{l}", fm(inp["ln_xa_g"][l]))
        put(f"ln_xa_b{l}", fm(inp["ln_xa_b"][l]))
    cw = inp["ssm_conv_w"][0]
    put("conv_w", np.stack([fm(cw[k]) for k in range(4)], axis=2))
    put("conv_b", fm(inp["ssm_conv_b"][0]))
    put("norm_g", fm(inp["ssm_norm_g"][0]))
    put("dskip", fm(np.repeat(inp["ssm_d"][0], 64)))
    for l in range(2):
        b = inp["moe_b_gate_up"][l]
        put(f"bg{l}", np.stack([fm(b[e, 0::2]) for e in range(NE)], axis=1))
        put(f"bu{l}", np.stack([fm(b[e, 1::2]) for e in range(NE)], axis=1))
    return p


class K:
    pass


def build(phases=None, dbg=False):
    nc = bass.Bass("TRN2", target_bir_lowering=False)
    k = K()
    k.nc = nc
    k.phases = phases

    def din(name, shape, dtype=F32):
        return Buf(name, nc.dram_tensor(name, list(shape), dtype, kind="ExternalInput"))

    def dscr(name, shape, dtype=F32):
        return Buf(name, nc.dram_tensor(name, list(shape), dtype,
                                        kind="ExternalOutput" if dbg else "Internal"))

    k.xT = din("xT", [D, T])
    k.memT = din("memT", [D, 256])
    k.cst = din("cst", [128, NCST])
    k.pv = din("pv", [128, NPV])
    k.pool_w_in = din("pool_w_in", [D, D])
    k.pool_w_grp = din("pool_w_grp", [4, 256, 256])
    k.pool_w_out = din("pool_w_out", [D, D])
    k.ssm_w_in = din("ssm_w_in", [D, 6176])
    k.ssm_w_out = din("ssm_w_out", [2048, D])
    k.rowp = din("rowp", [1, 64])
    k.xa_wq = din("xa_wq", [2, D, D])
    k.xa_wk = din("xa_wk", [2, D, D])
    k.xa_wv = din("xa_wv", [2, D, D])
    k.xa_wo = din("xa_wo", [2, D, D])
    k.w_router = din("moe_w_router", [2, D, NE])
    k.b_router = din("moe_b_router", [2, NE])
    k.w_gu = din("moe_w_gate_up", [2, NE, D, 2 * D])
    k.w_dn = din("moe_w_down", [2, NE, D, D])
    k.b_dn = din("moe_b_down", [2, NE, D])
    k.ln_mix_g = din("ln_mix_g", [2, D])
    k.ln_mix_b = din("ln_mix_b", [2, D])
    k.ln_ffn_g = din("ln_ffn_g", [2, D])
    k.ln_ffn_b = din("ln_ffn_b", [2, D])
    k.out = Buf("out", nc.dram_tensor("out", [T, D], F32, kind="ExternalOutput"))
    k.HA = dscr("HA", [D, T])
    k.HB = dscr("HB", [D, T])
    k.HC = dscr("HC", [D, T])
    k.HTOK = dscr("HTOK", [T, D])
    k.HCTOK = dscr("HCTOK", [T, D])
    k.ZSD = dscr("ZSD", [2048, T])
    k.XBCD = dscr("XBCD", [4096, T])
    k.XS = dscr("XS", [NE * CAP, D])
    k.YS = dscr("YS", [NE * CAP, D])

    with ExitStack() as st:
        S = Sched(nc, st)
        k.S = S
        k.cs = S.sbuf("cst_s", [128, NCST], glob=True)
        k.pvs = S.sbuf("pv_s", [128, NPV], glob=True)
        k.dest = S.sbuf("dest_i", [128, NTL, 4], I32, glob=True)
        k.gates = S.sbuf("gates", [128, NTL, 4], glob=True)
        LOAD(S, "sp", k.cs, k.cs[:, :], k.cst, k.cst.t.ap())
        k.csr = S.sbuf("csr_s", [128, 384], RT, glob=True)
        for ci, cn in enumerate(("onesD", "strictlt", "ones")):
            LOAD(S, "pool" if FAST else "sp", k.csr, k.csr[:, ci * 128:(ci + 1) * 128], k.cst,
                 k.cst.t.ap()[:, CST[cn][0]:CST[cn][0] + 128], group=(ci > 0))
        LOAD(S, "sp", k.pvs, k.pvs[:, :], k.pv, k.pv.t.ap())

        def want(p):
            return phases is None or p in phases

        if want("pool0"):
            phase_pool(k, k.xT, k.HA)
        if want("xa0"):
            phase_xattn(k, 0, k.HA, k.HB, k.HTOK)
        if want("moe0"):
            phase_moe(k, 0, k.HB, k.HTOK, k.HC, None)
        if want("ssd1"):
            phase_ssd_a(k, k.HC)
            phase_ssd_b(k, k.HC, k.HA)
        if want("xa1"):
            phase_xattn(k, 1, k.HA, k.HB, k.HTOK)
        if want("moe1"):
            phase_moe(k, 1, k.HB, k.HTOK, None, k.out)
        S.barrier()
        S.emit()
    return nc


def C(k, name):
    o, n = CST[name]
    return k.cs[:, o:o + n]


def P(k, name, j=None):
    o, n = PV[name]
    if j is None:
        return k.pvs[:, o:o + n]
    return k.pvs[:, o + j:o + j + 1]


def fmview(buf):
    return buf.t.ap().rearrange("(c p) t -> p c t", p=128)


def wview(ap2d):
    return ap2d.rearrange("(c p) f -> p c f", p=128)


class WStream:
    def __init__(self, S, n=4):
        self.S = S
        self.rot = Rot([S.sbuf(f"wt{i}", [128, 8, 512], RT) for i in range(n)])

    def get(self, wbuf, view, k0, c0, ncols=512, nk=8):
        b = self.rot.next()
        LOAD(self.S, "pool" if FAST else "sp", b, R(b[:, 0:nk, 0:ncols]), wbuf, view[:, k0:k0 + nk, c0:c0 + ncols])
        return b


def proj_fm(k, ws, psr, wbuf, view, xb, x_of_kc, N, n_oc, evac, kchunks=8, col0=0):
    S = k.S
    for wc in range((n_oc + 3) // 4):
        wts = [ws.get(wbuf, view, kh * 8, col0 + wc * 512) for kh in range(kchunks // 8)]
        for o4 in range(min(4, n_oc - wc * 4)):
            oc = wc * 4 + o4
            ps = psr.next()
            for kc in range(kchunks):
                MM(S, ps, ps[:, 0:N], wts[kc // 8], R(wts[kc // 8][:, kc % 8, o4 * 128:(o4 + 1) * 128]),
                   xb, R(x_of_kc(kc)), start=(kc == 0), stop=(kc == kchunks - 1))
            evac(oc, ps, ps[:, 0:N])


def proj_fm16(k, ws, psr, wbuf, view, xb, x_of_kc, N, n_oc, evac):
    S = k.S
    for wc in range(n_oc // 4):
        pss = [psr.next() for _ in range(4)]
        for kh in range(2):
            w = ws.get(wbuf, view, kh * 8, wc * 512)
            for o4 in range(4):
                for kc in range(8):
                    MM(S, pss[o4], pss[o4][:, 0:N], w, w[:, kc, o4 * 128:(o4 + 1) * 128], xb, x_of_kc(kh * 8 + kc),
                       start=(kh == 0 and kc == 0), stop=(kh == 1 and kc == 7))
        for o4 in range(4):
            evac(wc * 4 + o4, pss[o4], pss[o4][:, 0:N])


def ln_fm(k, psr, Z, N, gname, bname, OUT, scr):
    S = k.S
    onesD = C(k, "onesD")
    pm = psr.next()
    for c in range(8):
        MM(S, pm, pm[:, 0:N], k.cs, onesD, Z, Z[:, c, 0:N], start=(c == 0), stop=(c == 7))
    mean = k.ln_mean
    CP(S, "act", mean, mean[:, 0:N], pm, pm[:, 0:N])
    TT(S, "dve", Z, Z[:, :, 0:N], Z, Z[:, :, 0:N], mean, mean[:, 0:N].unsqueeze(1).broadcast_to([128, 8, N]),
       ALU.subtract)
    ACT(S, scr, scr[:, :, 0:N], Z, Z[:, :, 0:N], AF.Square)
    pv = psr.next()
    for c in range(8):
        MM(S, pv, pv[:, 0:N], k.csr, k.csr[:, 0:128], scr, scr[:, c, 0:N], start=(c == 0), stop=(c == 7))
    rstd = k.ln_rstd
    ACT(S, rstd, rstd[:, 0:N], pv, pv[:, 0:N], AF.Ln, bias=k.eps_t[:, 0:1], xr=[k.eps_t])
    ACT(S, rstd, rstd[:, 0:N], rstd, rstd[:, 0:N], AF.Exp, scale=-0.5)
    TT(S, "dve", Z, Z[:, :, 0:N], Z, Z[:, :, 0:N], rstd, rstd[:, 0:N].unsqueeze(1).broadcast_to([128, 8, N]),
       ALU.mult)
    for c in range(8):
        ACT(S, OUT, OUT[:, c, 0:N], Z, Z[:, c, 0:N], AF.Identity, bias=P(k, bname, c), scale=P(k, gname, c),
            xr=[k.pvs])


def common_small(k):
    S = k.S
    k.ln_mean = S.sbuf("ln_mean", [128, 512])
    k.ln_rstd = S.sbuf("ln_rstd", [128, 512])
    k.eps_t = S.sbuf("eps_t", [128, 1])
    S.op("dve", lambda e: e.memset(k.eps_t[:, :], EPS), writes=[k.eps_t])


def phase_pool(k, HIN, HOUT):
    S = k.S
    S.begin_phase()
    common_small(k)
    N = 512
    ws = WStream(S, 2)
    psr = Rot([S.psum(f"ps{i}", [128, 512]) for i in range(8)])
    xr = Rot([S.sbuf(f"xt{i}", [128, 8, N]) for i in range(2)])
    U = Rot([S.sbuf(f"U{i}", [128, 8, 16 + N]) for i in range(2)])
    TA = S.sbuf("TA", [128, 2, 16 + N])
    TB = S.sbuf("TB", [128, 2, 16 + N])
    M = S.sbuf("M", [128, 8, N], RT)
    Y = S.sbuf("Y", [128, 8, N], RT)
    Z = S.sbuf("Z", [128, 8, N])
    xtr = S.sbuf("xtr", [128, 8, N], RT)
    wg = S.sbuf("wg", [128, 4, 2, 256], RT)
    LOAD(S, "pool" if FAST else "sp", wg, R(wg[:, :, :, :]), k.pool_w_grp,
         k.pool_w_grp.t.ap().rearrange("g (kc p) d -> p g kc d", p=128))
    vin = wview(k.pool_w_in.t.ap())
    vout = wview(k.pool_w_out.t.ap())
    hin = fmview(HIN)
    hout = fmview(HOUT)
    o_rc, _ = CST["rcnt"]
    prevU = None
    for tb in range(T // N):
        xt = xr.next()
        LOAD(S, "sp", xt, xt[:, :, :], HIN, hin[:, :, tb * N:(tb + 1) * N])
        LOAD(S, "pool" if FAST else "sp", xtr, R(xtr[:, :, :]), HIN, hin[:, :, tb * N:(tb + 1) * N])
        u = U.next()
        if tb == 0:
            S.op("pool", lambda e, u=u: e.memset(u[:, :, 0:16], 0.0), writes=[u])
        else:
            CP(S, "dve", u, u[:, :, 0:16], prevU, prevU[:, :, N:N + 16])

        def ev_u(oc, ps, pa, u=u):
            CP(S, "act", u, u[:, oc, 16:16 + N], ps, pa)
        proj_fm(k, ws, psr, k.pool_w_in, vin, xtr, lambda kc: xtr[:, kc, :], N, 8, ev_u)
        W_ = 16 + N
        for g in range(4):
            w = 2 << g
            cur_b, cur = u, (lambda a, b, g=g: u[:, 2 * g:2 * g + 2, a:b])
            sh = 1
            tgt = [TA, TB]
            for step in range(g + 1):
                t_ = tgt[step % 2]
                lo = 2 * sh - 1
                TT(S, "dve", t_, t_[:, :, lo:W_], cur_b, cur(lo, W_), cur_b, cur(lo - sh, W_ - sh), ALU.add)
                cur_b, cur = t_, (lambda a, b, t_=t_: t_[:, :, a:b])
                sh *= 2
            if tb == 0:
                rc = k.cs[:, o_rc + g * 16:o_rc + (g + 1) * 16]
                TS(S, "dve", cur_b, cur(32, W_), cur_b, cur(32, W_), 1.0 / w, None, ALU.mult)
                TT(S, "dve", cur_b, cur(16, 32), cur_b, cur(16, 32), k.cs,
                   rc.unsqueeze(1).broadcast_to([128, 2, 16]), ALU.mult)
                TT(S, "dve", M, R(M[:, 2 * g:2 * g + 2, :]), cur_b, cur(16, W_), u, u[:, 2 * g:2 * g + 2, 16:W_],
                   ALU.subtract)
            else:
                STT(S, M, R(M[:, 2 * g:2 * g + 2, :]), cur_b, cur(16, W_), 1.0 / w, u, u[:, 2 * g:2 * g + 2, 16:W_],
                    ALU.mult, ALU.subtract)
        for g in range(4):
            for j in range(2):
                ps = psr.next()
                for kc in range(2):
                    MM(S, ps, ps[:, 0:N], wg, R(wg[:, g, kc, j * 128:(j + 1) * 128]), M, R(M[:, 2 * g + kc, :]),
                       start=(kc == 0), stop=(kc == 1))
                oc = 2 * g + j
                ACT(S, Y, R(Y[:, oc, :]), ps, ps[:, 0:N], AF.Identity, scale=P(k, "pool_scale", oc), xr=[k.pvs])

        def ev_z(oc, ps, pa, xt=xt):
            STT(S, Z, Z[:, oc, :], xt, xt[:, oc, :], ALPHA, ps, pa, ALU.mult, ALU.add)
        proj_fm(k, ws, psr, k.pool_w_out, vout, Y, lambda kc: Y[:, kc, :], N, 8, ev_z)
        o = Z
        ln_fm(k, psr, Z, N, "ln_mix_g0", "ln_mix_b0", o, M)
        STORE(S, "act", HOUT, hout[:, :, tb * N:(tb + 1) * N], o, o[:, :, :])
        prevU = u
    S.end_phase()


def phase_xattn(k, l, HIN, HOUT, HTOK):
    S = k.S
    S.begin_phase()
    common_small(k)
    N = 512
    ws = WStream(S, 2)
    psr = Rot([S.psum(f"ps{i}", [128, 512]) for i in range(8)])
    xr = Rot([S.sbuf(f"xt{i}", [128, 8, N]) for i in range(2)])
    memT = S.sbuf("memT", [128, 8, 256], RT)
    kT = S.sbuf("kT", [128, 8, 256], RT)
    v = S.sbuf("v", [128, 2, D], RT)
    qT = S.sbuf("qT", [128, 8, N], RT)
    Pm = Rot([S.sbuf(f"Pm{i}", [128, 4, 256]) for i in range(2)])
    pT = S.sbuf("pT", [128, 2, 4, N], RT)
    OT = S.sbuf("OT", [128, 8, N], RT)
    Z = S.sbuf("Z", [128, 8, N])
    xtr = S.sbuf("xtr", [128, 8, N], RT)
    htk = Rot([S.sbuf(f"htk{i}", [128, D]) for i in range(2)])
    sm = Rot([S.sbuf(f"sm{i}", [128, 16]) for i in range(2)])
    ident = C(k, "ident")
    LOAD(S, "pool" if FAST else "sp", memT, R(memT[:, :, :]), k.memT, fmview(k.memT))
    vq = wview(k.xa_wq.t.ap()[l])
    vk = wview(k.xa_wk.t.ap()[l])
    vv = wview(k.xa_wv.t.ap()[l])
    vo = wview(k.xa_wo.t.ap()[l])

    def ev_k(oc, ps, pa):
        CP(S, "act", kT, R(kT[:, oc, :]), ps, pa)
    proj_fm(k, ws, psr, k.xa_wk, vk, memT, lambda kc: memT[:, kc, :], 256, 8, ev_k)
    for wc in range(2):
        w = ws.get(k.xa_wv, vv, 0, wc * 512)
        for mc in range(2):
            ps = psr.next()
            for kc in range(8):
                MM(S, ps, ps[:, :], memT, R(memT[:, kc, mc * 128:(mc + 1) * 128]), w, R(w[:, kc, :]),
                   start=(kc == 0), stop=(kc == 7))
            CP(S, "act", v, R(v[:, mc, wc * 512:(wc + 1) * 512]), ps, ps[:, :])
    hin = fmview(HIN)
    hout = fmview(HOUT)
    gname, bname = f"ln_xa_g{l}", f"ln_xa_b{l}"
    for tb in range(T // N):
        xt = xr.next()
        LOAD(S, "sp", xt, xt[:, :, :], HIN, hin[:, :, tb * N:(tb + 1) * N])
        LOAD(S, "pool" if FAST else "sp", xtr, R(xtr[:, :, :]), HIN, hin[:, :, tb * N:(tb + 1) * N])

        def ev_q(oc, ps, pa):
            S.op("act", lambda e: e.activation(out=R(qT[:, oc, :]), in_=pa, func=AF.Identity, scale=0.0625),
                 reads=[ps], writes=[qT])
        proj_fm(k, ws, psr, k.xa_wq, vq, xtr, lambda kc: xtr[:, kc, :], N, 8, ev_q)
        def xa_stage_a(tt):
                tsl = slice(tt * 128, (tt + 1) * 128)
                pm = Pm.next()
                s_ = sm.next()
                banks = [psr.next(), psr.next()]
                for h in range(4):
                    ps = banks[h // 2]
                    for dc in range(2):
                        MM(S, ps, ps[:, (h % 2) * 256:(h % 2 + 1) * 256], qT, R(qT[:, 2 * h + dc, tsl]),
                           kT, R(kT[:, 2 * h + dc, :]), start=(dc == 0), stop=(dc == 1))
                for b2 in range(2):
                    ps = banks[b2]
                    S.op("dve", lambda e, ps=ps, s_=s_, b2=b2: e.tensor_reduce(
                        out=s_[:, 2 * b2:2 * b2 + 2], in_=ps[:, :].rearrange("p (a b) -> p a b", a=2),
                        axis=AX.X, op=ALU.max), reads=[ps], writes=[s_])
                TS(S, "dve", s_, s_[:, 4:8], s_, s_[:, 0:4], -1.0, None, ALU.mult)
                for h in range(4):
                    ps = banks[h // 2]
                    ACT(S, pm, pm[:, h, :], ps, ps[:, (h % 2) * 256:(h % 2 + 1) * 256], AF.Exp,
                        bias=s_[:, 4 + h:5 + h], xr=[s_], accum=s_[:, 8 + h:9 + h], xw=[s_])
                S.op("dve", lambda e, s_=s_: e.reciprocal(out=s_[:, 12:16], in_=s_[:, 8:12]), reads=[s_], writes=[s_])
                TT(S, "dve", pm, pm[:, :, :], pm, pm[:, :, :], s_, s_[:, 12:16].unsqueeze(2).broadcast_to([128, 4, 256]),
                   ALU.mult)
                return tsl, pm

        def xa_stage_b(tt, tsl, pm):
                for mc in range(2):
                    ps = psr.next()
                    for h in range(4):
                        TRP(S, ps, ps[:, h * 128:(h + 1) * 128], pm, pm[:, h, mc * 128:(mc + 1) * 128], k.cs, ident)
                    CP(S, "act" if mc == 0 else "dve", pT, R(pT[:, mc, :, tsl]), ps,
                       ps[:, :].rearrange("p (h t) -> p h t", h=4))
        cur = xa_stage_a(0)
        for tt in range(4):
            nxt = xa_stage_a(tt + 1) if tt < 3 else None
            xa_stage_b(tt, *cur)
            cur = nxt
        for h in range(4):
            for dc in range(2):
                ps = psr.next()
                for mc in range(2):
                    MM(S, ps, ps[:, :], v, R(v[:, mc, h * 256 + dc * 128:h * 256 + (dc + 1) * 128]),
                       pT, R(pT[:, mc, h, :]), start=(mc == 0), stop=(mc == 1))
                CP(S, "act", OT, R(OT[:, 2 * h + dc, :]), ps, ps[:, :])

        def ev_z(oc, ps, pa, xt=xt):
            STT(S, Z, Z[:, oc, :], xt, xt[:, oc, :], ALPHA, ps, pa, ALU.mult, ALU.add)
        proj_fm(k, ws, psr, k.xa_wo, vo, OT, lambda kc: OT[:, kc, :], N, 8, ev_z)
        o = Z
        ln_fm(k, psr, Z, N, gname, bname, o, OT)
        STORE(S, "act", HOUT, hout[:, :, tb * N:(tb + 1) * N], o, o[:, :, :])
        for tt in range(4):
            ht = htk.next()
            for half in range(2):
                ps = psr.next()
                for c4 in range(4):
                    c = half * 4 + c4
                    TRP(S, ps, ps[:, c4 * 128:(c4 + 1) * 128], o, o[:, c, tt * 128:(tt + 1) * 128], k.cs, ident)
                CP(S, "act" if half == 0 else "dve", ht, ht[:, half * 512:(half + 1) * 512], ps, ps[:, :])
            r0 = tb * N + tt * 128
            STORE(S, "act", HTOK, HTOK.t.ap()[r0:r0 + 128, :], ht, ht[:, :])
    S.end_phase()


def phase_moe(k, l, HFM, HTOK, HOUT_FM, OUT_TOK):
    S = k.S
    ident = C(k, "ident")
    S.begin_phase()
    psr = Rot([S.psum(f"ps{i}", [128, 512]) for i in range(8)])
    xr = Rot([S.sbuf(f"xt{i}", [128, 8, 512]) for i in range(2)])
    hr = Rot([S.sbuf(f"hrow{i}", [128, D]) for i in range(3)])
    wr = S.sbuf("wr", [128, 8, NE])
    brt = S.sbuf("brt", [128, NE])
    cum = S.sbuf("cum", [128, NE])
    lgr = Rot([S.sbuf(f"lg{i}", [128, NE]) for i in range(2)])
    t8r = Rot([S.sbuf(f"t8{i}", [128, 16]) for i in range(2)])
    mkr = Rot([S.sbuf(f"mk{i}", [128, NE]) for i in range(2)])
    pcr = Rot([S.sbuf(f"pc{i}", [128, NE]) for i in range(2)])
    jnk = S.sbuf("jnk", [128, NE])
    dfr = Rot([S.sbuf(f"df{i}", [128, 4]) for i in range(2)])
    LOAD(S, "sp", wr, wr[:, :, :], k.w_router, wview(k.w_router.t.ap()[l]))
    LOAD(S, "sp", brt, brt[:, :], k.b_router, k.b_router.t.ap()[l:l + 1, :].broadcast_to([128, NE]))
    S.op("dve", lambda e: e.memset(cum[:, :], 0.0), writes=[cum])
    hfm = fmview(HFM)
    ustrict = C(k, "ustrict")
    ones = C(k, "ones")
    iotaec = C(k, "iotaec")
    xcur = [None]

    def rt_stage_a(ti):
            if ti % 4 == 0:
                xcur[0] = xr.next()
                LOAD(S, "sp", xcur[0], xcur[0][:, :, :], HFM, hfm[:, :, ti * 128:ti * 128 + 512])
            xt = xcur[0]
            tsl = slice((ti % 4) * 128, (ti % 4 + 1) * 128)
            hrow = hr.next()
            LOAD(S, "sp", hrow, hrow[:, :], HTOK, HTOK.t.ap()[ti * 128:(ti + 1) * 128, :])
            ps = psr.next()
            for kc in range(8):
                MM(S, ps, ps[:, 0:NE], xt, xt[:, kc, tsl], wr, wr[:, kc, :], start=(kc == 0), stop=(kc == 7))
            lg = lgr.next()
            TT(S, "dve", lg, lg[:, :], ps, ps[:, 0:NE], brt, brt[:, :], ALU.add)
            t8 = t8r.next()
            S.op("dve", lambda e, t8=t8, lg=lg: e.max(out=t8[:, 0:8], in_=lg[:, :]), reads=[lg], writes=[t8])
            mk = mkr.next()
            TS(S, "dve", mk, mk[:, :], lg, lg[:, :], t8[:, 3:4], None, ALU.is_ge, xr=[t8])
            pp = psr.next()
            MM(S, pp, pp[:, 0:NE], k.cs, ustrict, mk, mk[:, :])
            MM(S, pp, pp[:, NE:2 * NE], k.cs, ones, mk, mk[:, :])
            return hrow, lg, t8, pp

    def rt_stage_b(ti, hrow, lg, t8, pp):
            pc = pcr.next()
            TT(S, "dve", pc, pc[:, :], pp, pp[:, 0:NE], cum, cum[:, :], ALU.add)
            TT(S, "dve", pc, pc[:, :], pc, pc[:, :], k.cs, iotaec, ALU.add)
            TT(S, "dve", cum, cum[:, :], pp, pp[:, NE:2 * NE], cum, cum[:, :], ALU.add)
            df = dfr.next()
            for kk in range(4):
                STT(S, jnk, jnk[:, :], lg, lg[:, :], t8[:, kk:kk + 1], pc, pc[:, :], ALU.is_equal, ALU.mult,
                    xr=[t8], accum=df[:, kk:kk + 1], xw=[df])
            TS(S, "dve", df, df[:, :], df, df[:, :], 0.0, float(NE * CAP - 1), ALU.max, ALU.min)
            CP(S, "dve", k.dest, k.dest[:, ti, :], df, df[:, :])
            TS(S, "dve", t8, t8[:, 8:9], t8, t8[:, 0:1], -1.0, None, ALU.mult)
            ACT(S, t8, t8[:, 10:14], t8, t8[:, 0:4], AF.Exp, bias=t8[:, 8:9], accum=t8[:, 9:10])
            S.op("dve", lambda e, t8=t8: e.reciprocal(out=t8[:, 14:15], in_=t8[:, 9:10]), reads=[t8], writes=[t8])
            TS(S, "dve", k.gates, k.gates[:, ti, :], t8, t8[:, 10:14], t8[:, 14:15], None, ALU.mult)
            for kk in range(4):
                S.dma("pool", lambda e, hrow=hrow, ti=ti, kk=kk: e.indirect_dma_start(
                    out=k.XS.t.ap(), out_offset=bass.IndirectOffsetOnAxis(ap=k.dest[:, ti, kk:kk + 1], axis=0),
                    in_=hrow[:, :], in_offset=None),
                    reads=[hrow, k.dest, k.gates], writes=[k.XS], owner=hrow, group=(kk > 0))
    cur = rt_stage_a(0)
    for ti in range(NTL):
        nxt = rt_stage_a(ti + 1) if ti < NTL - 1 else None
        rt_stage_b(ti, *cur)
        cur = nxt
    S.end_phase()
    if k.phases is not None and f"moe{l}_route_only" in k.phases:
        return
    S.begin_phase()
    ws = WStream(S, 4)
    psr = Rot([S.psum(f"ps{i}", [128, 512]) for i in range(8)])
    xrow = Rot([S.sbuf(f"xrow{i}", [128, D]) for i in range(3)])
    yrow = Rot([S.sbuf(f"yrow{i}", [128, D]) for i in range(2)])
    XT = S.sbuf("XsT", [128, 8, CAP], RT)
    AT = S.sbuf("actT", [128, 8, CAP], RT)
    gp = Rot([S.sbuf(f"gp{i}", [128, CAP // 2]) for i in range(2)])
    sg = Rot([S.sbuf(f"sg{i}", [128, CAP // 2]) for i in range(2)])
    up = Rot([S.sbuf(f"up{i}", [128, CAP // 2]) for i in range(2)])
    bdr = Rot([S.sbuf(f"bd{i}", [1, D], RT) for i in range(2)])
    o_bg, _ = PV[f"bg{l}"]
    o_bu, _ = PV[f"bu{l}"]
    HN = CAP // 2
    NRB = CAP // 128
    ones = C(k, "ones")
    xtr2 = Rot([XT, S.sbuf("XsT2", [128, 8, CAP], RT)])

    def build_xt(e_):
        XT = xtr2.next()
        for rb in range(NRB):
            xw = xrow.next()
            r0 = e_ * CAP + rb * 128
            LOAD(S, "sp", xw, xw[:, :], k.XS, k.XS.t.ap()[r0:r0 + 128, :])
            for half in range(2):
                ps = psr.next()
                for c4 in range(4):
                    c = half * 4 + c4
                    TRP(S, ps, ps[:, c4 * 128:(c4 + 1) * 128], xw, xw[:, c * 128:(c + 1) * 128], k.cs, ident)
                CP(S, "act" if half == 0 else "dve", XT, R(XT[:, half * 4:half * 4 + 4, rb * 128:(rb + 1) * 128]),
                   ps, ps[:, :].rearrange("p (c t) -> p c t", c=4))
        return XT

    def gate_up(e_, XT):
        vgu = wview(k.w_gu.t.ap()[l, e_])
        for wc in range(4):
            w = ws.get(k.w_gu, vgu, 0, wc * 512)
            for f2 in range(2):
                fc = wc * 2 + f2
                base = f2 * 256
                for hf in range(2):
                    cs_ = slice(hf * HN, (hf + 1) * HN)
                    pg = psr.next()
                    pu = psr.next()
                    for kc in range(8):
                        MM(S, pg, pg[:, 0:HN], w, R(w[:, kc, base:base + 256:2]), XT, R(XT[:, kc, cs_]),
                           start=(kc == 0), stop=(kc == 7))
                    for kc in range(8):
                        MM(S, pu, pu[:, 0:HN], w, R(w[:, kc, base + 1:base + 256:2]), XT, R(XT[:, kc, cs_]),
                           start=(kc == 0), stop=(kc == 7))
                    g_ = gp.next()
                    s_ = sg.next()
                    u_ = up.next()
                    bgc = k.pvs[:, o_bg + e_ * 8 + fc:o_bg + e_ * 8 + fc + 1]
                    buc = k.pvs[:, o_bu + e_ * 8 + fc:o_bu + e_ * 8 + fc + 1]
                    TS(S, "dve", g_, g_[:, :], pg, pg[:, 0:HN], bgc, 7.0, ALU.add, ALU.min, xr=[k.pvs])
                    ACT(S, s_, s_[:, :], g_, g_[:, :], AF.Sigmoid, scale=1.702)
                    TS(S, "dve", u_, u_[:, :], pu, pu[:, 0:HN], buc, 7.0, ALU.add, ALU.min, xr=[k.pvs])
                    TS(S, "dve", u_, u_[:, :], u_, u_[:, :], -7.0, 1.0, ALU.max, ALU.add)
                    TT(S, "dve", g_, g_[:, :], g_, g_[:, :], s_, s_[:, :], ALU.mult)
                    TT(S, "dve", AT, R(AT[:, fc, cs_]), g_, g_[:, :], u_, u_[:, :], ALU.mult)

    def down(e_):
        bd = bdr.next()
        LOAD(S, "pool" if FAST else "sp", bd, bd[:, :], k.b_dn, k.b_dn.t.ap()[l, e_:e_ + 1, :])
        vdn = wview(k.w_dn.t.ap()[l, e_])
        wd = [ws.get(k.w_dn, vdn, 0, oc * 512) for oc in range(2)]
        for rb in range(NRB):
            yw = yrow.next()
            for oc in range(2):
                ps = psr.next()
                for fc in range(8):
                    MM(S, ps, ps[:, :], AT, R(AT[:, fc, rb * 128:(rb + 1) * 128]), wd[oc], R(wd[oc][:, fc, :]),
                       start=(fc == 0), stop=False)
                MM(S, ps, ps[:, :], k.csr, k.csr[0:1, 256:384], bd, bd[0:1, oc * 512:(oc + 1) * 512],
                   start=False, stop=True)
                CP(S, "act", yw, yw[:, oc * 512:(oc + 1) * 512], ps, ps[:, :])
            r0 = e_ * CAP + rb * 128
            STORE(S, "act", k.YS, k.YS.t.ap()[r0:r0 + 128, :], yw, yw[:, :])

    xt_cur = build_xt(0)
    for e_ in range(NE):
        gate_up(e_, xt_cur)
        xt_nxt = build_xt(e_ + 1) if e_ < NE - 1 else None
        down(e_)
        xt_cur = xt_nxt
    S.end_phase()
    S.begin_phase()
    psr = Rot([S.psum(f"ps{i}", [128, 512]) for i in range(8)])
    hr = Rot([S.sbuf(f"hrow{i}", [128, D]) for i in range(2)])
    yk = Rot([S.sbuf(f"yk{i}", [128, D]) for i in range(8)])
    acc = Rot([S.sbuf(f"acc{i}", [128, D]) for i in range(2)])
    ofm = Rot([S.sbuf(f"ofm{i}", [128, 8, 128]) for i in range(2)])
    gB = S.sbuf("gB", [128, D])
    bB = S.sbuf("bB", [128, D])
    st_ = Rot([S.sbuf(f"st{i}", [128, 16]) for i in range(2)])
    eps_t = S.sbuf("eps_t", [128, 1])
    S.op("dve", lambda e: e.memset(eps_t[:, :], EPS), writes=[eps_t])
    LOAD(S, "sp", gB, gB[:, :], k.ln_ffn_g, k.ln_ffn_g.t.ap()[l:l + 1, :].broadcast_to([128, D]))
    LOAD(S, "sp", bB, bB[:, :], k.ln_ffn_b, k.ln_ffn_b.t.ap()[l:l + 1, :].broadcast_to([128, D]))
    for ti in range(NTL):
        hrow = hr.next()
        LOAD(S, "sp", hrow, hrow[:, :], HTOK, HTOK.t.ap()[ti * 128:(ti + 1) * 128, :])
        ys = []
        for kk in range(4):
            y = yk.next()
            S.dma("pool", lambda e, y=y, ti=ti, kk=kk: e.indirect_dma_start(
                out=y[:, :], out_offset=None, in_=k.YS.t.ap(),
                in_offset=bass.IndirectOffsetOnAxis(ap=k.dest[:, ti, kk:kk + 1], axis=0)),
                reads=[k.YS, k.dest], writes=[y], owner=y)
            ys.append(y)
        a = acc.next()
        TS(S, "dve", a, a[:, :], ys[0], ys[0][:, :], k.gates[:, ti, 0:1], None, ALU.mult, xr=[k.gates])
        for kk in range(1, 4):
            STT(S, a, a[:, :], ys[kk], ys[kk][:, :], k.gates[:, ti, kk:kk + 1], a, a[:, :], ALU.mult, ALU.add,
                xr=[k.gates])
        STT(S, a, a[:, :], hrow, hrow[:, :], ALPHA, a, a[:, :], ALU.mult, ALU.add)
        s_ = st_.next()
        for hf in range(2):
            S.op("dve", lambda e, s_=s_, a=a, hf=hf: e.bn_stats(out=s_[:, hf * 6:(hf + 1) * 6],
                                                               in_=a[:, hf * 512:(hf + 1) * 512]),
                 reads=[a], writes=[s_])
        S.op("dve", lambda e, s_=s_: e.bn_aggr(out=s_[:, 12:14], in_=s_[:, 0:12]), reads=[s_], writes=[s_])
        ACT(S, s_, s_[:, 14:15], s_, s_[:, 13:14], AF.Sqrt, bias=eps_t[:, 0:1], xr=[eps_t])
        S.op("dve", lambda e, s_=s_: e.reciprocal(out=s_[:, 15:16], in_=s_[:, 14:15]), reads=[s_], writes=[s_])
        TS(S, "dve", a, a[:, :], a, a[:, :], s_[:, 12:13], s_[:, 15:16], ALU.subtract, ALU.mult, xr=[s_])
        TT(S, "dve", a, a[:, :], a, a[:, :], gB, gB[:, :], ALU.mult)
        TT(S, "dve", a, a[:, :], a, a[:, :], bB, bB[:, :], ALU.add)
        if OUT_TOK is not None:
            STORE(S, "act", OUT_TOK, OUT_TOK.t.ap()[ti * 128:(ti + 1) * 128, :], a, a[:, :])
        if HOUT_FM is not None:
            STORE(S, "act", k.HCTOK, k.HCTOK.t.ap()[ti * 128:(ti + 1) * 128, :], a, a[:, :])
            o = ofm.next()
            for half in range(2):
                ps = psr.next()
                for c4 in range(4):
                    c = half * 4 + c4
                    TRP(S, ps, ps[:, c4 * 128:(c4 + 1) * 128], a, a[:, c * 128:(c + 1) * 128], k.cs, ident)
                CP(S, "act", o, o[:, half * 4:half * 4 + 4, :], ps, ps[:, :].rearrange("p (c t) -> p c t", c=4))
            STORE(S, "act", HOUT_FM, fmview(HOUT_FM)[:, :, ti * 128:(ti + 1) * 128], o, o[:, :, :])
    S.end_phase()


def phase_ssd_a(k, HIN):
    S = k.S
    S.begin_phase()
    N = 512
    ws = WStream(S, 3)
    psr = Rot([S.psum(f"ps{i}", [128, 512]) for i in range(8)])
    xr = Rot([S.sbuf(f"xtr{i}", [128, 8, N], RT) for i in range(2)])
    zst = Rot([S.sbuf(f"zst{i}", [128, N]) for i in range(4)])
    ucr = Rot([S.sbuf(f"uc{i}", [128, 3 + N]) for i in range(4)])
    acr = Rot([S.sbuf(f"ac{i}", [128, N]) for i in range(4)])
    xst = Rot([S.sbuf(f"xst{i}", [128, N]) for i in range(4)])
    halo = S.sbuf("halo", [128, 32, 4])
    S.op("dve", lambda e: e.memset(halo[:, :, :], 0.0), writes=[halo])
    vin = wview(k.ssm_w_in.t.ap())
    hin = fmview(HIN)
    o_cw, _ = PV["conv_w"]
    zsd = k.ZSD.t.ap()
    xbd = k.XBCD.t.ap()
    for tb in range(T // N):
        xtr = xr.next()
        LOAD(S, "pool" if FAST else "sp", xtr, xtr[:, :, :], HIN, hin[:, :, tb * N:(tb + 1) * N])

        def ev_zs(oc, ps, pa, tb=tb):
            z_ = zst.next()
            ACT(S, z_, z_[:, :], ps, pa, AF.Silu)
            STORE(S, "act", k.ZSD, zsd[oc * 128:(oc + 1) * 128, tb * N:(tb + 1) * N], z_, z_[:, :])
        proj_fm(k, ws, psr, k.ssm_w_in, vin, xtr, lambda kc, xtr=xtr: xtr[:, kc, :], N, 16, ev_zs, col0=0)

        pend = []

        def flush():
            while pend:
                pend.pop(0)()

        def ev_u(c, ps, pa, tb=tb):
            uc = ucr.next()
            CP(S, "dve", uc, uc[:, 0:3], halo, halo[:, c, 0:3])
            CP(S, "act", uc, uc[:, 3:3 + N], ps, pa)
            CP(S, "dve", halo, halo[:, c, 0:3], uc, uc[:, N:N + 3])
            acc = acr.next()
            wcol = lambda kk: k.pvs[:, o_cw + c * 4 + kk:o_cw + c * 4 + kk + 1]
            ACT(S, acc, acc[:, :], uc, uc[:, 0:N], AF.Identity, bias=P(k, "conv_b", c), scale=wcol(0), xr=[k.pvs])
            flush()
            for kk in range(1, 4):
                STT(S, acc, acc[:, :], uc, uc[:, kk:kk + N], wcol(kk), acc, acc[:, :], ALU.mult, ALU.add, xr=[k.pvs])

            def fin(c=c, acc=acc, tb=tb):
                x_ = xst.next()
                ACT(S, x_, x_[:, :], acc, acc[:, :], AF.Silu)
                STORE(S, "act", k.XBCD, xbd[c * 128:(c + 1) * 128, tb * N:(tb + 1) * N], x_, x_[:, :])
            pend.append(fin)
        proj_fm(k, ws, psr, k.ssm_w_in, vin, xtr, lambda kc, xtr=xtr: xtr[:, kc, :], N, 32, ev_u, col0=2048)
        flush()
    S.end_phase()


def phase_ssd_b(k, HIN, HOUT):
    S = k.S
    S.begin_phase()
    common_small(k)
    N = 128
    ws = WStream(S, 2)
    psr = Rot([S.psum(f"ps{i}", [128, 512]) for i in range(4)])
    SPS = S.psum("SPS", [128, 2048])
    xr = Rot([S.sbuf(f"xt{i}", [128, 8, N]) for i in range(2)])
    wdt = S.sbuf("wdt", [128, 8, 32])
    rowb = S.sbuf("rowb", [128, 64])
    Abc = S.sbuf("Abc", [128, 32])
    one_t = S.sbuf("one_t", [128, 1])
    zsr = Rot([S.sbuf(f"ZS{i}", [128, 16, N]) for i in range(2)])
    xbr = Rot([S.sbuf(f"XBC{i}", [128, 32, N]) for i in range(2)])
    XD = S.sbuf("XD", [128, 2048])
    XDD = S.sbuf("XDD", [128, 2048])
    BTK = S.sbuf("BTK", [128, 8, 128])
    tk = Rot([S.sbuf(f"tk{i}", [128, 8, 32]) for i in range(2)])
    cdr = Rot([S.sbuf(f"cdr{i}", [128, 64]) for i in range(2)])
    arr = Rot([S.sbuf(f"ar{i}", [128, 4, 128], RT) for i in range(2)])
    cbr = Rot([S.sbuf(f"cbm{i}", [128, 128]) for i in range(2)])
    mtr = Rot([S.sbuf(f"mt{i}", [128, 4, 128]) for i in range(2)])
    cer = Rot([S.sbuf(f"ce{i}", [128, 4, 128]) for i in range(2)])
    H = [S.sbuf(f"H{i}", [128, 2048]) for i in range(2)]
    Yt = S.sbuf("Yt", [128, 16, N])
    SQ = S.sbuf("SQ", [128, 16, N])
    rst = S.sbuf("rst", [128, 8, N])
    YR = S.sbuf("YR", [128, 16, N], RT)
    xtk = S.sbuf("xtk", [128, D])
    ZT = S.sbuf("ZT", [128, D])
    gB = S.sbuf("gBm", [128, D])
    bB = S.sbuf("bBm", [128, D])
    st_ = Rot([S.sbuf(f"stm{i}", [128, 16]) for i in range(2)])
    LOAD(S, "sp", gB, gB[:, :], k.ln_mix_g, k.ln_mix_g.t.ap()[1:2, :].broadcast_to([128, D]))
    LOAD(S, "sp", bB, bB[:, :], k.ln_mix_b, k.ln_mix_b.t.ap()[1:2, :].broadcast_to([128, D]))
    ident = C(k, "ident")
    triinc, blockones, strictlt, mask01 = C(k, "triinc"), C(k, "blockones"), C(k, "strictlt"), C(k, "mask01")
    csel = [C(k, "csel0"), C(k, "csel1")]
    ones, ones256 = C(k, "ones"), C(k, "ones256")
    vin = wview(k.ssm_w_in.t.ap())
    vout = wview(k.ssm_w_out.t.ap())
    hin = fmview(HIN)
    hout = fmview(HOUT)
    o_cw, _ = PV["conv_w"]
    LOAD(S, "sp", wdt, wdt[:, :, :], k.ssm_w_in, vin[:, :, 6144:6176])
    LOAD(S, "sp", rowb, rowb[:, :], k.rowp, k.rowp.t.ap()[0:1, :].broadcast_to([128, 64]))
    S.op("dve", lambda e: e.memset(one_t[:, :], 1.0), writes=[one_t])
    S.op("dve", lambda e: e.memset(H[0][:, :], 0.0), writes=[H[0]])
    ACT(S, Abc, Abc[:, :], rowb, rowb[:, 32:64], AF.Exp)
    TS(S, "dve", Abc, Abc[:, :], Abc, Abc[:, :], -1.0, None, ALU.mult)
    def tile_gen(tb):
        xt = xr.next()
        LOAD(S, "sp", xt, xt[:, :, :], HIN, hin[:, :, tb * N:(tb + 1) * N])
        ZS = zsr.next()
        XBC = xbr.next()
        for q4 in range(2):
            LOAD(S, "sp", ZS, ZS[:, q4 * 8:(q4 + 1) * 8, :], k.ZSD,
                 fmview(k.ZSD)[:, q4 * 8:(q4 + 1) * 8, tb * N:(tb + 1) * N], group=(q4 > 0))
        for q4 in range(4):
            LOAD(S, "sp", XBC, XBC[:, q4 * 8:(q4 + 1) * 8, :], k.XBCD,
                 fmview(k.XBCD)[:, q4 * 8:(q4 + 1) * 8, tb * N:(tb + 1) * N], group=(q4 > 0))
        t_ = tk.next()
        dt, a_, acs, dte, w1, tmp = (t_[:, i, :] for i in range(6))
        ps = psr.next()
        for kc in range(8):
            MM(S, ps, ps[:, 0:32], xt, xt[:, kc, :], wdt, wdt[:, kc, :], start=(kc == 0), stop=(kc == 7))
        TT(S, "dve", t_, tmp, ps, ps[:, 0:32], rowb, rowb[:, 0:32], ALU.add)
        ACT(S, t_, tmp, t_, tmp, AF.Exp)
        ACT(S, t_, dt, t_, tmp, AF.Ln, bias=one_t[:, 0:1], xr=[one_t])
        TT(S, "dve", t_, a_, t_, dt, Abc, Abc[:, :], ALU.mult)
        pa = psr.next()
        MM(S, pa, pa[:, 0:32], k.cs, triinc, t_, a_)
        MM(S, pa, pa[:, 32:64], k.cs, blockones, t_, a_)
        MM(S, pa, pa[:, 64:96], k.cs, csel[0], t_, a_)
        MM(S, pa, pa[:, 96:128], k.cs, csel[1], t_, a_)
        CP(S, "act", t_, acs, pa, pa[:, 0:32])
        TT(S, "dve", t_, dte, pa, pa[:, 32:64], t_, acs, ALU.subtract)
        ACT(S, t_, dte, t_, dte, AF.Exp)
        TT(S, "dve", t_, w1, t_, dt, t_, dte, ALU.mult)
        cd = cdr.next()
        ACT(S, cd, cd[:, :], pa, pa[:, 64:128], AF.Exp)
        for q in range(4):
            ps = psr.next()
            for j in range(4):
                TRP(S, ps, ps[:, j * 128:(j + 1) * 128], XBC, XBC[:, q * 4 + j, :], k.cs, ident)
            pv3 = ps[:, :].rearrange("p (h d) -> p h d", h=8)
            for dst, sc in ((XD, dt), (XDD, w1)):
                TT(S, "dve", dst, dst[:, q * 512:(q + 1) * 512].rearrange("p (h d) -> p h d", h=8), ps, pv3,
                   t_, sc[:, q * 8:(q + 1) * 8].unsqueeze(2).broadcast_to([128, 8, 64]), ALU.mult)
        for q in range(2):
            ps = psr.next()
            for j in range(4):
                TRP(S, ps, ps[:, j * 128:(j + 1) * 128], XBC, XBC[:, 16 + q * 4 + j, :], k.cs, ident)
            CP(S, "act", BTK, BTK[:, q * 4:(q + 1) * 4, :], ps, ps[:, :].rearrange("p (g n) -> p g n", g=4))
        H0, H1 = H[0], H[1]

        def states(cp):
            for g in range(8):
                MM(S, SPS, SPS[:, g * 256:(g + 1) * 256], BTK, BTK[cp * 64:(cp + 1) * 64, g, :],
                   XDD, XDD[cp * 64:(cp + 1) * 64, g * 256:(g + 1) * 256])

        def hupd(dst, src, cp):
            TT(S, "dve", dst, dst[:, :].rearrange("p (h d) -> p h d", h=32), src,
               src[:, :].rearrange("p (h d) -> p h d", h=32), cd,
               cd[:, cp * 32:(cp + 1) * 32].unsqueeze(2).broadcast_to([128, 32, 64]), ALU.mult)
            TT(S, "dve", dst, dst[:, :], SPS, SPS[:, :], dst, dst[:, :], ALU.add)
        states(0)
        hupd(H1, H0, 0)
        Hc = [H0, H1]
        yield
        def stage1(g):
            ar = arr.next()
            TT(S, "dve", ar, ar[:, :, :], k.cs, triinc.unsqueeze(1).broadcast_to([128, 4, 128]), t_,
               a_[:, 4 * g:4 * g + 4].unsqueeze(2).broadcast_to([128, 4, 128]), ALU.mult)
            ar2 = ar[:, :, :].rearrange("p h i -> p (h i)")
            pcb = psr.next()
            MM(S, pcb, pcb[:, 0:128], XBC, XBC[:, 16 + g, :], XBC, XBC[:, 24 + g, :])
            pseg = psr.next()
            MM(S, pseg, pseg[:, :], k.csr, k.csr[:, 128:256], ar, ar2)
            pacs = psr.next()
            MM(S, pacs, pacs[:, :], k.csr, k.csr[:, 256:384], ar, ar2)
            cbm = cbr.next()
            TT(S, "dve", cbm, cbm[:, :], pcb, pcb[:, 0:128], k.cs, mask01, ALU.mult)
            mt = mtr.next()
            ACT(S, mt, mt[:, :, :], pseg, pseg[:, :].rearrange("p (h i) -> p h i", h=4), AF.Exp)
            ce = cer.next()
            ACT(S, ce, ce[:, :, :], pacs, pacs[:, :].rearrange("p (h i) -> p h i", h=4), AF.Exp)
            TT(S, "dve", mt, mt[:, :, :], mt, mt[:, :, :], cbm, cbm[:, :].unsqueeze(1).broadcast_to([128, 4, 128]),
               ALU.mult)
            TT(S, "dve", ce, ce[:, :, :], ce, ce[:, :, :], XBC,
               XBC[:, 24 + g, :].unsqueeze(1).broadcast_to([128, 4, 128]), ALU.mult)
            return mt, ce

        def stage2(g, mt, ce):
            py = psr.next()
            for r in range(4):
                hd = 4 * g + r
                prt = slice((r % 2) * 64, (r % 2) * 64 + 64)
                c0 = (r // 2) * 128
                MM(S, py, py[prt, c0:c0 + 128], XD, XD[:, hd * 64:(hd + 1) * 64], mt, mt[:, r, :],
                   start=True, stop=False)
                for cp in range(2):
                    MM(S, py, py[prt, c0 + cp * 64:c0 + (cp + 1) * 64], Hc[cp], Hc[cp][:, hd * 64:(hd + 1) * 64],
                       ce, ce[:, r, cp * 64:(cp + 1) * 64], start=False, stop=(cp == 1))
            for j in range(2):
                c = 2 * g + j
                STT(S, Yt, Yt[:, c, :], XBC, XBC[:, c, :], P(k, "dskip", c), py, py[:, j * 128:(j + 1) * 128],
                    ALU.mult, ALU.add, xr=[k.pvs])
        cur = stage1(0)
        for g in range(8):
            nxt = stage1(g + 1) if g < 7 else None
            stage2(g, *cur)
            cur = nxt
        states(1)
        hupd(H0, H1, 1)
        yield
        TT(S, "dve", Yt, Yt[:, :, :], Yt, Yt[:, :, :], ZS, ZS[:, :, :], ALU.mult)
        ACT(S, SQ, SQ[:, :, :], Yt, Yt[:, :, :], AF.Square)
        for b2 in range(2):
            pm = psr.next()
            for g4 in range(4):
                g = b2 * 4 + g4
                for j in range(2):
                    MM(S, pm, pm[:, g4 * 128:(g4 + 1) * 128], k.cs, ones256, SQ, SQ[:, 2 * g + j, :],
                       start=(j == 0), stop=(j == 1))
            ACT(S, rst, rst[:, b2 * 4:(b2 + 1) * 4, :], pm, pm[:, :].rearrange("p (g t) -> p g t", g=4), AF.Ln,
                bias=k.eps_t[:, 0:1], xr=[k.eps_t])
        ACT(S, rst, rst[:, :, :], rst, rst[:, :, :], AF.Exp, scale=-0.5)
        Yt4 = Yt[:, :, :].rearrange("p (g j) t -> p g j t", j=2)
        TT(S, "dve", Yt, Yt4, Yt, Yt4, rst, rst[:, :, :].unsqueeze(2).broadcast_to([128, 8, 2, N]), ALU.mult)
        TT(S, "dve", YR, YR[:, :, :], Yt, Yt[:, :, :], k.pvs,
           P(k, "norm_g").unsqueeze(2).broadcast_to([128, 16, N]), ALU.mult)

        LOAD(S, "sp", xtk, xtk[:, :], k.HCTOK, k.HCTOK.t.ap()[tb * N:(tb + 1) * N, :])
        for wc in range(2):
            ps = psr.next()
            for kh in range(2):
                w = ws.get(k.ssm_w_out, vout, kh * 8, wc * 512)
                for kc in range(8):
                    MM(S, ps, ps[:, :], YR, YR[:, kh * 8 + kc, :], w, w[:, kc, :],
                       start=(kh == 0 and kc == 0), stop=(kh == 1 and kc == 7))
            STT(S, ZT, ZT[:, wc * 512:(wc + 1) * 512], xtk, xtk[:, wc * 512:(wc + 1) * 512], ALPHA, ps, ps[:, :],
                ALU.mult, ALU.add)
        s_ = st_.next()
        for hf in range(2):
            S.op("dve", lambda e, s_=s_, hf=hf: e.bn_stats(out=s_[:, hf * 6:(hf + 1) * 6],
                                                          in_=ZT[:, hf * 512:(hf + 1) * 512]),
                 reads=[ZT], writes=[s_])
        S.op("dve", lambda e, s_=s_: e.bn_aggr(out=s_[:, 12:14], in_=s_[:, 0:12]), reads=[s_], writes=[s_])
        ACT(S, s_, s_[:, 14:15], s_, s_[:, 13:14], AF.Sqrt, bias=k.eps_t[:, 0:1], xr=[k.eps_t])
        S.op("dve", lambda e, s_=s_: e.reciprocal(out=s_[:, 15:16], in_=s_[:, 14:15]), reads=[s_], writes=[s_])
        TS(S, "dve", ZT, ZT[:, :], ZT, ZT[:, :], s_[:, 12:13], s_[:, 15:16], ALU.subtract, ALU.mult, xr=[s_])
        TT(S, "dve", ZT, ZT[:, :], ZT, ZT[:, :], gB, gB[:, :], ALU.mult)
        TT(S, "dve", ZT, ZT[:, :], ZT, ZT[:, :], bB, bB[:, :], ALU.add)
        for half in range(2):
            ps = psr.next()
            for c4 in range(4):
                c = half * 4 + c4
                TRP(S, ps, ps[:, c4 * 128:(c4 + 1) * 128], ZT, ZT[:, c * 128:(c + 1) * 128], k.cs, ident)
            CP(S, "act", SQ, SQ[:, half * 4:half * 4 + 4, :], ps, ps[:, :].rearrange("p (c t) -> p c t", c=4))
        STORE(S, "act", HOUT, hout[:, :, tb * N:(tb + 1) * N], SQ, SQ[:, 0:8, :])
    gens = [tile_gen(tb) for tb in range(T // N)]
    next(gens[0])
    for tb in range(T // N):
        next(gens[tb])
        if tb + 1 < T // N:
            next(gens[tb + 1])
        next(gens[tb], None)
    S.end_phase()


_NC_CACHE = {}


def make_in_maps(inp):
    cst = make_consts()
    pv = make_pv(inp)
    rowp = np.concatenate([inp["ssm_dt_bias"][0], inp["ssm_a_log"][0]])[None, :].astype(np.float32)
    shared = {
        "cst": cst, "pv": pv, "rowp": rowp,
        "pool_w_in": inp["pool_w_in"][0], "pool_w_grp": inp["pool_w_grp"][0], "pool_w_out": inp["pool_w_out"][0],
        "ssm_w_in": inp["ssm_w_in"][0], "ssm_w_out": inp["ssm_w_out"][0],
        "xa_wq": inp["xa_wq"], "xa_wk": inp["xa_wk"], "xa_wv": inp["xa_wv"], "xa_wo": inp["xa_wo"],
        "moe_w_router": inp["moe_w_router"], "moe_b_router": inp["moe_b_router"],
        "moe_w_gate_up": inp["moe_w_gate_up"], "moe_w_down": inp["moe_w_down"], "moe_b_down": inp["moe_b_down"],
        "ln_ffn_g": inp["ln_ffn_g"], "ln_ffn_b": inp["ln_ffn_b"],
        "ln_mix_g": inp["ln_mix_g"], "ln_mix_b": inp["ln_mix_b"],
    }
    shared = {kk: np.ascontiguousarray(np.asarray(vv, dtype=np.float32)) for kk, vv in shared.items()}
    maps = []
    for b in range(8):
        m = dict(shared)
        m["xT"] = np.ascontiguousarray(np.asarray(inp["x"][b]).T)
        m["memT"] = np.ascontiguousarray(np.asarray(inp["mem"][b]).T)
        maps.append(m)
    return maps


def kernel(**inputs):
    inp = {kk: np.asarray(vv) for kk, vv in inputs.items()}
    if "nc" not in _NC_CACHE:
        _NC_CACHE["nc"] = build()
    nc = _NC_CACHE["nc"]
    maps = make_in_maps(inp)
    res = run_bass_kernel_spmd(nc, maps, core_ids=list(range(8)))
    return np.stack([res.results[b]["out"] for b in range(8)], axis=0).astype(np.float32)
```
